# Optimizing a Trainium2 kernel written in Bass

```python
import functools
import math
import jax
import jax.numpy as jnp
from jax import lax
import numpy as np

D_MODEL = 1024
BATCH = 1
SEQ = 16384
DEPTH = 1
DEC_BATCH = 16
DEC_SEQ = 16
PAST_LEN = 1024

CHUNK = 64
Q_BLOCK = 128
EPS = 1e-6
H_A = 16
HD_A = 64
W_A = H_A * HD_A
D_INNER = 2 * D_MODEL
SSD_HEADDIM = 64
SSD_HEADS = D_INNER // SSD_HEADDIM
SSD_STATE = 128
SSD_GROUPS = 4
SSD_CHUNK = CHUNK
CONV_W = 4
CONV_DIM = D_INNER + 2 * SSD_GROUPS * SSD_STATE
D_FF = 2816
N_MOD = 9
IN_SPLITS = (W_A, W_A, W_A, H_A, D_INNER, CONV_DIM, SSD_HEADS, D_MODEL, D_MODEL)
D_IN_PROJ = 3 * W_A + H_A + D_INNER + CONV_DIM + SSD_HEADS + 2 * D_MODEL

kernel_name = 'fox_ssd_macaron_stream_step'


def rmsnorm(x, g):
    xf = x.astype(jnp.float32)
    y = xf * lax.rsqrt(jnp.mean(xf * xf, axis=-1, keepdims=True) + EPS)
    return (y * g.astype(jnp.float32)).astype(x.dtype)


def modulate(h, shift, scale):
    return h * (1 + scale[:, None, :]) + shift[:, None, :]


def swiglu(h, w1, w3, w2):
    return (jax.nn.silu(h @ w1) * (h @ w3)) @ w2


def split_cols(x, sizes):
    out, start = [], 0
    for s in sizes:
        out.append(x[..., start:start + s])
        start += s
    return out


def fox_attend(q, k, v, f_q, f_k, q_pos, k_pos):
    s = jnp.einsum('bqhd,bkhd->bhqk', q, k).astype(jnp.float32) * (HD_A ** -0.5)
    s = s + jnp.swapaxes(f_q, 1, 2)[:, :, :, None] - jnp.swapaxes(f_k, 1, 2)[:, :, None, :]
    s = jnp.where(k_pos[None, :] <= q_pos[:, None], s, -jnp.inf)
    p = jax.nn.softmax(s, axis=-1)
    return jnp.einsum('bhqk,bkhd->bqhd', p.astype(v.dtype), v)


def fox_prompt(q, k, v, logf):
    bsz, seq = q.shape[:2]
    f_cum = jnp.cumsum(logf, axis=1)
    k_pos = jnp.arange(seq)

    def block(i):
        start = i * Q_BLOCK
        qb = lax.dynamic_slice_in_dim(q, start, Q_BLOCK, axis=1)
        fb = lax.dynamic_slice_in_dim(f_cum, start, Q_BLOCK, axis=1)
        return fox_attend(qb, k, v, fb, f_cum, start + jnp.arange(Q_BLOCK), k_pos)

    out = lax.map(block, jnp.arange(seq // Q_BLOCK))
    return jnp.moveaxis(out, 0, 1).reshape(bsz, seq, H_A, HD_A)


def fox_sample(q, k, v, logf, ck, cv, clogf):
    past, n = ck.shape[1], q.shape[1]
    f_past = jnp.cumsum(clogf.astype(jnp.float32), axis=1)
    f_new = f_past[:, -1:] + jnp.cumsum(logf, axis=1)
    kk = jnp.concatenate([ck.astype(k.dtype), k], axis=1)
    vv = jnp.concatenate([cv.astype(v.dtype), v], axis=1)
    f_all = jnp.concatenate([f_past, f_new], axis=1)
    return fox_attend(q, kk, vv, f_new, f_all, past + jnp.arange(n), jnp.arange(past + n))


def causal_conv(xbc, conv_state, w, b):
    seq = xbc.shape[1]
    full = jnp.concatenate([conv_state.astype(xbc.dtype), xbc], axis=1)
    out = b + full[:, 0:seq] * w[0]
    for i in range(1, CONV_W):
        out = out + full[:, i:i + seq] * w[i]
    return jax.nn.silu(out), full[:, -(CONV_W - 1):]


def ssd_scan(x, dt, a, b_in, c_in, h0):
    bsz, seq = x.shape[:2]
    q = min(SSD_CHUNK, seq)
    nc = seq // q
    hg = SSD_HEADS // SSD_GROUPS
    f32 = jnp.float32
    x = x.astype(f32).reshape(bsz, nc, q, SSD_GROUPS, hg, SSD_HEADDIM)
    dt = dt.astype(f32).reshape(bsz, nc, q, SSD_GROUPS, hg)
    b_in = b_in.astype(f32).reshape(bsz, nc, q, SSD_GROUPS, SSD_STATE)
    c_in = c_in.astype(f32).reshape(bsz, nc, q, SSD_GROUPS, SSD_STATE)
    acs = jnp.moveaxis(jnp.cumsum(dt * a.reshape(SSD_GROUPS, hg), axis=2), 2, -1)
    dt_t = jnp.moveaxis(dt, 2, -1)
    causal = jnp.tril(jnp.ones((q, q), dtype=bool))
    seg = acs[..., :, None] - acs[..., None, :]
    decay = jnp.exp(jnp.where(causal, seg, -jnp.inf))
    cb = jnp.einsum('bctgn,bcsgn->bcgts', c_in, b_in)
    w = cb[:, :, :, None] * decay * dt_t[..., None, :]
    y_diag = jnp.einsum('bcghts,bcsghp->bctghp', w, x)
    decay_end = jnp.exp(acs[..., -1:] - acs) * dt_t
    states = jnp.einsum('bcsgn,bcghs,bcsghp->bcghpn', b_in, decay_end, x)
    chunk_decay = jnp.exp(acs[..., -1])
    h_init = h0.astype(f32).reshape(bsz, SSD_GROUPS, hg, SSD_HEADDIM, SSD_STATE)

    def step(h, inp):
        s_c, d_c = inp
        return d_c[..., None, None] * h + s_c, h

    h_last, h_in = lax.scan(step, h_init, (jnp.moveaxis(states, 1, 0), jnp.moveaxis(chunk_decay, 1, 0)))
    h_in = jnp.moveaxis(h_in, 0, 1)
    y_off = jnp.einsum('bctgn,bcghpn,bcght->bctghp', c_in, h_in, jnp.exp(acs))
    y = (y_diag + y_off).reshape(bsz, seq, SSD_HEADS, SSD_HEADDIM)
    return y, h_last.reshape(bsz, SSD_HEADS, SSD_HEADDIM, SSD_STATE).astype(h0.dtype)


def ssd_branch(z, xbc, dt_raw, conv_state, ssm_state, prm):
    bsz, seq = z.shape[:2]
    f32 = jnp.float32
    xbc, new_conv = causal_conv(xbc, conv_state, prm['conv_w'], prm['conv_b'])
    xs, b_in, c_in = split_cols(xbc, (D_INNER, SSD_GROUPS * SSD_STATE, SSD_GROUPS * SSD_STATE))
    xs = xs.reshape(bsz, seq, SSD_HEADS, SSD_HEADDIM)
    dt = jax.nn.softplus(dt_raw.astype(f32) + prm['dt_bias'].astype(f32))
    a = -jnp.exp(prm['a_log'].astype(f32))
    y, new_ssm = ssd_scan(xs, dt, a,
                          b_in.reshape(bsz, seq, SSD_GROUPS, SSD_STATE),
                          c_in.reshape(bsz, seq, SSD_GROUPS, SSD_STATE), ssm_state)
    y = y + prm['d_skip'].astype(f32)[:, None] * xs.astype(f32)
    y = y.reshape(bsz, seq, D_INNER).astype(z.dtype)
    y = rmsnorm(y * jax.nn.silu(z), prm['g_ssd'])
    return y @ prm['w_s'], new_conv, new_ssm


def token_mixer(u, prm, fox_fn, conv_state, ssm_state):
    bsz, seq = u.shape[:2]
    q, k, v, f_logit, z, xbc, dt_raw, gate_a, gate_s = split_cols(u @ prm['w_in'], IN_SPLITS)
    q = q.reshape(bsz, seq, H_A, HD_A)
    k = k.reshape(bsz, seq, H_A, HD_A)
    v = v.reshape(bsz, seq, H_A, HD_A)
    logf = jax.nn.log_sigmoid(f_logit.astype(jnp.float32) + prm['b_f'].astype(jnp.float32))
    o_a = fox_fn(q, k, v, logf).reshape(bsz, seq, W_A) @ prm['w_a']
    o_s, new_conv, new_ssm = ssd_branch(z, xbc, dt_raw, conv_state, ssm_state, prm)
    merged = jax.nn.sigmoid(gate_a) * o_a + jax.nn.sigmoid(gate_s) * o_s
    return merged @ prm['w_out'], (k, v, logf, new_ssm, new_conv)


def trunk_layer(x, c, prm, fox_fn, conv_state, ssm_state):
    mod = jax.nn.silu(c) @ prm['w_ada'] + prm['b_ada']
    sh1, sc1, ga1, sh2, sc2, ga2, sh3, sc3, ga3 = jnp.split(mod, N_MOD, axis=-1)
    h = modulate(rmsnorm(x, prm['g_ffn1']), sh1, sc1)
    x = x + 0.5 * (1 + ga1)[:, None, :] * swiglu(h, prm['w1_ffn1'], prm['w3_ffn1'], prm['w2_ffn1'])
    u = modulate(rmsnorm(x, prm['g_mix']), sh2, sc2)
    m, new_state = token_mixer(u, prm, fox_fn, conv_state, ssm_state)
    x = x + (1 + ga2)[:, None, :] * m
    h = modulate(rmsnorm(x, prm['g_ffn2']), sh3, sc3)
    x = x + 0.5 * (1 + ga3)[:, None, :] * swiglu(h, prm['w1_ffn2'], prm['w3_ffn2'], prm['w2_ffn2'])
    return x, new_state


def setup_inputs(seed: int = 0) -> dict:
    key = jax.random.key(seed)
    ks = iter(jax.random.split(key, 48))
    L = DEPTH
    f32 = jnp.float32

    def nrm(shape, scale):
        return scale * jax.random.normal(next(ks), shape, f32)

    def unif(shape, lo, hi):
        return jax.random.uniform(next(ks), shape, f32, minval=lo, maxval=hi)

    dt0 = jnp.exp(unif((L, SSD_HEADS), math.log(1e-3), math.log(1e-1)))
    inputs = {
        'x_prompt': nrm((BATCH, SEQ, D_MODEL), 1.0),
        'x_sample': nrm((DEC_BATCH, DEC_SEQ, D_MODEL), 1.0),
        'c_prompt': nrm((BATCH, D_MODEL), 1.0),
        'c_sample': nrm((DEC_BATCH, D_MODEL), 1.0),
        'cache_k': nrm((L, DEC_BATCH, PAST_LEN, H_A, HD_A), 1.0),
        'cache_v': nrm((L, DEC_BATCH, PAST_LEN, H_A, HD_A), 1.0),
        'cache_logf': jax.nn.log_sigmoid(unif((L, DEC_BATCH, PAST_LEN, H_A), 1.0, 6.0)
                                         + nrm((L, DEC_BATCH, PAST_LEN, H_A), 1.0)),
        'state_ssm': nrm((L, DEC_BATCH, SSD_HEADS, SSD_HEADDIM, SSD_STATE), 0.5),
        'state_conv': nrm((L, DEC_BATCH, CONV_W - 1, CONV_DIM), 1.0),
        'w_ada': nrm((L, D_MODEL, N_MOD * D_MODEL), 0.3 * D_MODEL ** -0.5),
        'b_ada': nrm((L, N_MOD * D_MODEL), 0.01),
        'g_ffn1': 1.0 + nrm((L, D_MODEL), 0.01),
        'w1_ffn1': nrm((L, D_MODEL, D_FF), D_MODEL ** -0.5),
        'w3_ffn1': nrm((L, D_MODEL, D_FF), D_MODEL ** -0.5),
        'w2_ffn1': nrm((L, D_FF, D_MODEL), D_FF ** -0.5),
        'g_mix': 1.0 + nrm((L, D_MODEL), 0.01),
        'w_in': nrm((L, D_MODEL, D_IN_PROJ), D_MODEL ** -0.5),
        'b_f': unif((L, H_A), 1.0, 6.0),
        'conv_w': nrm((L, CONV_W, CONV_DIM), CONV_W ** -0.5),
        'conv_b': nrm((L, CONV_DIM), 0.01),
        'dt_bias': dt0 + jnp.log(-jnp.expm1(-dt0)),
        'a_log': jnp.log(unif((L, SSD_HEADS), 1.0, 16.0)),
        'd_skip': 1.0 + nrm((L, SSD_HEADS), 0.01),
        'g_ssd': 1.0 + nrm((L, D_INNER), 0.01),
        'w_a': nrm((L, W_A, D_MODEL), W_A ** -0.5),
        'w_s': nrm((L, D_INNER, D_MODEL), D_INNER ** -0.5),
        'w_out': nrm((L, D_MODEL, D_MODEL), D_MODEL ** -0.5),
        'g_ffn2': 1.0 + nrm((L, D_MODEL), 0.01),
        'w1_ffn2': nrm((L, D_MODEL, D_FF), D_MODEL ** -0.5),
        'w3_ffn2': nrm((L, D_MODEL, D_FF), D_MODEL ** -0.5),
        'w2_ffn2': nrm((L, D_FF, D_MODEL), D_FF ** -0.5),
        'g_final': 1.0 + nrm((D_MODEL,), 0.01),
    }
    return inputs


def reference(x_prompt, x_sample, c_prompt, c_sample, cache_k, cache_v, cache_logf, state_ssm, state_conv,
              w_ada, b_ada, g_ffn1, w1_ffn1, w3_ffn1, w2_ffn1, g_mix, w_in, b_f, conv_w, conv_b,
              dt_bias, a_log, d_skip, g_ssd, w_a, w_s, w_out, g_ffn2, w1_ffn2, w3_ffn2, w2_ffn2, g_final):
    bp = x_prompt.shape[0]
    xp, xs = x_prompt, x_sample
    kp_l, vp_l, lp_l, sp_l, cp_l = [], [], [], [], []
    ks_l, vs_l, ls_l, ss_l, cs_l = [], [], [], [], []
    for l in range(DEPTH):
        prm = {
            'w_ada': w_ada[l], 'b_ada': b_ada[l],
            'g_ffn1': g_ffn1[l], 'w1_ffn1': w1_ffn1[l], 'w3_ffn1': w3_ffn1[l], 'w2_ffn1': w2_ffn1[l],
            'g_mix': g_mix[l], 'w_in': w_in[l], 'b_f': b_f[l],
            'conv_w': conv_w[l], 'conv_b': conv_b[l], 'dt_bias': dt_bias[l], 'a_log': a_log[l],
            'd_skip': d_skip[l], 'g_ssd': g_ssd[l], 'w_a': w_a[l], 'w_s': w_s[l], 'w_out': w_out[l],
            'g_ffn2': g_ffn2[l], 'w1_ffn2': w1_ffn2[l], 'w3_ffn2': w3_ffn2[l], 'w2_ffn2': w2_ffn2[l],
        }
        zero_conv = jnp.zeros((bp, CONV_W - 1, CONV_DIM), x_prompt.dtype)
        zero_ssm = jnp.zeros((bp, SSD_HEADS, SSD_HEADDIM, SSD_STATE), jnp.float32)
        xp, (kp, vp, lp, sp, cp) = trunk_layer(xp, c_prompt, prm, fox_prompt, zero_conv, zero_ssm)
        fox_s = functools.partial(fox_sample, ck=cache_k[l], cv=cache_v[l], clogf=cache_logf[l])
        xs, (ks_, vs_, ls_, ss_, cs_) = trunk_layer(xs, c_sample, prm, fox_s, state_conv[l], state_ssm[l])
        kp_l.append(kp); vp_l.append(vp); lp_l.append(lp); sp_l.append(sp); cp_l.append(cp)
        ks_l.append(ks_); vs_l.append(vs_); ls_l.append(ls_); ss_l.append(ss_); cs_l.append(cs_)
    y_prompt = rmsnorm(xp, g_final)
    y_sample = rmsnorm(xs, g_final)
    k_prompt = jnp.stack(kp_l)
    v_prompt = jnp.stack(vp_l)
    logf_prompt = jnp.stack(lp_l)
    ssm_prompt = jnp.stack(sp_l)
    conv_prompt = jnp.stack(cp_l)
    k_sample = jnp.stack(ks_l)
    v_sample = jnp.stack(vs_l)
    logf_sample = jnp.stack(ls_l)
    ssm_sample = jnp.stack(ss_l)
    conv_sample = jnp.stack(cs_l)
    return (y_prompt, y_sample, k_prompt, v_prompt, logf_prompt, ssm_prompt, conv_prompt,
            k_sample, v_sample, logf_sample, ssm_sample, conv_sample)
```

```python
from contextlib import ExitStack
import numpy as np
import concourse.bass as bass
import concourse.mybir as mybir
from concourse.bass_utils import run_bass_kernel_spmd

F32 = mybir.dt.float32
BF16 = mybir.dt.bfloat16
AF = mybir.ActivationFunctionType
ALU = mybir.AluOpType
AX = mybir.AxisListType

NCORES = 8
D = 1024
SEQ = 16384
T = SEQ // NCORES
TB = 512
DFF = 2816
NH = 22
H_A = 16
HD = 64
DIN = 2048
NSSD = 32
NST = 128
CONVD = 3072
DINP = 10288
C_Q, C_K, C_V, C_F, C_Z, C_X, C_DT, C_GA, C_GS = 0, 1024, 2048, 3072, 3088, 5136, 8208, 8240, 9264
EPS = 1e-6
NEG = -30000.0
import os
CTX = os.environ.get("CTX", "1") == "1"

ENG = ["pe", "act", "dve", "pool", "sp"]


class Res:
    __slots__ = ("lw", "rd", "excl")

    def __init__(self, excl=False):
        self.lw = None
        self.rd = []
        self.excl = excl


class Op:
    __slots__ = ("eng", "fn", "deps", "dma", "need", "sv", "dsem", "dval", "idx", "cc", "seng")


class Prog:
    NS = 12

    def __init__(self):
        self.ops = []
        self.dq = {"sp": [], "pool": [], "act": []}
        self.ncc = 0

    def add(self, eng, fn, r=(), w=(), dma=False, cc=False):
        op = Op()
        op.eng, op.fn, op.dma, op.need, op.idx = eng, fn, dma, False, len(self.ops)
        op.sv = 0
        op.cc = cc
        op.seng = eng
        if cc:
            op.dma = dma = True
        deps = {}
        w = list(w) + [R for R in r if R.excl]
        r = [R for R in r if not R.excl]

        def dep(p, war=False):
            if p is None:
                return
            if (not p.dma) and p.eng == eng and eng == "pe":
                return
            deps[p.idx] = p

        for R in r:
            dep(R.lw)
        for R in w:
            dep(R.lw)
            for q in R.rd:
                dep(q, True)
        if cc:
            op.seng = "cc"
            op.dsem = self.ncc
            op.dval = 1
            self.ncc += 1
        elif dma:
            lst = self.dq[eng]
            n = len(lst)
            op.dsem = n % self.NS
            op.dval = 16 * (n // self.NS + 1)
            if n >= self.NS:
                deps[lst[n - self.NS].idx] = lst[n - self.NS]
            lst.append(op)
        op.deps = list(deps.values())
        for p in op.deps:
            if not p.dma:
                p.need = True
        for R in r:
            R.rd.append(op)
        for R in w:
            R.lw = op
            R.rd = []
        self.ops.append(op)
        return op

    def emit(self, nc, sems, dsems):
        cnt = {e: 0 for e in ENG}
        for op in self.ops:
            if (not op.dma) and op.need:
                cnt[op.eng] += 1
                op.sv = cnt[op.eng]
        per = {e: [o for o in self.ops if o.eng == e] for e in ENG}

        def run(ename, e):
            waited = {}
            for op in per[ename]:
                for p in op.deps:
                    if p.dma:
                        key, val, sem = ("d", p.seng, p.dsem), p.dval, dsems[p.seng][p.dsem]
                    else:
                        key, val, sem = ("c", p.eng), p.sv, sems[p.eng]
                    if waited.get(key, 0) >= val:
                        continue
                    waited[key] = val
                    e.wait_ge(sem, val)
                ins = op.fn(e)
                if op.cc:
                    ins.then_inc(dsems["cc"][op.dsem], 1)
                elif op.dma:
                    ins.then_inc(dsems[ename][op.dsem], 16)
                elif op.need:
                    ins.then_inc(sems[ename], 1)
            if ename in self.dq:
                lst = self.dq[ename]
                last = {}
                for o in lst:
                    last[o.dsem] = o.dval
                for s, v in last.items():
                    e.wait_ge(dsems[ename][s], v)

        with nc.Block() as block:
            @block.tensor
            def _(e):
                run("pe", e)

            @block.scalar
            def _(e):
                run("act", e)

            @block.vector
            def _(e):
                run("dve", e)

            @block.gpsimd
            def _(e):
                run("pool", e)

            @block.sync
            def _(e):
                run("sp", e)


class StopBuild(Exception):
    pass


class Builder:
    def __init__(self, stage=99):
        import os
        self.cutn = float(os.environ.get("DBG_CUT", "999"))
        self.stage = stage
        self.nc = bass.Bass("TRN2", target_bir_lowering=False)
        try:
            self.nc.allow_low_precision("bf16 matmul operands by design")
        except Exception:
            pass
        self.P = Prog()
        self.es = ExitStack()
        self.ins = {}
        self.outs = {}
        self.uid = 0
        self.wrr = 0

    def finish(self):
        nc = self.nc
        sems = {e: self.es.enter_context(nc.semaphore(f"s_{e}")) for e in ENG}
        dsems = {q: [self.es.enter_context(nc.semaphore(f"d_{q}{i}")) for i in range(Prog.NS)]
                 for q in ("sp", "pool", "act")}
        dsems["cc"] = [self.es.enter_context(nc.semaphore(f"d_cc{i}")) for i in range(max(1, self.P.ncc))]
        self.P.emit(nc, sems, dsems)
        self.es.close()

    def cut(self, n):
        if self.cutn <= n:
            raise StopBuild()

    def din(self, name, shape, dt=F32):
        t = self.nc.dram_tensor(name, list(shape), dt, kind="ExternalInput").ap()
        self.ins[name] = t
        return t

    def dout(self, name, shape, dt=F32):
        t = self.nc.dram_tensor(name, list(shape), dt, kind="ExternalOutput").ap()
        self.outs[name] = t
        return t

    def dscr(self, name, shape, dt):
        return self.nc.dram_tensor(name, list(shape), dt).ap()

    def sb(self, name, shape, dt=F32):
        return self.es.enter_context(self.nc.sbuf_tensor("sb_" + name, list(shape), dt))

    def ps(self, name, shape, dt=F32):
        return self.es.enter_context(self.nc.psum_tensor("ps_" + name, list(shape), dt))

    def dma(self, out, in_, r=(), w=(), q="sp"):
        return self.P.add(q, lambda e: e.dma_start(out=out, in_=in_), r=r, w=w, dma=True)

    def act(self, out, in_, func, r=(), w=(), bias=None, scale=None):
        kw = {}
        if bias is not None:
            kw["bias"] = bias
        if scale is not None:
            kw["scale"] = scale
        return self.P.add("act", lambda e: e.activation(out=out, in_=in_, func=func, **kw), r=r, w=w)

    def tt(self, out, a, b, op, r=(), w=(), eng="dve"):
        return self.P.add(eng, lambda e: e.tensor_tensor(out=out, in0=a, in1=b, op=op), r=r, w=w)

    def ts(self, out, a, s1, s2, op0, op1=None, r=(), w=(), eng="dve"):
        if op1 is None:
            return self.P.add(eng, lambda e: e.tensor_scalar(out=out, in0=a, scalar1=s1, scalar2=None, op0=op0), r=r, w=w)
        return self.P.add(eng, lambda e: e.tensor_scalar(out=out, in0=a, scalar1=s1, scalar2=s2, op0=op0, op1=op1), r=r, w=w)

    def stt(self, out, a, s, b, op0, op1, r=(), w=(), eng="dve"):
        return self.P.add(eng, lambda e: e.scalar_tensor_tensor(out=out, in0=a, scalar=s, in1=b, op0=op0, op1=op1), r=r, w=w)

    def cp(self, out, in_, r=(), w=(), eng="dve"):
        if eng == "act":
            return self.P.add("act", lambda e: e.copy(out=out, in_=in_), r=r, w=w)
        return self.P.add(eng, lambda e: e.tensor_copy(out=out, in_=in_), r=r, w=w)

    def mms(self, out, pairs, r=(), w=()):
        n = len(pairs)

        def fn(e):
            ins = None
            for i, (l, rh) in enumerate(pairs):
                ins = e.matmul(out, l, rh, start=(i == 0), stop=(i == n - 1))
            return ins
        return self.P.add("pe", fn, r=r, w=w)

    def init_wstream(self):
        self.WSZ = 2048
        self.wst = [(self.sb(f"wst{i}", [128, self.WSZ], F32), Res()) for i in range(2)]
        self.wbf = [(self.sb(f"wbf{i}", [128, self.WSZ], BF16), Res()) for i in range(2)]
        self.wi = 0
        self.wj = 0

    def load_w(self, wd, kcn, c0, n, pn=128, k0=0):
        st, sr = self.wst[self.wi % 2]
        self.wi += 1
        bf, br = self.wbf[self.wj % 2]
        self.wj += 1
        sz = kcn * n
        assert sz <= self.WSZ
        stv = st[0:pn, 0:sz].rearrange("p (k n) -> p k n", k=kcn)
        bfv = bf[0:pn, 0:sz].rearrange("p (k n) -> p k n", k=kcn)
        self.dma(stv, wd[0:pn, k0:k0 + kcn, c0:c0 + n], w=[sr])
        ceng = "pool" if (self.wj % 2 == 0) else "act"
        self.cp(bf[0:pn, 0:sz], st[0:pn, 0:sz], r=[sr], w=[br], eng=ceng)
        return bfv, br


def build(stage=99):
    B = Builder(stage)
    nc, P = B.nc, B.P
    NBLK = T // TB
    NT = T // 128

    xT_d = B.din("xT", [D, T])
    cT_d = B.din("cT", [128, 8, 3])
    wada_d = B.din("w_ada_r", [128, 8, 9 * D])
    bada_d = B.din("b_ada_r", [128, 72])
    g1_d = B.din("g_ffn1_r", [128, 8])
    gm_d = B.din("g_mix_r", [128, 8])
    g2_d = B.din("g_ffn2_r", [128, 8])
    gf_d = B.din("g_final_r", [128, 8])
    w1a_d = B.din("w1a", [128, 8, DFF])
    w3a_d = B.din("w3a", [128, 8, DFF])
    w2a_d = B.din("w2a", [128, NH, D])
    w1b_d = B.din("w1b", [128, 8, DFF])
    w3b_d = B.din("w3b", [128, 8, DFF])
    w2b_d = B.din("w2b", [128, NH, D])
    win_d = B.din("win_r", [128, 8, DINP])
    bf_d = B.din("bf_bc", [128, 16])
    cw_d = B.din("convw_r", [128, 24, 4])
    cb_d = B.din("convb_r", [128, 24])
    dtb_d = B.din("dtb_bc", [128, 32])

    kT_o = B.dout("kT_o", [D, T])
    v_o = B.dout("v_o", [T, D])
    lf_o = B.dout("lf_o", [T, 16])
    cv_o = B.dout("cv_o", [CONVD, 3])
    xsT_d = B.din("xsT", [D, 35])
    flg_d = B.din("flags", [128, 24])
    ksT_o = B.dout("ksT_o", [D, 32])
    vs_o = B.dout("vs_o", [32, D])
    lfs_o = B.dout("lfs_o", [32, 16])
    cvs_o = B.dout("cvs_o", [CONVD, 6])
    x1_o = B.dout("x1_o", [D, T])
    scv_d = B.din("scv", [128, 24, 2, 3])
    ssmin_d = B.din("ssmin", [2, 128, DIN])
    ckT_d = B.din("ckT", [2, H_A, 64, 1024])
    cv_d = B.din("cvc", [2, 1024, D])
    clf_d = B.din("clf", [2, 1024, 16])
    ysT_o = B.dout("ysT_o", [D, 32])
    ssms_o = B.dout("ssms_o", [2, 128, DIN])
    q2_s = B.dscr("q2_s", [H_A, 66, 32], BF16)
    k2_s = B.dscr("k2_s", [H_A, 66, 32], BF16)
    v2_s = B.dscr("v2_s", [H_A, 2, 16, 65], BF16)
    z2_s = B.dscr("z2_s", [32, DIN], BF16)
    xs2_s = B.dscr("xs2_s", [32, DIN], BF16)
    Bt2_s = B.dscr("Bt2_s", [32, 512], BF16)
    BT2_s = B.dscr("BT2_s", [512, 32], BF16)
    CT2_s = B.dscr("CT2_s", [512, 32], BF16)
    ga2_s = B.dscr("ga2_s", [D, 32], BF16)
    gs2_s = B.dscr("gs2_s", [D, 32], BF16)
    R_s2 = Res()
    xTall_d = B.din("xTall", [D, SEQ])
    kgA = B.dscr("kgA", [NCORES * H_A * 66, T], BF16)
    vgA = B.dscr("vgA", [NCORES * H_A * NT * 128, 65], BF16)
    kgA_v = kgA.rearrange("(j h r) t -> j h r t", j=NCORES, h=H_A)
    vgA_v = vgA.rearrange("(j h t p) d -> j h t p d", j=NCORES, h=H_A, t=NT)
    R_kgA, R_vgA = Res(), Res()
    alog_d = B.din("alog_bc", [128, 32])
    dsk_d = B.din("dskip_bc", [128, 32])
    gssd_d = B.din("gssd_r", [128, 16])
    tri_d = B.din("tri_in", [128, 128])
    mneg_d = B.din("maskneg_in", [128, 128])
    e0_d = B.din("e0row_in", [128, 128])
    sel_d = B.din("sel_in", [16, 16, 66])
    selc_d = B.din("selc_in", [128, 2])
    wa_d = B.din("wa_r", [64, 16, D])
    ws_d = B.din("ws_r", [128, 16, D])
    wo_d = B.din("wout_r", [128, 8, D])
    yT_o = B.dout("yT_o", [D, T])
    ssm_o = B.dout("ssm_o", [128, NSSD, 64])
    dbg_y = B.dout("dbg_y", [T, DIN])
    dbg_att = B.dout("dbg_att", [D, T])

    q_s = B.dscr("q_s", [H_A, 66, T], BF16)
    kg_s = B.dscr("kg_s", [H_A, 66, T], BF16)
    vg_s = B.dscr("vg_s", [H_A, NT, 128, 65], BF16)
    z_s = B.dscr("z_s", [T, DIN], BF16)
    xs_s = B.dscr("xs_s", [T, DIN], BF16)
    Bt_s = B.dscr("Bt_s", [T, 512], BF16)
    BT_s = B.dscr("BT_s", [512, T], BF16)
    CT_s = B.dscr("CT_s", [512, T], BF16)
    ga_s = B.dscr("ga_s", [D, T], BF16)
    gs_s = B.dscr("gs_s", [D, T], BF16)
    R_q, R_kg, R_vg, R_z, R_xs, R_Bt, R_BT, R_CT, R_ga, R_gs, R_x1 = [Res() for _ in range(11)]

    ones_f = B.sb("ones_f", [128, 128], F32)
    ident_b = B.sb("ident_b", [128, 128], BF16)
    ident_f = B.sb("ident_f", [128, 128], F32)
    epsc = B.sb("epsc", [128, 1], F32)
    onec = B.sb("onec", [128, 1], F32)
    R_const = Res()
    P.add("pool", lambda e: e.memset(ones_f[:], 1.0), w=[R_const])
    P.add("pool", lambda e: e.memset(epsc[:], EPS), w=[R_const])
    P.add("pool", lambda e: e.memset(onec[:], 1.0), w=[R_const])
    identf_d = B.din("ident_in", [128, 128])
    B.dma(ident_f[:], identf_d[:, :], w=[R_const])
    B.cp(ident_b[:], ident_f[:], r=[R_const], w=[R_const], eng="pool")

    B.init_wstream()

    banks = [(B.ps(f"bank{i}", [128, 512], F32), Res(True)) for i in range(7)]
    ptT = B.ps("ptT", [128, 1024], BF16)
    prT = Res(True)

    try:
        _build_body(B, locals())
    except StopBuild:
        pass
    B.finish()
    return B


def _build_body(B, L):
    globals_ = L
    nc, P = B.nc, B.P
    NBLK = T // TB
    NT = T // 128
    for k_, v_ in L.items():
        if k_ not in ("B", "nc", "P"):
            globals()[k_] = v_
    cT = B.sb("cT", [128, 8, 3], F32)
    cs = B.sb("cs", [128, 8, 3], BF16)
    bada = B.sb("bada", [128, 72], F32)
    modT = B.sb("modT", [128, 72, 3], F32)
    gsb = B.sb("gsb", [128, 4, 8], F32)
    R_c, R_mod, R_g = Res(), Res(), Res()
    B.dma(cT[:], cT_d[:, :, :], w=[R_c])
    B.dma(bada[:], bada_d[:, :], w=[R_c])
    for i, gd in enumerate([g1_d, gm_d, g2_d, gf_d]):
        B.dma(gsb[:, i, :], gd[:, :], w=[R_g])
    B.act(cs[:], cT[:], AF.Silu, r=[R_c], w=[R_c])
    for ch in range(72):
        wt, wr = B.load_w(wada_d, 8, ch * 128, 128)
        pt, pr = banks[ch % 2]
        B.mms(pt[:, 0:3], [(wt[:, kc, :], cs[:, kc, :]) for kc in range(8)], r=[wr, R_c], w=[pr])
        B.ts(modT[:, ch, :], pt[:, 0:3], bada[:, ch:ch + 1], None, ALU.add, r=[pr, R_c], w=[R_mod])
    Am = B.sb("Am", [128, 3, 8, 3], F32)
    Gm = B.sb("Gm", [128, 3, 8, 3], F32)
    R_AG = Res()
    for s in range(3):
        coef = 1.0 if s == 1 else 0.5
        for kc in range(8):
            B.ts(Am[:, s, kc, :], modT[:, (3 * s + 1) * 8 + kc, :], 1.0, gsb[:, s, kc:kc + 1], ALU.add, ALU.mult,
                 r=[R_mod, R_g], w=[R_AG])
            B.ts(Gm[:, s, kc, :], modT[:, (3 * s + 2) * 8 + kc, :], 1.0, coef, ALU.add, ALU.mult,
                 r=[R_mod], w=[R_AG])

    B.cut(1)

    def shiftp(s, kc, m):
        return modT[:, (3 * s) * 8 + kc, m:m + 1]

    xT = B.sb("xT", [128, 8, TB], F32)
    uT = B.sb("uT", [128, 8, TB], BF16)
    gT = B.sb("gT", [128, NH, TB], BF16)
    sq = B.sb("sq", [128, TB], F32)
    rstd = B.sb("rstd", [128, TB], F32)
    tmpA = [(B.sb(f"tmpA{i}", [128, TB], F32), Res()) for i in range(5)]
    tmpB = [(B.sb(f"tmpB{i}", [128, TB], BF16), Res()) for i in range(3)]
    R_x, R_u, R_gT, R_sq, R_rstd = Res(), Res(), Res(), Res(), Res()
    tai = [0]
    tbi = [0]

    def nextA():
        tai[0] += 1
        return tmpA[tai[0] % 5]

    def nextB():
        tbi[0] += 1
        return tmpB[tbi[0] % 3]

    def rms_mod(src, s, groups, ncols):
        pt, pr = banks[6]
        for kc in range(8):
            ta, tr = nextA()
            B.tt(ta[:, 0:ncols], src[:, kc, 0:ncols], src[:, kc, 0:ncols], ALU.mult, r=[R_x], w=[tr])
            P.add("pe", lambda e, ta=ta, kc=kc: e.matmul(pt[:, 0:ncols], ones_f[:], ta[:, 0:ncols],
                                                            start=(kc == 0), stop=(kc == 7)),
                  r=[tr, R_const], w=[pr])
        B.act(sq[:, 0:ncols], pt[:, 0:ncols], AF.Sqrt, r=[pr, R_const], w=[R_sq], bias=epsc[:], scale=1.0 / D)
        P.add("dve", lambda e: e.reciprocal(out=rstd[:, 0:ncols], in_=sq[:, 0:ncols]), r=[R_sq], w=[R_rstd])
        for kc in range(8):
            ta, tr = nextA()
            B.tt(ta[:, 0:ncols], src[:, kc, 0:ncols], rstd[:, 0:ncols], ALU.mult, r=[R_x, R_rstd], w=[tr])
            for (c0, n, m) in groups:
                B.ts(uT[:, kc, c0:c0 + n], ta[:, c0:c0 + n], Am[:, s, kc, m:m + 1], shiftp(s, kc, m),
                     ALU.mult, ALU.add, r=[tr, R_AG, R_mod], w=[R_u])

    def ffn(s, w1d, w3d, w2d, groups, ncols):
        for hc in range(NH):
            w1t, w1r = B.load_w(w1d, 8, hc * 128, 128)
            w3t, w3r = B.load_w(w3d, 8, hc * 128, 128)
            p1, r1 = banks[hc % 2]
            p3, r3 = banks[2 + hc % 2]
            B.mms(p1[:, 0:ncols], [(w1t[:, kc, :], uT[:, kc, 0:ncols]) for kc in range(8)], r=[w1r, R_u], w=[r1])
            B.mms(p3[:, 0:ncols], [(w3t[:, kc, :], uT[:, kc, 0:ncols]) for kc in range(8)], r=[w3r, R_u], w=[r3])
            ta, tr = nextA()
            B.act(ta[:, 0:ncols], p1[:, 0:ncols], AF.Silu, r=[r1], w=[tr])
            B.tt(gT[:, hc, 0:ncols], ta[:, 0:ncols], p3[:, 0:ncols], ALU.mult, r=[tr, r3], w=[R_gT])
        for oc in range(8):
            w2t, w2r = B.load_w(w2d, 11, oc * 128, 128)
            w2u, w2s = B.load_w(w2d, 11, oc * 128, 128, k0=11)
            po, ro = banks[4 + oc % 2]
            B.mms(po[:, 0:ncols], [(w2t[:, hc, :], gT[:, hc, 0:ncols]) for hc in range(11)]
                  + [(w2u[:, hc, :], gT[:, 11 + hc, 0:ncols]) for hc in range(11)], r=[w2r, w2s, R_gT], w=[ro])
            for (c0, n, m) in groups:
                B.stt(xT[:, oc, c0:c0 + n], po[:, c0:c0 + n], Gm[:, s, oc, m:m + 1], xT[:, oc, c0:c0 + n],
                      ALU.mult, ALU.add, r=[ro, R_AG], w=[R_x])

    logf = B.sb("logf", [128, NT + 2, 16], F32)
    dtsb = B.sb("dtsb", [128, NT + 2, 32], F32)
    bfb = B.sb("bfb", [128, 16], F32)
    dtb = B.sb("dtb", [128, 32], F32)
    cw = B.sb("cw", [128, 24, 4], F32)
    cbv = B.sb("cbv", [128, 24], F32)
    halo = B.sb("halo", [128, 24, 3], F32)
    R_lf, R_dt, R_sm, R_halo = Res(), Res(), Res(), Res()
    B.dma(bfb[:], bf_d[:, :], w=[R_sm])
    B.dma(dtb[:], dtb_d[:, :], w=[R_sm])
    B.dma(cw[:], cw_d[:, :, :], w=[R_sm])
    B.dma(cbv[:], cb_d[:, :], w=[R_sm])
    P.add("pool", lambda e: e.memset(halo[:], 0.0), w=[R_halo])
    flg = B.sb("flg", [128, 24], F32)
    B.dma(flg[:], flg_d[:, :], w=[R_sm])

    xb = [(B.sb(f"xb{i}", [128, TB + 3], F32), Res()) for i in range(2)]
    vst = [(B.sb(f"vst{i}", [128, 4, 65], BF16), Res()) for i in range(2)]
    kst = [(B.sb("kst0", [64, TB], F32), Res())] * 2
    onesrow = B.sb("onesrow", [66, TB], BF16)
    P.add("pool", lambda e: e.memset(onesrow[:], 1.0), w=[R_const])
    for i in range(2):
        P.add("pool", lambda e, i=i: e.memset(vst[i][0][:], 1.0), w=[vst[i][1]])
    for h in range(H_A):
        for bb in range(T // TB):
            B.dma(kg_s[h, 64:66, bb * TB:(bb + 1) * TB], onesrow[64:66, :], r=[R_const], w=[R_kg], q="pool")

    def softplus_to(out, in_ps, biasbc, n, r, w, neg_in=False, pn=128):
        ta, tr = nextA()
        tb2, tr2 = nextA()
        B.tt(ta[0:pn, 0:n], in_ps, biasbc, ALU.add, r=r, w=[tr])
        if neg_in:
            B.ts(ta[0:pn, 0:n], ta[0:pn, 0:n], -1.0, None, ALU.mult, r=[tr], w=[tr])
        B.stt(tb2[0:pn, 0:n], ta[0:pn, 0:n], -1.0, ta[0:pn, 0:n], ALU.mult, ALU.max, r=[tr], w=[tr2])
        B.act(tb2[0:pn, 0:n], tb2[0:pn, 0:n], AF.Exp, r=[tr2], w=[tr2], scale=-1.0)
        B.act(tb2[0:pn, 0:n], tb2[0:pn, 0:n], AF.Ln, r=[tr2, R_const], w=[tr2], bias=onec[0:pn, :], scale=1.0)
        B.stt(out, ta[0:pn, 0:n], 0.0, tb2[0:pn, 0:n], ALU.max, ALU.add, r=[tr, tr2], w=w)

    def win_block(blk, groups, ncols, ctxj=None):
        t0 = blk * TB
        ntt = ncols // 128
        B.cut(5)
        for h in range(H_A):
            if ctxj is None:
                wt, wr = B.load_w(win_d, 8, C_Q + h * 64, 64)
                pt, pr = banks[h % 2]
                B.mms(pt[0:64, 0:ncols], [(wt[:, kc, :], uT[:, kc, 0:ncols]) for kc in range(8)], r=[wr, R_u], w=[pr])
                tb_, tbr = nextB()
                B.ts(tb_[0:64, 0:ncols], pt[0:64, 0:ncols], 0.125, None, ALU.mult, r=[pr], w=[tbr])
                B.dma(q_s[h, 0:64, t0:t0 + ncols], tb_[0:64, 0:ncols], r=[tbr], w=[R_q], q="pool")
            wt, wr = B.load_w(win_d, 8, C_K + h * 64, 64)
            pt, pr = banks[2 + h % 2]
            B.mms(pt[0:64, 0:ncols], [(wt[:, kc, :], uT[:, kc, 0:ncols]) for kc in range(8)], r=[wr, R_u], w=[pr])
            if ctxj is None:
                kf, kr = kst[h % 2]
                B.cp(kf[:, 0:ncols], pt[0:64, 0:ncols], r=[pr], w=[kr])
                B.dma(kT_o[h * 64:(h + 1) * 64, t0:t0 + ncols], kf[:, 0:ncols], r=[kr], w=[], q="pool")
            tb_, tbr = nextB()
            B.cp(tb_[0:64, 0:ncols], pt[0:64, 0:ncols], r=[pr], w=[tbr], eng="act")
            if ctxj is None:
                B.dma(kg_s[h, 0:64, t0:t0 + ncols], tb_[0:64, 0:ncols], r=[tbr], w=[R_kg], q="pool")
            else:
                B.dma(kgA_v[ctxj, h, 0:64, t0:t0 + ncols], tb_[0:64, 0:ncols], r=[tbr], w=[R_kgA], q="pool")
        B.cut(6)
        for cg in range(4):
            wt, wr = B.load_w(win_d, 8, C_V + cg * 256, 256)
            for tt_ in range(ntt):
                pt, pr = banks[4 + tt_ % 2]
                B.mms(pt[:, 0:256], [(uT[:, kc, tt_ * 128:(tt_ + 1) * 128], wt[:, kc, :]) for kc in range(8)],
                      r=[wr, R_u], w=[pr])
                if ctxj is None:
                    ta, tr = nextA()
                    B.cp(ta[:, 0:256], pt[:, 0:256], r=[pr], w=[tr])
                    B.dma(v_o[t0 + tt_ * 128:t0 + (tt_ + 1) * 128, cg * 256:(cg + 1) * 256], ta[:, 0:256], r=[tr], w=[], q="pool")
                vs, vr = vst[(cg * ntt + tt_) % 2]
                B.cp(vs[:, 0:4, 0:64], pt[:, 0:256].rearrange("p (h d) -> p h d", h=4), r=[pr], w=[vr], eng="act")
                gt = (t0 // 128) + tt_
                if ctxj is None:
                    B.dma(vg_s[cg * 4:(cg + 1) * 4, gt, :, :].rearrange("h p d -> p h d"), vs[:, 0:4, :], r=[vr], w=[R_vg], q="pool")
                else:
                    B.dma(vgA_v[ctxj, cg * 4:(cg + 1) * 4, gt, :, :].rearrange("h p d -> p h d"), vs[:, 0:4, :], r=[vr], w=[R_vgA], q="pool")
        B.cut(7)
        wt, wr = B.load_w(win_d, 8, C_F, 16)
        for tt_ in range(ntt):
            gt = (t0 // 128) + tt_
            pt, pr = banks[6]
            B.mms(pt[:, 0:16], [(uT[:, kc, tt_ * 128:(tt_ + 1) * 128], wt[:, kc, :]) for kc in range(8)], r=[wr, R_u], w=[pr])
            ta, tr = nextA()
            softplus_to(ta[:, 0:16], pt[:, 0:16], bfb[:], 16, r=[pr, R_sm], w=[tr], neg_in=True)
            B.ts(logf[:, gt, :], ta[:, 0:16], -1.0, None, ALU.mult, r=[tr], w=[R_lf])
            if ctxj is None:
                B.dma(lf_o[gt * 128:(gt + 1) * 128, :], logf[:, gt, :], r=[R_lf], w=[], q="pool")
        if B.stage < 2:
            return
        wt, wr = B.load_w(win_d, 8, C_DT, 32)
        for tt_ in range(ntt):
            gt = (t0 // 128) + tt_
            pt, pr = banks[6]
            B.mms(pt[:, 0:32], [(uT[:, kc, tt_ * 128:(tt_ + 1) * 128], wt[:, kc, :]) for kc in range(8)], r=[wr, R_u], w=[pr])
            softplus_to(dtsb[:, gt, :], pt[:, 0:32], dtb[:], 32, r=[pr, R_sm], w=[R_dt])
        for cg in range(8 if ctxj is None else 0):
            wt, wr = B.load_w(win_d, 8, C_Z + cg * 256, 256)
            for tt_ in range(ntt):
                pt, pr = banks[4 + tt_ % 2]
                B.mms(pt[:, 0:256], [(uT[:, kc, tt_ * 128:(tt_ + 1) * 128], wt[:, kc, :]) for kc in range(8)],
                      r=[wr, R_u], w=[pr])
                tb_, tbr = nextB()
                B.cp(tb_[:, 0:256], pt[:, 0:256], r=[pr], w=[tbr], eng="act")
                B.dma(z_s[t0 + tt_ * 128:t0 + (tt_ + 1) * 128, cg * 256:(cg + 1) * 256], tb_[:, 0:256], r=[tbr], w=[R_z], q="pool")
        for c in range(24 if ctxj is None else 20):
            wt, wr = B.load_w(win_d, 8, C_X + c * 128, 128)
            pt, pr = banks[c % 2]
            B.mms(pt[:, 0:ncols], [(wt[:, kc, :], uT[:, kc, 0:ncols]) for kc in range(8)], r=[wr, R_u], w=[pr])
            xbt, xr = xb[c % 2]
            B.cp(xbt[:, 0:3], halo[:, c, :], r=[R_halo], w=[xr])
            B.cp(xbt[:, 3:3 + ncols], pt[:, 0:ncols], r=[pr], w=[xr])
            B.cp(halo[:, c, :], xbt[:, ncols:ncols + 3], r=[xr], w=[R_halo])
            if blk == NBLK - 1 and ctxj is None:
                B.dma(cv_o[c * 128:(c + 1) * 128, :], xbt[:, ncols:ncols + 3], r=[xr], w=[], q="pool")
            ta, tr = nextA()
            B.ts(ta[:, 0:ncols], xbt[:, 0:ncols], cw[:, c, 0:1], cbv[:, c:c + 1], ALU.mult, ALU.add, r=[xr, R_sm], w=[tr])
            for i in range(1, 4):
                B.stt(ta[:, 0:ncols], xbt[:, i:i + ncols], cw[:, c, i:i + 1], ta[:, 0:ncols], ALU.mult, ALU.add,
                      r=[xr, R_sm, tr], w=[tr])
            tb_, tbr = nextB()
            B.act(tb_[:, 0:ncols], ta[:, 0:ncols], AF.Silu, r=[tr], w=[tbr])
            if c >= 20:
                B.dma(CT_s[(c - 20) * 128:(c - 19) * 128, t0:t0 + ncols], tb_[:, 0:ncols], r=[tbr], w=[R_CT], q="pool")
                continue
            if c >= 16 and ctxj is None:
                B.dma(BT_s[(c - 16) * 128:(c - 15) * 128, t0:t0 + ncols], tb_[:, 0:ncols], r=[tbr], w=[R_BT], q="pool")
            for tt_ in range(ntt):
                P.add("pe", lambda e, tt_=tt_, tb_=tb_: e.transpose(ptT[:, tt_ * 128:(tt_ + 1) * 128],
                                                                     tb_[:, tt_ * 128:(tt_ + 1) * 128], ident_b[:]),
                      r=[tbr, R_const], w=[prT])
            tb2, tbr2 = nextB()
            B.cp(tb2[:, 0:ncols], ptT[:, 0:ncols], r=[prT], w=[tbr2])
            for tt_ in range(ntt):
                rows = slice(t0 + tt_ * 128, t0 + (tt_ + 1) * 128)
                if c < 16:
                    B.dma(xs_s[rows, c * 128:(c + 1) * 128], tb2[:, tt_ * 128:(tt_ + 1) * 128], r=[tbr2], w=[R_xs], q="pool")
                else:
                    B.dma(Bt_s[rows, (c - 16) * 128:(c - 15) * 128], tb2[:, tt_ * 128:(tt_ + 1) * 128], r=[tbr2], w=[R_Bt], q="pool")
        for gi, (c0, dst, rr) in enumerate([(C_GA, ga_s, R_ga), (C_GS, gs_s, R_gs)] if ctxj is None else []):
            for c in range(8):
                wt, wr = B.load_w(win_d, 8, c0 + c * 128, 128)
                pt, pr = banks[2 + c % 2]
                B.mms(pt[:, 0:ncols], [(wt[:, kc, :], uT[:, kc, 0:ncols]) for kc in range(8)], r=[wr, R_u], w=[pr])
                tb_, tbr = nextB()
                B.act(tb_[:, 0:ncols], pt[:, 0:ncols], AF.Sigmoid, r=[pr], w=[tbr])
                B.dma(dst[c * 128:(c + 1) * 128, t0:t0 + ncols], tb_[:, 0:ncols], r=[tbr], w=[rr], q="pool")


    xTd_v = xT_d.rearrange("(k p) t -> p k t", p=128)
    x1o_v = x1_o.rearrange("(k p) t -> p k t", p=128)

    xs1 = B.sb("xs1", [128, 8, 32], F32)
    sconv = B.sb("sconv", [128, 24, 2, 3], F32)
    R_xs1 = Res()

    def run_own_p1():
        NS_ = 32
        NX_ = 35
        sgroups = [(0, 16, 1), (16, 16, 2), (32, 3, 0)]
        B.dma(xT[:, :, 0:NX_], xsT_d.rearrange("(k p) t -> p k t", p=128), w=[R_x])
        rms_mod(xT, 0, sgroups, NX_)
        ffn(0, w1a_d, w3a_d, w2a_d, sgroups, NX_)
        rms_mod(xT, 1, sgroups, NX_)
        B.dma(sconv[:], scv_d[:, :, :, :], w=[R_sm])
        B.cp(xs1[:], xT[:, :, 0:NS_], r=[R_x], w=[R_xs1])
        for h in range(H_A):
            wt, wr = B.load_w(win_d, 8, C_Q + h * 64, 64)
            pt, pr = banks[h % 2]
            B.mms(pt[0:64, 0:NS_], [(wt[:, kc, :], uT[:, kc, 0:NS_]) for kc in range(8)], r=[wr, R_u], w=[pr])
            tb_, tbr = nextB()
            B.ts(tb_[0:64, 0:NS_], pt[0:64, 0:NS_], 0.125, None, ALU.mult, r=[pr], w=[tbr])
            B.dma(q2_s[h, 0:64, :], tb_[0:64, 0:NS_], r=[tbr], w=[R_s2], q="pool")
            wt, wr = B.load_w(win_d, 8, C_K + h * 64, 64)
            pt, pr = banks[2 + h % 2]
            B.mms(pt[0:64, 0:NS_], [(wt[:, kc, :], uT[:, kc, 0:NS_]) for kc in range(8)], r=[wr, R_u], w=[pr])
            kf, kr = kst[h % 2]
            B.cp(kf[:, 0:NS_], pt[0:64, 0:NS_], r=[pr], w=[kr])
            B.dma(ksT_o[h * 64:(h + 1) * 64, :], kf[:, 0:NS_], r=[kr], w=[], q="pool")
            tb_, tbr = nextB()
            B.cp(tb_[0:64, 0:NS_], pt[0:64, 0:NS_], r=[pr], w=[tbr], eng="act")
            B.dma(k2_s[h, 0:64, :], tb_[0:64, 0:NS_], r=[tbr], w=[R_s2], q="pool")
            B.dma(k2_s[h, 64:66, :], onesrow[64:66, 0:NS_], r=[R_const], w=[R_s2], q="pool")
        for cg in range(4):
            wt, wr = B.load_w(win_d, 8, C_V + cg * 256, 256)
            for sq_ in range(2):
                pt, pr = banks[4 + sq_]
                B.mms(pt[0:16, 0:256], [(uT[:, kc, sq_ * 16:(sq_ + 1) * 16], wt[:, kc, :]) for kc in range(8)], r=[wr, R_u], w=[pr])
                ta, tr = nextA()
                B.cp(ta[0:16, 0:256], pt[0:16, 0:256], r=[pr], w=[tr])
                B.dma(vs_o[sq_ * 16:(sq_ + 1) * 16, cg * 256:(cg + 1) * 256], ta[0:16, 0:256], r=[tr], w=[], q="pool")
                vs, vr = vst[sq_]
                B.cp(vs[0:16, 0:4, 0:64], pt[0:16, 0:256].rearrange("p (h d) -> p h d", h=4), r=[pr], w=[vr], eng="act")
                B.dma(v2_s[cg * 4:(cg + 1) * 4, sq_, :, :].rearrange("h p d -> p h d"), vs[0:16, 0:4, :], r=[vr], w=[R_s2], q="pool")
        wt, wr = B.load_w(win_d, 8, C_F, 16)
        for sq_ in range(2):
            pt, pr = banks[6]
            B.mms(pt[0:16, 0:16], [(uT[:, kc, sq_ * 16:(sq_ + 1) * 16], wt[:, kc, :]) for kc in range(8)], r=[wr, R_u], w=[pr])
            ta, tr = nextA()
            softplus_to(ta[0:16, 0:16], pt[0:16, 0:16], bfb[0:16, :], 16, r=[pr, R_sm], w=[tr], neg_in=True, pn=16)
            B.ts(logf[0:16, NT + sq_, :], ta[0:16, 0:16], -1.0, None, ALU.mult, r=[tr], w=[R_lf])
            B.dma(lfs_o[sq_ * 16:(sq_ + 1) * 16, :], logf[0:16, NT + sq_, :], r=[R_lf], w=[], q="pool")
        if B.stage >= 2:
            wt, wr = B.load_w(win_d, 8, C_DT, 32)
            for sq_ in range(2):
                pt, pr = banks[6]
                B.mms(pt[0:16, 0:32], [(uT[:, kc, sq_ * 16:(sq_ + 1) * 16], wt[:, kc, :]) for kc in range(8)], r=[wr, R_u], w=[pr])
                softplus_to(dtsb[0:16, NT + sq_, :], pt[0:16, 0:32], dtb[0:16, :], 32, r=[pr, R_sm], w=[R_dt], pn=16)
            for cg in range(8):
                wt, wr = B.load_w(win_d, 8, C_Z + cg * 256, 256)
                for sq_ in range(2):
                    pt, pr = banks[4 + sq_]
                    B.mms(pt[0:16, 0:256], [(uT[:, kc, sq_ * 16:(sq_ + 1) * 16], wt[:, kc, :]) for kc in range(8)], r=[wr, R_u], w=[pr])
                    tb_, tbr = nextB()
                    B.cp(tb_[0:16, 0:256], pt[0:16, 0:256], r=[pr], w=[tbr], eng="act")
                    B.dma(z2_s[sq_ * 16:(sq_ + 1) * 16, cg * 256:(cg + 1) * 256], tb_[0:16, 0:256], r=[tbr], w=[R_s2], q="pool")
            for c in range(24):
                wt, wr = B.load_w(win_d, 8, C_X + c * 128, 128)
                pt, pr = banks[c % 2]
                B.mms(pt[:, 0:NX_], [(wt[:, kc, :], uT[:, kc, 0:NX_]) for kc in range(8)], r=[wr, R_u], w=[pr])
                xbt, xr = xb[c % 2]
                B.cp(xbt[:, 0:3], sconv[:, c, 0, :], r=[R_sm], w=[xr])
                B.cp(xbt[:, 3:19], pt[:, 0:16], r=[pr], w=[xr])
                B.cp(xbt[:, 19:22], sconv[:, c, 1, :], r=[R_sm], w=[xr])
                B.cp(xbt[:, 22:38], pt[:, 16:32], r=[pr], w=[xr])
                B.ts(halo[:, c, :], pt[:, 32:35], flg[:, 16:17], None, ALU.mult, r=[pr, R_sm], w=[R_halo])
                B.dma(cvs_o[c * 128:(c + 1) * 128, 0:3], xbt[:, 16:19], r=[xr], w=[], q="pool")
                B.dma(cvs_o[c * 128:(c + 1) * 128, 3:6], xbt[:, 35:38], r=[xr], w=[], q="pool")
                ta, tr = nextA()
                B.ts(ta[:, 0:35], xbt[:, 0:35], cw[:, c, 0:1], cbv[:, c:c + 1], ALU.mult, ALU.add, r=[xr, R_sm], w=[tr])
                for i in range(1, 4):
                    B.stt(ta[:, 0:35], xbt[:, i:i + 35], cw[:, c, i:i + 1], ta[:, 0:35], ALU.mult, ALU.add, r=[xr, R_sm, tr], w=[tr])
                tb_, tbr = nextB()
                B.act(tb_[:, 0:35], ta[:, 0:35], AF.Silu, r=[tr], w=[tbr])
                offs = (0, 19)
                if c >= 20:
                    for sq_ in range(2):
                        B.dma(CT2_s[(c - 20) * 128:(c - 19) * 128, sq_ * 16:(sq_ + 1) * 16], tb_[:, offs[sq_]:offs[sq_] + 16], r=[tbr], w=[R_s2], q="pool")
                    continue
                if c >= 16:
                    for sq_ in range(2):
                        B.dma(BT2_s[(c - 16) * 128:(c - 15) * 128, sq_ * 16:(sq_ + 1) * 16], tb_[:, offs[sq_]:offs[sq_] + 16], r=[tbr], w=[R_s2], q="pool")
                for sq_ in range(2):
                    P.add("pe", lambda e, sq_=sq_, tb_=tb_: e.transpose(ptT[0:16, sq_ * 128:(sq_ + 1) * 128], tb_[:, offs[sq_]:offs[sq_] + 16], ident_b[:]),
                          r=[tbr, R_const], w=[prT])
                tb2, tbr2 = nextB()
                B.cp(tb2[0:16, 0:256], ptT[0:16, 0:256], r=[prT], w=[tbr2])
                for sq_ in range(2):
                    rows2 = slice(sq_ * 16, (sq_ + 1) * 16)
                    if c < 16:
                        B.dma(xs2_s[rows2, c * 128:(c + 1) * 128], tb2[0:16, sq_ * 128:(sq_ + 1) * 128], r=[tbr2], w=[R_s2], q="pool")
                    else:
                        B.dma(Bt2_s[rows2, (c - 16) * 128:(c - 15) * 128], tb2[0:16, sq_ * 128:(sq_ + 1) * 128], r=[tbr2], w=[R_s2], q="pool")
            for (c0, dst) in [(C_GA, ga2_s), (C_GS, gs2_s)]:
                for c in range(8):
                    wt, wr = B.load_w(win_d, 8, c0 + c * 128, 128)
                    pt, pr = banks[2 + c % 2]
                    B.mms(pt[:, 0:NS_], [(wt[:, kc, :], uT[:, kc, 0:NS_]) for kc in range(8)], r=[wr, R_u], w=[pr])
                    tb_, tbr = nextB()
                    B.act(tb_[:, 0:NS_], pt[:, 0:NS_], AF.Sigmoid, r=[pr], w=[tbr])
                    B.dma(dst[c * 128:(c + 1) * 128, :], tb_[:, 0:NS_], r=[tbr], w=[R_s2], q="pool")

        xTd_v = xT_d.rearrange("(k p) t -> p k t", p=128)
        x1o_v = x1_o.rearrange("(k p) t -> p k t", p=128)
        for blk in range(NBLK):
            t0 = blk * TB
            groups = [(0, TB, 0)]
            B.dma(xT[:, :, :], xTd_v[:, :, t0:t0 + TB], w=[R_x])
            if B.cutn <= 2:
                B.dma(x1o_v[:, :, t0:t0 + TB], xT[:, :, :], r=[R_x], w=[R_x1], q="pool")
            B.cut(2)
            rms_mod(xT, 0, groups, TB)
            if B.cutn <= 3:
                B.cp(xT[:, :, :], uT[:, :, :], r=[R_u], w=[R_x])
                B.dma(x1o_v[:, :, t0:t0 + TB], xT[:, :, :], r=[R_x], w=[R_x1], q="pool")
            B.cut(3)
            ffn(0, w1a_d, w3a_d, w2a_d, groups, TB)
            B.dma(x1o_v[:, :, t0:t0 + TB], xT[:, :, :], r=[R_x], w=[R_x1], q="pool")
            B.cut(4)
            rms_mod(xT, 1, groups, TB)
            win_block(blk, groups, TB)


    if B.stage < 3:
        run_own_p1()
        return
    tri = B.sb("tri", [128, 128], F32)
    mneg = B.sb("mneg", [128, 128], F32)
    e0row = B.sb("e0row", [128, 128], F32)
    selt = B.sb("selt", [16, 16, 66], F32)
    selc = B.sb("selc", [128, 2], F32)
    a_bc = B.sb("a_bc", [128, 32], F32)
    dsk = B.sb("dsk", [128, 32], F32)
    gssd = B.sb("gssd", [128, 16], F32)
    R_c3 = Res()
    B.dma(tri[:], tri_d[:, :], w=[R_c3])
    B.dma(mneg[:], mneg_d[:, :], w=[R_c3])
    B.dma(e0row[:], e0_d[:, :], w=[R_c3])
    B.dma(selt[:], sel_d[:, :, :], w=[R_c3])
    B.dma(selc[:], selc_d[:, :], w=[R_c3])
    B.dma(a_bc[:], alog_d[:, :], w=[R_c3])
    B.dma(dsk[:], dsk_d[:, :], w=[R_c3])
    B.dma(gssd[:], gssd_d[:, :], w=[R_c3])
    B.act(a_bc[:], a_bc[:], AF.Exp, r=[R_c3], w=[R_c3])
    B.ts(a_bc[:], a_bc[:], -1.0, None, ALU.mult, r=[R_c3], w=[R_c3])

    Fc = B.sb("Fc", [128, NT, 16], F32)
    carry = B.sb("carry", [128, 16], F32)
    R_F, R_carry = Res(), Res()
    arena = B.sb("arena", [128, 8192], BF16)
    R_ar = [Res() for _ in range(4)]
    xtm_s = [arena[:, 0:2048], arena[:, 2048:4096]]
    ztm_s = [arena[:, 4096:6144], arena[:, 6144:8192]]
    oT = arena[0:64, :].rearrange("p (h t) -> p h t", h=16)
    bc_s = [(B.sb(f"bcs{i}", [128, 3, 512], BF16), Res()) for i in range(2)]
    Hst = B.sb("Hst", [128, DIN], F32)
    Hb = B.sb("Hb", [128, DIN], BF16)
    yz = B.sb("yz", [128, DIN], F32)
    sm = B.sb("sm", [128, 8, 32], F32)
    cbm = B.sb("cbm", [128, 4, 128], F32)
    xd = B.sb("xd", [128, DIN], BF16)
    ynb = xd
    wTb = [(B.sb(f"wTb{i}", [128, 4, 128], BF16), Res()) for i in range(2)]
    D4 = [(B.sb(f"D4{i}", [128, 4, 128], F32), Res()) for i in range(2)]
    sg4 = [(B.sb(f"sg4{i}", [128, 4, 128], F32), Res()) for i in range(2)]
    ssq = B.sb("ssq", [128, 4], F32)
    R_H, R_Hb, R_yz, R_ynb, R_sm, R_cbm, R_xd, R_ssq = [Res() for _ in range(8)]
    R_ynb = R_xd
    P.add("pool", lambda e: e.memset(Hst[:], 0.0), w=[R_H])
    ynT = gT[:, 0:16, :]
    BT_v = BT_s.rearrange("(g n) t -> n g t", n=128)
    CT_v = CT_s.rearrange("(g n) t -> n g t", n=128)

    def bc3(ap2, n, m):
        return ap2.unsqueeze(2).broadcast_to([128, n, m])

    def ssd_tile(gt, want_y=True, L=128, samp=None, maskj=None):
        sl = gt % 2
        if samp is None:
            rows = slice(gt * 128, (gt + 1) * 128)
            src_x, src_z, src_Bt, src_BT, src_CT = xs_s, z_s, Bt_s, BT_v, CT_v
            rx_, rz_, rbt_, rBT_, rCT_ = R_xs, R_z, R_Bt, R_BT, R_CT
            dti = gt
            ycol0 = (gt % 4) * 128
        else:
            sl = samp
            rows = slice(samp * 16, samp * 16 + 16)
            src_x, src_z, src_Bt, src_BT, src_CT = xs2_s, z2_s, Bt2_s, BT2_v, CT2_v
            rx_ = rz_ = rbt_ = rBT_ = rCT_ = R_s2
            dti = NT + samp
            ycol0 = samp * 16
        xtm, ztm = xtm_s[sl], ztm_s[sl]
        Rx_, Rz_ = R_ar[sl], R_ar[2 + sl]
        bct, Rb_ = bc_s[sl]
        B.dma(xtm[0:L, :], src_x[rows, :], r=[rx_], w=[Rx_])
        if want_y:
            B.dma(ztm[0:L, :], src_z[rows, :], r=[rz_], w=[Rz_])
        B.dma(bct[0:L, 0, :], src_Bt[rows, :], r=[rbt_], w=[Rb_])
        if want_y:
            B.dma(bct[:, 1, 0:4 * L].rearrange("p (g t) -> p g t", g=4), src_BT[:, :, rows], r=[rBT_], w=[Rb_])
            B.dma(bct[:, 2, 0:4 * L].rearrange("p (g t) -> p g t", g=4), src_CT[:, :, rows], r=[rCT_], w=[Rb_])
        Btm = bct[0:L, 0, :].rearrange("p (g n) -> p g n", g=4)
        BTf = bct[:, 1, 0:4 * L].rearrange("p (g t) -> p g t", g=4)
        CTf = bct[:, 2, 0:4 * L].rearrange("p (g t) -> p g t", g=4)
        xt3 = xtm[0:L, :].rearrange("p (h d) -> p h d", h=32)
        dA, acs, ea, de, tmpv = [sm[0:L, j, :] for j in (0, 1, 3, 4, 6)]
        al, cd = sm[:, 2, :], sm[:, 5, :]
        dtv = dtsb[0:L, dti, :]
        B.tt(dA, dtv, a_bc[0:L, :], ALU.mult, r=[R_dt, R_c3], w=[R_sm])
        pt, pr = banks[0]
        P.add("pe", lambda e: e.matmul(pt[0:L, 0:32], tri[0:L, 0:L], dA, start=True, stop=True), r=[R_c3, R_sm], w=[pr])
        P.add("pe", lambda e: e.matmul(pt[:, 32:64], ones_f[0:L, :], dA, start=True, stop=True), r=[R_const, R_sm], w=[pr])
        B.cp(acs, pt[0:L, 0:32], r=[pr], w=[R_sm])
        B.cp(al, pt[:, 32:64], r=[pr], w=[R_sm])
        B.act(ea, acs, AF.Exp, r=[R_sm], w=[R_sm])
        B.act(cd, al, AF.Exp, r=[R_sm], w=[R_sm])
        if maskj is not None:
            B.ts(cd, cd, -1.0, flg[:, maskj:maskj + 1], ALU.add, ALU.mult, r=[R_sm], w=[R_sm])
            B.ts(cd, cd, 1.0, None, ALU.add, r=[R_sm], w=[R_sm])
        B.tt(tmpv, al[0:L, :], acs, ALU.subtract, r=[R_sm], w=[R_sm])
        B.act(tmpv, tmpv, AF.Exp, r=[R_sm], w=[R_sm])
        B.tt(de, tmpv, dtv, ALU.mult, r=[R_sm, R_dt], w=[R_sm])
        B.cp(Hb[:], Hst[:], r=[R_H], w=[R_Hb], eng="pool")
        if want_y:
            pc, prc = banks[1]
            for g in range(4):
                P.add("pe", lambda e, g=g: e.matmul(pc[0:L, g * L:(g + 1) * L], BTf[:, g, :], CTf[:, g, :], start=True, stop=True),
                      r=[Rb_], w=[prc])
            cbv_ = cbm[:].rearrange("p g t -> p (g t)")[0:L, 0:4 * L].rearrange("p (g t) -> p g t", g=4)
            B.tt(cbv_, pc[0:L, 0:4 * L].rearrange("p (g t) -> p g t", g=4), tri[0:L, 0:L].unsqueeze(1).broadcast_to([L, 4, L]), ALU.mult,
                 r=[prc, R_c3], w=[R_cbm])
            for g in range(4):
                pyd, pryd = banks[4]
                for hb in range(2):
                    h0 = g * 8 + hb * 4
                    d4t, rd4 = D4[hb]
                    s4t, rs4 = sg4[hb]
                    wt4t, rw4 = wTb[hb]
                    d4 = d4t[:].rearrange("p g t -> p (g t)")[0:L, 0:4 * L].rearrange("p (g t) -> p g t", g=4)
                    s4 = s4t[:].rearrange("p g t -> p (g t)")[0:L, 0:4 * L].rearrange("p (g t) -> p g t", g=4)
                    wt4 = wt4t[:].rearrange("p g t -> p (g t)")[0:L, 0:4 * L].rearrange("p (g t) -> p g t", g=4)
                    B.tt(d4, ident_f[0:L, 0:L].unsqueeze(1).broadcast_to([L, 4, L]),
                         acs[:, h0:h0 + 4].unsqueeze(2).broadcast_to([L, 4, L]), ALU.mult, r=[R_const, R_sm], w=[rd4])
                    pb, prb = banks[2 + hb]
                    P.add("pe", lambda e, d4t=d4t, pb=pb: e.matmul(pb[0:L, 0:4 * L], ones_f[0:L, 0:L],
                                                                    d4t[:].rearrange("p g t -> p (g t)")[0:L, 0:4 * L], start=True, stop=True),
                          r=[R_const, rd4], w=[prb])
                    for j in range(4):
                        B.stt(s4[:, j, :], pb[0:L, j * L:(j + 1) * L], acs[:, h0 + j:h0 + j + 1], mneg[0:L, 0:L], ALU.subtract, ALU.add,
                              r=[prb, R_sm, R_c3], w=[rs4])
                    B.act(s4, s4, AF.Exp, r=[rs4], w=[rs4])
                    for j in range(4):
                        B.stt(wt4[:, j, :], s4[:, j, :], dtv[:, h0 + j:h0 + j + 1], cbv_[:, g, :], ALU.mult, ALU.mult,
                              r=[rs4, R_dt, R_cbm], w=[rw4])
                    for j in range(4):
                        hh = hb * 4 + j
                        P.add("pe", lambda e, j=j, hh=hh, wt4=wt4, h0=h0: e.matmul(pyd[0:L, hh * 64:(hh + 1) * 64], wt4[:, j, :], xt3[:, h0 + j, :],
                                                                                    start=True, stop=True), r=[rw4, Rx_], w=[pryd])
                pyo, pryo = banks[5]
                P.add("pe", lambda e, g=g, pyo=pyo: e.matmul(pyo[0:L, :], CTf[:, g, :], Hb[:, g * 512:(g + 1) * 512], start=True, stop=True),
                      r=[Rb_, R_Hb], w=[pryo])
                yg = yz[0:L, g * 512:(g + 1) * 512].rearrange("p (h d) -> p h d", h=8)
                B.tt(yg, pyo[0:L, :].rearrange("p (h d) -> p h d", h=8), ea[:, g * 8:(g + 1) * 8].unsqueeze(2).broadcast_to([L, 8, 64]),
                     ALU.mult, r=[pryo, R_sm], w=[R_yz])
                B.tt(yg, pyd[0:L, :].rearrange("p (h d) -> p h d", h=8), yg, ALU.add, r=[pryd, R_yz], w=[R_yz])
                ta, tr = nextA()
                ta3 = ta[0:L, :].rearrange("p (h d) -> p h d", h=8)
                B.tt(ta3, xt3[:, g * 8:(g + 1) * 8, :], dsk[0:L, g * 8:(g + 1) * 8].unsqueeze(2).broadcast_to([L, 8, 64]), ALU.mult,
                     r=[Rx_, R_c3], w=[tr])
                B.tt(yg, yg, ta3, ALU.add, r=[tr, R_yz], w=[R_yz])
        B.tt(xd[0:L, :].rearrange("p (h d) -> p h d", h=32), xt3, de.unsqueeze(2).broadcast_to([L, 32, 64]), ALU.mult,
             r=[Rx_, R_sm], w=[R_xd])
        for g in range(4):
            pS, prS = banks[6]
            P.add("pe", lambda e, g=g, pS=pS: e.matmul(pS[:, :], Btm[:, g, :], xd[0:L, g * 512:(g + 1) * 512], start=True, stop=True),
                  r=[Rb_, R_xd], w=[prS])
            Hg = Hst[:, g * 512:(g + 1) * 512].rearrange("p (h d) -> p h d", h=8)
            B.tt(Hg, Hg, bc3(cd[:, g * 8:(g + 1) * 8], 8, 64), ALU.mult, r=[R_sm, R_Hb], w=[R_H])
            if maskj is not None:
                B.stt(Hg, pS[:, :].rearrange("p (h d) -> p h d", h=8), flg[:, maskj:maskj + 1], Hg, ALU.mult, ALU.add, r=[prS], w=[R_H])
            else:
                B.tt(Hg, Hg, pS[:, :].rearrange("p (h d) -> p h d", h=8), ALU.add, r=[prS], w=[R_H])
        if not want_y:
            return
        if samp is None:
            B.dma(dbg_y[rows, :], yz[:], r=[R_yz], w=[], q="pool")
        for g in range(4):
            ta, tr = nextA()
            B.act(ta[0:L, :], ztm[0:L, g * 512:(g + 1) * 512], AF.Silu, r=[Rz_], w=[tr])
            B.tt(yz[0:L, g * 512:(g + 1) * 512], yz[0:L, g * 512:(g + 1) * 512], ta[0:L, :], ALU.mult, r=[tr, R_yz], w=[R_yz])
            ta2, tr2 = nextA()
            P.add("act", lambda e, g=g, ta2=ta2: e.activation(out=ta2[0:L, :], in_=yz[0:L, g * 512:(g + 1) * 512], func=AF.Square,
                                                               accum_out=ssq[0:L, g:g + 1]), r=[R_yz], w=[tr2, R_ssq])
        rs_ = sm[0:L, 7, 0:1]
        P.add("dve", lambda e: e.tensor_reduce(out=rs_, in_=ssq[0:L, 0:4], axis=AX.X, op=ALU.add), r=[R_ssq], w=[R_sm])
        B.act(rs_, rs_, AF.Sqrt, r=[R_sm, R_const], w=[R_sm], bias=epsc[0:L, :], scale=1.0 / DIN)
        P.add("dve", lambda e: e.reciprocal(out=rs_, in_=rs_), r=[R_sm], w=[R_sm])
        B.ts(ynb[0:L, :], yz[0:L, :], rs_, None, ALU.mult, r=[R_yz, R_sm], w=[R_ynb])
        for half in range(2):
            for c in range(8):
                cc = half * 8 + c
                P.add("pe", lambda e, c=c, cc=cc: e.transpose(ptT[:, c * 128:c * 128 + L], ynb[0:L, cc * 128:(cc + 1) * 128], ident_b[0:L, 0:L]),
                      r=[R_ynb, R_const], w=[prT])
            for c in range(8):
                cc = half * 8 + c
                B.ts(ynT[:, cc, ycol0:ycol0 + L], ptT[:, c * 128:c * 128 + L], gssd[:, cc:cc + 1], None, ALU.mult,
                     r=[prT, R_c3], w=[R_gT])

    Kt = [(B.sb(f"Kt{i}", [66, T], BF16), Res()) for i in range(2)]
    Vt = [(B.sb(f"Vt{i}", [128, NT, 65], BF16), Res()) for i in range(2)]
    Qa = [(B.sb(f"Qa{i}", [66, TB], BF16), Res()) for i in range(2)]
    Pt = [(B.sb(f"Pt{i}", [128, TB], BF16), Res()) for i in range(3)]
    FT = sq[0:16, :]
    rbc = B.sb("rbc", [128, 16], F32)
    biasT = B.sb("biasT", [128, NT, 16], F32)
    arb = B.sb("arb", [66, TB], BF16)
    R_FT, R_rbc, R_bias, R_arow, R_rl, R_rlb = [Res() for _ in range(6)]
    R_FT = R_sq
    pti = [0]

    def attn_block(blk):
        t0 = blk * TB
        nkt = 4 * (blk + 1)
        pf, prf = banks[6]
        for j in range(4):
            P.add("pe", lambda e, j=j: e.transpose(pf[0:16, j * 128:(j + 1) * 128], Fc[:, 4 * blk + j, :], ident_f[:]),
                  r=[R_F, R_const], w=[prf])
        B.cp(FT[:, :], pf[0:16, :], r=[prf], w=[R_FT])
        P.add("pe", lambda e: e.matmul(pf[:, 0:16], e0row[:], Fc[:, 4 * blk, :], start=True, stop=True), r=[R_c3, R_F], w=[prf])
        B.cp(rbc[:], pf[:, 0:16], r=[prf], w=[R_rbc])
        B.tt(biasT[:, 0:nkt, :], rbc[:].unsqueeze(1).broadcast_to([128, nkt, 16]), Fc[:, 0:nkt, :], ALU.subtract,
             r=[R_rbc, R_F], w=[R_bias])
        B.tt(rdj[:], delta[:, 0:NCORES, :], rbc[:].unsqueeze(1).broadcast_to([128, NCORES, 16]), ALU.add,
             r=[R_delta, R_rbc], w=[R_rdj])
        kvi = [0]
        for h in range(H_A):
            qa_, qr_ = Qa[h % 2]
            B.dma(qa_[0:64, :], q_s[h, 0:64, t0:t0 + TB], r=[R_q], w=[qr_])
            pa, pra = banks[4]
            P.add("pe", lambda e, h=h: e.matmul(pa[0:66, :], selt[:, h, :], FT[:, :], start=True, stop=True), r=[R_c3, R_FT], w=[pra])
            ar0, rr0 = nextA()
            ar1, rr1 = nextA()
            ar2, rr2 = nextA()
            B.cp(ar0[64:66, :], pa[64:66, :], r=[pra], w=[rr0])
            B.ts(ar1[64:66, :], ar0[64:66, :], ar0[64:66, 0:1], None, ALU.subtract, r=[rr0], w=[rr1])
            B.cp(arb[64:66, :], ar1[64:66, :], r=[rr1], w=[R_arow])
            B.tt(ar0[64:66, :], ar1[64:66, :], arb[64:66, :], ALU.subtract, r=[rr1, R_arow], w=[rr0])
            B.ts(ar2[64:66, :], arb[64:66, :], selc[64:66, 0:1], None, ALU.mult, r=[R_arow, R_c3], w=[rr2])
            B.stt(qa_[64:66, :], ar0[64:66, :], selc[64:66, 1:2], ar2[64:66, :], ALU.mult, ALU.add,
                  r=[rr0, rr2, R_c3], w=[qr_])
            po, pro = banks[2 + h % 2]
            first = [True]
            if CTX:
                for j in range(NCORES):
                    kvi[0] += 1
                    kt_, kr_ = Kt[kvi[0] % 2]
                    vt_, vr_ = Vt[kvi[0] % 2]
                    B.dma(kt_[:, :], kgA_v[j, h, :, :], r=[R_kgA], w=[kr_])
                    B.dma(vt_[:, :, :], vgA_v[j, h, :, :, :].rearrange("t p d -> p t d"), r=[R_vgA], w=[vr_])
                    bj, bjr = biasJ[kvi[0] % 2]
                    B.ts(bj[:, :], FcA[:, j, :].rearrange("p (t h) -> p t h", h=16)[:, :, h], -1.0, rdj[:, j, h:h + 1], ALU.mult, ALU.add,
                         r=[R_FcA, R_rdj], w=[bjr])
                    B.ts(bj[:, :], bj[:, :], flg[:, j:j + 1], flg[:, 8 + j:9 + j], ALU.mult, ALU.add, r=[R_sm], w=[bjr])
                    for kt in range(NT):
                        ps_, prs = banks[kt % 2]
                        P.add("pe", lambda e, kt=kt, ps_=ps_, kt_=kt_, qa_=qa_: e.matmul(ps_[:, :], kt_[:, kt * 128:(kt + 1) * 128], qa_[:, :],
                                                                                         start=True, stop=True), r=[kr_, qr_], w=[prs])
                        pti[0] += 1
                        pT_, prp = Pt[pti[0] % 3]
                        B.act(pT_[:, :], ps_[:, :], AF.Exp, r=[prs, bjr], w=[prp], bias=bj[:, kt:kt + 1], scale=1.0)
                        st_ = first[0]
                        first[0] = False
                        P.add("pe", lambda e, kt=kt, pT_=pT_, vt_=vt_, po=po, st_=st_: e.matmul(po[0:65, :], vt_[:, kt, :], pT_[:, :],
                                                                                              start=st_, stop=False, skip_group_check=True),
                              r=[vr_, prp], w=[pro])
            kvi[0] += 1
            kt_, kr_ = Kt[kvi[0] % 2]
            vt_, vr_ = Vt[kvi[0] % 2]
            B.dma(kt_[:, 0:t0 + TB], kg_s[h, :, 0:t0 + TB], r=[R_kg], w=[kr_])
            B.dma(vt_[:, 0:nkt, :], vg_s[h, 0:nkt, :, :].rearrange("t p d -> p t d"), r=[R_vg], w=[vr_])
            for kt in range(nkt):
                j = kt - 4 * blk
                c0 = 128 * j if j > 0 else 0
                ps_, prs = banks[kt % 2]
                P.add("pe", lambda e, kt=kt, c0=c0, ps_=ps_, kt_=kt_, qa_=qa_: e.matmul(ps_[:, c0:TB], kt_[:, kt * 128:(kt + 1) * 128], qa_[:, c0:TB],
                                                                                       start=True, stop=True), r=[kr_, qr_], w=[prs])
                pti[0] += 1
                pT_, prp = Pt[pti[0] % 3]
                bias_ap = biasT[:, kt, h:h + 1]
                if j >= 0:
                    ta, tr = nextA()
                    B.tt(ta[:, c0:c0 + 128], ps_[:, c0:c0 + 128], mneg[:], ALU.add, r=[prs, R_c3], w=[tr])
                    B.act(pT_[:, c0:c0 + 128], ta[:, c0:c0 + 128], AF.Exp, r=[tr, R_bias], w=[prp], bias=bias_ap, scale=1.0)
                    if c0 + 128 < TB:
                        B.act(pT_[:, c0 + 128:TB], ps_[:, c0 + 128:TB], AF.Exp, r=[prs, R_bias], w=[prp], bias=bias_ap, scale=1.0)
                else:
                    B.act(pT_[:, :], ps_[:, :], AF.Exp, r=[prs, R_bias], w=[prp], bias=bias_ap, scale=1.0)
                P.add("pe", lambda e, kt=kt, c0=c0, pT_=pT_, vt_=vt_, po=po: e.matmul(po[0:65, c0:TB], vt_[:, kt, :], pT_[:, c0:TB],
                                                                                    start=(kt == 0 and first[0]), stop=(kt == nkt - 1),
                                                                                    skip_group_check=True),
                      r=[vr_, prp], w=[pro])
            rl, R_rl = nextA()
            rlb, R_rlb = nextA()
            P.add("dve", lambda e, po=po, rl=rl: e.reciprocal(out=rl[64:65, :], in_=po[64:65, :]), r=[pro], w=[R_rl])
            pb2, prb2 = banks[5]
            P.add("pe", lambda e, rl=rl: e.matmul(pb2[0:64, :], ones_f[64:65, 0:64], rl[64:65, :], start=True, stop=True), r=[R_const, R_rl], w=[prb2])
            B.cp(rlb[0:64, :], pb2[0:64, :], r=[prb2], w=[R_rlb])
            B.tt(oT[:, h, :], po[0:64, :], rlb[0:64, :], ALU.mult, r=[pro, R_rlb], w=R_ar)
            ta, tr = nextA()
            B.tt(ta[0:64, :], po[0:64, :], rlb[0:64, :], ALU.mult, r=[pro, R_rlb], w=[tr])
            B.dma(dbg_att[h * 64:(h + 1) * 64, t0:t0 + TB], ta[0:64, :], r=[tr], w=[], q="pool")

    gab = [(B.sb(f"gab{i}", [128, 2, TB], BF16), Res()) for i in range(2)]
    mT = uT
    R_mT = R_u
    ga_v = ga_s.rearrange("(k p) t -> p k t", p=128)
    gs_v = gs_s.rearrange("(k p) t -> p k t", p=128)
    yTo_v = yT_o.rearrange("(k p) t -> p k t", p=128)

    def dense_block(blk, ncols=TB, groups=None, samp=False):
        t0 = 0 if samp else blk * TB
        if groups is None:
            groups = [(0, TB, 0)]
        gav = ga2_v if samp else ga_v
        gsv = gs2_v if samp else gs_v
        rga, rgs = (R_s2, R_s2) if samp else (R_ga, R_gs)
        for oc in range(8):
            wat, war = B.load_w(wa_d, 16, oc * 128, 128, pn=64)
            wst_, wsr = B.load_w(ws_d, 16, oc * 128, 128)
            pa_, pra_ = banks[oc % 2]
            ps2, prs2 = banks[2 + oc % 2]
            B.mms(pa_[:, 0:ncols], [(wat[:, h, :], oT[:, h, 0:ncols]) for h in range(16)], r=[war] + R_ar, w=[pra_])
            B.mms(ps2[:, 0:ncols], [(wst_[:, kc, :], ynT[:, kc, 0:ncols]) for kc in range(16)], r=[wsr, R_gT], w=[prs2])
            gt_, gr_ = gab[oc % 2]
            B.dma(gt_[:, 0, 0:ncols], gav[:, oc, t0:t0 + ncols], r=[rga], w=[gr_])
            B.dma(gt_[:, 1, 0:ncols], gsv[:, oc, t0:t0 + ncols], r=[rgs], w=[gr_])
            ta, tr = nextA()
            B.tt(ta[:, 0:ncols], pa_[:, 0:ncols], gt_[:, 0, 0:ncols], ALU.mult, r=[pra_, gr_], w=[tr])
            ta2, tr2 = nextA()
            B.tt(ta2[:, 0:ncols], ps2[:, 0:ncols], gt_[:, 1, 0:ncols], ALU.mult, r=[prs2, gr_], w=[tr2])
            B.tt(mT[:, oc, 0:ncols], ta[:, 0:ncols], ta2[:, 0:ncols], ALU.add, r=[tr, tr2], w=[R_mT])
        if samp:
            B.cp(xT[:, :, 0:ncols], xs1[:, :, 0:ncols], r=[R_xs1], w=[R_x])
        else:
            B.dma(xT[:, :, :], x1o_v[:, :, t0:t0 + TB], r=[R_x1], w=[R_x])
        for oc in range(8):
            wot, wor = B.load_w(wo_d, 8, oc * 128, 128)
            po_, pro_ = banks[4 + oc % 2]
            B.mms(po_[:, 0:ncols], [(wot[:, kc, :], mT[:, kc, 0:ncols]) for kc in range(8)], r=[wor, R_mT], w=[pro_])
            for (c0_, n_, m_) in groups:
                B.stt(xT[:, oc, c0_:c0_ + n_], po_[:, c0_:c0_ + n_], Gm[:, 1, oc, m_:m_ + 1], xT[:, oc, c0_:c0_ + n_],
                      ALU.mult, ALU.add, r=[pro_, R_AG], w=[R_x])
        rms_mod(xT, 2, groups, ncols)
        ffn(2, w1b_d, w3b_d, w2b_d, groups, ncols)
        pt, pr = banks[6]
        for kc in range(8):
            ta, tr = nextA()
            B.tt(ta[:, 0:ncols], xT[:, kc, 0:ncols], xT[:, kc, 0:ncols], ALU.mult, r=[R_x], w=[tr])
            P.add("pe", lambda e, ta=ta, kc=kc: e.matmul(pt[:, 0:ncols], ones_f[:], ta[:, 0:ncols], start=(kc == 0), stop=(kc == 7)),
                  r=[tr, R_const], w=[pr])
        B.act(sq[:, 0:ncols], pt[:, 0:ncols], AF.Sqrt, r=[pr, R_const], w=[R_sq], bias=epsc[:], scale=1.0 / D)
        P.add("dve", lambda e: e.reciprocal(out=rstd[:, 0:ncols], in_=sq[:, 0:ncols]), r=[R_sq], w=[R_rstd])
        for kc in range(8):
            B.stt(xT[:, kc, 0:ncols], xT[:, kc, 0:ncols], gsb[:, 3, kc:kc + 1], rstd[:, 0:ncols], ALU.mult, ALU.mult,
                  r=[R_x, R_g, R_rstd], w=[R_x])
        if samp:
            B.dma(ysT_o.rearrange("(k p) t -> p k t", p=128), xT[:, :, 0:ncols], r=[R_x], w=[], q="pool")
        else:
            B.dma(yTo_v[:, :, t0:t0 + TB], xT[:, :, :], r=[R_x], w=[], q="pool")

    ga2_v = ga2_s.rearrange("(k p) t -> p k t", p=128)
    gs2_v = gs2_s.rearrange("(k p) t -> p k t", p=128)
    BT2_v = BT2_s.rearrange("(g n) t -> n g t", n=128)
    CT2_v = CT2_s.rearrange("(g n) t -> n g t", n=128)

    FcA = B.sb("FcA", [128, NCORES, 256], F32)
    smAll = B.sb("smAll", [128, NCORES, 16], F32)
    delta = B.sb("delta", [128, NCORES + 1, 16], F32)
    R_FcA, R_smAll, R_delta = Res(), Res(), Res()
    xTall_v = xTall_d.rearrange("(k p) t -> p k t", p=128)
    for j in range(NCORES):
        for h in range(H_A):
            for bb in range(T // TB):
                B.dma(kgA_v[j, h, 64:66, bb * TB:(bb + 1) * TB], onesrow[64:66, :], r=[R_const], w=[R_kgA], q="pool")
    P.add("pool", lambda e: e.memset(halo[:], 0.0), w=[R_halo])
    for j in range(NCORES):
        for blk in range(NBLK):
            c0_ = j * T + blk * TB
            B.dma(xT[:, :, :], xTall_v[:, :, c0_:c0_ + TB], w=[R_x])
            rms_mod(xT, 0, [(0, TB, 0)], TB)
            ffn(0, w1a_d, w3a_d, w2a_d, [(0, TB, 0)], TB)
            rms_mod(xT, 1, [(0, TB, 0)], TB)
            win_block(blk, [(0, TB, 0)], TB, ctxj=j)
        P.add("dve", lambda e: e.memset(carry[:], 0.0), w=[R_carry])
        for i in range(NT):
            pt, pr = banks[6]
            P.add("pe", lambda e, i=i: e.matmul(pt[:, 0:16], tri[:], logf[:, i, :], start=True, stop=True), r=[R_c3, R_lf], w=[pr])
            B.tt(FcA[:, j, i * 16:(i + 1) * 16], pt[:, 0:16], carry[:], ALU.add, r=[pr, R_carry], w=[R_FcA])
            P.add("pe", lambda e, i=i: e.matmul(pt[:, 16:32], ones_f[:], logf[:, i, :], start=True, stop=True), r=[R_const, R_lf], w=[pr])
            B.tt(carry[:], pt[:, 16:32], carry[:], ALU.add, r=[pr], w=[R_carry])
        B.cp(smAll[:, j, :], carry[:], r=[R_carry], w=[R_smAll])
        for gt in range(NT):
            ssd_tile(gt, want_y=False, maskj=j)
    P.add("dve", lambda e: e.memset(delta[:], 0.0), w=[R_delta])
    for j in range(NCORES - 1, -1, -1):
        B.stt(delta[:, j, :], smAll[:, j, 0:16], flg[:, j:j + 1], delta[:, j + 1, :], ALU.mult, ALU.add,
              r=[R_smAll, R_delta], w=[R_delta])
    run_own_p1()
    P.add("dve", lambda e: e.memset(carry[:], 0.0), w=[R_carry])
    for i in range(NT):
        pt, pr = banks[6]
        P.add("pe", lambda e, i=i: e.matmul(pt[:, 0:16], tri[:], logf[:, i, :], start=True, stop=True), r=[R_c3, R_lf], w=[pr])
        B.tt(Fc[:, i, :], pt[:, 0:16], carry[:], ALU.add, r=[pr, R_carry], w=[R_F])
        P.add("pe", lambda e, i=i: e.matmul(pt[:, 16:32], ones_f[:], logf[:, i, :], start=True, stop=True), r=[R_const, R_lf], w=[pr])
        B.tt(carry[:], pt[:, 16:32], carry[:], ALU.add, r=[pr], w=[R_carry])
    biasJ = [(B.sb(f"biasJ{i}", [128, NT], F32), Res()) for i in range(2)]
    rdj = B.sb("rdj", [128, NCORES, 16], F32)
    R_rdj = Res()
    for blk in range(NBLK):
        for lt in range(4):
            ssd_tile(blk * 4 + lt)
        attn_block(blk)
        dense_block(blk)
    B.dma(ssm_o[:, :, :], Hst[:].rearrange("p (h d) -> p h d", h=32), r=[R_H], w=[], q="pool")

    lfc = B.sb("lfc", [128, 8, 16], F32)
    Fs = B.sb("Fs", [128, 9, 16], F32)
    biasS = B.sb("biasS", [128, 9, 16], F32)
    R_lfc, R_Fs, R_bS = Res(), Res(), Res()

    def attn_sample(sq_):
        cs_ = slice(sq_ * 16, (sq_ + 1) * 16)
        B.dma(lfc[:], clf_d[sq_, :, :].rearrange("(t p) h -> p t h", p=128), w=[R_lfc])
        P.add("dve", lambda e: e.memset(carry[:], 0.0), w=[R_carry])
        P.add("dve", lambda e: e.memset(Fs[:], 0.0), w=[R_Fs])
        pt, pr = banks[6]
        for i in range(8):
            P.add("pe", lambda e, i=i: e.matmul(pt[:, 0:16], tri[:], lfc[:, i, :], start=True, stop=True), r=[R_c3, R_lfc], w=[pr])
            B.tt(Fs[:, i, :], pt[:, 0:16], carry[:], ALU.add, r=[pr, R_carry], w=[R_Fs])
            P.add("pe", lambda e, i=i: e.matmul(pt[:, 16:32], ones_f[:], lfc[:, i, :], start=True, stop=True), r=[R_const, R_lfc], w=[pr])
            B.tt(carry[:], pt[:, 16:32], carry[:], ALU.add, r=[pr], w=[R_carry])
        P.add("pe", lambda e: e.matmul(pt[0:16, 0:16], tri[0:16, 0:16], logf[0:16, NT + sq_, :], start=True, stop=True), r=[R_c3, R_lf], w=[pr])
        B.tt(Fs[0:16, 8, :], pt[0:16, 0:16], carry[0:16, :], ALU.add, r=[pr, R_carry], w=[R_Fs])
        pf, prf = banks[6]
        P.add("pe", lambda e: e.transpose(pf[0:16, 0:16], Fs[0:16, 8, :], ident_f[0:16, 0:16]), r=[R_Fs, R_const], w=[prf])
        B.cp(FT[:, 0:16], pf[0:16, 0:16], r=[prf], w=[R_FT])
        P.add("pe", lambda e: e.matmul(pf[:, 0:16], e0row[0:16, :], Fs[0:16, 8, :], start=True, stop=True), r=[R_c3, R_Fs], w=[prf])
        B.cp(rbc[:], pf[:, 0:16], r=[prf], w=[R_rbc])
        B.tt(biasS[:], rbc[:].unsqueeze(1).broadcast_to([128, 9, 16]), Fs[:], ALU.subtract, r=[R_rbc, R_Fs], w=[R_bS])
        for h in range(H_A):
            kt_, kr_ = Kt[h % 2]
            vt_, vr_ = Vt[h % 2]
            qa_, qr_ = Qa[h % 2]
            B.dma(yz[0:64, 0:1024], ckT_d[sq_, h, :, :], w=[R_yz])
            B.cp(kt_[0:64, 0:1024], yz[0:64, 0:1024], r=[R_yz], w=[kr_], eng="pool")
            P.add("pool", lambda e, kt_=kt_: e.memset(kt_[64:66, 0:1040], 1.0), w=[kr_])
            B.dma(kt_[0:64, 1024:1040], k2_s[h, 0:64, cs_], r=[R_s2], w=[kr_])
            ta, tr = nextA()
            B.dma(ta[:, :].rearrange("p (t d) -> p t d", t=8),
                  cv_d[sq_, :, :].rearrange("(t p) (h d) -> p t h d", p=128, h=16)[:, :, h, :], w=[tr])
            B.cp(vt_[:, 0:8, 0:64], ta[:, :].rearrange("p (t d) -> p t d", t=8), r=[tr], w=[vr_], eng="pool")
            P.add("pool", lambda e, vt_=vt_: e.memset(vt_[:, 0:9, 64:65], 1.0), w=[vr_])
            B.dma(vt_[0:16, 8, :], v2_s[h, sq_, :, :], r=[R_s2], w=[vr_])
            B.dma(qa_[0:64, 0:16], q2_s[h, 0:64, cs_], r=[R_s2], w=[qr_])
            pa, pra = banks[4]
            P.add("pe", lambda e, h=h: e.matmul(pa[0:66, 0:16], selt[:, h, :], FT[:, 0:16], start=True, stop=True), r=[R_c3, R_FT], w=[pra])
            ar0, rr0 = nextA()
            ar1, rr1 = nextA()
            ar2, rr2 = nextA()
            B.cp(ar0[64:66, 0:16], pa[64:66, 0:16], r=[pra], w=[rr0])
            B.ts(ar1[64:66, 0:16], ar0[64:66, 0:16], ar0[64:66, 0:1], None, ALU.subtract, r=[rr0], w=[rr1])
            B.cp(arb[64:66, 0:16], ar1[64:66, 0:16], r=[rr1], w=[R_arow])
            B.tt(ar0[64:66, 0:16], ar1[64:66, 0:16], arb[64:66, 0:16], ALU.subtract, r=[rr1, R_arow], w=[rr0])
            B.ts(ar2[64:66, 0:16], arb[64:66, 0:16], selc[64:66, 0:1], None, ALU.mult, r=[R_arow, R_c3], w=[rr2])
            B.stt(qa_[64:66, 0:16], ar0[64:66, 0:16], selc[64:66, 1:2], ar2[64:66, 0:16], ALU.mult, ALU.add, r=[rr0, rr2, R_c3], w=[qr_])
            po, pro = banks[2 + h % 2]
            for kt in range(9):
                Lk = 128 if kt < 8 else 16
                ps_, prs = banks[kt % 2]
                P.add("pe", lambda e, kt=kt, Lk=Lk, ps_=ps_, kt_=kt_, qa_=qa_: e.matmul(ps_[0:Lk, 0:16], kt_[:, kt * 128:kt * 128 + Lk], qa_[:, 0:16],
                                                                                       start=True, stop=True), r=[kr_, qr_], w=[prs])
                pti[0] += 1
                pT_, prp = Pt[pti[0] % 3]
                if kt == 8:
                    ta, tr = nextA()
                    B.tt(ta[0:16, 0:16], ps_[0:16, 0:16], mneg[0:16, 0:16], ALU.add, r=[prs, R_c3], w=[tr])
                    B.act(pT_[0:16, 0:16], ta[0:16, 0:16], AF.Exp, r=[tr, R_bS], w=[prp], bias=biasS[0:16, 8, h:h + 1], scale=1.0)
                else:
                    B.act(pT_[:, 0:16], ps_[:, 0:16], AF.Exp, r=[prs, R_bS], w=[prp], bias=biasS[:, kt, h:h + 1], scale=1.0)
                P.add("pe", lambda e, kt=kt, Lk=Lk, pT_=pT_, vt_=vt_, po=po: e.matmul(po[0:65, 0:16], vt_[0:Lk, kt, :], pT_[0:Lk, 0:16],
                                                                                    start=(kt == 0), stop=(kt == 8), skip_group_check=True),
                      r=[vr_, prp], w=[pro])
            rl, R_rl = nextA()
            rlb, R_rlb = nextA()
            P.add("dve", lambda e, po=po, rl=rl: e.reciprocal(out=rl[64:65, 0:16], in_=po[64:65, 0:16]), r=[pro], w=[R_rl])
            pb2, prb2 = banks[5]
            P.add("pe", lambda e, rl=rl: e.matmul(pb2[0:64, 0:16], ones_f[64:65, 0:64], rl[64:65, 0:16], start=True, stop=True), r=[R_const, R_rl], w=[prb2])
            B.cp(rlb[0:64, 0:16], pb2[0:64, 0:16], r=[prb2], w=[R_rlb])
            B.tt(oT[:, h, cs_], po[0:64, 0:16], rlb[0:64, 0:16], ALU.mult, r=[pro, R_rlb], w=R_ar)

    for sq_ in range(2):
        B.dma(Hst[:], ssmin_d[sq_, :, :], w=[R_H])
        ssd_tile(0, want_y=True, L=16, samp=sq_)
        B.dma(ssms_o[sq_, :, :], Hst[:], r=[R_H], w=[], q="pool")
    for sq_ in range(2):
        attn_sample(sq_)
    dense_block(0, ncols=32, groups=[(0, 16, 1), (16, 16, 2)], samp=True)


_CACHE = {}


def _r(w, kc):
    K, N = w.shape
    return np.ascontiguousarray(w.reshape(kc, K // kc, N).transpose(1, 0, 2))


def prep_inputs(inp, stage):
    f = np.float32
    xp = np.asarray(inp["x_prompt"], f)[0]
    maps = []
    shared = {}
    shared["w_ada_r"] = _r(np.asarray(inp["w_ada"], f)[0], 8)
    shared["b_ada_r"] = np.ascontiguousarray(np.asarray(inp["b_ada"], f)[0].reshape(72, 128).T)
    for nm, key in [("g_ffn1_r", "g_ffn1"), ("g_mix_r", "g_mix"), ("g_ffn2_r", "g_ffn2")]:
        shared[nm] = np.ascontiguousarray(np.asarray(inp[key], f)[0].reshape(8, 128).T)
    shared["g_final_r"] = np.ascontiguousarray(np.asarray(inp["g_final"], f).reshape(8, 128).T)
    shared["w1a"] = _r(np.asarray(inp["w1_ffn1"], f)[0], 8)
    shared["w3a"] = _r(np.asarray(inp["w3_ffn1"], f)[0], 8)
    shared["w2a"] = _r(np.asarray(inp["w2_ffn1"], f)[0], NH)
    shared["w1b"] = _r(np.asarray(inp["w1_ffn2"], f)[0], 8)
    shared["w3b"] = _r(np.asarray(inp["w3_ffn2"], f)[0], 8)
    shared["w2b"] = _r(np.asarray(inp["w2_ffn2"], f)[0], NH)
    shared["win_r"] = _r(np.asarray(inp["w_in"], f)[0], 8)
    shared["bf_bc"] = np.ascontiguousarray(np.broadcast_to(np.asarray(inp["b_f"], f)[0][None, :], (128, 16)))
    shared["convw_r"] = np.ascontiguousarray(np.asarray(inp["conv_w"], f)[0].reshape(4, 24, 128).transpose(2, 1, 0))
    shared["convb_r"] = np.ascontiguousarray(np.asarray(inp["conv_b"], f)[0].reshape(24, 128).T)
    shared["dtb_bc"] = np.ascontiguousarray(np.broadcast_to(np.asarray(inp["dt_bias"], f)[0][None, :], (128, 32)))
    shared["ident_in"] = np.eye(128, dtype=f)
    shared["alog_bc"] = np.ascontiguousarray(np.broadcast_to(np.asarray(inp["a_log"], f)[0][None, :], (128, 32)))
    shared["dskip_bc"] = np.ascontiguousarray(np.broadcast_to(np.asarray(inp["d_skip"], f)[0][None, :], (128, 32)))
    shared["gssd_r"] = np.ascontiguousarray(np.asarray(inp["g_ssd"], f)[0].reshape(16, 128).T)
    tri = np.triu(np.ones((128, 128), f))
    shared["tri_in"] = tri
    shared["maskneg_in"] = ((1.0 - tri) * -1e9).astype(f)
    e0 = np.zeros((128, 128), f); e0[0, :] = 1.0
    shared["e0row_in"] = e0
    sel = np.zeros((16, 16, 66), f)
    for h in range(16):
        sel[h, h, 64] = 1.0; sel[h, h, 65] = 1.0
    shared["sel_in"] = sel
    selc = np.zeros((128, 2), f); selc[64, 0] = 1.0; selc[65, 1] = 1.0
    shared["selc_in"] = selc
    shared["wa_r"] = np.ascontiguousarray(np.asarray(inp["w_a"], f)[0].reshape(16, 64, D).transpose(1, 0, 2))
    shared["ws_r"] = _r(np.asarray(inp["w_s"], f)[0], 16)
    shared["wout_r"] = _r(np.asarray(inp["w_out"], f)[0], 8)
    xpT = np.ascontiguousarray(xp.T)
    xsm = np.asarray(inp["x_sample"], f)
    cp = np.asarray(inp["c_prompt"], f)[0]
    cs = np.asarray(inp["c_sample"], f)
    for c in range(NCORES):
        m = dict(shared)
        m["xT"] = np.ascontiguousarray(xp[c * T:(c + 1) * T].T)
        m["xTall"] = xpT
        hal = xp[c * T - 3:c * T] if c > 0 else np.zeros((3, D), f)
        m["xsT"] = np.ascontiguousarray(np.concatenate([xsm[2 * c:2 * c + 2].reshape(32, D), hal], 0).T)
        fl = np.zeros((128, 24), f)
        for j in range(8):
            fl[:, j] = 1.0 if j < c else 0.0
            fl[:, 8 + j] = 0.0 if j < c else NEG
        fl[:, 16] = 1.0 if c > 0 else 0.0
        m["flags"] = fl
        sc = np.asarray(inp["state_conv"], f)[0, 2 * c:2 * c + 2]
        m["scv"] = np.ascontiguousarray(sc.transpose(2, 0, 1).reshape(24, 128, 2, 3).transpose(1, 0, 2, 3))
        ss = np.asarray(inp["state_ssm"], f)[0, 2 * c:2 * c + 2]
        m["ssmin"] = np.ascontiguousarray(ss.transpose(0, 3, 1, 2).reshape(2, 128, DIN))
        ck = np.asarray(inp["cache_k"], f)[0, 2 * c:2 * c + 2]
        m["ckT"] = np.ascontiguousarray(ck.transpose(0, 2, 3, 1))
        m["cvc"] = np.ascontiguousarray(np.asarray(inp["cache_v"], f)[0, 2 * c:2 * c + 2].reshape(2, 1024, D))
        m["clf"] = np.ascontiguousarray(np.asarray(inp["cache_logf"], f)[0, 2 * c:2 * c + 2])
        c3 = np.stack([cp, cs[2 * c], cs[2 * c + 1]], axis=1)
        m["cT"] = np.ascontiguousarray(c3.reshape(8, 128, 3).transpose(1, 0, 2))
        maps.append(m)
    return maps


def run(inp, stage=99, ncores=NCORES):
    if stage not in _CACHE:
        B = build(stage)
        _CACHE[stage] = B
    B = _CACHE[stage]
    maps = prep_inputs(inp, stage)
    maps = [{k: v for k, v in m.items() if k in B.ins} for m in maps]
    res = run_bass_kernel_spmd(B.nc, maps[:ncores], core_ids=list(range(ncores)))
    return res.results


def kernel(**inp):
    res = run(inp, stage=3)
    f = np.float32
    y_prompt = np.zeros((1, SEQ, D), f)
    y_sample = np.zeros((16, 16, D), f)
    k_prompt = np.zeros((1, 1, SEQ, H_A, HD), f)
    v_prompt = np.zeros((1, 1, SEQ, H_A, HD), f)
    logf_prompt = np.zeros((1, 1, SEQ, H_A), f)
    ssm_prompt = np.zeros((1, 1, NSSD, 64, NST), f)
    conv_prompt = np.zeros((1, 1, 3, CONVD), f)
    k_sample = np.zeros((1, 16, 16, H_A, HD), f)
    v_sample = np.zeros((1, 16, 16, H_A, HD), f)
    logf_sample = np.zeros((1, 16, 16, H_A), f)
    ssm_sample = np.zeros((1, 16, NSSD, 64, NST), f)
    conv_sample = np.zeros((1, 16, 3, CONVD), f)
    for c in range(NCORES):
        r = res[c]
        sl = slice(c * T, (c + 1) * T)
        y_prompt[0, sl] = np.asarray(r["yT_o"]).T
        k_prompt[0, 0, sl] = np.asarray(r["kT_o"]).T.reshape(T, H_A, HD)
        v_prompt[0, 0, sl] = np.asarray(r["v_o"]).reshape(T, H_A, HD)
        logf_prompt[0, 0, sl] = np.asarray(r["lf_o"])
        k_sample[0, 2 * c:2 * c + 2] = np.asarray(r["ksT_o"]).T.reshape(2, 16, H_A, HD)
        v_sample[0, 2 * c:2 * c + 2] = np.asarray(r["vs_o"]).reshape(2, 16, H_A, HD)
        logf_sample[0, 2 * c:2 * c + 2] = np.asarray(r["lfs_o"]).reshape(2, 16, H_A)
        y_sample[2 * c:2 * c + 2] = np.asarray(r["ysT_o"]).T.reshape(2, 16, D)
        ssm_sample[0, 2 * c:2 * c + 2] = np.asarray(r["ssms_o"]).reshape(2, 128, NSSD, 64).transpose(0, 2, 3, 1)
        cvs = np.asarray(r["cvs_o"])
        conv_sample[0, 2 * c] = cvs[:, 0:3].T
        conv_sample[0, 2 * c + 1] = cvs[:, 3:6].T
    conv_prompt[0, 0] = np.asarray(res[NCORES - 1]["cv_o"]).T
    ssm_prompt[0, 0] = np.asarray(res[NCORES - 1]["ssm_o"]).transpose(1, 2, 0)
    return (y_prompt, y_sample, k_prompt, v_prompt, logf_prompt, ssm_prompt, conv_prompt,
            k_sample, v_sample, logf_sample, ssm_sample, conv_sample)
```

```python
from contextlib import ExitStack
import numpy as np
import concourse.bass as bass
import concourse.mybir as mybir
from concourse.bass_utils import run_bass_kernel_spmd

F32 = mybir.dt.float32
BF16 = mybir.dt.bfloat16
AF = mybir.ActivationFunctionType
ALU = mybir.AluOpType
AX = mybir.AxisListType

NCORES = 8
D = 1024
SEQ = 16384
T = SEQ // NCORES
TB = 512
DFF = 2816
NH = 22
H_A = 16
HD = 64
DIN = 2048
NSSD = 32
NST = 128
CONVD = 3072
DINP = 10288
C_Q, C_K, C_V, C_F, C_Z, C_X, C_DT, C_GA, C_GS = 0, 1024, 2048, 3072, 3088, 5136, 8208, 8240, 9264
EPS = 1e-6
NEG = -30000.0
import os
CTX = os.environ.get("CTX", "1") == "1"

ENG = ["pe", "act", "dve", "pool", "sp"]


class Res:
    __slots__ = ("lw", "rd", "excl")

    def __init__(self, excl=False):
        self.lw = None
        self.rd = []
        self.excl = excl


class Op:
    __slots__ = ("eng", "fn", "deps", "dma", "need", "sv", "dsem", "dval", "idx", "cc", "seng")


class Prog:
    NS = 12

    def __init__(self):
        self.ops = []
        self.dq = {"sp": [], "pool": [], "act": []}
        self.ncc = 0

    def add(self, eng, fn, r=(), w=(), dma=False, cc=False):
        op = Op()
        op.eng, op.fn, op.dma, op.need, op.idx = eng, fn, dma, False, len(self.ops)
        op.sv = 0
        op.cc = cc
        op.seng = eng
        if cc:
            op.dma = dma = True
        deps = {}
        w = list(w) + [R for R in r if R.excl]
        r = [R for R in r if not R.excl]

        def dep(p, war=False):
            if p is None:
                return
            if (not p.dma) and p.eng == eng and eng == "pe":
                return
            deps[p.idx] = p

        for R in r:
            dep(R.lw)
        for R in w:
            dep(R.lw)
            for q in R.rd:
                dep(q, True)
        if cc:
            op.seng = "cc"
            op.dsem = self.ncc
            op.dval = 1
            self.ncc += 1
        elif dma:
            lst = self.dq[eng]
            n = len(lst)
            op.dsem = n % self.NS
            op.dval = 16 * (n // self.NS + 1)
            if n >= self.NS:
                deps[lst[n - self.NS].idx] = lst[n - self.NS]
            lst.append(op)
        op.deps = list(deps.values())
        for p in op.deps:
            if not p.dma:
                p.need = True
        for R in r:
            R.rd.append(op)
        for R in w:
            R.lw = op
            R.rd = []
        self.ops.append(op)
        return op

    def emit(self, nc, sems, dsems):
        cnt = {e: 0 for e in ENG}
        for op in self.ops:
            if (not op.dma) and op.need:
                cnt[op.eng] += 1
                op.sv = cnt[op.eng]
        per = {e: [o for o in self.ops if o.eng == e] for e in ENG}

        def run(ename, e):
            waited = {}
            for op in per[ename]:
                for p in op.deps:
                    if p.dma:
                        key, val, sem = ("d", p.seng, p.dsem), p.dval, dsems[p.seng][p.dsem]
                    else:
                        key, val, sem = ("c", p.eng), p.sv, sems[p.eng]
                    if waited.get(key, 0) >= val:
                        continue
                    waited[key] = val
                    e.wait_ge(sem, val)
                ins = op.fn(e)
                if op.cc:
                    ins.then_inc(dsems["cc"][op.dsem], 1)
                elif op.dma:
                    ins.then_inc(dsems[ename][op.dsem], 16)
                elif op.need:
                    ins.then_inc(sems[ename], 1)
            if ename in self.dq:
                lst = self.dq[ename]
                last = {}
                for o in lst:
                    last[o.dsem] = o.dval
                for s, v in last.items():
                    e.wait_ge(dsems[ename][s], v)

        with nc.Block() as block:
            @block.tensor
            def _(e):
                run("pe", e)

            @block.scalar
            def _(e):
                run("act", e)

            @block.vector
            def _(e):
                run("dve", e)

            @block.gpsimd
            def _(e):
                run("pool", e)

            @block.sync
            def _(e):
                run("sp", e)


class StopBuild(Exception):
    pass


class Builder:
    def __init__(self, stage=99):
        import os
        self.cutn = float(os.environ.get("DBG_CUT", "999"))
        self.stage = stage
        self.nc = bass.Bass("TRN2", target_bir_lowering=False)
        try:
            self.nc.allow_low_precision("bf16 matmul operands by design")
        except Exception:
            pass
        self.P = Prog()
        self.es = ExitStack()
        self.ins = {}
        self.outs = {}
        self.uid = 0
        self.wrr = 0

    def finish(self):
        nc = self.nc
        sems = {e: self.es.enter_context(nc.semaphore(f"s_{e}")) for e in ENG}
        dsems = {q: [self.es.enter_context(nc.semaphore(f"d_{q}{i}")) for i in range(Prog.NS)]
                 for q in ("sp", "pool", "act")}
        dsems["cc"] = [self.es.enter_context(nc.semaphore(f"d_cc{i}")) for i in range(max(1, self.P.ncc))]
        self.P.emit(nc, sems, dsems)
        self.es.close()

    def cut(self, n):
        if self.cutn <= n:
            raise StopBuild()

    def din(self, name, shape, dt=F32):
        t = self.nc.dram_tensor(name, list(shape), dt, kind="ExternalInput").ap()
        self.ins[name] = t
        return t

    def dout(self, name, shape, dt=F32):
        t = self.nc.dram_tensor(name, list(shape), dt, kind="ExternalOutput").ap()
        self.outs[name] = t
        return t

    def dscr(self, name, shape, dt):
        return self.nc.dram_tensor(name, list(shape), dt).ap()

    def sb(self, name, shape, dt=F32):
        return self.es.enter_context(self.nc.sbuf_tensor("sb_" + name, list(shape), dt))

    def ps(self, name, shape, dt=F32):
        return self.es.enter_context(self.nc.psum_tensor("ps_" + name, list(shape), dt))

    def dma(self, out, in_, r=(), w=(), q="sp"):
        return self.P.add(q, lambda e: e.dma_start(out=out, in_=in_), r=r, w=w, dma=True)

    def act(self, out, in_, func, r=(), w=(), bias=None, scale=None):
        kw = {}
        if bias is not None:
            kw["bias"] = bias
        if scale is not None:
            kw["scale"] = scale
        return self.P.add("act", lambda e: e.activation(out=out, in_=in_, func=func, **kw), r=r, w=w)

    def tt(self, out, a, b, op, r=(), w=(), eng="dve"):
        return self.P.add(eng, lambda e: e.tensor_tensor(out=out, in0=a, in1=b, op=op), r=r, w=w)

    def ts(self, out, a, s1, s2, op0, op1=None, r=(), w=(), eng="dve"):
        if op1 is None:
            return self.P.add(eng, lambda e: e.tensor_scalar(out=out, in0=a, scalar1=s1, scalar2=None, op0=op0), r=r, w=w)
        return self.P.add(eng, lambda e: e.tensor_scalar(out=out, in0=a, scalar1=s1, scalar2=s2, op0=op0, op1=op1), r=r, w=w)

    def stt(self, out, a, s, b, op0, op1, r=(), w=(), eng="dve"):
        return self.P.add(eng, lambda e: e.scalar_tensor_tensor(out=out, in0=a, scalar=s, in1=b, op0=op0, op1=op1), r=r, w=w)

    def cp(self, out, in_, r=(), w=(), eng="dve"):
        if eng == "act":
            return self.P.add("act", lambda e: e.copy(out=out, in_=in_), r=r, w=w)
        return self.P.add(eng, lambda e: e.tensor_copy(out=out, in_=in_), r=r, w=w)

    def mms(self, out, pairs, r=(), w=()):
        n = len(pairs)

        def fn(e):
            ins = None
            for i, (l, rh) in enumerate(pairs):
                ins = e.matmul(out, l, rh, start=(i == 0), stop=(i == n - 1))
            return ins
        return self.P.add("pe", fn, r=r, w=w)

    def init_wstream(self):
        self.WSZ = 2048
        self.wst = [(self.sb(f"wst{i}", [128, self.WSZ], F32), Res()) for i in range(2)]
        self.wbf = [(self.sb(f"wbf{i}", [128, self.WSZ], BF16), Res()) for i in range(2)]
        self.wi = 0
        self.wj = 0

    def load_w(self, wd, kcn, c0, n, pn=128, k0=0):
        st, sr = self.wst[self.wi % 2]
        self.wi += 1
        bf, br = self.wbf[self.wj % 2]
        self.wj += 1
        sz = kcn * n
        assert sz <= self.WSZ
        stv = st[0:pn, 0:sz].rearrange("p (k n) -> p k n", k=kcn)
        bfv = bf[0:pn, 0:sz].rearrange("p (k n) -> p k n", k=kcn)
        self.dma(stv, wd[0:pn, k0:k0 + kcn, c0:c0 + n], w=[sr])
        ceng = "dve" if (self.wj % 3 == 0) else "act"
        self.cp(bf[0:pn, 0:sz], st[0:pn, 0:sz], r=[sr], w=[br], eng=ceng)
        return bfv, br


def build(stage=99):
    B = Builder(stage)
    nc, P = B.nc, B.P
    NBLK = T // TB
    NT = T // 128

    xT_d = B.din("xT", [D, T])
    cT_d = B.din("cT", [128, 8, 3])
    wada_d = B.din("w_ada_r", [128, 8, 9 * D])
    bada_d = B.din("b_ada_r", [128, 72])
    g1_d = B.din("g_ffn1_r", [128, 8])
    gm_d = B.din("g_mix_r", [128, 8])
    g2_d = B.din("g_ffn2_r", [128, 8])
    gf_d = B.din("g_final_r", [128, 8])
    w1a_d = B.din("w1a", [128, 8, DFF])
    w3a_d = B.din("w3a", [128, 8, DFF])
    w2a_d = B.din("w2a", [128, NH, D])
    w1b_d = B.din("w1b", [128, 8, DFF])
    w3b_d = B.din("w3b", [128, 8, DFF])
    w2b_d = B.din("w2b", [128, NH, D])
    win_d = B.din("win_r", [128, 8, DINP])
    bf_d = B.din("bf_bc", [128, 16])
    cw_d = B.din("convw_r", [128, 24, 4])
    cb_d = B.din("convb_r", [128, 24])
    dtb_d = B.din("dtb_bc", [128, 32])

    kT_o = B.dout("kT_o", [D, T])
    v_o = B.dout("v_o", [T, D])
    lf_o = B.dout("lf_o", [T, 16])
    cv_o = B.dout("cv_o", [CONVD, 3])
    xsT_d = B.din("xsT", [D, 35])
    flg_d = B.din("flags", [128, 24])
    ksT_o = B.dout("ksT_o", [D, 32])
    vs_o = B.dout("vs_o", [32, D])
    lfs_o = B.dout("lfs_o", [32, 16])
    cvs_o = B.dout("cvs_o", [CONVD, 6])
    x1_o = B.dout("x1_o", [D, T])
    scv_d = B.din("scv", [128, 24, 2, 3])
    ssmin_d = B.din("ssmin", [2, 128, DIN])
    ckT_d = B.din("ckT", [2, H_A, 64, 1024])
    cv_d = B.din("cvc", [2, 1024, D])
    clf_d = B.din("clf", [2, 1024, 16])
    ysT_o = B.dout("ysT_o", [D, 32])
    ssms_o = B.dout("ssms_o", [2, 128, DIN])
    q2_s = B.dscr("q2_s", [H_A, 66, 32], BF16)
    k2_s = B.dscr("k2_s", [H_A, 66, 32], BF16)
    v2_s = B.dscr("v2_s", [H_A, 2, 16, 65], BF16)
    z2_s = B.dscr("z2_s", [32, DIN], BF16)
    xs2_s = B.dscr("xs2_s", [32, DIN], BF16)
    Bt2_s = B.dscr("Bt2_s", [32, 512], BF16)
    BT2_s = B.dscr("BT2_s", [512, 32], BF16)
    CT2_s = B.dscr("CT2_s", [512, 32], BF16)
    ga2_s = B.dscr("ga2_s", [D, 32], BF16)
    gs2_s = B.dscr("gs2_s", [D, 32], BF16)
    R_s2 = Res()
    xTall_d = B.din("xTall", [D, SEQ])
    kgA = B.dscr("kgA", [NCORES * H_A * 66, T], BF16)
    vgA = B.dscr("vgA", [NCORES * H_A * NT * 128, 65], BF16)
    kgA_v = kgA.rearrange("(j h r) t -> j h r t", j=NCORES, h=H_A)
    vgA_v = vgA.rearrange("(j h t p) d -> j h t p d", j=NCORES, h=H_A, t=NT)
    R_kgA, R_vgA = Res(), Res()
    alog_d = B.din("alog_bc", [128, 32])
    dsk_d = B.din("dskip_bc", [128, 32])
    gssd_d = B.din("gssd_r", [128, 16])
    tri_d = B.din("tri_in", [128, 128])
    mneg_d = B.din("maskneg_in", [128, 128])
    e0_d = B.din("e0row_in", [128, 128])
    sel_d = B.din("sel_in", [16, 16, 66])
    selc_d = B.din("selc_in", [128, 2])
    wa_d = B.din("wa_r", [64, 16, D])
    ws_d = B.din("ws_r", [128, 16, D])
    wo_d = B.din("wout_r", [128, 8, D])
    yT_o = B.dout("yT_o", [D, T])
    ssm_o = B.dout("ssm_o", [128, NSSD, 64])
    dbg_y = B.dout("dbg_y", [T, DIN])
    dbg_att = B.dout("dbg_att", [D, T])

    q_s = B.dscr("q_s", [H_A, 66, T], BF16)
    kg_s = B.dscr("kg_s", [H_A, 66, T], BF16)
    vg_s = B.dscr("vg_s", [H_A, NT, 128, 65], BF16)
    z_s = B.dscr("z_s", [T, DIN], BF16)
    xs_s = B.dscr("xs_s", [T, DIN], BF16)
    Bt_s = B.dscr("Bt_s", [T, 512], BF16)
    BT_s = B.dscr("BT_s", [512, T], BF16)
    CT_s = B.dscr("CT_s", [512, T], BF16)
    ga_s = B.dscr("ga_s", [D, T], BF16)
    gs_s = B.dscr("gs_s", [D, T], BF16)
    R_q, R_kg, R_vg, R_z, R_xs, R_Bt, R_BT, R_CT, R_ga, R_gs, R_x1 = [Res() for _ in range(11)]

    ones_f = B.sb("ones_f", [128, 128], F32)
    ident_b = B.sb("ident_b", [128, 128], BF16)
    ident_f = B.sb("ident_f", [128, 128], F32)
    epsc = B.sb("epsc", [128, 1], F32)
    onec = B.sb("onec", [128, 1], F32)
    R_const = Res()
    P.add("pool", lambda e: e.memset(ones_f[:], 1.0), w=[R_const])
    P.add("pool", lambda e: e.memset(epsc[:], EPS), w=[R_const])
    P.add("pool", lambda e: e.memset(onec[:], 1.0), w=[R_const])
    identf_d = B.din("ident_in", [128, 128])
    B.dma(ident_f[:], identf_d[:, :], w=[R_const])
    B.cp(ident_b[:], ident_f[:], r=[R_const], w=[R_const], eng="pool")

    B.init_wstream()

    banks = [(B.ps(f"bank{i}", [128, 512], F32), Res(True)) for i in range(7)]
    ptT = B.ps("ptT", [128, 1024], BF16)
    prT = Res(True)

    try:
        _build_body(B, locals())
    except StopBuild:
        pass
    B.finish()
    return B


def _build_body(B, L):
    globals_ = L
    nc, P = B.nc, B.P
    NBLK = T // TB
    NT = T // 128
    for k_, v_ in L.items():
        if k_ not in ("B", "nc", "P"):
            globals()[k_] = v_
    cT = B.sb("cT", [128, 8, 3], F32)
    cs = B.sb("cs", [128, 8, 3], BF16)
    bada = B.sb("bada", [128, 72], F32)
    modT = B.sb("modT", [128, 72, 3], F32)
    gsb = B.sb("gsb", [128, 4, 8], F32)
    R_c, R_mod, R_g = Res(), Res(), Res()
    B.dma(cT[:], cT_d[:, :, :], w=[R_c])
    B.dma(bada[:], bada_d[:, :], w=[R_c])
    for i, gd in enumerate([g1_d, gm_d, g2_d, gf_d]):
        B.dma(gsb[:, i, :], gd[:, :], w=[R_g])
    B.act(cs[:], cT[:], AF.Silu, r=[R_c], w=[R_c])
    for ch in range(72):
        wt, wr = B.load_w(wada_d, 8, ch * 128, 128)
        pt, pr = banks[ch % 2]
        B.mms(pt[:, 0:3], [(wt[:, kc, :], cs[:, kc, :]) for kc in range(8)], r=[wr, R_c], w=[pr])
        B.ts(modT[:, ch, :], pt[:, 0:3], bada[:, ch:ch + 1], None, ALU.add, r=[pr, R_c], w=[R_mod])
    Am = B.sb("Am", [128, 3, 8, 3], F32)
    Gm = B.sb("Gm", [128, 3, 8, 3], F32)
    R_AG = Res()
    for s in range(3):
        coef = 1.0 if s == 1 else 0.5
        for kc in range(8):
            B.ts(Am[:, s, kc, :], modT[:, (3 * s + 1) * 8 + kc, :], 1.0, gsb[:, s, kc:kc + 1], ALU.add, ALU.mult,
                 r=[R_mod, R_g], w=[R_AG])
            B.ts(Gm[:, s, kc, :], modT[:, (3 * s + 2) * 8 + kc, :], 1.0, coef, ALU.add, ALU.mult,
                 r=[R_mod], w=[R_AG])

    B.cut(1)

    def shiftp(s, kc, m):
        return modT[:, (3 * s) * 8 + kc, m:m + 1]

    xT = B.sb("xT", [128, 8, TB], F32)
    uT = B.sb("uT", [128, 8, TB], BF16)
    gT = B.sb("gT", [128, NH, TB], BF16)
    sq = B.sb("sq", [128, TB], F32)
    rstd = B.sb("rstd", [128, TB], F32)
    tmpA = [(B.sb(f"tmpA{i}", [128, TB], F32), Res()) for i in range(5)]
    tmpB = [(B.sb(f"tmpB{i}", [128, TB], BF16), Res()) for i in range(3)]
    R_x, R_u, R_gT, R_sq, R_rstd = Res(), Res(), Res(), Res(), Res()
    tai = [0]
    tbi = [0]

    def nextA():
        tai[0] += 1
        return tmpA[tai[0] % 5]

    def nextB():
        tbi[0] += 1
        return tmpB[tbi[0] % 3]

    def rms_mod(src, s, groups, ncols):
        pt, pr = banks[6]
        for kc in range(8):
            ta, tr = nextA()
            B.tt(ta[:, 0:ncols], src[:, kc, 0:ncols], src[:, kc, 0:ncols], ALU.mult, r=[R_x], w=[tr])
            P.add("pe", lambda e, ta=ta, kc=kc: e.matmul(pt[:, 0:ncols], ones_f[:], ta[:, 0:ncols],
                                                            start=(kc == 0), stop=(kc == 7)),
                  r=[tr, R_const], w=[pr])
        B.act(sq[:, 0:ncols], pt[:, 0:ncols], AF.Sqrt, r=[pr, R_const], w=[R_sq], bias=epsc[:], scale=1.0 / D)
        P.add("dve", lambda e: e.reciprocal(out=rstd[:, 0:ncols], in_=sq[:, 0:ncols]), r=[R_sq], w=[R_rstd])
        for kc in range(8):
            ta, tr = nextA()
            B.tt(ta[:, 0:ncols], src[:, kc, 0:ncols], rstd[:, 0:ncols], ALU.mult, r=[R_x, R_rstd], w=[tr])
            for (c0, n, m) in groups:
                B.ts(uT[:, kc, c0:c0 + n], ta[:, c0:c0 + n], Am[:, s, kc, m:m + 1], shiftp(s, kc, m),
                     ALU.mult, ALU.add, r=[tr, R_AG, R_mod], w=[R_u])

    def ffn(s, w1d, w3d, w2d, groups, ncols):
        for hc in range(NH):
            w1t, w1r = B.load_w(w1d, 8, hc * 128, 128)
            w3t, w3r = B.load_w(w3d, 8, hc * 128, 128)
            p1, r1 = banks[hc % 2]
            p3, r3 = banks[2 + hc % 2]
            B.mms(p1[:, 0:ncols], [(w1t[:, kc, :], uT[:, kc, 0:ncols]) for kc in range(8)], r=[w1r, R_u], w=[r1])
            B.mms(p3[:, 0:ncols], [(w3t[:, kc, :], uT[:, kc, 0:ncols]) for kc in range(8)], r=[w3r, R_u], w=[r3])
            ta, tr = nextA()
            B.act(ta[:, 0:ncols], p1[:, 0:ncols], AF.Silu, r=[r1], w=[tr])
            B.tt(gT[:, hc, 0:ncols], ta[:, 0:ncols], p3[:, 0:ncols], ALU.mult, r=[tr, r3], w=[R_gT])
        for oc in range(8):
            w2t, w2r = B.load_w(w2d, 11, oc * 128, 128)
            w2u, w2s = B.load_w(w2d, 11, oc * 128, 128, k0=11)
            po, ro = banks[4 + oc % 2]
            B.mms(po[:, 0:ncols], [(w2t[:, hc, :], gT[:, hc, 0:ncols]) for hc in range(11)]
                  + [(w2u[:, hc, :], gT[:, 11 + hc, 0:ncols]) for hc in range(11)], r=[w2r, w2s, R_gT], w=[ro])
            for (c0, n, m) in groups:
                B.stt(xT[:, oc, c0:c0 + n], po[:, c0:c0 + n], Gm[:, s, oc, m:m + 1], xT[:, oc, c0:c0 + n],
                      ALU.mult, ALU.add, r=[ro, R_AG], w=[R_x])

    logf = B.sb("logf", [128, NT + 2, 16], F32)
    dtsb = B.sb("dtsb", [128, NT + 2, 32], F32)
    bfb = B.sb("bfb", [128, 16], F32)
    dtb = B.sb("dtb", [128, 32], F32)
    cw = B.sb("cw", [128, 24, 4], F32)
    cbv = B.sb("cbv", [128, 24], F32)
    halo = B.sb("halo", [128, 24, 3], F32)
    R_lf, R_dt, R_sm, R_halo = Res(), Res(), Res(), Res()
    B.dma(bfb[:], bf_d[:, :], w=[R_sm])
    B.dma(dtb[:], dtb_d[:, :], w=[R_sm])
    B.dma(cw[:], cw_d[:, :, :], w=[R_sm])
    B.dma(cbv[:], cb_d[:, :], w=[R_sm])
    P.add("pool", lambda e: e.memset(halo[:], 0.0), w=[R_halo])
    flg = B.sb("flg", [128, 24], F32)
    B.dma(flg[:], flg_d[:, :], w=[R_sm])

    xb = [(B.sb(f"xb{i}", [128, TB + 3], F32), Res()) for i in range(2)]
    vst = [(B.sb(f"vst{i}", [128, 4, 65], BF16), Res()) for i in range(2)]
    kst = [(B.sb("kst0", [64, TB], F32), Res())] * 2
    onesrow = B.sb("onesrow", [66, TB], BF16)
    P.add("pool", lambda e: e.memset(onesrow[:], 1.0), w=[R_const])
    for i in range(2):
        P.add("pool", lambda e, i=i: e.memset(vst[i][0][:], 1.0), w=[vst[i][1]])
    for h in range(H_A):
        for bb in range(T // TB):
            B.dma(kg_s[h, 64:66, bb * TB:(bb + 1) * TB], onesrow[64:66, :], r=[R_const], w=[R_kg], q="pool")

    def softplus_to(out, in_ps, biasbc, n, r, w, neg_in=False, pn=128):
        ta, tr = nextA()
        tb2, tr2 = nextA()
        B.tt(ta[0:pn, 0:n], in_ps, biasbc, ALU.add, r=r, w=[tr])
        if neg_in:
            B.ts(ta[0:pn, 0:n], ta[0:pn, 0:n], -1.0, None, ALU.mult, r=[tr], w=[tr])
        B.stt(tb2[0:pn, 0:n], ta[0:pn, 0:n], -1.0, ta[0:pn, 0:n], ALU.mult, ALU.max, r=[tr], w=[tr2])
        B.act(tb2[0:pn, 0:n], tb2[0:pn, 0:n], AF.Exp, r=[tr2], w=[tr2], scale=-1.0)
        B.act(tb2[0:pn, 0:n], tb2[0:pn, 0:n], AF.Ln, r=[tr2, R_const], w=[tr2], bias=onec[0:pn, :], scale=1.0)
        B.stt(out, ta[0:pn, 0:n], 0.0, tb2[0:pn, 0:n], ALU.max, ALU.add, r=[tr, tr2], w=w)

    def win_block(blk, groups, ncols, ctxj=None):
        t0 = blk * TB
        ntt = ncols // 128
        B.cut(5)
        for h in range(H_A):
            if ctxj is None:
                wt, wr = B.load_w(win_d, 8, C_Q + h * 64, 64)
                pt, pr = banks[h % 2]
                B.mms(pt[0:64, 0:ncols], [(wt[:, kc, :], uT[:, kc, 0:ncols]) for kc in range(8)], r=[wr, R_u], w=[pr])
                tb_, tbr = nextB()
                B.ts(tb_[0:64, 0:ncols], pt[0:64, 0:ncols], 0.125, None, ALU.mult, r=[pr], w=[tbr])
                B.dma(q_s[h, 0:64, t0:t0 + ncols], tb_[0:64, 0:ncols], r=[tbr], w=[R_q], q="pool")
            wt, wr = B.load_w(win_d, 8, C_K + h * 64, 64)
            pt, pr = banks[2 + h % 2]
            B.mms(pt[0:64, 0:ncols], [(wt[:, kc, :], uT[:, kc, 0:ncols]) for kc in range(8)], r=[wr, R_u], w=[pr])
            if ctxj is None:
                kf, kr = kst[h % 2]
                B.cp(kf[:, 0:ncols], pt[0:64, 0:ncols], r=[pr], w=[kr])
                B.dma(kT_o[h * 64:(h + 1) * 64, t0:t0 + ncols], kf[:, 0:ncols], r=[kr], w=[], q="pool")
            tb_, tbr = nextB()
            B.cp(tb_[0:64, 0:ncols], pt[0:64, 0:ncols], r=[pr], w=[tbr], eng="act")
            if ctxj is None:
                B.dma(kg_s[h, 0:64, t0:t0 + ncols], tb_[0:64, 0:ncols], r=[tbr], w=[R_kg], q="pool")
            else:
                B.dma(kgA_v[ctxj, h, 0:64, t0:t0 + ncols], tb_[0:64, 0:ncols], r=[tbr], w=[R_kgA], q="pool")
        B.cut(6)
        for cg in range(4):
            wt, wr = B.load_w(win_d, 8, C_V + cg * 256, 256)
            for tt_ in range(ntt):
                pt, pr = banks[4 + tt_ % 2]
                B.mms(pt[:, 0:256], [(uT[:, kc, tt_ * 128:(tt_ + 1) * 128], wt[:, kc, :]) for kc in range(8)],
                      r=[wr, R_u], w=[pr])
                if ctxj is None:
                    ta, tr = nextA()
                    B.cp(ta[:, 0:256], pt[:, 0:256], r=[pr], w=[tr])
                    B.dma(v_o[t0 + tt_ * 128:t0 + (tt_ + 1) * 128, cg * 256:(cg + 1) * 256], ta[:, 0:256], r=[tr], w=[], q="pool")
                vs, vr = vst[(cg * ntt + tt_) % 2]
                B.cp(vs[:, 0:4, 0:64], pt[:, 0:256].rearrange("p (h d) -> p h d", h=4), r=[pr], w=[vr], eng="act")
                gt = (t0 // 128) + tt_
                if ctxj is None:
                    B.dma(vg_s[cg * 4:(cg + 1) * 4, gt, :, :].rearrange("h p d -> p h d"), vs[:, 0:4, :], r=[vr], w=[R_vg], q="pool")
                else:
                    B.dma(vgA_v[ctxj, cg * 4:(cg + 1) * 4, gt, :, :].rearrange("h p d -> p h d"), vs[:, 0:4, :], r=[vr], w=[R_vgA], q="pool")
        B.cut(7)
        wt, wr = B.load_w(win_d, 8, C_F, 16)
        for tt_ in range(ntt):
            gt = (t0 // 128) + tt_
            pt, pr = banks[6]
            B.mms(pt[:, 0:16], [(uT[:, kc, tt_ * 128:(tt_ + 1) * 128], wt[:, kc, :]) for kc in range(8)], r=[wr, R_u], w=[pr])
            ta, tr = nextA()
            softplus_to(ta[:, 0:16], pt[:, 0:16], bfb[:], 16, r=[pr, R_sm], w=[tr], neg_in=True)
            B.ts(logf[:, gt, :], ta[:, 0:16], -1.0, None, ALU.mult, r=[tr], w=[R_lf])
            if ctxj is None:
                B.dma(lf_o[gt * 128:(gt + 1) * 128, :], logf[:, gt, :], r=[R_lf], w=[], q="pool")
        if B.stage < 2:
            return
        wt, wr = B.load_w(win_d, 8, C_DT, 32)
        for tt_ in range(ntt):
            gt = (t0 // 128) + tt_
            pt, pr = banks[6]
            B.mms(pt[:, 0:32], [(uT[:, kc, tt_ * 128:(tt_ + 1) * 128], wt[:, kc, :]) for kc in range(8)], r=[wr, R_u], w=[pr])
            softplus_to(dtsb[:, gt, :], pt[:, 0:32], dtb[:], 32, r=[pr, R_sm], w=[R_dt])
        for cg in range(8 if ctxj is None else 0):
            wt, wr = B.load_w(win_d, 8, C_Z + cg * 256, 256)
            for tt_ in range(ntt):
                pt, pr = banks[4 + tt_ % 2]
                B.mms(pt[:, 0:256], [(uT[:, kc, tt_ * 128:(tt_ + 1) * 128], wt[:, kc, :]) for kc in range(8)],
                      r=[wr, R_u], w=[pr])
                tb_, tbr = nextB()
                B.cp(tb_[:, 0:256], pt[:, 0:256], r=[pr], w=[tbr], eng="act")
                B.dma(z_s[t0 + tt_ * 128:t0 + (tt_ + 1) * 128, cg * 256:(cg + 1) * 256], tb_[:, 0:256], r=[tbr], w=[R_z], q="pool")
        for c in range(24 if ctxj is None else 20):
            wt, wr = B.load_w(win_d, 8, C_X + c * 128, 128)
            pt, pr = banks[c % 2]
            B.mms(pt[:, 0:ncols], [(wt[:, kc, :], uT[:, kc, 0:ncols]) for kc in range(8)], r=[wr, R_u], w=[pr])
            xbt, xr = xb[c % 2]
            B.cp(xbt[:, 0:3], halo[:, c, :], r=[R_halo], w=[xr])
            B.cp(xbt[:, 3:3 + ncols], pt[:, 0:ncols], r=[pr], w=[xr])
            B.cp(halo[:, c, :], xbt[:, ncols:ncols + 3], r=[xr], w=[R_halo])
            if blk == NBLK - 1 and ctxj is None:
                B.dma(cv_o[c * 128:(c + 1) * 128, :], xbt[:, ncols:ncols + 3], r=[xr], w=[], q="pool")
            ta, tr = nextA()
            B.ts(ta[:, 0:ncols], xbt[:, 0:ncols], cw[:, c, 0:1], cbv[:, c:c + 1], ALU.mult, ALU.add, r=[xr, R_sm], w=[tr])
            for i in range(1, 4):
                B.stt(ta[:, 0:ncols], xbt[:, i:i + ncols], cw[:, c, i:i + 1], ta[:, 0:ncols], ALU.mult, ALU.add,
                      r=[xr, R_sm, tr], w=[tr])
            tb_, tbr = nextB()
            B.act(tb_[:, 0:ncols], ta[:, 0:ncols], AF.Silu, r=[tr], w=[tbr])
            if c >= 20:
                B.dma(CT_s[(c - 20) * 128:(c - 19) * 128, t0:t0 + ncols], tb_[:, 0:ncols], r=[tbr], w=[R_CT], q="pool")
                continue
            if c >= 16 and ctxj is None:
                B.dma(BT_s[(c - 16) * 128:(c - 15) * 128, t0:t0 + ncols], tb_[:, 0:ncols], r=[tbr], w=[R_BT], q="pool")
            for tt_ in range(ntt):
                P.add("pe", lambda e, tt_=tt_, tb_=tb_: e.transpose(ptT[:, tt_ * 128:(tt_ + 1) * 128],
                                                                     tb_[:, tt_ * 128:(tt_ + 1) * 128], ident_b[:]),
                      r=[tbr, R_const], w=[prT])
            tb2, tbr2 = nextB()
            B.cp(tb2[:, 0:ncols], ptT[:, 0:ncols], r=[prT], w=[tbr2])
            for tt_ in range(ntt):
                rows = slice(t0 + tt_ * 128, t0 + (tt_ + 1) * 128)
                if c < 16:
                    B.dma(xs_s[rows, c * 128:(c + 1) * 128], tb2[:, tt_ * 128:(tt_ + 1) * 128], r=[tbr2], w=[R_xs], q="pool")
                else:
                    B.dma(Bt_s[rows, (c - 16) * 128:(c - 15) * 128], tb2[:, tt_ * 128:(tt_ + 1) * 128], r=[tbr2], w=[R_Bt], q="pool")
        for gi, (c0, dst, rr) in enumerate([(C_GA, ga_s, R_ga), (C_GS, gs_s, R_gs)] if ctxj is None else []):
            for c in range(8):
                wt, wr = B.load_w(win_d, 8, c0 + c * 128, 128)
                pt, pr = banks[2 + c % 2]
                B.mms(pt[:, 0:ncols], [(wt[:, kc, :], uT[:, kc, 0:ncols]) for kc in range(8)], r=[wr, R_u], w=[pr])
                tb_, tbr = nextB()
                B.act(tb_[:, 0:ncols], pt[:, 0:ncols], AF.Sigmoid, r=[pr], w=[tbr])
                B.dma(dst[c * 128:(c + 1) * 128, t0:t0 + ncols], tb_[:, 0:ncols], r=[tbr], w=[rr], q="pool")


    xTd_v = xT_d.rearrange("(k p) t -> p k t", p=128)
    x1o_v = x1_o.rearrange("(k p) t -> p k t", p=128)

    xs1 = B.sb("xs1", [128, 8, 32], F32)
    sconv = B.sb("sconv", [128, 24, 2, 3], F32)
    R_xs1 = Res()

    def run_own_p1():
        NS_ = 32
        NX_ = 35
        sgroups = [(0, 16, 1), (16, 16, 2), (32, 3, 0)]
        B.dma(xT[:, :, 0:NX_], xsT_d.rearrange("(k p) t -> p k t", p=128), w=[R_x])
        rms_mod(xT, 0, sgroups, NX_)
        ffn(0, w1a_d, w3a_d, w2a_d, sgroups, NX_)
        rms_mod(xT, 1, sgroups, NX_)
        B.dma(sconv[:], scv_d[:, :, :, :], w=[R_sm])
        B.cp(xs1[:], xT[:, :, 0:NS_], r=[R_x], w=[R_xs1])
        for h in range(H_A):
            wt, wr = B.load_w(win_d, 8, C_Q + h * 64, 64)
            pt, pr = banks[h % 2]
            B.mms(pt[0:64, 0:NS_], [(wt[:, kc, :], uT[:, kc, 0:NS_]) for kc in range(8)], r=[wr, R_u], w=[pr])
            tb_, tbr = nextB()
            B.ts(tb_[0:64, 0:NS_], pt[0:64, 0:NS_], 0.125, None, ALU.mult, r=[pr], w=[tbr])
            B.dma(q2_s[h, 0:64, :], tb_[0:64, 0:NS_], r=[tbr], w=[R_s2], q="pool")
            wt, wr = B.load_w(win_d, 8, C_K + h * 64, 64)
            pt, pr = banks[2 + h % 2]
            B.mms(pt[0:64, 0:NS_], [(wt[:, kc, :], uT[:, kc, 0:NS_]) for kc in range(8)], r=[wr, R_u], w=[pr])
            kf, kr = kst[h % 2]
            B.cp(kf[:, 0:NS_], pt[0:64, 0:NS_], r=[pr], w=[kr])
            B.dma(ksT_o[h * 64:(h + 1) * 64, :], kf[:, 0:NS_], r=[kr], w=[], q="pool")
            tb_, tbr = nextB()
            B.cp(tb_[0:64, 0:NS_], pt[0:64, 0:NS_], r=[pr], w=[tbr], eng="act")
            B.dma(k2_s[h, 0:64, :], tb_[0:64, 0:NS_], r=[tbr], w=[R_s2], q="pool")
            B.dma(k2_s[h, 64:66, :], onesrow[64:66, 0:NS_], r=[R_const], w=[R_s2], q="pool")
        for cg in range(4):
            wt, wr = B.load_w(win_d, 8, C_V + cg * 256, 256)
            for sq_ in range(2):
                pt, pr = banks[4 + sq_]
                B.mms(pt[0:16, 0:256], [(uT[:, kc, sq_ * 16:(sq_ + 1) * 16], wt[:, kc, :]) for kc in range(8)], r=[wr, R_u], w=[pr])
                ta, tr = nextA()
                B.cp(ta[0:16, 0:256], pt[0:16, 0:256], r=[pr], w=[tr])
                B.dma(vs_o[sq_ * 16:(sq_ + 1) * 16, cg * 256:(cg + 1) * 256], ta[0:16, 0:256], r=[tr], w=[], q="pool")
                vs, vr = vst[sq_]
                B.cp(vs[0:16, 0:4, 0:64], pt[0:16, 0:256].rearrange("p (h d) -> p h d", h=4), r=[pr], w=[vr], eng="act")
                B.dma(v2_s[cg * 4:(cg + 1) * 4, sq_, :, :].rearrange("h p d -> p h d"), vs[0:16, 0:4, :], r=[vr], w=[R_s2], q="pool")
        wt, wr = B.load_w(win_d, 8, C_F, 16)
        for sq_ in range(2):
            pt, pr = banks[6]
            B.mms(pt[0:16, 0:16], [(uT[:, kc, sq_ * 16:(sq_ + 1) * 16], wt[:, kc, :]) for kc in range(8)], r=[wr, R_u], w=[pr])
            ta, tr = nextA()
            softplus_to(ta[0:16, 0:16], pt[0:16, 0:16], bfb[0:16, :], 16, r=[pr, R_sm], w=[tr], neg_in=True, pn=16)
            B.ts(logf[0:16, NT + sq_, :], ta[0:16, 0:16], -1.0, None, ALU.mult, r=[tr], w=[R_lf])
            B.dma(lfs_o[sq_ * 16:(sq_ + 1) * 16, :], logf[0:16, NT + sq_, :], r=[R_lf], w=[], q="pool")
        if B.stage >= 2:
            wt, wr = B.load_w(win_d, 8, C_DT, 32)
            for sq_ in range(2):
                pt, pr = banks[6]
                B.mms(pt[0:16, 0:32], [(uT[:, kc, sq_ * 16:(sq_ + 1) * 16], wt[:, kc, :]) for kc in range(8)], r=[wr, R_u], w=[pr])
                softplus_to(dtsb[0:16, NT + sq_, :], pt[0:16, 0:32], dtb[0:16, :], 32, r=[pr, R_sm], w=[R_dt], pn=16)
            for cg in range(8):
                wt, wr = B.load_w(win_d, 8, C_Z + cg * 256, 256)
                for sq_ in range(2):
                    pt, pr = banks[4 + sq_]
                    B.mms(pt[0:16, 0:256], [(uT[:, kc, sq_ * 16:(sq_ + 1) * 16], wt[:, kc, :]) for kc in range(8)], r=[wr, R_u], w=[pr])
                    tb_, tbr = nextB()
                    B.cp(tb_[0:16, 0:256], pt[0:16, 0:256], r=[pr], w=[tbr], eng="act")
                    B.dma(z2_s[sq_ * 16:(sq_ + 1) * 16, cg * 256:(cg + 1) * 256], tb_[0:16, 0:256], r=[tbr], w=[R_s2], q="pool")
            for c in range(24):
                wt, wr = B.load_w(win_d, 8, C_X + c * 128, 128)
                pt, pr = banks[c % 2]
                B.mms(pt[:, 0:NX_], [(wt[:, kc, :], uT[:, kc, 0:NX_]) for kc in range(8)], r=[wr, R_u], w=[pr])
                xbt, xr = xb[c % 2]
                B.cp(xbt[:, 0:3], sconv[:, c, 0, :], r=[R_sm], w=[xr])
                B.cp(xbt[:, 3:19], pt[:, 0:16], r=[pr], w=[xr])
                B.cp(xbt[:, 19:22], sconv[:, c, 1, :], r=[R_sm], w=[xr])
                B.cp(xbt[:, 22:38], pt[:, 16:32], r=[pr], w=[xr])
                B.ts(halo[:, c, :], pt[:, 32:35], flg[:, 16:17], None, ALU.mult, r=[pr, R_sm], w=[R_halo])
                B.dma(cvs_o[c * 128:(c + 1) * 128, 0:3], xbt[:, 16:19], r=[xr], w=[], q="pool")
                B.dma(cvs_o[c * 128:(c + 1) * 128, 3:6], xbt[:, 35:38], r=[xr], w=[], q="pool")
                ta, tr = nextA()
                B.ts(ta[:, 0:35], xbt[:, 0:35], cw[:, c, 0:1], cbv[:, c:c + 1], ALU.mult, ALU.add, r=[xr, R_sm], w=[tr])
                for i in range(1, 4):
                    B.stt(ta[:, 0:35], xbt[:, i:i + 35], cw[:, c, i:i + 1], ta[:, 0:35], ALU.mult, ALU.add, r=[xr, R_sm, tr], w=[tr])
                tb_, tbr = nextB()
                B.act(tb_[:, 0:35], ta[:, 0:35], AF.Silu, r=[tr], w=[tbr])
                offs = (0, 19)
                if c >= 20:
                    for sq_ in range(2):
                        B.dma(CT2_s[(c - 20) * 128:(c - 19) * 128, sq_ * 16:(sq_ + 1) * 16], tb_[:, offs[sq_]:offs[sq_] + 16], r=[tbr], w=[R_s2], q="pool")
                    continue
                if c >= 16:
                    for sq_ in range(2):
                        B.dma(BT2_s[(c - 16) * 128:(c - 15) * 128, sq_ * 16:(sq_ + 1) * 16], tb_[:, offs[sq_]:offs[sq_] + 16], r=[tbr], w=[R_s2], q="pool")
                for sq_ in range(2):
                    P.add("pe", lambda e, sq_=sq_, tb_=tb_: e.transpose(ptT[0:16, sq_ * 128:(sq_ + 1) * 128], tb_[:, offs[sq_]:offs[sq_] + 16], ident_b[:]),
                          r=[tbr, R_const], w=[prT])
                tb2, tbr2 = nextB()
                B.cp(tb2[0:16, 0:256], ptT[0:16, 0:256], r=[prT], w=[tbr2])
                for sq_ in range(2):
                    rows2 = slice(sq_ * 16, (sq_ + 1) * 16)
                    if c < 16:
                        B.dma(xs2_s[rows2, c * 128:(c + 1) * 128], tb2[0:16, sq_ * 128:(sq_ + 1) * 128], r=[tbr2], w=[R_s2], q="pool")
                    else:
                        B.dma(Bt2_s[rows2, (c - 16) * 128:(c - 15) * 128], tb2[0:16, sq_ * 128:(sq_ + 1) * 128], r=[tbr2], w=[R_s2], q="pool")
            for (c0, dst) in [(C_GA, ga2_s), (C_GS, gs2_s)]:
                for c in range(8):
                    wt, wr = B.load_w(win_d, 8, c0 + c * 128, 128)
                    pt, pr = banks[2 + c % 2]
                    B.mms(pt[:, 0:NS_], [(wt[:, kc, :], uT[:, kc, 0:NS_]) for kc in range(8)], r=[wr, R_u], w=[pr])
                    tb_, tbr = nextB()
                    B.act(tb_[:, 0:NS_], pt[:, 0:NS_], AF.Sigmoid, r=[pr], w=[tbr])
                    B.dma(dst[c * 128:(c + 1) * 128, :], tb_[:, 0:NS_], r=[tbr], w=[R_s2], q="pool")

        xTd_v = xT_d.rearrange("(k p) t -> p k t", p=128)
        x1o_v = x1_o.rearrange("(k p) t -> p k t", p=128)
        for blk in range(NBLK):
            t0 = blk * TB
            groups = [(0, TB, 0)]
            B.dma(xT[:, :, :], xTd_v[:, :, t0:t0 + TB], w=[R_x])
            if B.cutn <= 2:
                B.dma(x1o_v[:, :, t0:t0 + TB], xT[:, :, :], r=[R_x], w=[R_x1], q="pool")
            B.cut(2)
            rms_mod(xT, 0, groups, TB)
            if B.cutn <= 3:
                B.cp(xT[:, :, :], uT[:, :, :], r=[R_u], w=[R_x])
                B.dma(x1o_v[:, :, t0:t0 + TB], xT[:, :, :], r=[R_x], w=[R_x1], q="pool")
            B.cut(3)
            ffn(0, w1a_d, w3a_d, w2a_d, groups, TB)
            B.dma(x1o_v[:, :, t0:t0 + TB], xT[:, :, :], r=[R_x], w=[R_x1], q="pool")
            B.cut(4)
            rms_mod(xT, 1, groups, TB)
            win_block(blk, groups, TB)


    if B.stage < 3:
        run_own_p1()
        return
    tri = B.sb("tri", [128, 128], F32)
    mneg = B.sb("mneg", [128, 128], F32)
    e0row = B.sb("e0row", [128, 128], F32)
    selt = B.sb("selt", [16, 16, 66], F32)
    selc = B.sb("selc", [128, 2], F32)
    a_bc = B.sb("a_bc", [128, 32], F32)
    dsk = B.sb("dsk", [128, 32], F32)
    gssd = B.sb("gssd", [128, 16], F32)
    R_c3 = Res()
    B.dma(tri[:], tri_d[:, :], w=[R_c3])
    B.dma(mneg[:], mneg_d[:, :], w=[R_c3])
    B.dma(e0row[:], e0_d[:, :], w=[R_c3])
    B.dma(selt[:], sel_d[:, :, :], w=[R_c3])
    B.dma(selc[:], selc_d[:, :], w=[R_c3])
    B.dma(a_bc[:], alog_d[:, :], w=[R_c3])
    B.dma(dsk[:], dsk_d[:, :], w=[R_c3])
    B.dma(gssd[:], gssd_d[:, :], w=[R_c3])
    B.act(a_bc[:], a_bc[:], AF.Exp, r=[R_c3], w=[R_c3])
    B.ts(a_bc[:], a_bc[:], -1.0, None, ALU.mult, r=[R_c3], w=[R_c3])

    Fc = B.sb("Fc", [128, NT, 16], F32)
    carry = B.sb("carry", [128, 16], F32)
    R_F, R_carry = Res(), Res()
    arena = B.sb("arena", [128, 8192], BF16)
    R_ar = [Res() for _ in range(4)]
    xtm_s = [arena[:, 0:2048], arena[:, 2048:4096]]
    ztm_s = [arena[:, 4096:6144], arena[:, 6144:8192]]
    oT = arena[0:64, :].rearrange("p (h t) -> p h t", h=16)
    bc_s = [(B.sb(f"bcs{i}", [128, 3, 512], BF16), Res()) for i in range(2)]
    Hst = B.sb("Hst", [128, DIN], F32)
    Hb = B.sb("Hb", [128, DIN], BF16)
    yz = B.sb("yz", [128, DIN], F32)
    sm = B.sb("sm", [128, 8, 32], F32)
    cbm = B.sb("cbm", [128, 4, 128], F32)
    xd = B.sb("xd", [128, DIN], BF16)
    ynb = xd
    wTb = [(B.sb(f"wTb{i}", [128, 4, 128], BF16), Res()) for i in range(2)]
    D4 = [(B.sb(f"D4{i}", [128, 4, 128], F32), Res()) for i in range(2)]
    sg4 = [(B.sb(f"sg4{i}", [128, 4, 128], F32), Res()) for i in range(2)]
    ssq = B.sb("ssq", [128, 4], F32)
    R_H, R_Hb, R_yz, R_ynb, R_sm, R_cbm, R_xd, R_ssq = [Res() for _ in range(8)]
    R_ynb = R_xd
    P.add("pool", lambda e: e.memset(Hst[:], 0.0), w=[R_H])
    ynT = gT[:, 0:16, :]
    BT_v = BT_s.rearrange("(g n) t -> n g t", n=128)
    CT_v = CT_s.rearrange("(g n) t -> n g t", n=128)

    def bc3(ap2, n, m):
        return ap2.unsqueeze(2).broadcast_to([128, n, m])

    def ssd_tile(gt, want_y=True, L=128, samp=None, maskj=None):
        sl = gt % 2
        if samp is None:
            rows = slice(gt * 128, (gt + 1) * 128)
            src_x, src_z, src_Bt, src_BT, src_CT = xs_s, z_s, Bt_s, BT_v, CT_v
            rx_, rz_, rbt_, rBT_, rCT_ = R_xs, R_z, R_Bt, R_BT, R_CT
            dti = gt
            ycol0 = (gt % 4) * 128
        else:
            sl = samp
            rows = slice(samp * 16, samp * 16 + 16)
            src_x, src_z, src_Bt, src_BT, src_CT = xs2_s, z2_s, Bt2_s, BT2_v, CT2_v
            rx_ = rz_ = rbt_ = rBT_ = rCT_ = R_s2
            dti = NT + samp
            ycol0 = samp * 16
        xtm, ztm = xtm_s[sl], ztm_s[sl]
        Rx_, Rz_ = R_ar[sl], R_ar[2 + sl]
        bct, Rb_ = bc_s[sl]
        B.dma(xtm[0:L, :], src_x[rows, :], r=[rx_], w=[Rx_])
        if want_y:
            B.dma(ztm[0:L, :], src_z[rows, :], r=[rz_], w=[Rz_])
        B.dma(bct[0:L, 0, :], src_Bt[rows, :], r=[rbt_], w=[Rb_])
        if want_y:
            B.dma(bct[:, 1, 0:4 * L].rearrange("p (g t) -> p g t", g=4), src_BT[:, :, rows], r=[rBT_], w=[Rb_])
            B.dma(bct[:, 2, 0:4 * L].rearrange("p (g t) -> p g t", g=4), src_CT[:, :, rows], r=[rCT_], w=[Rb_])
        Btm = bct[0:L, 0, :].rearrange("p (g n) -> p g n", g=4)
        BTf = bct[:, 1, 0:4 * L].rearrange("p (g t) -> p g t", g=4)
        CTf = bct[:, 2, 0:4 * L].rearrange("p (g t) -> p g t", g=4)
        xt3 = xtm[0:L, :].rearrange("p (h d) -> p h d", h=32)
        dA, acs, ea, de, tmpv = [sm[0:L, j, :] for j in (0, 1, 3, 4, 6)]
        al, cd = sm[:, 2, :], sm[:, 5, :]
        dtv = dtsb[0:L, dti, :]
        B.tt(dA, dtv, a_bc[0:L, :], ALU.mult, r=[R_dt, R_c3], w=[R_sm])
        pt, pr = banks[0]
        P.add("pe", lambda e: e.matmul(pt[0:L, 0:32], tri[0:L, 0:L], dA, start=True, stop=True), r=[R_c3, R_sm], w=[pr])
        P.add("pe", lambda e: e.matmul(pt[:, 32:64], ones_f[0:L, :], dA, start=True, stop=True), r=[R_const, R_sm], w=[pr])
        B.cp(acs, pt[0:L, 0:32], r=[pr], w=[R_sm])
        B.cp(al, pt[:, 32:64], r=[pr], w=[R_sm])
        B.act(ea, acs, AF.Exp, r=[R_sm], w=[R_sm])
        B.act(cd, al, AF.Exp, r=[R_sm], w=[R_sm])
        if maskj is not None:
            B.ts(cd, cd, -1.0, flg[:, maskj:maskj + 1], ALU.add, ALU.mult, r=[R_sm], w=[R_sm])
            B.ts(cd, cd, 1.0, None, ALU.add, r=[R_sm], w=[R_sm])
        B.tt(tmpv, al[0:L, :], acs, ALU.subtract, r=[R_sm], w=[R_sm])
        B.act(tmpv, tmpv, AF.Exp, r=[R_sm], w=[R_sm])
        B.tt(de, tmpv, dtv, ALU.mult, r=[R_sm, R_dt], w=[R_sm])
        if want_y:
            B.cp(Hb[:], Hst[:], r=[R_H], w=[R_Hb], eng="act")
            pc, prc = banks[1]
            for g in range(4):
                P.add("pe", lambda e, g=g: e.matmul(pc[0:L, g * L:(g + 1) * L], BTf[:, g, :], CTf[:, g, :], start=True, stop=True),
                      r=[Rb_], w=[prc])
            cbv_ = cbm[:].rearrange("p g t -> p (g t)")[0:L, 0:4 * L].rearrange("p (g t) -> p g t", g=4)
            B.tt(cbv_, pc[0:L, 0:4 * L].rearrange("p (g t) -> p g t", g=4), tri[0:L, 0:L].unsqueeze(1).broadcast_to([L, 4, L]), ALU.mult,
                 r=[prc, R_c3], w=[R_cbm])
            for g in range(4):
                pyd, pryd = banks[4]
                for hb in range(2):
                    h0 = g * 8 + hb * 4
                    d4t, rd4 = D4[hb]
                    s4t, rs4 = sg4[hb]
                    wt4t, rw4 = wTb[hb]
                    d4 = d4t[:].rearrange("p g t -> p (g t)")[0:L, 0:4 * L].rearrange("p (g t) -> p g t", g=4)
                    s4 = s4t[:].rearrange("p g t -> p (g t)")[0:L, 0:4 * L].rearrange("p (g t) -> p g t", g=4)
                    wt4 = wt4t[:].rearrange("p g t -> p (g t)")[0:L, 0:4 * L].rearrange("p (g t) -> p g t", g=4)
                    B.tt(d4, ident_f[0:L, 0:L].unsqueeze(1).broadcast_to([L, 4, L]),
                         acs[:, h0:h0 + 4].unsqueeze(2).broadcast_to([L, 4, L]), ALU.mult, r=[R_const, R_sm], w=[rd4])
                    pb, prb = banks[2 + hb]
                    P.add("pe", lambda e, d4t=d4t, pb=pb: e.matmul(pb[0:L, 0:4 * L], ones_f[0:L, 0:L],
                                                                    d4t[:].rearrange("p g t -> p (g t)")[0:L, 0:4 * L], start=True, stop=True),
                          r=[R_const, rd4], w=[prb])
                    for j in range(4):
                        B.stt(s4[:, j, :], pb[0:L, j * L:(j + 1) * L], acs[:, h0 + j:h0 + j + 1], mneg[0:L, 0:L], ALU.subtract, ALU.add,
                              r=[prb, R_sm, R_c3], w=[rs4])
                    B.act(s4, s4, AF.Exp, r=[rs4], w=[rs4])
                    for j in range(4):
                        B.stt(wt4[:, j, :], s4[:, j, :], dtv[:, h0 + j:h0 + j + 1], cbv_[:, g, :], ALU.mult, ALU.mult,
                              r=[rs4, R_dt, R_cbm], w=[rw4])
                    for j in range(4):
                        hh = hb * 4 + j
                        P.add("pe", lambda e, j=j, hh=hh, wt4=wt4, h0=h0: e.matmul(pyd[0:L, hh * 64:(hh + 1) * 64], wt4[:, j, :], xt3[:, h0 + j, :],
                                                                                    start=True, stop=True), r=[rw4, Rx_], w=[pryd])
                pyo, pryo = banks[5]
                P.add("pe", lambda e, g=g, pyo=pyo: e.matmul(pyo[0:L, :], CTf[:, g, :], Hb[:, g * 512:(g + 1) * 512], start=True, stop=True),
                      r=[Rb_, R_Hb], w=[pryo])
                yg = yz[0:L, g * 512:(g + 1) * 512].rearrange("p (h d) -> p h d", h=8)
                B.tt(yg, pyo[0:L, :].rearrange("p (h d) -> p h d", h=8), ea[:, g * 8:(g + 1) * 8].unsqueeze(2).broadcast_to([L, 8, 64]),
                     ALU.mult, r=[pryo, R_sm], w=[R_yz])
                B.tt(yg, pyd[0:L, :].rearrange("p (h d) -> p h d", h=8), yg, ALU.add, r=[pryd, R_yz], w=[R_yz])
                ta, tr = nextA()
                ta3 = ta[0:L, :].rearrange("p (h d) -> p h d", h=8)
                B.tt(ta3, xt3[:, g * 8:(g + 1) * 8, :], dsk[0:L, g * 8:(g + 1) * 8].unsqueeze(2).broadcast_to([L, 8, 64]), ALU.mult,
                     r=[Rx_, R_c3], w=[tr])
                B.tt(yg, yg, ta3, ALU.add, r=[tr, R_yz], w=[R_yz])
        B.tt(xd[0:L, :].rearrange("p (h d) -> p h d", h=32), xt3, de.unsqueeze(2).broadcast_to([L, 32, 64]), ALU.mult,
             r=[Rx_, R_sm], w=[R_xd])
        for g in range(4):
            pS, prS = banks[6]
            P.add("pe", lambda e, g=g, pS=pS: e.matmul(pS[:, :], Btm[:, g, :], xd[0:L, g * 512:(g + 1) * 512], start=True, stop=True),
                  r=[Rb_, R_xd], w=[prS])
            Hg = Hst[:, g * 512:(g + 1) * 512].rearrange("p (h d) -> p h d", h=8)
            B.tt(Hg, Hg, bc3(cd[:, g * 8:(g + 1) * 8], 8, 64), ALU.mult, r=[R_sm, R_Hb], w=[R_H])
            if maskj is not None:
                B.stt(Hg, pS[:, :].rearrange("p (h d) -> p h d", h=8), flg[:, maskj:maskj + 1], Hg, ALU.mult, ALU.add, r=[prS], w=[R_H])
            else:
                B.tt(Hg, Hg, pS[:, :].rearrange("p (h d) -> p h d", h=8), ALU.add, r=[prS], w=[R_H])
        if not want_y:
            return
        if samp is None:
            B.dma(dbg_y[rows, :], yz[:], r=[R_yz], w=[], q="pool")
        for g in range(4):
            ta, tr = nextA()
            B.act(ta[0:L, :], ztm[0:L, g * 512:(g + 1) * 512], AF.Silu, r=[Rz_], w=[tr])
            B.tt(yz[0:L, g * 512:(g + 1) * 512], yz[0:L, g * 512:(g + 1) * 512], ta[0:L, :], ALU.mult, r=[tr, R_yz], w=[R_yz])
            ta2, tr2 = nextA()
            P.add("act", lambda e, g=g, ta2=ta2: e.activation(out=ta2[0:L, :], in_=yz[0:L, g * 512:(g + 1) * 512], func=AF.Square,
                                                               accum_out=ssq[0:L, g:g + 1]), r=[R_yz], w=[tr2, R_ssq])
        rs_ = sm[0:L, 7, 0:1]
        P.add("dve", lambda e: e.tensor_reduce(out=rs_, in_=ssq[0:L, 0:4], axis=AX.X, op=ALU.add), r=[R_ssq], w=[R_sm])
        B.act(rs_, rs_, AF.Sqrt, r=[R_sm, R_const], w=[R_sm], bias=epsc[0:L, :], scale=1.0 / DIN)
        P.add("dve", lambda e: e.reciprocal(out=rs_, in_=rs_), r=[R_sm], w=[R_sm])
        B.ts(ynb[0:L, :], yz[0:L, :], rs_, None, ALU.mult, r=[R_yz, R_sm], w=[R_ynb])
        for half in range(2):
            for c in range(8):
                cc = half * 8 + c
                P.add("pe", lambda e, c=c, cc=cc: e.transpose(ptT[:, c * 128:c * 128 + L], ynb[0:L, cc * 128:(cc + 1) * 128], ident_b[0:L, 0:L]),
                      r=[R_ynb, R_const], w=[prT])
            for c in range(8):
                cc = half * 8 + c
                B.ts(ynT[:, cc, ycol0:ycol0 + L], ptT[:, c * 128:c * 128 + L], gssd[:, cc:cc + 1], None, ALU.mult,
                     r=[prT, R_c3], w=[R_gT])

    Kt = [(B.sb(f"Kt{i}", [66, T], BF16), Res()) for i in range(2)]
    Vt = [(B.sb(f"Vt{i}", [128, NT, 65], BF16), Res()) for i in range(2)]
    Qa = [(B.sb(f"Qa{i}", [66, TB], BF16), Res()) for i in range(2)]
    Pt = [(B.sb(f"Pt{i}", [128, TB], BF16), Res()) for i in range(3)]
    FT = sq[0:16, :]
    rbc = B.sb("rbc", [128, 16], F32)
    biasT = B.sb("biasT", [128, NT, 16], F32)
    arb = B.sb("arb", [66, TB], BF16)
    R_FT, R_rbc, R_bias, R_arow, R_rl, R_rlb = [Res() for _ in range(6)]
    R_FT = R_sq
    pti = [0]

    def attn_block(blk):
        t0 = blk * TB
        nkt = 4 * (blk + 1)
        pf, prf = banks[6]
        for j in range(4):
            P.add("pe", lambda e, j=j: e.transpose(pf[0:16, j * 128:(j + 1) * 128], Fc[:, 4 * blk + j, :], ident_f[:]),
                  r=[R_F, R_const], w=[prf])
        B.cp(FT[:, :], pf[0:16, :], r=[prf], w=[R_FT])
        P.add("pe", lambda e: e.matmul(pf[:, 0:16], e0row[:], Fc[:, 4 * blk, :], start=True, stop=True), r=[R_c3, R_F], w=[prf])
        B.cp(rbc[:], pf[:, 0:16], r=[prf], w=[R_rbc])
        B.tt(biasT[:, 0:nkt, :], rbc[:].unsqueeze(1).broadcast_to([128, nkt, 16]), Fc[:, 0:nkt, :], ALU.subtract,
             r=[R_rbc, R_F], w=[R_bias])
        B.tt(rdj[:], delta[:, 0:NCORES, :], rbc[:].unsqueeze(1).broadcast_to([128, NCORES, 16]), ALU.add,
             r=[R_delta, R_rbc], w=[R_rdj])
        kvi = [0]
        for h in range(H_A):
            qa_, qr_ = Qa[h % 2]
            B.dma(qa_[0:64, :], q_s[h, 0:64, t0:t0 + TB], r=[R_q], w=[qr_])
            pa, pra = banks[4]
            P.add("pe", lambda e, h=h: e.matmul(pa[0:66, :], selt[:, h, :], FT[:, :], start=True, stop=True), r=[R_c3, R_FT], w=[pra])
            ar0, rr0 = nextA()
            ar1, rr1 = nextA()
            ar2, rr2 = nextA()
            B.cp(ar0[64:66, :], pa[64:66, :], r=[pra], w=[rr0])
            B.ts(ar1[64:66, :], ar0[64:66, :], ar0[64:66, 0:1], None, ALU.subtract, r=[rr0], w=[rr1])
            B.cp(arb[64:66, :], ar1[64:66, :], r=[rr1], w=[R_arow])
            B.tt(ar0[64:66, :], ar1[64:66, :], arb[64:66, :], ALU.subtract, r=[rr1, R_arow], w=[rr0])
            B.ts(ar2[64:66, :], arb[64:66, :], selc[64:66, 0:1], None, ALU.mult, r=[R_arow, R_c3], w=[rr2])
            B.stt(qa_[64:66, :], ar0[64:66, :], selc[64:66, 1:2], ar2[64:66, :], ALU.mult, ALU.add,
                  r=[rr0, rr2, R_c3], w=[qr_])
            po, pro = banks[2 + h % 2]
            first = [True]
            if CTX:
                for j in range(NCORES - 1):
                    kvi[0] += 1
                    kt_, kr_ = Kt[kvi[0] % 2]
                    vt_, vr_ = Vt[kvi[0] % 2]
                    B.dma(kt_[:, :], kgA_v[j, h, :, :], r=[R_kgA], w=[kr_])
                    B.dma(vt_[:, :, :], vgA_v[j, h, :, :, :].rearrange("t p d -> p t d"), r=[R_vgA], w=[vr_])
                    bj, bjr = biasJ[kvi[0] % 2]
                    B.ts(bj[:, :], FcA[:, j, :].rearrange("p (t h) -> p t h", h=16)[:, :, h], -1.0, rdj[:, j, h:h + 1], ALU.mult, ALU.add,
                         r=[R_FcA, R_rdj], w=[bjr])
                    B.ts(bj[:, :], bj[:, :], flg[:, j:j + 1], flg[:, 8 + j:9 + j], ALU.mult, ALU.add, r=[R_sm], w=[bjr])
                    for kt in range(NT):
                        ps_, prs = banks[kt % 2]
                        P.add("pe", lambda e, kt=kt, ps_=ps_, kt_=kt_, qa_=qa_: e.matmul(ps_[:, :], kt_[:, kt * 128:(kt + 1) * 128], qa_[:, :],
                                                                                         start=True, stop=True), r=[kr_, qr_], w=[prs])
                        pti[0] += 1
                        pT_, prp = Pt[pti[0] % 3]
                        B.act(pT_[:, :], ps_[:, :], AF.Exp, r=[prs, bjr], w=[prp], bias=bj[:, kt:kt + 1], scale=1.0)
                        st_ = first[0]
                        first[0] = False
                        P.add("pe", lambda e, kt=kt, pT_=pT_, vt_=vt_, po=po, st_=st_: e.matmul(po[0:65, :], vt_[:, kt, :], pT_[:, :],
                                                                                              start=st_, stop=False, skip_group_check=True),
                              r=[vr_, prp], w=[pro])
            kvi[0] += 1
            kt_, kr_ = Kt[kvi[0] % 2]
            vt_, vr_ = Vt[kvi[0] % 2]
            B.dma(kt_[:, 0:t0 + TB], kg_s[h, :, 0:t0 + TB], r=[R_kg], w=[kr_])
            B.dma(vt_[:, 0:nkt, :], vg_s[h, 0:nkt, :, :].rearrange("t p d -> p t d"), r=[R_vg], w=[vr_])
            for kt in range(nkt):
                j = kt - 4 * blk
                c0 = 128 * j if j > 0 else 0
                ps_, prs = banks[kt % 2]
                P.add("pe", lambda e, kt=kt, c0=c0, ps_=ps_, kt_=kt_, qa_=qa_: e.matmul(ps_[:, c0:TB], kt_[:, kt * 128:(kt + 1) * 128], qa_[:, c0:TB],
                                                                                       start=True, stop=True), r=[kr_, qr_], w=[prs])
                pti[0] += 1
                pT_, prp = Pt[pti[0] % 3]
                bias_ap = biasT[:, kt, h:h + 1]
                if j >= 0:
                    ta, tr = nextA()
                    B.tt(ta[:, c0:c0 + 128], ps_[:, c0:c0 + 128], mneg[:], ALU.add, r=[prs, R_c3], w=[tr])
                    B.act(pT_[:, c0:c0 + 128], ta[:, c0:c0 + 128], AF.Exp, r=[tr, R_bias], w=[prp], bias=bias_ap, scale=1.0)
                    if c0 + 128 < TB:
                        B.act(pT_[:, c0 + 128:TB], ps_[:, c0 + 128:TB], AF.Exp, r=[prs, R_bias], w=[prp], bias=bias_ap, scale=1.0)
                else:
                    B.act(pT_[:, :], ps_[:, :], AF.Exp, r=[prs, R_bias], w=[prp], bias=bias_ap, scale=1.0)
                P.add("pe", lambda e, kt=kt, c0=c0, pT_=pT_, vt_=vt_, po=po: e.matmul(po[0:65, c0:TB], vt_[:, kt, :], pT_[:, c0:TB],
                                                                                    start=(kt == 0 and first[0]), stop=(kt == nkt - 1),
                                                                                    skip_group_check=True),
                      r=[vr_, prp], w=[pro])
            rl, R_rl = nextA()
            rlb, R_rlb = nextA()
            P.add("dve", lambda e, po=po, rl=rl: e.reciprocal(out=rl[64:65, :], in_=po[64:65, :]), r=[pro], w=[R_rl])
            pb2, prb2 = banks[5]
            P.add("pe", lambda e, rl=rl: e.matmul(pb2[0:64, :], ones_f[64:65, 0:64], rl[64:65, :], start=True, stop=True), r=[R_const, R_rl], w=[prb2])
            B.cp(rlb[0:64, :], pb2[0:64, :], r=[prb2], w=[R_rlb])
            B.tt(oT[:, h, :], po[0:64, :], rlb[0:64, :], ALU.mult, r=[pro, R_rlb], w=R_ar)
            ta, tr = nextA()
            B.tt(ta[0:64, :], po[0:64, :], rlb[0:64, :], ALU.mult, r=[pro, R_rlb], w=[tr])
            B.dma(dbg_att[h * 64:(h + 1) * 64, t0:t0 + TB], ta[0:64, :], r=[tr], w=[], q="pool")

    gab = [(B.sb(f"gab{i}", [128, 2, TB], BF16), Res()) for i in range(2)]
    mT = uT
    R_mT = R_u
    ga_v = ga_s.rearrange("(k p) t -> p k t", p=128)
    gs_v = gs_s.rearrange("(k p) t -> p k t", p=128)
    yTo_v = yT_o.rearrange("(k p) t -> p k t", p=128)

    def dense_block(blk, ncols=TB, groups=None, samp=False):
        t0 = 0 if samp else blk * TB
        if groups is None:
            groups = [(0, TB, 0)]
        gav = ga2_v if samp else ga_v
        gsv = gs2_v if samp else gs_v
        rga, rgs = (R_s2, R_s2) if samp else (R_ga, R_gs)
        for oc in range(8):
            wat, war = B.load_w(wa_d, 16, oc * 128, 128, pn=64)
            wst_, wsr = B.load_w(ws_d, 16, oc * 128, 128)
            pa_, pra_ = banks[oc % 2]
            ps2, prs2 = banks[2 + oc % 2]
            B.mms(pa_[:, 0:ncols], [(wat[:, h, :], oT[:, h, 0:ncols]) for h in range(16)], r=[war] + R_ar, w=[pra_])
            B.mms(ps2[:, 0:ncols], [(wst_[:, kc, :], ynT[:, kc, 0:ncols]) for kc in range(16)], r=[wsr, R_gT], w=[prs2])
            gt_, gr_ = gab[oc % 2]
            B.dma(gt_[:, 0, 0:ncols], gav[:, oc, t0:t0 + ncols], r=[rga], w=[gr_])
            B.dma(gt_[:, 1, 0:ncols], gsv[:, oc, t0:t0 + ncols], r=[rgs], w=[gr_])
            ta, tr = nextA()
            B.tt(ta[:, 0:ncols], pa_[:, 0:ncols], gt_[:, 0, 0:ncols], ALU.mult, r=[pra_, gr_], w=[tr])
            ta2, tr2 = nextA()
            B.tt(ta2[:, 0:ncols], ps2[:, 0:ncols], gt_[:, 1, 0:ncols], ALU.mult, r=[prs2, gr_], w=[tr2])
            B.tt(mT[:, oc, 0:ncols], ta[:, 0:ncols], ta2[:, 0:ncols], ALU.add, r=[tr, tr2], w=[R_mT])
        if samp:
            B.cp(xT[:, :, 0:ncols], xs1[:, :, 0:ncols], r=[R_xs1], w=[R_x])
        else:
            B.dma(xT[:, :, :], x1o_v[:, :, t0:t0 + TB], r=[R_x1], w=[R_x])
        for oc in range(8):
            wot, wor = B.load_w(wo_d, 8, oc * 128, 128)
            po_, pro_ = banks[4 + oc % 2]
            B.mms(po_[:, 0:ncols], [(wot[:, kc, :], mT[:, kc, 0:ncols]) for kc in range(8)], r=[wor, R_mT], w=[pro_])
            for (c0_, n_, m_) in groups:
                B.stt(xT[:, oc, c0_:c0_ + n_], po_[:, c0_:c0_ + n_], Gm[:, 1, oc, m_:m_ + 1], xT[:, oc, c0_:c0_ + n_],
                      ALU.mult, ALU.add, r=[pro_, R_AG], w=[R_x])
        rms_mod(xT, 2, groups, ncols)
        ffn(2, w1b_d, w3b_d, w2b_d, groups, ncols)
        pt, pr = banks[6]
        for kc in range(8):
            ta, tr = nextA()
            B.tt(ta[:, 0:ncols], xT[:, kc, 0:ncols], xT[:, kc, 0:ncols], ALU.mult, r=[R_x], w=[tr])
            P.add("pe", lambda e, ta=ta, kc=kc: e.matmul(pt[:, 0:ncols], ones_f[:], ta[:, 0:ncols], start=(kc == 0), stop=(kc == 7)),
                  r=[tr, R_const], w=[pr])
        B.act(sq[:, 0:ncols], pt[:, 0:ncols], AF.Sqrt, r=[pr, R_const], w=[R_sq], bias=epsc[:], scale=1.0 / D)
        P.add("dve", lambda e: e.reciprocal(out=rstd[:, 0:ncols], in_=sq[:, 0:ncols]), r=[R_sq], w=[R_rstd])
        for kc in range(8):
            B.stt(xT[:, kc, 0:ncols], xT[:, kc, 0:ncols], gsb[:, 3, kc:kc + 1], rstd[:, 0:ncols], ALU.mult, ALU.mult,
                  r=[R_x, R_g, R_rstd], w=[R_x])
        if samp:
            B.dma(ysT_o.rearrange("(k p) t -> p k t", p=128), xT[:, :, 0:ncols], r=[R_x], w=[], q="pool")
        else:
            B.dma(yTo_v[:, :, t0:t0 + TB], xT[:, :, :], r=[R_x], w=[], q="pool")

    ga2_v = ga2_s.rearrange("(k p) t -> p k t", p=128)
    gs2_v = gs2_s.rearrange("(k p) t -> p k t", p=128)
    BT2_v = BT2_s.rearrange("(g n) t -> n g t", n=128)
    CT2_v = CT2_s.rearrange("(g n) t -> n g t", n=128)

    FcA = B.sb("FcA", [128, NCORES, 256], F32)
    smAll = B.sb("smAll", [128, NCORES, 16], F32)
    delta = B.sb("delta", [128, NCORES + 1, 16], F32)
    R_FcA, R_smAll, R_delta = Res(), Res(), Res()
    xTall_v = xTall_d.rearrange("(k p) t -> p k t", p=128)
    NCTX = NCORES - 1
    P.add("dve", lambda e: e.memset(smAll[:], 0.0), w=[R_smAll])
    for j in range(NCTX):
        for h in range(H_A):
            for bb in range(T // TB):
                B.dma(kgA_v[j, h, 64:66, bb * TB:(bb + 1) * TB], onesrow[64:66, :], r=[R_const], w=[R_kgA], q="pool")
    P.add("pool", lambda e: e.memset(halo[:], 0.0), w=[R_halo])
    for j in range(NCTX):
        for blk in range(NBLK):
            c0_ = j * T + blk * TB
            B.dma(xT[:, :, :], xTall_v[:, :, c0_:c0_ + TB], w=[R_x])
            rms_mod(xT, 0, [(0, TB, 0)], TB)
            ffn(0, w1a_d, w3a_d, w2a_d, [(0, TB, 0)], TB)
            rms_mod(xT, 1, [(0, TB, 0)], TB)
            win_block(blk, [(0, TB, 0)], TB, ctxj=j)
        P.add("dve", lambda e: e.memset(carry[:], 0.0), w=[R_carry])
        for i in range(NT):
            pt, pr = banks[6]
            P.add("pe", lambda e, i=i: e.matmul(pt[:, 0:16], tri[:], logf[:, i, :], start=True, stop=True), r=[R_c3, R_lf], w=[pr])
            B.tt(FcA[:, j, i * 16:(i + 1) * 16], pt[:, 0:16], carry[:], ALU.add, r=[pr, R_carry], w=[R_FcA])
            P.add("pe", lambda e, i=i: e.matmul(pt[:, 16:32], ones_f[:], logf[:, i, :], start=True, stop=True), r=[R_const, R_lf], w=[pr])
            B.tt(carry[:], pt[:, 16:32], carry[:], ALU.add, r=[pr], w=[R_carry])
        B.cp(smAll[:, j, :], carry[:], r=[R_carry], w=[R_smAll])
        for gt in range(NT):
            ssd_tile(gt, want_y=False, maskj=j)
    P.add("dve", lambda e: e.memset(delta[:], 0.0), w=[R_delta])
    for j in range(NCORES - 1, -1, -1):
        B.stt(delta[:, j, :], smAll[:, j, 0:16], flg[:, j:j + 1], delta[:, j + 1, :], ALU.mult, ALU.add,
              r=[R_smAll, R_delta], w=[R_delta])
    run_own_p1()
    P.add("dve", lambda e: e.memset(carry[:], 0.0), w=[R_carry])
    for i in range(NT):
        pt, pr = banks[6]
        P.add("pe", lambda e, i=i: e.matmul(pt[:, 0:16], tri[:], logf[:, i, :], start=True, stop=True), r=[R_c3, R_lf], w=[pr])
        B.tt(Fc[:, i, :], pt[:, 0:16], carry[:], ALU.add, r=[pr, R_carry], w=[R_F])
        P.add("pe", lambda e, i=i: e.matmul(pt[:, 16:32], ones_f[:], logf[:, i, :], start=True, stop=True), r=[R_const, R_lf], w=[pr])
        B.tt(carry[:], pt[:, 16:32], carry[:], ALU.add, r=[pr], w=[R_carry])
    biasJ = [(B.sb(f"biasJ{i}", [128, NT], F32), Res()) for i in range(2)]
    rdj = B.sb("rdj", [128, NCORES, 16], F32)
    R_rdj = Res()
    for blk in range(NBLK):
        for lt in range(4):
            ssd_tile(blk * 4 + lt)
        attn_block(blk)
        dense_block(blk)
    B.dma(ssm_o[:, :, :], Hst[:].rearrange("p (h d) -> p h d", h=32), r=[R_H], w=[], q="pool")

    lfc = B.sb("lfc", [128, 8, 16], F32)
    Fs = B.sb("Fs", [128, 9, 16], F32)
    biasS = B.sb("biasS", [128, 9, 16], F32)
    R_lfc, R_Fs, R_bS = Res(), Res(), Res()

    def attn_sample(sq_):
        cs_ = slice(sq_ * 16, (sq_ + 1) * 16)
        B.dma(lfc[:], clf_d[sq_, :, :].rearrange("(t p) h -> p t h", p=128), w=[R_lfc])
        P.add("dve", lambda e: e.memset(carry[:], 0.0), w=[R_carry])
        P.add("dve", lambda e: e.memset(Fs[:], 0.0), w=[R_Fs])
        pt, pr = banks[6]
        for i in range(8):
            P.add("pe", lambda e, i=i: e.matmul(pt[:, 0:16], tri[:], lfc[:, i, :], start=True, stop=True), r=[R_c3, R_lfc], w=[pr])
            B.tt(Fs[:, i, :], pt[:, 0:16], carry[:], ALU.add, r=[pr, R_carry], w=[R_Fs])
            P.add("pe", lambda e, i=i: e.matmul(pt[:, 16:32], ones_f[:], lfc[:, i, :], start=True, stop=True), r=[R_const, R_lfc], w=[pr])
            B.tt(carry[:], pt[:, 16:32], carry[:], ALU.add, r=[pr], w=[R_carry])
        P.add("pe", lambda e: e.matmul(pt[0:16, 0:16], tri[0:16, 0:16], logf[0:16, NT + sq_, :], start=True, stop=True), r=[R_c3, R_lf], w=[pr])
        B.tt(Fs[0:16, 8, :], pt[0:16, 0:16], carry[0:16, :], ALU.add, r=[pr, R_carry], w=[R_Fs])
        pf, prf = banks[6]
        P.add("pe", lambda e: e.transpose(pf[0:16, 0:16], Fs[0:16, 8, :], ident_f[0:16, 0:16]), r=[R_Fs, R_const], w=[prf])
        B.cp(FT[:, 0:16], pf[0:16, 0:16], r=[prf], w=[R_FT])
        P.add("pe", lambda e: e.matmul(pf[:, 0:16], e0row[0:16, :], Fs[0:16, 8, :], start=True, stop=True), r=[R_c3, R_Fs], w=[prf])
        B.cp(rbc[:], pf[:, 0:16], r=[prf], w=[R_rbc])
        B.tt(biasS[:], rbc[:].unsqueeze(1).broadcast_to([128, 9, 16]), Fs[:], ALU.subtract, r=[R_rbc, R_Fs], w=[R_bS])
        for h in range(H_A):
            kt_, kr_ = Kt[h % 2]
            vt_, vr_ = Vt[h % 2]
            qa_, qr_ = Qa[h % 2]
            B.dma(yz[0:64, 0:1024], ckT_d[sq_, h, :, :], w=[R_yz])
            B.cp(kt_[0:64, 0:1024], yz[0:64, 0:1024], r=[R_yz], w=[kr_], eng="pool")
            P.add("pool", lambda e, kt_=kt_: e.memset(kt_[64:66, 0:1040], 1.0), w=[kr_])
            B.dma(kt_[0:64, 1024:1040], k2_s[h, 0:64, cs_], r=[R_s2], w=[kr_])
            ta, tr = nextA()
            B.dma(ta[:, :].rearrange("p (t d) -> p t d", t=8),
                  cv_d[sq_, :, :].rearrange("(t p) (h d) -> p t h d", p=128, h=16)[:, :, h, :], w=[tr])
            B.cp(vt_[:, 0:8, 0:64], ta[:, :].rearrange("p (t d) -> p t d", t=8), r=[tr], w=[vr_], eng="pool")
            P.add("pool", lambda e, vt_=vt_: e.memset(vt_[:, 0:9, 64:65], 1.0), w=[vr_])
            B.dma(vt_[0:16, 8, :], v2_s[h, sq_, :, :], r=[R_s2], w=[vr_])
            B.dma(qa_[0:64, 0:16], q2_s[h, 0:64, cs_], r=[R_s2], w=[qr_])
            pa, pra = banks[4]
            P.add("pe", lambda e, h=h: e.matmul(pa[0:66, 0:16], selt[:, h, :], FT[:, 0:16], start=True, stop=True), r=[R_c3, R_FT], w=[pra])
            ar0, rr0 = nextA()
            ar1, rr1 = nextA()
            ar2, rr2 = nextA()
            B.cp(ar0[64:66, 0:16], pa[64:66, 0:16], r=[pra], w=[rr0])
            B.ts(ar1[64:66, 0:16], ar0[64:66, 0:16], ar0[64:66, 0:1], None, ALU.subtract, r=[rr0], w=[rr1])
            B.cp(arb[64:66, 0:16], ar1[64:66, 0:16], r=[rr1], w=[R_arow])
            B.tt(ar0[64:66, 0:16], ar1[64:66, 0:16], arb[64:66, 0:16], ALU.subtract, r=[rr1, R_arow], w=[rr0])
            B.ts(ar2[64:66, 0:16], arb[64:66, 0:16], selc[64:66, 0:1], None, ALU.mult, r=[R_arow, R_c3], w=[rr2])
            B.stt(qa_[64:66, 0:16], ar0[64:66, 0:16], selc[64:66, 1:2], ar2[64:66, 0:16], ALU.mult, ALU.add, r=[rr0, rr2, R_c3], w=[qr_])
            po, pro = banks[2 + h % 2]
            for kt in range(9):
                Lk = 128 if kt < 8 else 16
                ps_, prs = banks[kt % 2]
                P.add("pe", lambda e, kt=kt, Lk=Lk, ps_=ps_, kt_=kt_, qa_=qa_: e.matmul(ps_[0:Lk, 0:16], kt_[:, kt * 128:kt * 128 + Lk], qa_[:, 0:16],
                                                                                       start=True, stop=True), r=[kr_, qr_], w=[prs])
                pti[0] += 1
                pT_, prp = Pt[pti[0] % 3]
                if kt == 8:
                    ta, tr = nextA()
                    B.tt(ta[0:16, 0:16], ps_[0:16, 0:16], mneg[0:16, 0:16], ALU.add, r=[prs, R_c3], w=[tr])
                    B.act(pT_[0:16, 0:16], ta[0:16, 0:16], AF.Exp, r=[tr, R_bS], w=[prp], bias=biasS[0:16, 8, h:h + 1], scale=1.0)
                else:
                    B.act(pT_[:, 0:16], ps_[:, 0:16], AF.Exp, r=[prs, R_bS], w=[prp], bias=biasS[:, kt, h:h + 1], scale=1.0)
                P.add("pe", lambda e, kt=kt, Lk=Lk, pT_=pT_, vt_=vt_, po=po: e.matmul(po[0:65, 0:16], vt_[0:Lk, kt, :], pT_[0:Lk, 0:16],
                                                                                    start=(kt == 0), stop=(kt == 8), skip_group_check=True),
                      r=[vr_, prp], w=[pro])
            rl, R_rl = nextA()
            rlb, R_rlb = nextA()
            P.add("dve", lambda e, po=po, rl=rl: e.reciprocal(out=rl[64:65, 0:16], in_=po[64:65, 0:16]), r=[pro], w=[R_rl])
            pb2, prb2 = banks[5]
            P.add("pe", lambda e, rl=rl: e.matmul(pb2[0:64, 0:16], ones_f[64:65, 0:64], rl[64:65, 0:16], start=True, stop=True), r=[R_const, R_rl], w=[prb2])
            B.cp(rlb[0:64, 0:16], pb2[0:64, 0:16], r=[prb2], w=[R_rlb])
            B.tt(oT[:, h, cs_], po[0:64, 0:16], rlb[0:64, 0:16], ALU.mult, r=[pro, R_rlb], w=R_ar)

    for sq_ in range(2):
        B.dma(Hst[:], ssmin_d[sq_, :, :], w=[R_H])
        ssd_tile(0, want_y=True, L=16, samp=sq_)
        B.dma(ssms_o[sq_, :, :], Hst[:], r=[R_H], w=[], q="pool")
    for sq_ in range(2):
        attn_sample(sq_)
    dense_block(0, ncols=32, groups=[(0, 16, 1), (16, 16, 2)], samp=True)


_CACHE = {}


def _r(w, kc):
    K, N = w.shape
    return np.ascontiguousarray(w.reshape(kc, K // kc, N).transpose(1, 0, 2))


def prep_inputs(inp, stage):
    f = np.float32
    xp = np.asarray(inp["x_prompt"], f)[0]
    maps = []
    shared = {}
    shared["w_ada_r"] = _r(np.asarray(inp["w_ada"], f)[0], 8)
    shared["b_ada_r"] = np.ascontiguousarray(np.asarray(inp["b_ada"], f)[0].reshape(72, 128).T)
    for nm, key in [("g_ffn1_r", "g_ffn1"), ("g_mix_r", "g_mix"), ("g_ffn2_r", "g_ffn2")]:
        shared[nm] = np.ascontiguousarray(np.asarray(inp[key], f)[0].reshape(8, 128).T)
    shared["g_final_r"] = np.ascontiguousarray(np.asarray(inp["g_final"], f).reshape(8, 128).T)
    shared["w1a"] = _r(np.asarray(inp["w1_ffn1"], f)[0], 8)
    shared["w3a"] = _r(np.asarray(inp["w3_ffn1"], f)[0], 8)
    shared["w2a"] = _r(np.asarray(inp["w2_ffn1"], f)[0], NH)
    shared["w1b"] = _r(np.asarray(inp["w1_ffn2"], f)[0], 8)
    shared["w3b"] = _r(np.asarray(inp["w3_ffn2"], f)[0], 8)
    shared["w2b"] = _r(np.asarray(inp["w2_ffn2"], f)[0], NH)
    shared["win_r"] = _r(np.asarray(inp["w_in"], f)[0], 8)
    shared["bf_bc"] = np.ascontiguousarray(np.broadcast_to(np.asarray(inp["b_f"], f)[0][None, :], (128, 16)))
    shared["convw_r"] = np.ascontiguousarray(np.asarray(inp["conv_w"], f)[0].reshape(4, 24, 128).transpose(2, 1, 0))
    shared["convb_r"] = np.ascontiguousarray(np.asarray(inp["conv_b"], f)[0].reshape(24, 128).T)
    shared["dtb_bc"] = np.ascontiguousarray(np.broadcast_to(np.asarray(inp["dt_bias"], f)[0][None, :], (128, 32)))
    shared["ident_in"] = np.eye(128, dtype=f)
    shared["alog_bc"] = np.ascontiguousarray(np.broadcast_to(np.asarray(inp["a_log"], f)[0][None, :], (128, 32)))
    shared["dskip_bc"] = np.ascontiguousarray(np.broadcast_to(np.asarray(inp["d_skip"], f)[0][None, :], (128, 32)))
    shared["gssd_r"] = np.ascontiguousarray(np.asarray(inp["g_ssd"], f)[0].reshape(16, 128).T)
    tri = np.triu(np.ones((128, 128), f))
    shared["tri_in"] = tri
    shared["maskneg_in"] = ((1.0 - tri) * -1e9).astype(f)
    e0 = np.zeros((128, 128), f); e0[0, :] = 1.0
    shared["e0row_in"] = e0
    sel = np.zeros((16, 16, 66), f)
    for h in range(16):
        sel[h, h, 64] = 1.0; sel[h, h, 65] = 1.0
    shared["sel_in"] = sel
    selc = np.zeros((128, 2), f); selc[64, 0] = 1.0; selc[65, 1] = 1.0
    shared["selc_in"] = selc
    shared["wa_r"] = np.ascontiguousarray(np.asarray(inp["w_a"], f)[0].reshape(16, 64, D).transpose(1, 0, 2))
    shared["ws_r"] = _r(np.asarray(inp["w_s"], f)[0], 16)
    shared["wout_r"] = _r(np.asarray(inp["w_out"], f)[0], 8)
    xpT = np.ascontiguousarray(xp.T)
    xsm = np.asarray(inp["x_sample"], f)
    cp = np.asarray(inp["c_prompt"], f)[0]
    cs = np.asarray(inp["c_sample"], f)
    for c in range(NCORES):
        m = dict(shared)
        m["xT"] = np.ascontiguousarray(xp[c * T:(c + 1) * T].T)
        m["xTall"] = xpT
        hal = xp[c * T - 3:c * T] if c > 0 else np.zeros((3, D), f)
        m["xsT"] = np.ascontiguousarray(np.concatenate([xsm[2 * c:2 * c + 2].reshape(32, D), hal], 0).T)
        fl = np.zeros((128, 24), f)
        for j in range(8):
            fl[:, j] = 1.0 if j < c else 0.0
            fl[:, 8 + j] = 0.0 if j < c else NEG
        fl[:, 16] = 1.0 if c > 0 else 0.0
        m["flags"] = fl
        sc = np.asarray(inp["state_conv"], f)[0, 2 * c:2 * c + 2]
        m["scv"] = np.ascontiguousarray(sc.transpose(2, 0, 1).reshape(24, 128, 2, 3).transpose(1, 0, 2, 3))
        ss = np.asarray(inp["state_ssm"], f)[0, 2 * c:2 * c + 2]
        m["ssmin"] = np.ascontiguousarray(ss.transpose(0, 3, 1, 2).reshape(2, 128, DIN))
        ck = np.asarray(inp["cache_k"], f)[0, 2 * c:2 * c + 2]
        m["ckT"] = np.ascontiguousarray(ck.transpose(0, 2, 3, 1))
        m["cvc"] = np.ascontiguousarray(np.asarray(inp["cache_v"], f)[0, 2 * c:2 * c + 2].reshape(2, 1024, D))
        m["clf"] = np.ascontiguousarray(np.asarray(inp["cache_logf"], f)[0, 2 * c:2 * c + 2])
        c3 = np.stack([cp, cs[2 * c], cs[2 * c + 1]], axis=1)
        m["cT"] = np.ascontiguousarray(c3.reshape(8, 128, 3).transpose(1, 0, 2))
        maps.append(m)
    return maps


def run(inp, stage=99, ncores=NCORES):
    if stage not in _CACHE:
        B = build(stage)
        _CACHE[stage] = B
    B = _CACHE[stage]
    maps = prep_inputs(inp, stage)
    maps = [{k: v for k, v in m.items() if k in B.ins} for m in maps]
    res = run_bass_kernel_spmd(B.nc, maps[:ncores], core_ids=list(range(ncores)))
    return res.results


def kernel(**inp):
    res = run(inp, stage=3)
    f = np.float32
    y_prompt = np.zeros((1, SEQ, D), f)
    y_sample = np.zeros((16, 16, D), f)
    k_prompt = np.zeros((1, 1, SEQ, H_A, HD), f)
    v_prompt = np.zeros((1, 1, SEQ, H_A, HD), f)
    logf_prompt = np.zeros((1, 1, SEQ, H_A), f)
    ssm_prompt = np.zeros((1, 1, NSSD, 64, NST), f)
    conv_prompt = np.zeros((1, 1, 3, CONVD), f)
    k_sample = np.zeros((1, 16, 16, H_A, HD), f)
    v_sample = np.zeros((1, 16, 16, H_A, HD), f)
    logf_sample = np.zeros((1, 16, 16, H_A), f)
    ssm_sample = np.zeros((1, 16, NSSD, 64, NST), f)
    conv_sample = np.zeros((1, 16, 3, CONVD), f)
    for c in range(NCORES):
        r = res[c]
        sl = slice(c * T, (c + 1) * T)
        y_prompt[0, sl] = np.asarray(r["yT_o"]).T
        k_prompt[0, 0, sl] = np.asarray(r["kT_o"]).T.reshape(T, H_A, HD)
        v_prompt[0, 0, sl] = np.asarray(r["v_o"]).reshape(T, H_A, HD)
        logf_prompt[0, 0, sl] = np.asarray(r["lf_o"])
        k_sample[0, 2 * c:2 * c + 2] = np.asarray(r["ksT_o"]).T.reshape(2, 16, H_A, HD)
        v_sample[0, 2 * c:2 * c + 2] = np.asarray(r["vs_o"]).reshape(2, 16, H_A, HD)
        logf_sample[0, 2 * c:2 * c + 2] = np.asarray(r["lfs_o"]).reshape(2, 16, H_A)
        y_sample[2 * c:2 * c + 2] = np.asarray(r["ysT_o"]).T.reshape(2, 16, D)
        ssm_sample[0, 2 * c:2 * c + 2] = np.asarray(r["ssms_o"]).reshape(2, 128, NSSD, 64).transpose(0, 2, 3, 1)
        cvs = np.asarray(r["cvs_o"])
        conv_sample[0, 2 * c] = cvs[:, 0:3].T
        conv_sample[0, 2 * c + 1] = cvs[:, 3:6].T
    conv_prompt[0, 0] = np.asarray(res[NCORES - 1]["cv_o"]).T
    ssm_prompt[0, 0] = np.asarray(res[NCORES - 1]["ssm_o"]).transpose(1, 2, 0)
    return (y_prompt, y_sample, k_prompt, v_prompt, logf_prompt, ssm_prompt, conv_prompt,
            k_sample, v_sample, logf_sample, ssm_sample, conv_sample)
```

```python
from contextlib import ExitStack
import numpy as np
import concourse.bass as bass
import concourse.mybir as mybir
from concourse.bass_utils import run_bass_kernel_spmd

F32 = mybir.dt.float32
BF16 = mybir.dt.bfloat16
AF = mybir.ActivationFunctionType
ALU = mybir.AluOpType
AX = mybir.AxisListType

NCORES = 8
D = 1024
SEQ = 16384
T = SEQ // NCORES
TB = 512
DFF = 2816
NH = 22
H_A = 16
HD = 64
DIN = 2048
NSSD = 32
NST = 128
CONVD = 3072
DINP = 10288
C_Q, C_K, C_V, C_F, C_Z, C_X, C_DT, C_GA, C_GS = 0, 1024, 2048, 3072, 3088, 5136, 8208, 8240, 9264
EPS = 1e-6
NEG = -30000.0
import os
CTX = os.environ.get("CTX", "1") == "1"

ENG = ["pe", "act", "dve", "pool", "sp"]


class Res:
    __slots__ = ("lw", "rd", "excl")

    def __init__(self, excl=False):
        self.lw = None
        self.rd = []
        self.excl = excl


class Op:
    __slots__ = ("eng", "fn", "deps", "dma", "need", "sv", "dsem", "dval", "idx", "cc", "seng")


class Prog:
    NS = 16

    def __init__(self):
        self.ops = []
        self.dq = {"sp": [], "pool": [], "act": []}
        self.ncc = 0

    def add(self, eng, fn, r=(), w=(), dma=False, cc=False):
        op = Op()
        op.eng, op.fn, op.dma, op.need, op.idx = eng, fn, dma, False, len(self.ops)
        op.sv = 0
        op.cc = cc
        op.seng = eng
        if cc:
            op.dma = dma = True
        deps = {}
        w = list(w) + [R for R in r if R.excl]
        r = [R for R in r if not R.excl]

        def dep(p, war=False):
            if p is None:
                return
            if (not p.dma) and p.eng == eng and eng == "pe":
                return
            deps[p.idx] = p

        for R in r:
            dep(R.lw)
        for R in w:
            dep(R.lw)
            for q in R.rd:
                dep(q, True)
        if cc:
            op.seng = "cc"
            op.dsem = self.ncc
            op.dval = 1
            self.ncc += 1
        elif dma:
            lst = self.dq[eng]
            n = len(lst)
            op.dsem = n % self.NS
            op.dval = 16 * (n // self.NS + 1)
            if n >= self.NS:
                deps[lst[n - self.NS].idx] = lst[n - self.NS]
            lst.append(op)
        op.deps = list(deps.values())
        for p in op.deps:
            if not p.dma:
                p.need = True
        for R in r:
            R.rd.append(op)
        for R in w:
            R.lw = op
            R.rd = []
        self.ops.append(op)
        return op

    def emit(self, nc, sems, dsems):
        cnt = {e: 0 for e in ENG}
        for op in self.ops:
            if (not op.dma) and op.need:
                cnt[op.eng] += 1
                op.sv = cnt[op.eng]
        per = {e: [o for o in self.ops if o.eng == e] for e in ENG}

        def run(ename, e):
            waited = {}
            for op in per[ename]:
                for p in op.deps:
                    if p.dma:
                        key, val, sem = ("d", p.seng, p.dsem), p.dval, dsems[p.seng][p.dsem]
                    else:
                        key, val, sem = ("c", p.eng), p.sv, sems[p.eng]
                    if waited.get(key, 0) >= val:
                        continue
                    waited[key] = val
                    e.wait_ge(sem, val)
                ins = op.fn(e)
                if op.cc:
                    ins.then_inc(dsems["cc"][op.dsem], 1)
                elif op.dma:
                    ins.then_inc(dsems[ename][op.dsem], 16)
                elif op.need:
                    ins.then_inc(sems[ename], 1)
            if ename in self.dq:
                lst = self.dq[ename]
                last = {}
                for o in lst:
                    last[o.dsem] = o.dval
                for s, v in last.items():
                    e.wait_ge(dsems[ename][s], v)

        with nc.Block() as block:
            @block.tensor
            def _(e):
                run("pe", e)

            @block.scalar
            def _(e):
                run("act", e)

            @block.vector
            def _(e):
                run("dve", e)

            @block.gpsimd
            def _(e):
                run("pool", e)

            @block.sync
            def _(e):
                run("sp", e)


class StopBuild(Exception):
    pass


class Builder:
    def __init__(self, stage=99):
        import os
        self.cutn = float(os.environ.get("DBG_CUT", "999"))
        self.stage = stage
        self.nc = bass.Bass("TRN2", target_bir_lowering=False)
        try:
            self.nc.allow_low_precision("bf16 matmul operands by design")
        except Exception:
            pass
        self.P = Prog()
        self.es = ExitStack()
        self.ins = {}
        self.outs = {}
        self.uid = 0
        self.wrr = 0

    def finish(self):
        nc = self.nc
        sems = {e: self.es.enter_context(nc.semaphore(f"s_{e}")) for e in ENG}
        dsems = {q: [self.es.enter_context(nc.semaphore(f"d_{q}{i}")) for i in range(Prog.NS)]
                 for q in ("sp", "pool", "act")}
        dsems["cc"] = [self.es.enter_context(nc.semaphore(f"d_cc{i}")) for i in range(max(1, self.P.ncc))]
        self.P.emit(nc, sems, dsems)
        self.es.close()

    def cut(self, n):
        if self.cutn <= n:
            raise StopBuild()

    def din(self, name, shape, dt=F32):
        t = self.nc.dram_tensor(name, list(shape), dt, kind="ExternalInput").ap()
        self.ins[name] = t
        return t

    def dout(self, name, shape, dt=F32):
        t = self.nc.dram_tensor(name, list(shape), dt, kind="ExternalOutput").ap()
        self.outs[name] = t
        return t

    def dscr(self, name, shape, dt):
        return self.nc.dram_tensor(name, list(shape), dt).ap()

    def sb(self, name, shape, dt=F32):
        return self.es.enter_context(self.nc.sbuf_tensor("sb_" + name, list(shape), dt))

    def ps(self, name, shape, dt=F32):
        return self.es.enter_context(self.nc.psum_tensor("ps_" + name, list(shape), dt))

    def dma(self, out, in_, r=(), w=(), q="sp"):
        return self.P.add(q, lambda e: e.dma_start(out=out, in_=in_), r=r, w=w, dma=True)

    def act(self, out, in_, func, r=(), w=(), bias=None, scale=None):
        kw = {}
        if bias is not None:
            kw["bias"] = bias
        if scale is not None:
            kw["scale"] = scale
        return self.P.add("act", lambda e: e.activation(out=out, in_=in_, func=func, **kw), r=r, w=w)

    def tt(self, out, a, b, op, r=(), w=(), eng="dve"):
        return self.P.add(eng, lambda e: e.tensor_tensor(out=out, in0=a, in1=b, op=op), r=r, w=w)

    def ts(self, out, a, s1, s2, op0, op1=None, r=(), w=(), eng="dve"):
        if op1 is None:
            return self.P.add(eng, lambda e: e.tensor_scalar(out=out, in0=a, scalar1=s1, scalar2=None, op0=op0), r=r, w=w)
        return self.P.add(eng, lambda e: e.tensor_scalar(out=out, in0=a, scalar1=s1, scalar2=s2, op0=op0, op1=op1), r=r, w=w)

    def stt(self, out, a, s, b, op0, op1, r=(), w=(), eng="dve"):
        return self.P.add(eng, lambda e: e.scalar_tensor_tensor(out=out, in0=a, scalar=s, in1=b, op0=op0, op1=op1), r=r, w=w)

    def cp(self, out, in_, r=(), w=(), eng="dve"):
        if eng == "act":
            return self.P.add("act", lambda e: e.copy(out=out, in_=in_), r=r, w=w)
        return self.P.add(eng, lambda e: e.tensor_copy(out=out, in_=in_), r=r, w=w)

    def mms(self, out, pairs, r=(), w=()):
        n = len(pairs)

        def fn(e):
            ins = None
            for i, (l, rh) in enumerate(pairs):
                ins = e.matmul(out, l, rh, start=(i == 0), stop=(i == n - 1))
            return ins
        return self.P.add("pe", fn, r=r, w=w)

    def init_wstream(self):
        self.WSZ = 2048
        self.wst = [(self.sb(f"wst{i}", [128, self.WSZ], F32), Res()) for i in range(2)]
        self.wbf = [(self.sb(f"wbf{i}", [128, self.WSZ], BF16), Res()) for i in range(2)]
        self.wi = 0
        self.wj = 0

    def load_w(self, wd, kcn, c0, n, pn=128, k0=0):
        st, sr = self.wst[self.wi % 2]
        self.wi += 1
        bf, br = self.wbf[self.wj % 2]
        self.wj += 1
        sz = kcn * n
        assert sz <= self.WSZ
        stv = st[0:pn, 0:sz].rearrange("p (k n) -> p k n", k=kcn)
        bfv = bf[0:pn, 0:sz].rearrange("p (k n) -> p k n", k=kcn)
        self.dma(stv, wd[0:pn, k0:k0 + kcn, c0:c0 + n], w=[sr])
        ceng = "dve" if (self.wj % 3 == 0) else "act"
        self.cp(bf[0:pn, 0:sz], st[0:pn, 0:sz], r=[sr], w=[br], eng=ceng)
        return bfv, br


def build(stage=99):
    B = Builder(stage)
    nc, P = B.nc, B.P
    NBLK = T // TB
    NT = T // 128

    xT_d = B.din("xT", [D, T])
    cT_d = B.din("cT", [128, 8, 3])
    wada_d = B.din("w_ada_r", [128, 8, 9 * D])
    bada_d = B.din("b_ada_r", [128, 72])
    g1_d = B.din("g_ffn1_r", [128, 8])
    gm_d = B.din("g_mix_r", [128, 8])
    g2_d = B.din("g_ffn2_r", [128, 8])
    gf_d = B.din("g_final_r", [128, 8])
    w1a_d = B.din("w1a", [128, 8, DFF])
    w3a_d = B.din("w3a", [128, 8, DFF])
    w2a_d = B.din("w2a", [128, NH, D])
    w1b_d = B.din("w1b", [128, 8, DFF])
    w3b_d = B.din("w3b", [128, 8, DFF])
    w2b_d = B.din("w2b", [128, NH, D])
    win_d = B.din("win_r", [128, 8, DINP])
    bf_d = B.din("bf_bc", [128, 16])
    cw_d = B.din("convw_r", [128, 24, 4])
    cb_d = B.din("convb_r", [128, 24])
    dtb_d = B.din("dtb_bc", [128, 32])

    kT_o = B.dout("kT_o", [D, T])
    v_o = B.dout("v_o", [T, D])
    lf_o = B.dout("lf_o", [T, 16])
    cv_o = B.dout("cv_o", [CONVD, 3])
    xsT_d = B.din("xsT", [D, 35])
    flg_d = B.din("flags", [128, 24])
    ksT_o = B.dout("ksT_o", [D, 32])
    vs_o = B.dout("vs_o", [32, D])
    lfs_o = B.dout("lfs_o", [32, 16])
    cvs_o = B.dout("cvs_o", [CONVD, 6])
    x1_o = B.dout("x1_o", [D, T])
    scv_d = B.din("scv", [128, 24, 2, 3])
    ssmin_d = B.din("ssmin", [2, 128, DIN])
    ckT_d = B.din("ckT", [2, H_A, 64, 1024])
    cv_d = B.din("cvc", [2, 1024, D])
    clf_d = B.din("clf", [2, 1024, 16])
    ysT_o = B.dout("ysT_o", [D, 32])
    ssms_o = B.dout("ssms_o", [2, 128, DIN])
    q2_s = B.dscr("q2_s", [H_A, 66, 32], BF16)
    k2_s = B.dscr("k2_s", [H_A, 66, 32], BF16)
    v2_s = B.dscr("v2_s", [H_A, 2, 16, 65], BF16)
    z2_s = B.dscr("z2_s", [32, DIN], BF16)
    xs2_s = B.dscr("xs2_s", [32, DIN], BF16)
    Bt2_s = B.dscr("Bt2_s", [32, 512], BF16)
    BT2_s = B.dscr("BT2_s", [512, 32], BF16)
    CT2_s = B.dscr("CT2_s", [512, 32], BF16)
    ga2_s = B.dscr("ga2_s", [D, 32], BF16)
    gs2_s = B.dscr("gs2_s", [D, 32], BF16)
    R_s2 = Res()
    xTall_d = B.din("xTall", [D, SEQ])
    kgA = B.dscr("kgA", [NCORES * H_A * 66, T], BF16)
    vgA = B.dscr("vgA", [NCORES * H_A * NT * 128, 65], BF16)
    kgA_v = kgA.rearrange("(j h r) t -> j h r t", j=NCORES, h=H_A)
    vgA_v = vgA.rearrange("(j h t p) d -> j h t p d", j=NCORES, h=H_A, t=NT)
    R_kgA, R_vgA = Res(), Res()
    alog_d = B.din("alog_bc", [128, 32])
    dsk_d = B.din("dskip_bc", [128, 32])
    gssd_d = B.din("gssd_r", [128, 16])
    tri_d = B.din("tri_in", [128, 128])
    mneg_d = B.din("maskneg_in", [128, 128])
    e0_d = B.din("e0row_in", [128, 128])
    sel_d = B.din("sel_in", [16, 16, 66])
    selc_d = B.din("selc_in", [128, 2])
    wa_d = B.din("wa_r", [64, 16, D])
    ws_d = B.din("ws_r", [128, 16, D])
    wo_d = B.din("wout_r", [128, 8, D])
    yT_o = B.dout("yT_o", [D, T])
    ssm_o = B.dout("ssm_o", [128, NSSD, 64])

    q_s = B.dscr("q_s", [H_A, 66, T], BF16)
    kg_s = B.dscr("kg_s", [H_A, 66, T], BF16)
    vg_s = B.dscr("vg_s", [H_A, NT, 128, 65], BF16)
    z_s = B.dscr("z_s", [T, DIN], BF16)
    xs_s = B.dscr("xs_s", [T, DIN], BF16)
    Bt_s = B.dscr("Bt_s", [T, 512], BF16)
    BT_s = B.dscr("BT_s", [512, T], BF16)
    CT_s = B.dscr("CT_s", [512, T], BF16)
    ga_s = B.dscr("ga_s", [D, T], BF16)
    gs_s = B.dscr("gs_s", [D, T], BF16)
    R_q, R_kg, R_vg, R_z, R_xs, R_Bt, R_BT, R_CT, R_ga, R_gs, R_x1 = [Res() for _ in range(11)]

    ones_f = B.sb("ones_f", [128, 128], F32)
    ident_b = B.sb("ident_b", [128, 128], BF16)
    ident_f = B.sb("ident_f", [128, 128], F32)
    epsc = B.sb("epsc", [128, 1], F32)
    onec = B.sb("onec", [128, 1], F32)
    R_const = Res()
    P.add("pool", lambda e: e.memset(ones_f[:], 1.0), w=[R_const])
    P.add("pool", lambda e: e.memset(epsc[:], EPS), w=[R_const])
    P.add("pool", lambda e: e.memset(onec[:], 1.0), w=[R_const])
    identf_d = B.din("ident_in", [128, 128])
    B.dma(ident_f[:], identf_d[:, :], w=[R_const])
    B.cp(ident_b[:], ident_f[:], r=[R_const], w=[R_const], eng="pool")

    B.init_wstream()

    banks = [(B.ps(f"bank{i}", [128, 512], F32), Res(True)) for i in range(7)]
    ptT = B.ps("ptT", [128, 1024], BF16)
    prT = Res(True)

    try:
        _build_body(B, locals())
    except StopBuild:
        pass
    B.finish()
    return B


def _build_body(B, L):
    globals_ = L
    nc, P = B.nc, B.P
    NBLK = T // TB
    NT = T // 128
    for k_, v_ in L.items():
        if k_ not in ("B", "nc", "P"):
            globals()[k_] = v_
    cT = B.sb("cT", [128, 8, 3], F32)
    cs = B.sb("cs", [128, 8, 3], BF16)
    bada = B.sb("bada", [128, 72], F32)
    modT = B.sb("modT", [128, 72, 3], F32)
    gsb = B.sb("gsb", [128, 4, 8], F32)
    R_c, R_mod, R_g = Res(), Res(), Res()
    B.dma(cT[:], cT_d[:, :, :], w=[R_c])
    B.dma(bada[:], bada_d[:, :], w=[R_c])
    for i, gd in enumerate([g1_d, gm_d, g2_d, gf_d]):
        B.dma(gsb[:, i, :], gd[:, :], w=[R_g])
    B.act(cs[:], cT[:], AF.Silu, r=[R_c], w=[R_c])
    for ch in range(72):
        wt, wr = B.load_w(wada_d, 8, ch * 128, 128)
        pt, pr = banks[ch % 2]
        B.mms(pt[:, 0:3], [(wt[:, kc, :], cs[:, kc, :]) for kc in range(8)], r=[wr, R_c], w=[pr])
        B.ts(modT[:, ch, :], pt[:, 0:3], bada[:, ch:ch + 1], None, ALU.add, r=[pr, R_c], w=[R_mod])
    Am = B.sb("Am", [128, 3, 8, 3], F32)
    Gm = B.sb("Gm", [128, 3, 8, 3], F32)
    R_AG = Res()
    for s in range(3):
        coef = 1.0 if s == 1 else 0.5
        for kc in range(8):
            B.ts(Am[:, s, kc, :], modT[:, (3 * s + 1) * 8 + kc, :], 1.0, gsb[:, s, kc:kc + 1], ALU.add, ALU.mult,
                 r=[R_mod, R_g], w=[R_AG])
            B.ts(Gm[:, s, kc, :], modT[:, (3 * s + 2) * 8 + kc, :], 1.0, coef, ALU.add, ALU.mult,
                 r=[R_mod], w=[R_AG])

    B.cut(1)

    def shiftp(s, kc, m):
        return modT[:, (3 * s) * 8 + kc, m:m + 1]

    xT = B.sb("xT", [128, 8, TB], F32)
    uT = B.sb("uT", [128, 8, TB], BF16)
    gT = B.sb("gT", [128, NH, TB], BF16)
    sq = B.sb("sq", [128, TB], F32)
    rstd = B.sb("rstd", [128, TB], F32)
    tmpA = [(B.sb(f"tmpA{i}", [128, TB], F32), Res()) for i in range(5)]
    tmpB = [(B.sb(f"tmpB{i}", [128, TB], BF16), Res()) for i in range(5)]
    R_x, R_u, R_gT, R_sq, R_rstd = Res(), Res(), Res(), Res(), Res()
    tai = [0]
    tbi = [0]

    def nextA():
        tai[0] += 1
        return tmpA[tai[0] % 5]

    def nextB():
        tbi[0] += 1
        return tmpB[tbi[0] % 5]

    def rms_mod(src, s, groups, ncols):
        pt, pr = banks[6]
        for kc in range(8):
            ta, tr = nextA()
            B.tt(ta[:, 0:ncols], src[:, kc, 0:ncols], src[:, kc, 0:ncols], ALU.mult, r=[R_x], w=[tr])
            P.add("pe", lambda e, ta=ta, kc=kc: e.matmul(pt[:, 0:ncols], ones_f[:], ta[:, 0:ncols],
                                                            start=(kc == 0), stop=(kc == 7)),
                  r=[tr, R_const], w=[pr])
        B.act(sq[:, 0:ncols], pt[:, 0:ncols], AF.Sqrt, r=[pr, R_const], w=[R_sq], bias=epsc[:], scale=1.0 / D)
        P.add("dve", lambda e: e.reciprocal(out=rstd[:, 0:ncols], in_=sq[:, 0:ncols]), r=[R_sq], w=[R_rstd])
        for kc in range(8):
            ta, tr = nextA()
            B.tt(ta[:, 0:ncols], src[:, kc, 0:ncols], rstd[:, 0:ncols], ALU.mult, r=[R_x, R_rstd], w=[tr])
            for (c0, n, m) in groups:
                B.ts(uT[:, kc, c0:c0 + n], ta[:, c0:c0 + n], Am[:, s, kc, m:m + 1], shiftp(s, kc, m),
                     ALU.mult, ALU.add, r=[tr, R_AG, R_mod], w=[R_u])

    def ffn(s, w1d, w3d, w2d, groups, ncols):
        for hc in range(NH):
            w1t, w1r = B.load_w(w1d, 8, hc * 128, 128)
            w3t, w3r = B.load_w(w3d, 8, hc * 128, 128)
            p1, r1 = banks[hc % 2]
            p3, r3 = banks[2 + hc % 2]
            B.mms(p1[:, 0:ncols], [(w1t[:, kc, :], uT[:, kc, 0:ncols]) for kc in range(8)], r=[w1r, R_u], w=[r1])
            B.mms(p3[:, 0:ncols], [(w3t[:, kc, :], uT[:, kc, 0:ncols]) for kc in range(8)], r=[w3r, R_u], w=[r3])
            ta, tr = nextA()
            B.act(ta[:, 0:ncols], p1[:, 0:ncols], AF.Silu, r=[r1], w=[tr])
            B.tt(gT[:, hc, 0:ncols], ta[:, 0:ncols], p3[:, 0:ncols], ALU.mult, r=[tr, r3], w=[R_gT])
        for oc in range(8):
            w2t, w2r = B.load_w(w2d, 11, oc * 128, 128)
            w2u, w2s = B.load_w(w2d, 11, oc * 128, 128, k0=11)
            po, ro = banks[4 + oc % 2]
            B.mms(po[:, 0:ncols], [(w2t[:, hc, :], gT[:, hc, 0:ncols]) for hc in range(11)]
                  + [(w2u[:, hc, :], gT[:, 11 + hc, 0:ncols]) for hc in range(11)], r=[w2r, w2s, R_gT], w=[ro])
            for (c0, n, m) in groups:
                B.stt(xT[:, oc, c0:c0 + n], po[:, c0:c0 + n], Gm[:, s, oc, m:m + 1], xT[:, oc, c0:c0 + n],
                      ALU.mult, ALU.add, r=[ro, R_AG], w=[R_x])

    logf = B.sb("logf", [128, NT + 2, 16], F32)
    dtsb = B.sb("dtsb", [128, NT + 2, 32], F32)
    bfb = B.sb("bfb", [128, 16], F32)
    dtb = B.sb("dtb", [128, 32], F32)
    cw = B.sb("cw", [128, 24, 4], F32)
    cbv = B.sb("cbv", [128, 24], F32)
    halo = B.sb("halo", [128, 24, 3], F32)
    R_lf, R_dt, R_sm, R_halo = Res(), Res(), Res(), Res()
    B.dma(bfb[:], bf_d[:, :], w=[R_sm])
    B.dma(dtb[:], dtb_d[:, :], w=[R_sm])
    B.dma(cw[:], cw_d[:, :, :], w=[R_sm])
    B.dma(cbv[:], cb_d[:, :], w=[R_sm])
    P.add("pool", lambda e: e.memset(halo[:], 0.0), w=[R_halo])
    flg = B.sb("flg", [128, 24], F32)
    B.dma(flg[:], flg_d[:, :], w=[R_sm])

    xb = [(B.sb(f"xb{i}", [128, TB + 3], F32), Res()) for i in range(2)]
    vst = [(B.sb(f"vst{i}", [128, 4, 65], BF16), Res()) for i in range(2)]
    kst = [(B.sb("kst0", [64, TB], F32), Res())] * 2
    onesrow = B.sb("onesrow", [66, TB], BF16)
    P.add("pool", lambda e: e.memset(onesrow[:], 1.0), w=[R_const])
    for i in range(2):
        P.add("pool", lambda e, i=i: e.memset(vst[i][0][:], 1.0), w=[vst[i][1]])
    for h in range(H_A):
        for bb in range(T // TB):
            B.dma(kg_s[h, 64:66, bb * TB:(bb + 1) * TB], onesrow[64:66, :], r=[R_const], w=[R_kg], q="pool")

    def softplus_to(out, in_ps, biasbc, n, r, w, neg_in=False, pn=128):
        ta, tr = nextA()
        tb2, tr2 = nextA()
        B.tt(ta[0:pn, 0:n], in_ps, biasbc, ALU.add, r=r, w=[tr])
        if neg_in:
            B.ts(ta[0:pn, 0:n], ta[0:pn, 0:n], -1.0, None, ALU.mult, r=[tr], w=[tr])
        B.stt(tb2[0:pn, 0:n], ta[0:pn, 0:n], -1.0, ta[0:pn, 0:n], ALU.mult, ALU.max, r=[tr], w=[tr2])
        B.act(tb2[0:pn, 0:n], tb2[0:pn, 0:n], AF.Exp, r=[tr2], w=[tr2], scale=-1.0)
        B.act(tb2[0:pn, 0:n], tb2[0:pn, 0:n], AF.Ln, r=[tr2, R_const], w=[tr2], bias=onec[0:pn, :], scale=1.0)
        B.stt(out, ta[0:pn, 0:n], 0.0, tb2[0:pn, 0:n], ALU.max, ALU.add, r=[tr, tr2], w=w)

    def win_block(blk, groups, ncols, ctxj=None):
        t0 = blk * TB
        ntt = ncols // 128
        B.cut(5)
        for h in range(H_A):
            if ctxj is None:
                wt, wr = B.load_w(win_d, 8, C_Q + h * 64, 64)
                pt, pr = banks[h % 2]
                B.mms(pt[0:64, 0:ncols], [(wt[:, kc, :], uT[:, kc, 0:ncols]) for kc in range(8)], r=[wr, R_u], w=[pr])
                tb_, tbr = nextB()
                B.ts(tb_[0:64, 0:ncols], pt[0:64, 0:ncols], 0.125, None, ALU.mult, r=[pr], w=[tbr])
                B.dma(q_s[h, 0:64, t0:t0 + ncols], tb_[0:64, 0:ncols], r=[tbr], w=[R_q], q="pool")
            wt, wr = B.load_w(win_d, 8, C_K + h * 64, 64)
            pt, pr = banks[2 + h % 2]
            B.mms(pt[0:64, 0:ncols], [(wt[:, kc, :], uT[:, kc, 0:ncols]) for kc in range(8)], r=[wr, R_u], w=[pr])
            if ctxj is None:
                kf, kr = kst[h % 2]
                B.cp(kf[:, 0:ncols], pt[0:64, 0:ncols], r=[pr], w=[kr])
                B.dma(kT_o[h * 64:(h + 1) * 64, t0:t0 + ncols], kf[:, 0:ncols], r=[kr], w=[], q="pool")
            tb_, tbr = nextB()
            B.cp(tb_[0:64, 0:ncols], pt[0:64, 0:ncols], r=[pr], w=[tbr], eng="act")
            if ctxj is None:
                B.dma(kg_s[h, 0:64, t0:t0 + ncols], tb_[0:64, 0:ncols], r=[tbr], w=[R_kg], q="pool")
            else:
                B.dma(kgA_v[ctxj, h, 0:64, t0:t0 + ncols], tb_[0:64, 0:ncols], r=[tbr], w=[R_kgA], q="pool")
        B.cut(6)
        for cg in range(4):
            wt, wr = B.load_w(win_d, 8, C_V + cg * 256, 256)
            for tt_ in range(ntt):
                pt, pr = banks[4 + tt_ % 2]
                B.mms(pt[:, 0:256], [(uT[:, kc, tt_ * 128:(tt_ + 1) * 128], wt[:, kc, :]) for kc in range(8)],
                      r=[wr, R_u], w=[pr])
                if ctxj is None:
                    ta, tr = nextA()
                    B.cp(ta[:, 0:256], pt[:, 0:256], r=[pr], w=[tr])
                    B.dma(v_o[t0 + tt_ * 128:t0 + (tt_ + 1) * 128, cg * 256:(cg + 1) * 256], ta[:, 0:256], r=[tr], w=[], q="pool")
                vs, vr = vst[(cg * ntt + tt_) % 2]
                B.cp(vs[:, 0:4, 0:64], pt[:, 0:256].rearrange("p (h d) -> p h d", h=4), r=[pr], w=[vr], eng="act")
                gt = (t0 // 128) + tt_
                if ctxj is None:
                    B.dma(vg_s[cg * 4:(cg + 1) * 4, gt, :, :].rearrange("h p d -> p h d"), vs[:, 0:4, :], r=[vr], w=[R_vg], q="pool")
                else:
                    B.dma(vgA_v[ctxj, cg * 4:(cg + 1) * 4, gt, :, :].rearrange("h p d -> p h d"), vs[:, 0:4, :], r=[vr], w=[R_vgA], q="pool")
        B.cut(7)
        wt, wr = B.load_w(win_d, 8, C_F, 16)
        for tt_ in range(ntt):
            gt = (t0 // 128) + tt_
            pt, pr = banks[6]
            B.mms(pt[:, 0:16], [(uT[:, kc, tt_ * 128:(tt_ + 1) * 128], wt[:, kc, :]) for kc in range(8)], r=[wr, R_u], w=[pr])
            ta, tr = nextA()
            softplus_to(ta[:, 0:16], pt[:, 0:16], bfb[:], 16, r=[pr, R_sm], w=[tr], neg_in=True)
            B.ts(logf[:, gt, :], ta[:, 0:16], -1.0, None, ALU.mult, r=[tr], w=[R_lf])
            if ctxj is None:
                B.dma(lf_o[gt * 128:(gt + 1) * 128, :], logf[:, gt, :], r=[R_lf], w=[], q="pool")
        if B.stage < 2:
            return
        wt, wr = B.load_w(win_d, 8, C_DT, 32)
        for tt_ in range(ntt):
            gt = (t0 // 128) + tt_
            pt, pr = banks[6]
            B.mms(pt[:, 0:32], [(uT[:, kc, tt_ * 128:(tt_ + 1) * 128], wt[:, kc, :]) for kc in range(8)], r=[wr, R_u], w=[pr])
            softplus_to(dtsb[:, gt, :], pt[:, 0:32], dtb[:], 32, r=[pr, R_sm], w=[R_dt])
        for cg in range(8 if ctxj is None else 0):
            wt, wr = B.load_w(win_d, 8, C_Z + cg * 256, 256)
            for tt_ in range(ntt):
                pt, pr = banks[4 + tt_ % 2]
                B.mms(pt[:, 0:256], [(uT[:, kc, tt_ * 128:(tt_ + 1) * 128], wt[:, kc, :]) for kc in range(8)],
                      r=[wr, R_u], w=[pr])
                tb_, tbr = nextB()
                B.cp(tb_[:, 0:256], pt[:, 0:256], r=[pr], w=[tbr], eng="act")
                B.dma(z_s[t0 + tt_ * 128:t0 + (tt_ + 1) * 128, cg * 256:(cg + 1) * 256], tb_[:, 0:256], r=[tbr], w=[R_z], q="pool")
        for c in range(24 if ctxj is None else 20):
            wt, wr = B.load_w(win_d, 8, C_X + c * 128, 128)
            pt, pr = banks[c % 2]
            B.mms(pt[:, 0:ncols], [(wt[:, kc, :], uT[:, kc, 0:ncols]) for kc in range(8)], r=[wr, R_u], w=[pr])
            xbt, xr = xb[c % 2]
            B.cp(xbt[:, 0:3], halo[:, c, :], r=[R_halo], w=[xr])
            B.cp(xbt[:, 3:3 + ncols], pt[:, 0:ncols], r=[pr], w=[xr])
            B.cp(halo[:, c, :], xbt[:, ncols:ncols + 3], r=[xr], w=[R_halo])
            if blk == NBLK - 1 and ctxj is None:
                B.dma(cv_o[c * 128:(c + 1) * 128, :], xbt[:, ncols:ncols + 3], r=[xr], w=[], q="pool")
            ta, tr = nextA()
            B.ts(ta[:, 0:ncols], xbt[:, 0:ncols], cw[:, c, 0:1], cbv[:, c:c + 1], ALU.mult, ALU.add, r=[xr, R_sm], w=[tr])
            for i in range(1, 4):
                B.stt(ta[:, 0:ncols], xbt[:, i:i + ncols], cw[:, c, i:i + 1], ta[:, 0:ncols], ALU.mult, ALU.add,
                      r=[xr, R_sm, tr], w=[tr])
            tb_, tbr = nextB()
            B.act(tb_[:, 0:ncols], ta[:, 0:ncols], AF.Silu, r=[tr], w=[tbr])
            if c >= 20:
                B.dma(CT_s[(c - 20) * 128:(c - 19) * 128, t0:t0 + ncols], tb_[:, 0:ncols], r=[tbr], w=[R_CT], q="pool")
                continue
            if c >= 16 and ctxj is None:
                B.dma(BT_s[(c - 16) * 128:(c - 15) * 128, t0:t0 + ncols], tb_[:, 0:ncols], r=[tbr], w=[R_BT], q="pool")
            for tt_ in range(ntt):
                P.add("pe", lambda e, tt_=tt_, tb_=tb_: e.transpose(ptT[:, tt_ * 128:(tt_ + 1) * 128],
                                                                     tb_[:, tt_ * 128:(tt_ + 1) * 128], ident_b[:]),
                      r=[tbr, R_const], w=[prT])
            tb2, tbr2 = nextB()
            B.cp(tb2[:, 0:ncols], ptT[:, 0:ncols], r=[prT], w=[tbr2])
            for tt_ in range(ntt):
                rows = slice(t0 + tt_ * 128, t0 + (tt_ + 1) * 128)
                if c < 16:
                    B.dma(xs_s[rows, c * 128:(c + 1) * 128], tb2[:, tt_ * 128:(tt_ + 1) * 128], r=[tbr2], w=[R_xs], q="pool")
                else:
                    B.dma(Bt_s[rows, (c - 16) * 128:(c - 15) * 128], tb2[:, tt_ * 128:(tt_ + 1) * 128], r=[tbr2], w=[R_Bt], q="pool")
        for gi, (c0, dst, rr) in enumerate([(C_GA, ga_s, R_ga), (C_GS, gs_s, R_gs)] if ctxj is None else []):
            for c in range(8):
                wt, wr = B.load_w(win_d, 8, c0 + c * 128, 128)
                pt, pr = banks[2 + c % 2]
                B.mms(pt[:, 0:ncols], [(wt[:, kc, :], uT[:, kc, 0:ncols]) for kc in range(8)], r=[wr, R_u], w=[pr])
                tb_, tbr = nextB()
                B.act(tb_[:, 0:ncols], pt[:, 0:ncols], AF.Sigmoid, r=[pr], w=[tbr])
                B.dma(dst[c * 128:(c + 1) * 128, t0:t0 + ncols], tb_[:, 0:ncols], r=[tbr], w=[rr], q="pool")


    xTd_v = xT_d.rearrange("(k p) t -> p k t", p=128)
    x1o_v = x1_o.rearrange("(k p) t -> p k t", p=128)

    xs1 = B.sb("xs1", [128, 8, 32], F32)
    sconv = B.sb("sconv", [128, 24, 2, 3], F32)
    R_xs1 = Res()

    def run_own_p1():
        NS_ = 32
        NX_ = 35
        sgroups = [(0, 16, 1), (16, 16, 2), (32, 3, 0)]
        B.dma(xT[:, :, 0:NX_], xsT_d.rearrange("(k p) t -> p k t", p=128), w=[R_x])
        rms_mod(xT, 0, sgroups, NX_)
        ffn(0, w1a_d, w3a_d, w2a_d, sgroups, NX_)
        rms_mod(xT, 1, sgroups, NX_)
        B.dma(sconv[:], scv_d[:, :, :, :], w=[R_sm])
        B.cp(xs1[:], xT[:, :, 0:NS_], r=[R_x], w=[R_xs1])
        for h in range(H_A):
            wt, wr = B.load_w(win_d, 8, C_Q + h * 64, 64)
            pt, pr = banks[h % 2]
            B.mms(pt[0:64, 0:NS_], [(wt[:, kc, :], uT[:, kc, 0:NS_]) for kc in range(8)], r=[wr, R_u], w=[pr])
            tb_, tbr = nextB()
            B.ts(tb_[0:64, 0:NS_], pt[0:64, 0:NS_], 0.125, None, ALU.mult, r=[pr], w=[tbr])
            B.dma(q2_s[h, 0:64, :], tb_[0:64, 0:NS_], r=[tbr], w=[R_s2], q="pool")
            wt, wr = B.load_w(win_d, 8, C_K + h * 64, 64)
            pt, pr = banks[2 + h % 2]
            B.mms(pt[0:64, 0:NS_], [(wt[:, kc, :], uT[:, kc, 0:NS_]) for kc in range(8)], r=[wr, R_u], w=[pr])
            kf, kr = kst[h % 2]
            B.cp(kf[:, 0:NS_], pt[0:64, 0:NS_], r=[pr], w=[kr])
            B.dma(ksT_o[h * 64:(h + 1) * 64, :], kf[:, 0:NS_], r=[kr], w=[], q="pool")
            tb_, tbr = nextB()
            B.cp(tb_[0:64, 0:NS_], pt[0:64, 0:NS_], r=[pr], w=[tbr], eng="act")
            B.dma(k2_s[h, 0:64, :], tb_[0:64, 0:NS_], r=[tbr], w=[R_s2], q="pool")
            B.dma(k2_s[h, 64:66, :], onesrow[64:66, 0:NS_], r=[R_const], w=[R_s2], q="pool")
        for cg in range(4):
            wt, wr = B.load_w(win_d, 8, C_V + cg * 256, 256)
            for sq_ in range(2):
                pt, pr = banks[4 + sq_]
                B.mms(pt[0:16, 0:256], [(uT[:, kc, sq_ * 16:(sq_ + 1) * 16], wt[:, kc, :]) for kc in range(8)], r=[wr, R_u], w=[pr])
                ta, tr = nextA()
                B.cp(ta[0:16, 0:256], pt[0:16, 0:256], r=[pr], w=[tr])
                B.dma(vs_o[sq_ * 16:(sq_ + 1) * 16, cg * 256:(cg + 1) * 256], ta[0:16, 0:256], r=[tr], w=[], q="pool")
                vs, vr = vst[sq_]
                B.cp(vs[0:16, 0:4, 0:64], pt[0:16, 0:256].rearrange("p (h d) -> p h d", h=4), r=[pr], w=[vr], eng="act")
                B.dma(v2_s[cg * 4:(cg + 1) * 4, sq_, :, :].rearrange("h p d -> p h d"), vs[0:16, 0:4, :], r=[vr], w=[R_s2], q="pool")
        wt, wr = B.load_w(win_d, 8, C_F, 16)
        for sq_ in range(2):
            pt, pr = banks[6]
            B.mms(pt[0:16, 0:16], [(uT[:, kc, sq_ * 16:(sq_ + 1) * 16], wt[:, kc, :]) for kc in range(8)], r=[wr, R_u], w=[pr])
            ta, tr = nextA()
            softplus_to(ta[0:16, 0:16], pt[0:16, 0:16], bfb[0:16, :], 16, r=[pr, R_sm], w=[tr], neg_in=True, pn=16)
            B.ts(logf[0:16, NT + sq_, :], ta[0:16, 0:16], -1.0, None, ALU.mult, r=[tr], w=[R_lf])
            B.dma(lfs_o[sq_ * 16:(sq_ + 1) * 16, :], logf[0:16, NT + sq_, :], r=[R_lf], w=[], q="pool")
        if B.stage >= 2:
            wt, wr = B.load_w(win_d, 8, C_DT, 32)
            for sq_ in range(2):
                pt, pr = banks[6]
                B.mms(pt[0:16, 0:32], [(uT[:, kc, sq_ * 16:(sq_ + 1) * 16], wt[:, kc, :]) for kc in range(8)], r=[wr, R_u], w=[pr])
                softplus_to(dtsb[0:16, NT + sq_, :], pt[0:16, 0:32], dtb[0:16, :], 32, r=[pr, R_sm], w=[R_dt], pn=16)
            for cg in range(8):
                wt, wr = B.load_w(win_d, 8, C_Z + cg * 256, 256)
                for sq_ in range(2):
                    pt, pr = banks[4 + sq_]
                    B.mms(pt[0:16, 0:256], [(uT[:, kc, sq_ * 16:(sq_ + 1) * 16], wt[:, kc, :]) for kc in range(8)], r=[wr, R_u], w=[pr])
                    tb_, tbr = nextB()
                    B.cp(tb_[0:16, 0:256], pt[0:16, 0:256], r=[pr], w=[tbr], eng="act")
                    B.dma(z2_s[sq_ * 16:(sq_ + 1) * 16, cg * 256:(cg + 1) * 256], tb_[0:16, 0:256], r=[tbr], w=[R_s2], q="pool")
            for c in range(24):
                wt, wr = B.load_w(win_d, 8, C_X + c * 128, 128)
                pt, pr = banks[c % 2]
                B.mms(pt[:, 0:NX_], [(wt[:, kc, :], uT[:, kc, 0:NX_]) for kc in range(8)], r=[wr, R_u], w=[pr])
                xbt, xr = xb[c % 2]
                B.cp(xbt[:, 0:3], sconv[:, c, 0, :], r=[R_sm], w=[xr])
                B.cp(xbt[:, 3:19], pt[:, 0:16], r=[pr], w=[xr])
                B.cp(xbt[:, 19:22], sconv[:, c, 1, :], r=[R_sm], w=[xr])
                B.cp(xbt[:, 22:38], pt[:, 16:32], r=[pr], w=[xr])
                B.ts(halo[:, c, :], pt[:, 32:35], flg[:, 16:17], None, ALU.mult, r=[pr, R_sm], w=[R_halo])
                B.dma(cvs_o[c * 128:(c + 1) * 128, 0:3], xbt[:, 16:19], r=[xr], w=[], q="pool")
                B.dma(cvs_o[c * 128:(c + 1) * 128, 3:6], xbt[:, 35:38], r=[xr], w=[], q="pool")
                ta, tr = nextA()
                B.ts(ta[:, 0:35], xbt[:, 0:35], cw[:, c, 0:1], cbv[:, c:c + 1], ALU.mult, ALU.add, r=[xr, R_sm], w=[tr])
                for i in range(1, 4):
                    B.stt(ta[:, 0:35], xbt[:, i:i + 35], cw[:, c, i:i + 1], ta[:, 0:35], ALU.mult, ALU.add, r=[xr, R_sm, tr], w=[tr])
                tb_, tbr = nextB()
                B.act(tb_[:, 0:35], ta[:, 0:35], AF.Silu, r=[tr], w=[tbr])
                offs = (0, 19)
                if c >= 20:
                    for sq_ in range(2):
                        B.dma(CT2_s[(c - 20) * 128:(c - 19) * 128, sq_ * 16:(sq_ + 1) * 16], tb_[:, offs[sq_]:offs[sq_] + 16], r=[tbr], w=[R_s2], q="pool")
                    continue
                if c >= 16:
                    for sq_ in range(2):
                        B.dma(BT2_s[(c - 16) * 128:(c - 15) * 128, sq_ * 16:(sq_ + 1) * 16], tb_[:, offs[sq_]:offs[sq_] + 16], r=[tbr], w=[R_s2], q="pool")
                for sq_ in range(2):
                    P.add("pe", lambda e, sq_=sq_, tb_=tb_: e.transpose(ptT[0:16, sq_ * 128:(sq_ + 1) * 128], tb_[:, offs[sq_]:offs[sq_] + 16], ident_b[:]),
                          r=[tbr, R_const], w=[prT])
                tb2, tbr2 = nextB()
                B.cp(tb2[0:16, 0:256], ptT[0:16, 0:256], r=[prT], w=[tbr2])
                for sq_ in range(2):
                    rows2 = slice(sq_ * 16, (sq_ + 1) * 16)
                    if c < 16:
                        B.dma(xs2_s[rows2, c * 128:(c + 1) * 128], tb2[0:16, sq_ * 128:(sq_ + 1) * 128], r=[tbr2], w=[R_s2], q="pool")
                    else:
                        B.dma(Bt2_s[rows2, (c - 16) * 128:(c - 15) * 128], tb2[0:16, sq_ * 128:(sq_ + 1) * 128], r=[tbr2], w=[R_s2], q="pool")
            for (c0, dst) in [(C_GA, ga2_s), (C_GS, gs2_s)]:
                for c in range(8):
                    wt, wr = B.load_w(win_d, 8, c0 + c * 128, 128)
                    pt, pr = banks[2 + c % 2]
                    B.mms(pt[:, 0:NS_], [(wt[:, kc, :], uT[:, kc, 0:NS_]) for kc in range(8)], r=[wr, R_u], w=[pr])
                    tb_, tbr = nextB()
                    B.act(tb_[:, 0:NS_], pt[:, 0:NS_], AF.Sigmoid, r=[pr], w=[tbr])
                    B.dma(dst[c * 128:(c + 1) * 128, :], tb_[:, 0:NS_], r=[tbr], w=[R_s2], q="pool")

        xTd_v = xT_d.rearrange("(k p) t -> p k t", p=128)
        x1o_v = x1_o.rearrange("(k p) t -> p k t", p=128)
        for blk in range(NBLK):
            t0 = blk * TB
            groups = [(0, TB, 0)]
            B.dma(xT[:, :, :], xTd_v[:, :, t0:t0 + TB], w=[R_x])
            if B.cutn <= 2:
                B.dma(x1o_v[:, :, t0:t0 + TB], xT[:, :, :], r=[R_x], w=[R_x1], q="pool")
            B.cut(2)
            rms_mod(xT, 0, groups, TB)
            if B.cutn <= 3:
                B.cp(xT[:, :, :], uT[:, :, :], r=[R_u], w=[R_x])
                B.dma(x1o_v[:, :, t0:t0 + TB], xT[:, :, :], r=[R_x], w=[R_x1], q="pool")
            B.cut(3)
            ffn(0, w1a_d, w3a_d, w2a_d, groups, TB)
            B.dma(x1o_v[:, :, t0:t0 + TB], xT[:, :, :], r=[R_x], w=[R_x1], q="pool")
            B.cut(4)
            rms_mod(xT, 1, groups, TB)
            win_block(blk, groups, TB)


    if B.stage < 3:
        run_own_p1()
        return
    tri = B.sb("tri", [128, 128], F32)
    mneg = B.sb("mneg", [128, 128], F32)
    e0row = B.sb("e0row", [128, 128], F32)
    selt = B.sb("selt", [16, 16, 66], F32)
    selc = B.sb("selc", [128, 2], F32)
    a_bc = B.sb("a_bc", [128, 32], F32)
    dsk = B.sb("dsk", [128, 32], F32)
    gssd = B.sb("gssd", [128, 16], F32)
    R_c3 = Res()
    B.dma(tri[:], tri_d[:, :], w=[R_c3])
    B.dma(mneg[:], mneg_d[:, :], w=[R_c3])
    B.dma(e0row[:], e0_d[:, :], w=[R_c3])
    B.dma(selt[:], sel_d[:, :, :], w=[R_c3])
    B.dma(selc[:], selc_d[:, :], w=[R_c3])
    B.dma(a_bc[:], alog_d[:, :], w=[R_c3])
    B.dma(dsk[:], dsk_d[:, :], w=[R_c3])
    B.dma(gssd[:], gssd_d[:, :], w=[R_c3])
    B.act(a_bc[:], a_bc[:], AF.Exp, r=[R_c3], w=[R_c3])
    B.ts(a_bc[:], a_bc[:], -1.0, None, ALU.mult, r=[R_c3], w=[R_c3])

    Fc = B.sb("Fc", [128, NT, 16], F32)
    carry = B.sb("carry", [128, 16], F32)
    R_F, R_carry = Res(), Res()
    arena = B.sb("arena", [128, 8192], BF16)
    R_ar = [Res() for _ in range(4)]
    xtm_s = [arena[:, 0:2048], arena[:, 2048:4096]]
    ztm_s = [arena[:, 4096:6144], arena[:, 6144:8192]]
    oT = arena[0:64, :].rearrange("p (h t) -> p h t", h=16)
    bc_s = [(B.sb(f"bcs{i}", [128, 3, 512], BF16), Res()) for i in range(2)]
    Hst = B.sb("Hst", [128, DIN], F32)
    Hb = B.sb("Hb", [128, DIN], BF16)
    yz = B.sb("yz", [128, DIN], F32)
    sm = B.sb("sm", [128, 8, 32], F32)
    cbm = B.sb("cbm", [128, 4, 128], F32)
    xd = B.sb("xd", [128, DIN], BF16)
    ynb = xd
    wTb = [(B.sb(f"wTb{i}", [128, 4, 128], BF16), Res()) for i in range(2)]
    D4 = [(B.sb(f"D4{i}", [128, 4, 128], F32), Res()) for i in range(2)]
    sg4 = [(B.sb(f"sg4{i}", [128, 4, 128], F32), Res()) for i in range(2)]
    ssq = B.sb("ssq", [128, 4], F32)
    R_H, R_Hb, R_yz, R_ynb, R_sm, R_cbm, R_xd, R_ssq = [Res() for _ in range(8)]
    R_ynb = R_xd
    P.add("pool", lambda e: e.memset(Hst[:], 0.0), w=[R_H])
    ynT = gT[:, 0:16, :]
    BT_v = BT_s.rearrange("(g n) t -> n g t", n=128)
    CT_v = CT_s.rearrange("(g n) t -> n g t", n=128)

    def bc3(ap2, n, m):
        return ap2.unsqueeze(2).broadcast_to([128, n, m])

    def ssd_tile(gt, want_y=True, L=128, samp=None, maskj=None):
        sl = gt % 2
        if samp is None:
            rows = slice(gt * 128, (gt + 1) * 128)
            src_x, src_z, src_Bt, src_BT, src_CT = xs_s, z_s, Bt_s, BT_v, CT_v
            rx_, rz_, rbt_, rBT_, rCT_ = R_xs, R_z, R_Bt, R_BT, R_CT
            dti = gt
            ycol0 = (gt % 4) * 128
        else:
            sl = samp
            rows = slice(samp * 16, samp * 16 + 16)
            src_x, src_z, src_Bt, src_BT, src_CT = xs2_s, z2_s, Bt2_s, BT2_v, CT2_v
            rx_ = rz_ = rbt_ = rBT_ = rCT_ = R_s2
            dti = NT + samp
            ycol0 = samp * 16
        xtm, ztm = xtm_s[sl], ztm_s[sl]
        Rx_, Rz_ = R_ar[sl], R_ar[2 + sl]
        bct, Rb_ = bc_s[sl]
        B.dma(xtm[0:L, :], src_x[rows, :], r=[rx_], w=[Rx_])
        if want_y:
            B.dma(ztm[0:L, :], src_z[rows, :], r=[rz_], w=[Rz_])
        B.dma(bct[0:L, 0, :], src_Bt[rows, :], r=[rbt_], w=[Rb_])
        if want_y:
            B.dma(bct[:, 1, 0:4 * L].rearrange("p (g t) -> p g t", g=4), src_BT[:, :, rows], r=[rBT_], w=[Rb_])
            B.dma(bct[:, 2, 0:4 * L].rearrange("p (g t) -> p g t", g=4), src_CT[:, :, rows], r=[rCT_], w=[Rb_])
        Btm = bct[0:L, 0, :].rearrange("p (g n) -> p g n", g=4)
        BTf = bct[:, 1, 0:4 * L].rearrange("p (g t) -> p g t", g=4)
        CTf = bct[:, 2, 0:4 * L].rearrange("p (g t) -> p g t", g=4)
        xt3 = xtm[0:L, :].rearrange("p (h d) -> p h d", h=32)
        dA, acs, ea, de, tmpv = [sm[0:L, j, :] for j in (0, 1, 3, 4, 6)]
        al, cd = sm[:, 2, :], sm[:, 5, :]
        dtv = dtsb[0:L, dti, :]
        B.tt(dA, dtv, a_bc[0:L, :], ALU.mult, r=[R_dt, R_c3], w=[R_sm])
        pt, pr = banks[0]
        P.add("pe", lambda e: e.matmul(pt[0:L, 0:32], tri[0:L, 0:L], dA, start=True, stop=True), r=[R_c3, R_sm], w=[pr])
        P.add("pe", lambda e: e.matmul(pt[:, 32:64], ones_f[0:L, :], dA, start=True, stop=True), r=[R_const, R_sm], w=[pr])
        B.cp(acs, pt[0:L, 0:32], r=[pr], w=[R_sm])
        B.cp(al, pt[:, 32:64], r=[pr], w=[R_sm])
        B.act(ea, acs, AF.Exp, r=[R_sm], w=[R_sm])
        B.act(cd, al, AF.Exp, r=[R_sm], w=[R_sm])
        if maskj is not None:
            B.ts(cd, cd, -1.0, flg[:, maskj:maskj + 1], ALU.add, ALU.mult, r=[R_sm], w=[R_sm])
            B.ts(cd, cd, 1.0, None, ALU.add, r=[R_sm], w=[R_sm])
        B.tt(tmpv, al[0:L, :], acs, ALU.subtract, r=[R_sm], w=[R_sm])
        B.act(tmpv, tmpv, AF.Exp, r=[R_sm], w=[R_sm])
        B.tt(de, tmpv, dtv, ALU.mult, r=[R_sm, R_dt], w=[R_sm])
        if want_y:
            B.cp(Hb[:], Hst[:], r=[R_H], w=[R_Hb], eng="act")
            pc, prc = banks[1]
            for g in range(4):
                P.add("pe", lambda e, g=g: e.matmul(pc[0:L, g * L:(g + 1) * L], BTf[:, g, :], CTf[:, g, :], start=True, stop=True),
                      r=[Rb_], w=[prc])
            cbv_ = cbm[:].rearrange("p g t -> p (g t)")[0:L, 0:4 * L].rearrange("p (g t) -> p g t", g=4)
            B.tt(cbv_, pc[0:L, 0:4 * L].rearrange("p (g t) -> p g t", g=4), tri[0:L, 0:L].unsqueeze(1).broadcast_to([L, 4, L]), ALU.mult,
                 r=[prc, R_c3], w=[R_cbm])
            for g in range(4):
                pyd, pryd = banks[4]
                for hb in range(2):
                    h0 = g * 8 + hb * 4
                    d4t, rd4 = D4[hb]
                    s4t, rs4 = sg4[hb]
                    wt4t, rw4 = wTb[hb]
                    d4 = d4t[:].rearrange("p g t -> p (g t)")[0:L, 0:4 * L].rearrange("p (g t) -> p g t", g=4)
                    s4 = s4t[:].rearrange("p g t -> p (g t)")[0:L, 0:4 * L].rearrange("p (g t) -> p g t", g=4)
                    wt4 = wt4t[:].rearrange("p g t -> p (g t)")[0:L, 0:4 * L].rearrange("p (g t) -> p g t", g=4)
                    B.tt(d4, ident_f[0:L, 0:L].unsqueeze(1).broadcast_to([L, 4, L]),
                         acs[:, h0:h0 + 4].unsqueeze(2).broadcast_to([L, 4, L]), ALU.mult, r=[R_const, R_sm], w=[rd4])
                    pb, prb = banks[2 + hb]
                    P.add("pe", lambda e, d4t=d4t, pb=pb: e.matmul(pb[0:L, 0:4 * L], ones_f[0:L, 0:L],
                                                                    d4t[:].rearrange("p g t -> p (g t)")[0:L, 0:4 * L], start=True, stop=True),
                          r=[R_const, rd4], w=[prb])
                    for j in range(4):
                        B.stt(s4[:, j, :], pb[0:L, j * L:(j + 1) * L], acs[:, h0 + j:h0 + j + 1], mneg[0:L, 0:L], ALU.subtract, ALU.add,
                              r=[prb, R_sm, R_c3], w=[rs4])
                    B.act(s4, s4, AF.Exp, r=[rs4], w=[rs4])
                    for j in range(4):
                        B.stt(wt4[:, j, :], s4[:, j, :], dtv[:, h0 + j:h0 + j + 1], cbv_[:, g, :], ALU.mult, ALU.mult,
                              r=[rs4, R_dt, R_cbm], w=[rw4])
                    for j in range(4):
                        hh = hb * 4 + j
                        P.add("pe", lambda e, j=j, hh=hh, wt4=wt4, h0=h0: e.matmul(pyd[0:L, hh * 64:(hh + 1) * 64], wt4[:, j, :], xt3[:, h0 + j, :],
                                                                                    start=True, stop=True), r=[rw4, Rx_], w=[pryd])
                pyo, pryo = banks[5]
                P.add("pe", lambda e, g=g, pyo=pyo: e.matmul(pyo[0:L, :], CTf[:, g, :], Hb[:, g * 512:(g + 1) * 512], start=True, stop=True),
                      r=[Rb_, R_Hb], w=[pryo])
                yg = yz[0:L, g * 512:(g + 1) * 512].rearrange("p (h d) -> p h d", h=8)
                B.tt(yg, pyo[0:L, :].rearrange("p (h d) -> p h d", h=8), ea[:, g * 8:(g + 1) * 8].unsqueeze(2).broadcast_to([L, 8, 64]),
                     ALU.mult, r=[pryo, R_sm], w=[R_yz])
                B.tt(yg, pyd[0:L, :].rearrange("p (h d) -> p h d", h=8), yg, ALU.add, r=[pryd, R_yz], w=[R_yz])
                ta, tr = nextA()
                ta3 = ta[0:L, :].rearrange("p (h d) -> p h d", h=8)
                B.tt(ta3, xt3[:, g * 8:(g + 1) * 8, :], dsk[0:L, g * 8:(g + 1) * 8].unsqueeze(2).broadcast_to([L, 8, 64]), ALU.mult,
                     r=[Rx_, R_c3], w=[tr])
                B.tt(yg, yg, ta3, ALU.add, r=[tr, R_yz], w=[R_yz])
        B.tt(xd[0:L, :].rearrange("p (h d) -> p h d", h=32), xt3, de.unsqueeze(2).broadcast_to([L, 32, 64]), ALU.mult,
             r=[Rx_, R_sm], w=[R_xd])
        for g in range(4):
            pS, prS = banks[6]
            P.add("pe", lambda e, g=g, pS=pS: e.matmul(pS[:, :], Btm[:, g, :], xd[0:L, g * 512:(g + 1) * 512], start=True, stop=True),
                  r=[Rb_, R_xd], w=[prS])
            Hg = Hst[:, g * 512:(g + 1) * 512].rearrange("p (h d) -> p h d", h=8)
            B.tt(Hg, Hg, bc3(cd[:, g * 8:(g + 1) * 8], 8, 64), ALU.mult, r=[R_sm, R_Hb], w=[R_H])
            if maskj is not None:
                B.stt(Hg, pS[:, :].rearrange("p (h d) -> p h d", h=8), flg[:, maskj:maskj + 1], Hg, ALU.mult, ALU.add, r=[prS], w=[R_H])
            else:
                B.tt(Hg, Hg, pS[:, :].rearrange("p (h d) -> p h d", h=8), ALU.add, r=[prS], w=[R_H])
        if not want_y:
            return
        for g in range(4):
            ta, tr = nextA()
            B.act(ta[0:L, :], ztm[0:L, g * 512:(g + 1) * 512], AF.Silu, r=[Rz_], w=[tr])
            B.tt(yz[0:L, g * 512:(g + 1) * 512], yz[0:L, g * 512:(g + 1) * 512], ta[0:L, :], ALU.mult, r=[tr, R_yz], w=[R_yz])
            ta2, tr2 = nextA()
            P.add("act", lambda e, g=g, ta2=ta2: e.activation(out=ta2[0:L, :], in_=yz[0:L, g * 512:(g + 1) * 512], func=AF.Square,
                                                               accum_out=ssq[0:L, g:g + 1]), r=[R_yz], w=[tr2, R_ssq])
        rs_ = sm[0:L, 7, 0:1]
        P.add("dve", lambda e: e.tensor_reduce(out=rs_, in_=ssq[0:L, 0:4], axis=AX.X, op=ALU.add), r=[R_ssq], w=[R_sm])
        B.act(rs_, rs_, AF.Sqrt, r=[R_sm, R_const], w=[R_sm], bias=epsc[0:L, :], scale=1.0 / DIN)
        P.add("dve", lambda e: e.reciprocal(out=rs_, in_=rs_), r=[R_sm], w=[R_sm])
        B.ts(ynb[0:L, :], yz[0:L, :], rs_, None, ALU.mult, r=[R_yz, R_sm], w=[R_ynb])
        for half in range(2):
            for c in range(8):
                cc = half * 8 + c
                P.add("pe", lambda e, c=c, cc=cc: e.transpose(ptT[:, c * 128:c * 128 + L], ynb[0:L, cc * 128:(cc + 1) * 128], ident_b[0:L, 0:L]),
                      r=[R_ynb, R_const], w=[prT])
            for c in range(8):
                cc = half * 8 + c
                B.ts(ynT[:, cc, ycol0:ycol0 + L], ptT[:, c * 128:c * 128 + L], gssd[:, cc:cc + 1], None, ALU.mult,
                     r=[prT, R_c3], w=[R_gT])

    Kt = [(B.sb(f"Kt{i}", [66, T], BF16), Res()) for i in range(2)]
    Vt = [(B.sb(f"Vt{i}", [128, NT, 65], BF16), Res()) for i in range(2)]
    Qa = [(B.sb(f"Qa{i}", [66, TB], BF16), Res()) for i in range(2)]
    Pt = [(B.sb(f"Pt{i}", [128, TB], BF16), Res()) for i in range(3)]
    FT = sq[0:16, :]
    rbc = B.sb("rbc", [128, 16], F32)
    biasT = B.sb("biasT", [128, NT, 16], F32)
    arb = B.sb("arb", [66, TB], BF16)
    R_FT, R_rbc, R_bias, R_arow, R_rl, R_rlb = [Res() for _ in range(6)]
    R_FT = R_sq
    pti = [0]

    def attn_block(blk):
        t0 = blk * TB
        nkt = 4 * (blk + 1)
        pf, prf = banks[6]
        for j in range(4):
            P.add("pe", lambda e, j=j: e.transpose(pf[0:16, j * 128:(j + 1) * 128], Fc[:, 4 * blk + j, :], ident_f[:]),
                  r=[R_F, R_const], w=[prf])
        B.cp(FT[:, :], pf[0:16, :], r=[prf], w=[R_FT])
        P.add("pe", lambda e: e.matmul(pf[:, 0:16], e0row[:], Fc[:, 4 * blk, :], start=True, stop=True), r=[R_c3, R_F], w=[prf])
        B.cp(rbc[:], pf[:, 0:16], r=[prf], w=[R_rbc])
        B.tt(biasT[:, 0:nkt, :], rbc[:].unsqueeze(1).broadcast_to([128, nkt, 16]), Fc[:, 0:nkt, :], ALU.subtract,
             r=[R_rbc, R_F], w=[R_bias])
        B.tt(rdj[:], delta[:, 0:NCORES, :], rbc[:].unsqueeze(1).broadcast_to([128, NCORES, 16]), ALU.add,
             r=[R_delta, R_rbc], w=[R_rdj])
        kvi = [0]
        for h in range(H_A):
            qa_, qr_ = Qa[h % 2]
            B.dma(qa_[0:64, :], q_s[h, 0:64, t0:t0 + TB], r=[R_q], w=[qr_])
            pa, pra = banks[4]
            P.add("pe", lambda e, h=h: e.matmul(pa[0:66, :], selt[:, h, :], FT[:, :], start=True, stop=True), r=[R_c3, R_FT], w=[pra])
            ar0, rr0 = nextA()
            ar1, rr1 = nextA()
            ar2, rr2 = nextA()
            B.cp(ar0[64:66, :], pa[64:66, :], r=[pra], w=[rr0])
            B.ts(ar1[64:66, :], ar0[64:66, :], ar0[64:66, 0:1], None, ALU.subtract, r=[rr0], w=[rr1])
            B.cp(arb[64:66, :], ar1[64:66, :], r=[rr1], w=[R_arow])
            B.tt(ar0[64:66, :], ar1[64:66, :], arb[64:66, :], ALU.subtract, r=[rr1, R_arow], w=[rr0])
            B.ts(ar2[64:66, :], arb[64:66, :], selc[64:66, 0:1], None, ALU.mult, r=[R_arow, R_c3], w=[rr2])
            B.stt(qa_[64:66, :], ar0[64:66, :], selc[64:66, 1:2], ar2[64:66, :], ALU.mult, ALU.add,
                  r=[rr0, rr2, R_c3], w=[qr_])
            po, pro = banks[2 + h % 2]
            first = [True]
            if CTX:
                for j in range(NCORES - 1):
                    kvi[0] += 1
                    kt_, kr_ = Kt[kvi[0] % 2]
                    vt_, vr_ = Vt[kvi[0] % 2]
                    B.dma(kt_[:, :], kgA_v[j, h, :, :], r=[R_kgA], w=[kr_])
                    B.dma(vt_[:, :, :], vgA_v[j, h, :, :, :].rearrange("t p d -> p t d"), r=[R_vgA], w=[vr_])
                    bj, bjr = biasJ[kvi[0] % 2]
                    B.ts(bj[:, :], FcA[:, j, :].rearrange("p (t h) -> p t h", h=16)[:, :, h], -1.0, rdj[:, j, h:h + 1], ALU.mult, ALU.add,
                         r=[R_FcA, R_rdj], w=[bjr])
                    B.ts(bj[:, :], bj[:, :], flg[:, j:j + 1], flg[:, 8 + j:9 + j], ALU.mult, ALU.add, r=[R_sm], w=[bjr])
                    for kt in range(NT):
                        ps_, prs = banks[kt % 2]
                        P.add("pe", lambda e, kt=kt, ps_=ps_, kt_=kt_, qa_=qa_: e.matmul(ps_[:, :], kt_[:, kt * 128:(kt + 1) * 128], qa_[:, :],
                                                                                         start=True, stop=True), r=[kr_, qr_], w=[prs])
                        pti[0] += 1
                        pT_, prp = Pt[pti[0] % 3]
                        B.act(pT_[:, :], ps_[:, :], AF.Exp, r=[prs, bjr], w=[prp], bias=bj[:, kt:kt + 1], scale=1.0)
                        st_ = first[0]
                        first[0] = False
                        P.add("pe", lambda e, kt=kt, pT_=pT_, vt_=vt_, po=po, st_=st_: e.matmul(po[0:65, :], vt_[:, kt, :], pT_[:, :],
                                                                                              start=st_, stop=False, skip_group_check=True),
                              r=[vr_, prp], w=[pro])
            kvi[0] += 1
            kt_, kr_ = Kt[kvi[0] % 2]
            vt_, vr_ = Vt[kvi[0] % 2]
            B.dma(kt_[:, 0:t0 + TB], kg_s[h, :, 0:t0 + TB], r=[R_kg], w=[kr_])
            B.dma(vt_[:, 0:nkt, :], vg_s[h, 0:nkt, :, :].rearrange("t p d -> p t d"), r=[R_vg], w=[vr_])
            for kt in range(nkt):
                j = kt - 4 * blk
                c0 = 128 * j if j > 0 else 0
                ps_, prs = banks[kt % 2]
                P.add("pe", lambda e, kt=kt, c0=c0, ps_=ps_, kt_=kt_, qa_=qa_: e.matmul(ps_[:, c0:TB], kt_[:, kt * 128:(kt + 1) * 128], qa_[:, c0:TB],
                                                                                       start=True, stop=True), r=[kr_, qr_], w=[prs])
                pti[0] += 1
                pT_, prp = Pt[pti[0] % 3]
                bias_ap = biasT[:, kt, h:h + 1]
                if j >= 0:
                    ta, tr = nextA()
                    B.tt(ta[:, c0:c0 + 128], ps_[:, c0:c0 + 128], mneg[:], ALU.add, r=[prs, R_c3], w=[tr])
                    B.act(pT_[:, c0:c0 + 128], ta[:, c0:c0 + 128], AF.Exp, r=[tr, R_bias], w=[prp], bias=bias_ap, scale=1.0)
                    if c0 + 128 < TB:
                        B.act(pT_[:, c0 + 128:TB], ps_[:, c0 + 128:TB], AF.Exp, r=[prs, R_bias], w=[prp], bias=bias_ap, scale=1.0)
                else:
                    B.act(pT_[:, :], ps_[:, :], AF.Exp, r=[prs, R_bias], w=[prp], bias=bias_ap, scale=1.0)
                P.add("pe", lambda e, kt=kt, c0=c0, pT_=pT_, vt_=vt_, po=po: e.matmul(po[0:65, c0:TB], vt_[:, kt, :], pT_[:, c0:TB],
                                                                                    start=(kt == 0 and first[0]), stop=(kt == nkt - 1),
                                                                                    skip_group_check=True),
                      r=[vr_, prp], w=[pro])
            rl, R_rl = nextA()
            rlb, R_rlb = nextA()
            P.add("dve", lambda e, po=po, rl=rl: e.reciprocal(out=rl[64:65, :], in_=po[64:65, :]), r=[pro], w=[R_rl])
            pb2, prb2 = banks[5]
            P.add("pe", lambda e, rl=rl: e.matmul(pb2[0:64, :], ones_f[64:65, 0:64], rl[64:65, :], start=True, stop=True), r=[R_const, R_rl], w=[prb2])
            B.cp(rlb[0:64, :], pb2[0:64, :], r=[prb2], w=[R_rlb])
            B.tt(oT[:, h, :], po[0:64, :], rlb[0:64, :], ALU.mult, r=[pro, R_rlb], w=R_ar)

    gab = [(B.sb(f"gab{i}", [128, 2, TB], BF16), Res()) for i in range(2)]
    mT = uT
    R_mT = R_u
    ga_v = ga_s.rearrange("(k p) t -> p k t", p=128)
    gs_v = gs_s.rearrange("(k p) t -> p k t", p=128)
    yTo_v = yT_o.rearrange("(k p) t -> p k t", p=128)

    def dense_block(blk, ncols=TB, groups=None, samp=False):
        t0 = 0 if samp else blk * TB
        if groups is None:
            groups = [(0, TB, 0)]
        gav = ga2_v if samp else ga_v
        gsv = gs2_v if samp else gs_v
        rga, rgs = (R_s2, R_s2) if samp else (R_ga, R_gs)
        for oc in range(8):
            wat, war = B.load_w(wa_d, 16, oc * 128, 128, pn=64)
            wst_, wsr = B.load_w(ws_d, 16, oc * 128, 128)
            pa_, pra_ = banks[oc % 2]
            ps2, prs2 = banks[2 + oc % 2]
            B.mms(pa_[:, 0:ncols], [(wat[:, h, :], oT[:, h, 0:ncols]) for h in range(16)], r=[war] + R_ar, w=[pra_])
            B.mms(ps2[:, 0:ncols], [(wst_[:, kc, :], ynT[:, kc, 0:ncols]) for kc in range(16)], r=[wsr, R_gT], w=[prs2])
            gt_, gr_ = gab[oc % 2]
            B.dma(gt_[:, 0, 0:ncols], gav[:, oc, t0:t0 + ncols], r=[rga], w=[gr_])
            B.dma(gt_[:, 1, 0:ncols], gsv[:, oc, t0:t0 + ncols], r=[rgs], w=[gr_])
            ta, tr = nextA()
            B.tt(ta[:, 0:ncols], pa_[:, 0:ncols], gt_[:, 0, 0:ncols], ALU.mult, r=[pra_, gr_], w=[tr])
            ta2, tr2 = nextA()
            B.tt(ta2[:, 0:ncols], ps2[:, 0:ncols], gt_[:, 1, 0:ncols], ALU.mult, r=[prs2, gr_], w=[tr2])
            B.tt(mT[:, oc, 0:ncols], ta[:, 0:ncols], ta2[:, 0:ncols], ALU.add, r=[tr, tr2], w=[R_mT])
        if samp:
            B.cp(xT[:, :, 0:ncols], xs1[:, :, 0:ncols], r=[R_xs1], w=[R_x])
        else:
            B.dma(xT[:, :, :], x1o_v[:, :, t0:t0 + TB], r=[R_x1], w=[R_x])
        for oc in range(8):
            wot, wor = B.load_w(wo_d, 8, oc * 128, 128)
            po_, pro_ = banks[4 + oc % 2]
            B.mms(po_[:, 0:ncols], [(wot[:, kc, :], mT[:, kc, 0:ncols]) for kc in range(8)], r=[wor, R_mT], w=[pro_])
            for (c0_, n_, m_) in groups:
                B.stt(xT[:, oc, c0_:c0_ + n_], po_[:, c0_:c0_ + n_], Gm[:, 1, oc, m_:m_ + 1], xT[:, oc, c0_:c0_ + n_],
                      ALU.mult, ALU.add, r=[pro_, R_AG], w=[R_x])
        rms_mod(xT, 2, groups, ncols)
        ffn(2, w1b_d, w3b_d, w2b_d, groups, ncols)
        pt, pr = banks[6]
        for kc in range(8):
            ta, tr = nextA()
            B.tt(ta[:, 0:ncols], xT[:, kc, 0:ncols], xT[:, kc, 0:ncols], ALU.mult, r=[R_x], w=[tr])
            P.add("pe", lambda e, ta=ta, kc=kc: e.matmul(pt[:, 0:ncols], ones_f[:], ta[:, 0:ncols], start=(kc == 0), stop=(kc == 7)),
                  r=[tr, R_const], w=[pr])
        B.act(sq[:, 0:ncols], pt[:, 0:ncols], AF.Sqrt, r=[pr, R_const], w=[R_sq], bias=epsc[:], scale=1.0 / D)
        P.add("dve", lambda e: e.reciprocal(out=rstd[:, 0:ncols], in_=sq[:, 0:ncols]), r=[R_sq], w=[R_rstd])
        for kc in range(8):
            B.stt(xT[:, kc, 0:ncols], xT[:, kc, 0:ncols], gsb[:, 3, kc:kc + 1], rstd[:, 0:ncols], ALU.mult, ALU.mult,
                  r=[R_x, R_g, R_rstd], w=[R_x])
        if samp:
            B.dma(ysT_o.rearrange("(k p) t -> p k t", p=128), xT[:, :, 0:ncols], r=[R_x], w=[], q="pool")
        else:
            B.dma(yTo_v[:, :, t0:t0 + TB], xT[:, :, :], r=[R_x], w=[], q="pool")

    ga2_v = ga2_s.rearrange("(k p) t -> p k t", p=128)
    gs2_v = gs2_s.rearrange("(k p) t -> p k t", p=128)
    BT2_v = BT2_s.rearrange("(g n) t -> n g t", n=128)
    CT2_v = CT2_s.rearrange("(g n) t -> n g t", n=128)

    FcA = B.sb("FcA", [128, NCORES, 256], F32)
    smAll = B.sb("smAll", [128, NCORES, 16], F32)
    delta = B.sb("delta", [128, NCORES + 1, 16], F32)
    R_FcA, R_smAll, R_delta = Res(), Res(), Res()
    xTall_v = xTall_d.rearrange("(k p) t -> p k t", p=128)
    NCTX = NCORES - 1
    P.add("dve", lambda e: e.memset(smAll[:], 0.0), w=[R_smAll])
    for j in range(NCTX):
        for h in range(H_A):
            for bb in range(T // TB):
                B.dma(kgA_v[j, h, 64:66, bb * TB:(bb + 1) * TB], onesrow[64:66, :], r=[R_const], w=[R_kgA], q="pool")
    P.add("pool", lambda e: e.memset(halo[:], 0.0), w=[R_halo])
    for j in range(NCTX):
        for blk in range(NBLK):
            c0_ = j * T + blk * TB
            B.dma(xT[:, :, :], xTall_v[:, :, c0_:c0_ + TB], w=[R_x])
            rms_mod(xT, 0, [(0, TB, 0)], TB)
            ffn(0, w1a_d, w3a_d, w2a_d, [(0, TB, 0)], TB)
            rms_mod(xT, 1, [(0, TB, 0)], TB)
            win_block(blk, [(0, TB, 0)], TB, ctxj=j)
        P.add("dve", lambda e: e.memset(carry[:], 0.0), w=[R_carry])
        for i in range(NT):
            pt, pr = banks[6]
            P.add("pe", lambda e, i=i: e.matmul(pt[:, 0:16], tri[:], logf[:, i, :], start=True, stop=True), r=[R_c3, R_lf], w=[pr])
            B.tt(FcA[:, j, i * 16:(i + 1) * 16], pt[:, 0:16], carry[:], ALU.add, r=[pr, R_carry], w=[R_FcA])
            P.add("pe", lambda e, i=i: e.matmul(pt[:, 16:32], ones_f[:], logf[:, i, :], start=True, stop=True), r=[R_const, R_lf], w=[pr])
            B.tt(carry[:], pt[:, 16:32], carry[:], ALU.add, r=[pr], w=[R_carry])
        B.cp(smAll[:, j, :], carry[:], r=[R_carry], w=[R_smAll])
        for gt in range(NT):
            ssd_tile(gt, want_y=False, maskj=j)
    P.add("dve", lambda e: e.memset(delta[:], 0.0), w=[R_delta])
    for j in range(NCORES - 1, -1, -1):
        B.stt(delta[:, j, :], smAll[:, j, 0:16], flg[:, j:j + 1], delta[:, j + 1, :], ALU.mult, ALU.add,
              r=[R_smAll, R_delta], w=[R_delta])
    run_own_p1()
    P.add("dve", lambda e: e.memset(carry[:], 0.0), w=[R_carry])
    for i in range(NT):
        pt, pr = banks[6]
        P.add("pe", lambda e, i=i: e.matmul(pt[:, 0:16], tri[:], logf[:, i, :], start=True, stop=True), r=[R_c3, R_lf], w=[pr])
        B.tt(Fc[:, i, :], pt[:, 0:16], carry[:], ALU.add, r=[pr, R_carry], w=[R_F])
        P.add("pe", lambda e, i=i: e.matmul(pt[:, 16:32], ones_f[:], logf[:, i, :], start=True, stop=True), r=[R_const, R_lf], w=[pr])
        B.tt(carry[:], pt[:, 16:32], carry[:], ALU.add, r=[pr], w=[R_carry])
    biasJ = [(B.sb(f"biasJ{i}", [128, NT], F32), Res()) for i in range(2)]
    rdj = B.sb("rdj", [128, NCORES, 16], F32)
    R_rdj = Res()
    for blk in range(NBLK):
        for lt in range(4):
            ssd_tile(blk * 4 + lt)
        attn_block(blk)
        dense_block(blk)
    B.dma(ssm_o[:, :, :], Hst[:].rearrange("p (h d) -> p h d", h=32), r=[R_H], w=[], q="pool")

    lfc = B.sb("lfc", [128, 8, 16], F32)
    Fs = B.sb("Fs", [128, 9, 16], F32)
    biasS = B.sb("biasS", [128, 9, 16], F32)
    R_lfc, R_Fs, R_bS = Res(), Res(), Res()

    def attn_sample(sq_):
        cs_ = slice(sq_ * 16, (sq_ + 1) * 16)
        B.dma(lfc[:], clf_d[sq_, :, :].rearrange("(t p) h -> p t h", p=128), w=[R_lfc])
        P.add("dve", lambda e: e.memset(carry[:], 0.0), w=[R_carry])
        P.add("dve", lambda e: e.memset(Fs[:], 0.0), w=[R_Fs])
        pt, pr = banks[6]
        for i in range(8):
            P.add("pe", lambda e, i=i: e.matmul(pt[:, 0:16], tri[:], lfc[:, i, :], start=True, stop=True), r=[R_c3, R_lfc], w=[pr])
            B.tt(Fs[:, i, :], pt[:, 0:16], carry[:], ALU.add, r=[pr, R_carry], w=[R_Fs])
            P.add("pe", lambda e, i=i: e.matmul(pt[:, 16:32], ones_f[:], lfc[:, i, :], start=True, stop=True), r=[R_const, R_lfc], w=[pr])
            B.tt(carry[:], pt[:, 16:32], carry[:], ALU.add, r=[pr], w=[R_carry])
        P.add("pe", lambda e: e.matmul(pt[0:16, 0:16], tri[0:16, 0:16], logf[0:16, NT + sq_, :], start=True, stop=True), r=[R_c3, R_lf], w=[pr])
        B.tt(Fs[0:16, 8, :], pt[0:16, 0:16], carry[0:16, :], ALU.add, r=[pr, R_carry], w=[R_Fs])
        pf, prf = banks[6]
        P.add("pe", lambda e: e.transpose(pf[0:16, 0:16], Fs[0:16, 8, :], ident_f[0:16, 0:16]), r=[R_Fs, R_const], w=[prf])
        B.cp(FT[:, 0:16], pf[0:16, 0:16], r=[prf], w=[R_FT])
        P.add("pe", lambda e: e.matmul(pf[:, 0:16], e0row[0:16, :], Fs[0:16, 8, :], start=True, stop=True), r=[R_c3, R_Fs], w=[prf])
        B.cp(rbc[:], pf[:, 0:16], r=[prf], w=[R_rbc])
        B.tt(biasS[:], rbc[:].unsqueeze(1).broadcast_to([128, 9, 16]), Fs[:], ALU.subtract, r=[R_rbc, R_Fs], w=[R_bS])
        for h in range(H_A):
            kt_, kr_ = Kt[h % 2]
            vt_, vr_ = Vt[h % 2]
            qa_, qr_ = Qa[h % 2]
            B.dma(yz[0:64, 0:1024], ckT_d[sq_, h, :, :], w=[R_yz])
            B.cp(kt_[0:64, 0:1024], yz[0:64, 0:1024], r=[R_yz], w=[kr_], eng="pool")
            P.add("pool", lambda e, kt_=kt_: e.memset(kt_[64:66, 0:1040], 1.0), w=[kr_])
            B.dma(kt_[0:64, 1024:1040], k2_s[h, 0:64, cs_], r=[R_s2], w=[kr_])
            ta, tr = nextA()
            B.dma(ta[:, :].rearrange("p (t d) -> p t d", t=8),
                  cv_d[sq_, :, :].rearrange("(t p) (h d) -> p t h d", p=128, h=16)[:, :, h, :], w=[tr])
            B.cp(vt_[:, 0:8, 0:64], ta[:, :].rearrange("p (t d) -> p t d", t=8), r=[tr], w=[vr_], eng="pool")
            P.add("pool", lambda e, vt_=vt_: e.memset(vt_[:, 0:9, 64:65], 1.0), w=[vr_])
            B.dma(vt_[0:16, 8, :], v2_s[h, sq_, :, :], r=[R_s2], w=[vr_])
            B.dma(qa_[0:64, 0:16], q2_s[h, 0:64, cs_], r=[R_s2], w=[qr_])
            pa, pra = banks[4]
            P.add("pe", lambda e, h=h: e.matmul(pa[0:66, 0:16], selt[:, h, :], FT[:, 0:16], start=True, stop=True), r=[R_c3, R_FT], w=[pra])
            ar0, rr0 = nextA()
            ar1, rr1 = nextA()
            ar2, rr2 = nextA()
            B.cp(ar0[64:66, 0:16], pa[64:66, 0:16], r=[pra], w=[rr0])
            B.ts(ar1[64:66, 0:16], ar0[64:66, 0:16], ar0[64:66, 0:1], None, ALU.subtract, r=[rr0], w=[rr1])
            B.cp(arb[64:66, 0:16], ar1[64:66, 0:16], r=[rr1], w=[R_arow])
            B.tt(ar0[64:66, 0:16], ar1[64:66, 0:16], arb[64:66, 0:16], ALU.subtract, r=[rr1, R_arow], w=[rr0])
            B.ts(ar2[64:66, 0:16], arb[64:66, 0:16], selc[64:66, 0:1], None, ALU.mult, r=[R_arow, R_c3], w=[rr2])
            B.stt(qa_[64:66, 0:16], ar0[64:66, 0:16], selc[64:66, 1:2], ar2[64:66, 0:16], ALU.mult, ALU.add, r=[rr0, rr2, R_c3], w=[qr_])
            po, pro = banks[2 + h % 2]
            for kt in range(9):
                Lk = 128 if kt < 8 else 16
                ps_, prs = banks[kt % 2]
                P.add("pe", lambda e, kt=kt, Lk=Lk, ps_=ps_, kt_=kt_, qa_=qa_: e.matmul(ps_[0:Lk, 0:16], kt_[:, kt * 128:kt * 128 + Lk], qa_[:, 0:16],
                                                                                       start=True, stop=True), r=[kr_, qr_], w=[prs])
                pti[0] += 1
                pT_, prp = Pt[pti[0] % 3]
                if kt == 8:
                    ta, tr = nextA()
                    B.tt(ta[0:16, 0:16], ps_[0:16, 0:16], mneg[0:16, 0:16], ALU.add, r=[prs, R_c3], w=[tr])
                    B.act(pT_[0:16, 0:16], ta[0:16, 0:16], AF.Exp, r=[tr, R_bS], w=[prp], bias=biasS[0:16, 8, h:h + 1], scale=1.0)
                else:
                    B.act(pT_[:, 0:16], ps_[:, 0:16], AF.Exp, r=[prs, R_bS], w=[prp], bias=biasS[:, kt, h:h + 1], scale=1.0)
                P.add("pe", lambda e, kt=kt, Lk=Lk, pT_=pT_, vt_=vt_, po=po: e.matmul(po[0:65, 0:16], vt_[0:Lk, kt, :], pT_[0:Lk, 0:16],
                                                                                    start=(kt == 0), stop=(kt == 8), skip_group_check=True),
                      r=[vr_, prp], w=[pro])
            rl, R_rl = nextA()
            rlb, R_rlb = nextA()
            P.add("dve", lambda e, po=po, rl=rl: e.reciprocal(out=rl[64:65, 0:16], in_=po[64:65, 0:16]), r=[pro], w=[R_rl])
            pb2, prb2 = banks[5]
            P.add("pe", lambda e, rl=rl: e.matmul(pb2[0:64, 0:16], ones_f[64:65, 0:64], rl[64:65, 0:16], start=True, stop=True), r=[R_const, R_rl], w=[prb2])
            B.cp(rlb[0:64, 0:16], pb2[0:64, 0:16], r=[prb2], w=[R_rlb])
            B.tt(oT[:, h, cs_], po[0:64, 0:16], rlb[0:64, 0:16], ALU.mult, r=[pro, R_rlb], w=R_ar)

    for sq_ in range(2):
        B.dma(Hst[:], ssmin_d[sq_, :, :], w=[R_H])
        ssd_tile(0, want_y=True, L=16, samp=sq_)
        B.dma(ssms_o[sq_, :, :], Hst[:], r=[R_H], w=[], q="pool")
    for sq_ in range(2):
        attn_sample(sq_)
    dense_block(0, ncols=32, groups=[(0, 16, 1), (16, 16, 2)], samp=True)


_CACHE = {}


def _r(w, kc):
    K, N = w.shape
    return np.ascontiguousarray(w.reshape(kc, K // kc, N).transpose(1, 0, 2))


def prep_inputs(inp, stage):
    f = np.float32
    xp = np.asarray(inp["x_prompt"], f)[0]
    maps = []
    shared = {}
    shared["w_ada_r"] = _r(np.asarray(inp["w_ada"], f)[0], 8)
    shared["b_ada_r"] = np.ascontiguousarray(np.asarray(inp["b_ada"], f)[0].reshape(72, 128).T)
    for nm, key in [("g_ffn1_r", "g_ffn1"), ("g_mix_r", "g_mix"), ("g_ffn2_r", "g_ffn2")]:
        shared[nm] = np.ascontiguousarray(np.asarray(inp[key], f)[0].reshape(8, 128).T)
    shared["g_final_r"] = np.ascontiguousarray(np.asarray(inp["g_final"], f).reshape(8, 128).T)
    shared["w1a"] = _r(np.asarray(inp["w1_ffn1"], f)[0], 8)
    shared["w3a"] = _r(np.asarray(inp["w3_ffn1"], f)[0], 8)
    shared["w2a"] = _r(np.asarray(inp["w2_ffn1"], f)[0], NH)
    shared["w1b"] = _r(np.asarray(inp["w1_ffn2"], f)[0], 8)
    shared["w3b"] = _r(np.asarray(inp["w3_ffn2"], f)[0], 8)
    shared["w2b"] = _r(np.asarray(inp["w2_ffn2"], f)[0], NH)
    shared["win_r"] = _r(np.asarray(inp["w_in"], f)[0], 8)
    shared["bf_bc"] = np.ascontiguousarray(np.broadcast_to(np.asarray(inp["b_f"], f)[0][None, :], (128, 16)))
    shared["convw_r"] = np.ascontiguousarray(np.asarray(inp["conv_w"], f)[0].reshape(4, 24, 128).transpose(2, 1, 0))
    shared["convb_r"] = np.ascontiguousarray(np.asarray(inp["conv_b"], f)[0].reshape(24, 128).T)
    shared["dtb_bc"] = np.ascontiguousarray(np.broadcast_to(np.asarray(inp["dt_bias"], f)[0][None, :], (128, 32)))
    shared["ident_in"] = np.eye(128, dtype=f)
    shared["alog_bc"] = np.ascontiguousarray(np.broadcast_to(np.asarray(inp["a_log"], f)[0][None, :], (128, 32)))
    shared["dskip_bc"] = np.ascontiguousarray(np.broadcast_to(np.asarray(inp["d_skip"], f)[0][None, :], (128, 32)))
    shared["gssd_r"] = np.ascontiguousarray(np.asarray(inp["g_ssd"], f)[0].reshape(16, 128).T)
    tri = np.triu(np.ones((128, 128), f))
    shared["tri_in"] = tri
    shared["maskneg_in"] = ((1.0 - tri) * -1e9).astype(f)
    e0 = np.zeros((128, 128), f); e0[0, :] = 1.0
    shared["e0row_in"] = e0
    sel = np.zeros((16, 16, 66), f)
    for h in range(16):
        sel[h, h, 64] = 1.0; sel[h, h, 65] = 1.0
    shared["sel_in"] = sel
    selc = np.zeros((128, 2), f); selc[64, 0] = 1.0; selc[65, 1] = 1.0
    shared["selc_in"] = selc
    shared["wa_r"] = np.ascontiguousarray(np.asarray(inp["w_a"], f)[0].reshape(16, 64, D).transpose(1, 0, 2))
    shared["ws_r"] = _r(np.asarray(inp["w_s"], f)[0], 16)
    shared["wout_r"] = _r(np.asarray(inp["w_out"], f)[0], 8)
    xpT = np.ascontiguousarray(xp.T)
    xsm = np.asarray(inp["x_sample"], f)
    cp = np.asarray(inp["c_prompt"], f)[0]
    cs = np.asarray(inp["c_sample"], f)
    for c in range(NCORES):
        m = dict(shared)
        m["xT"] = np.ascontiguousarray(xp[c * T:(c + 1) * T].T)
        m["xTall"] = xpT
        hal = xp[c * T - 3:c * T] if c > 0 else np.zeros((3, D), f)
        m["xsT"] = np.ascontiguousarray(np.concatenate([xsm[2 * c:2 * c + 2].reshape(32, D), hal], 0).T)
        fl = np.zeros((128, 24), f)
        for j in range(8):
            fl[:, j] = 1.0 if j < c else 0.0
            fl[:, 8 + j] = 0.0 if j < c else NEG
        fl[:, 16] = 1.0 if c > 0 else 0.0
        m["flags"] = fl
        sc = np.asarray(inp["state_conv"], f)[0, 2 * c:2 * c + 2]
        m["scv"] = np.ascontiguousarray(sc.transpose(2, 0, 1).reshape(24, 128, 2, 3).transpose(1, 0, 2, 3))
        ss = np.asarray(inp["state_ssm"], f)[0, 2 * c:2 * c + 2]
        m["ssmin"] = np.ascontiguousarray(ss.transpose(0, 3, 1, 2).reshape(2, 128, DIN))
        ck = np.asarray(inp["cache_k"], f)[0, 2 * c:2 * c + 2]
        m["ckT"] = np.ascontiguousarray(ck.transpose(0, 2, 3, 1))
        m["cvc"] = np.ascontiguousarray(np.asarray(inp["cache_v"], f)[0, 2 * c:2 * c + 2].reshape(2, 1024, D))
        m["clf"] = np.ascontiguousarray(np.asarray(inp["cache_logf"], f)[0, 2 * c:2 * c + 2])
        c3 = np.stack([cp, cs[2 * c], cs[2 * c + 1]], axis=1)
        m["cT"] = np.ascontiguousarray(c3.reshape(8, 128, 3).transpose(1, 0, 2))
        maps.append(m)
    return maps


def run(inp, stage=99, ncores=NCORES):
    if stage not in _CACHE:
        B = build(stage)
        _CACHE[stage] = B
    B = _CACHE[stage]
    maps = prep_inputs(inp, stage)
    maps = [{k: v for k, v in m.items() if k in B.ins} for m in maps]
    res = run_bass_kernel_spmd(B.nc, maps[:ncores], core_ids=list(range(ncores)))
    return res.results


def kernel(**inp):
    res = run(inp, stage=3)
    f = np.float32
    y_prompt = np.zeros((1, SEQ, D), f)
    y_sample = np.zeros((16, 16, D), f)
    k_prompt = np.zeros((1, 1, SEQ, H_A, HD), f)
    v_prompt = np.zeros((1, 1, SEQ, H_A, HD), f)
    logf_prompt = np.zeros((1, 1, SEQ, H_A), f)
    ssm_prompt = np.zeros((1, 1, NSSD, 64, NST), f)
    conv_prompt = np.zeros((1, 1, 3, CONVD), f)
    k_sample = np.zeros((1, 16, 16, H_A, HD), f)
    v_sample = np.zeros((1, 16, 16, H_A, HD), f)
    logf_sample = np.zeros((1, 16, 16, H_A), f)
    ssm_sample = np.zeros((1, 16, NSSD, 64, NST), f)
    conv_sample = np.zeros((1, 16, 3, CONVD), f)
    for c in range(NCORES):
        r = res[c]
        sl = slice(c * T, (c + 1) * T)
        y_prompt[0, sl] = np.asarray(r["yT_o"]).T
        k_prompt[0, 0, sl] = np.asarray(r["kT_o"]).T.reshape(T, H_A, HD)
        v_prompt[0, 0, sl] = np.asarray(r["v_o"]).reshape(T, H_A, HD)
        logf_prompt[0, 0, sl] = np.asarray(r["lf_o"])
        k_sample[0, 2 * c:2 * c + 2] = np.asarray(r["ksT_o"]).T.reshape(2, 16, H_A, HD)
        v_sample[0, 2 * c:2 * c + 2] = np.asarray(r["vs_o"]).reshape(2, 16, H_A, HD)
        logf_sample[0, 2 * c:2 * c + 2] = np.asarray(r["lfs_o"]).reshape(2, 16, H_A)
        y_sample[2 * c:2 * c + 2] = np.asarray(r["ysT_o"]).T.reshape(2, 16, D)
        ssm_sample[0, 2 * c:2 * c + 2] = np.asarray(r["ssms_o"]).reshape(2, 128, NSSD, 64).transpose(0, 2, 3, 1)
        cvs = np.asarray(r["cvs_o"])
        conv_sample[0, 2 * c] = cvs[:, 0:3].T
        conv_sample[0, 2 * c + 1] = cvs[:, 3:6].T
    conv_prompt[0, 0] = np.asarray(res[NCORES - 1]["cv_o"]).T
    ssm_prompt[0, 0] = np.asarray(res[NCORES - 1]["ssm_o"]).transpose(1, 2, 0)
    return (y_prompt, y_sample, k_prompt, v_prompt, logf_prompt, ssm_prompt, conv_prompt,
            k_sample, v_sample, logf_sample, ssm_sample, conv_sample)
```

```python
from contextlib import ExitStack
import numpy as np
import concourse.bass as bass
import concourse.mybir as mybir
from concourse.bass_utils import run_bass_kernel_spmd

F32 = mybir.dt.float32
BF16 = mybir.dt.bfloat16
AF = mybir.ActivationFunctionType
ALU = mybir.AluOpType
AX = mybir.AxisListType

NCORES = 8
D = 1024
SEQ = 16384
T = SEQ // NCORES
TB = 512
DFF = 2816
NH = 22
H_A = 16
HD = 64
DIN = 2048
NSSD = 32
NST = 128
CONVD = 3072
DINP = 10288
C_Q, C_K, C_V, C_F, C_Z, C_X, C_DT, C_GA, C_GS = 0, 1024, 2048, 3072, 3088, 5136, 8208, 8240, 9264
EPS = 1e-6
NEG = -30000.0
import os
CTX = os.environ.get("CTX", "1") == "1"

ENG = ["pe", "act", "dve", "pool", "sp"]


class Res:
    __slots__ = ("lw", "rd", "excl")

    def __init__(self, excl=False):
        self.lw = None
        self.rd = []
        self.excl = excl


class Op:
    __slots__ = ("eng", "fn", "deps", "dma", "need", "sv", "dsem", "dval", "idx", "cc", "seng")


class Prog:
    NS = 16

    def __init__(self):
        self.ops = []
        self.dq = {"sp": [], "pool": [], "act": []}
        self.ncc = 0

    def add(self, eng, fn, r=(), w=(), dma=False, cc=False):
        op = Op()
        op.eng, op.fn, op.dma, op.need, op.idx = eng, fn, dma, False, len(self.ops)
        op.sv = 0
        op.cc = cc
        op.seng = eng
        if cc:
            op.dma = dma = True
        deps = {}
        w = list(w) + [R for R in r if R.excl]
        r = [R for R in r if not R.excl]

        def dep(p, war=False):
            if p is None:
                return
            if (not p.dma) and p.eng == eng and eng == "pe":
                return
            deps[p.idx] = p

        for R in r:
            dep(R.lw)
        for R in w:
            dep(R.lw)
            for q in R.rd:
                dep(q, True)
        if cc:
            op.seng = "cc"
            op.dsem = self.ncc
            op.dval = 1
            self.ncc += 1
        elif dma:
            lst = self.dq[eng]
            n = len(lst)
            op.dsem = n % self.NS
            op.dval = 16 * (n // self.NS + 1)
            if n >= self.NS:
                deps[lst[n - self.NS].idx] = lst[n - self.NS]
            lst.append(op)
        op.deps = list(deps.values())
        for p in op.deps:
            if not p.dma:
                p.need = True
        for R in r:
            R.rd.append(op)
        for R in w:
            R.lw = op
            R.rd = []
        self.ops.append(op)
        return op

    def emit(self, nc, sems, dsems):
        cnt = {e: 0 for e in ENG}
        for op in self.ops:
            if (not op.dma) and op.need:
                cnt[op.eng] += 1
                op.sv = cnt[op.eng]
        per = {e: [o for o in self.ops if o.eng == e] for e in ENG}

        def run(ename, e):
            waited = {}
            for op in per[ename]:
                for p in op.deps:
                    if p.dma:
                        key, val, sem = ("d", p.seng, p.dsem), p.dval, dsems[p.seng][p.dsem]
                    else:
                        key, val, sem = ("c", p.eng), p.sv, sems[p.eng]
                    if waited.get(key, 0) >= val:
                        continue
                    waited[key] = val
                    e.wait_ge(sem, val)
                ins = op.fn(e)
                if op.cc:
                    ins.then_inc(dsems["cc"][op.dsem], 1)
                elif op.dma:
                    ins.then_inc(dsems[ename][op.dsem], 16)
                elif op.need:
                    ins.then_inc(sems[ename], 1)
            if ename in self.dq:
                lst = self.dq[ename]
                last = {}
                for o in lst:
                    last[o.dsem] = o.dval
                for s, v in last.items():
                    e.wait_ge(dsems[ename][s], v)

        with nc.Block() as block:
            @block.tensor
            def _(e):
                run("pe", e)

            @block.scalar
            def _(e):
                run("act", e)

            @block.vector
            def _(e):
                run("dve", e)

            @block.gpsimd
            def _(e):
                run("pool", e)

            @block.sync
            def _(e):
                run("sp", e)


class StopBuild(Exception):
    pass


class Builder:
    def __init__(self, stage=99):
        import os
        self.cutn = float(os.environ.get("DBG_CUT", "999"))
        self.stage = stage
        self.nc = bass.Bass("TRN2", target_bir_lowering=False)
        try:
            self.nc.allow_low_precision("bf16 matmul operands by design")
        except Exception:
            pass
        self.P = Prog()
        self.es = ExitStack()
        self.ins = {}
        self.outs = {}
        self.uid = 0
        self.wrr = 0

    def finish(self):
        nc = self.nc
        sems = {e: self.es.enter_context(nc.semaphore(f"s_{e}")) for e in ENG}
        dsems = {q: [self.es.enter_context(nc.semaphore(f"d_{q}{i}")) for i in range(Prog.NS)]
                 for q in ("sp", "pool", "act")}
        dsems["cc"] = [self.es.enter_context(nc.semaphore(f"d_cc{i}")) for i in range(max(1, self.P.ncc))]
        self.P.emit(nc, sems, dsems)
        self.es.close()

    def cut(self, n):
        if self.cutn <= n:
            raise StopBuild()

    def din(self, name, shape, dt=F32):
        t = self.nc.dram_tensor(name, list(shape), dt, kind="ExternalInput").ap()
        self.ins[name] = t
        return t

    def dout(self, name, shape, dt=F32):
        t = self.nc.dram_tensor(name, list(shape), dt, kind="ExternalOutput").ap()
        self.outs[name] = t
        return t

    def dscr(self, name, shape, dt):
        return self.nc.dram_tensor(name, list(shape), dt).ap()

    def sb(self, name, shape, dt=F32):
        return self.es.enter_context(self.nc.sbuf_tensor("sb_" + name, list(shape), dt))

    def ps(self, name, shape, dt=F32):
        return self.es.enter_context(self.nc.psum_tensor("ps_" + name, list(shape), dt))

    def dma(self, out, in_, r=(), w=(), q="sp"):
        return self.P.add(q, lambda e: e.dma_start(out=out, in_=in_), r=r, w=w, dma=True)

    def act(self, out, in_, func, r=(), w=(), bias=None, scale=None):
        kw = {}
        if bias is not None:
            kw["bias"] = bias
        if scale is not None:
            kw["scale"] = scale
        return self.P.add("act", lambda e: e.activation(out=out, in_=in_, func=func, **kw), r=r, w=w)

    def tt(self, out, a, b, op, r=(), w=(), eng="dve"):
        return self.P.add(eng, lambda e: e.tensor_tensor(out=out, in0=a, in1=b, op=op), r=r, w=w)

    def ts(self, out, a, s1, s2, op0, op1=None, r=(), w=(), eng="dve"):
        if op1 is None:
            return self.P.add(eng, lambda e: e.tensor_scalar(out=out, in0=a, scalar1=s1, scalar2=None, op0=op0), r=r, w=w)
        return self.P.add(eng, lambda e: e.tensor_scalar(out=out, in0=a, scalar1=s1, scalar2=s2, op0=op0, op1=op1), r=r, w=w)

    def stt(self, out, a, s, b, op0, op1, r=(), w=(), eng="dve"):
        return self.P.add(eng, lambda e: e.scalar_tensor_tensor(out=out, in0=a, scalar=s, in1=b, op0=op0, op1=op1), r=r, w=w)

    def cp(self, out, in_, r=(), w=(), eng="dve"):
        if eng == "act":
            return self.P.add("act", lambda e: e.copy(out=out, in_=in_), r=r, w=w)
        return self.P.add(eng, lambda e: e.tensor_copy(out=out, in_=in_), r=r, w=w)

    def mms(self, out, pairs, r=(), w=()):
        n = len(pairs)

        def fn(e):
            ins = None
            for i, (l, rh) in enumerate(pairs):
                ins = e.matmul(out, l, rh, start=(i == 0), stop=(i == n - 1))
            return ins
        return self.P.add("pe", fn, r=r, w=w)

    def init_wstream(self):
        self.WSZ = 2048
        self.wst = [(self.sb(f"wst{i}", [128, self.WSZ], F32), Res()) for i in range(2)]
        self.wbf = [(self.sb(f"wbf{i}", [128, self.WSZ], BF16), Res()) for i in range(2)]
        self.wi = 0
        self.wj = 0

    def load_w(self, wd, kcn, c0, n, pn=128, k0=0):
        st, sr = self.wst[self.wi % 2]
        self.wi += 1
        bf, br = self.wbf[self.wj % 2]
        self.wj += 1
        sz = kcn * n
        assert sz <= self.WSZ
        stv = st[0:pn, 0:sz].rearrange("p (k n) -> p k n", k=kcn)
        bfv = bf[0:pn, 0:sz].rearrange("p (k n) -> p k n", k=kcn)
        self.dma(stv, wd[0:pn, k0:k0 + kcn, c0:c0 + n], w=[sr])
        ceng = "dve" if (self.wj % 3 == 0) else "act"
        self.cp(bf[0:pn, 0:sz], st[0:pn, 0:sz], r=[sr], w=[br], eng=ceng)
        return bfv, br


def build(stage=99):
    B = Builder(stage)
    nc, P = B.nc, B.P
    NBLK = T // TB
    NT = T // 128

    xT_d = B.din("xT", [D, T])
    cT_d = B.din("cT", [128, 8, 3])
    wada_d = B.din("w_ada_r", [128, 8, 9 * D])
    bada_d = B.din("b_ada_r", [128, 72])
    g1_d = B.din("g_ffn1_r", [128, 8])
    gm_d = B.din("g_mix_r", [128, 8])
    g2_d = B.din("g_ffn2_r", [128, 8])
    gf_d = B.din("g_final_r", [128, 8])
    w1a_d = B.din("w1a", [128, 8, DFF])
    w3a_d = B.din("w3a", [128, 8, DFF])
    w2a_d = B.din("w2a", [128, NH, D])
    w1b_d = B.din("w1b", [128, 8, DFF])
    w3b_d = B.din("w3b", [128, 8, DFF])
    w2b_d = B.din("w2b", [128, NH, D])
    win_d = B.din("win_r", [128, 8, DINP])
    bf_d = B.din("bf_bc", [128, 16])
    cw_d = B.din("convw_r", [128, 24, 4])
    cb_d = B.din("convb_r", [128, 24])
    dtb_d = B.din("dtb_bc", [128, 32])

    kT_o = B.dout("kT_o", [D, T])
    v_o = B.dout("v_o", [T, D])
    lf_o = B.dout("lf_o", [T, 16])
    cv_o = B.dout("cv_o", [CONVD, 3])
    xsT_d = B.din("xsT", [D, 35])
    flg_d = B.din("flags", [128, 24])
    ksT_o = B.dout("ksT_o", [D, 32])
    vs_o = B.dout("vs_o", [32, D])
    lfs_o = B.dout("lfs_o", [32, 16])
    cvs_o = B.dout("cvs_o", [CONVD, 6])
    x1_o = B.dout("x1_o", [D, T])
    scv_d = B.din("scv", [128, 24, 2, 3])
    ssmin_d = B.din("ssmin", [2, 128, DIN])
    ckT_d = B.din("ckT", [2, H_A, 64, 1024])
    cv_d = B.din("cvc", [2, 1024, D])
    clf_d = B.din("clf", [2, 1024, 16])
    ysT_o = B.dout("ysT_o", [D, 32])
    ssms_o = B.dout("ssms_o", [2, 128, DIN])
    q2_s = B.dscr("q2_s", [H_A, 66, 32], BF16)
    k2_s = B.dscr("k2_s", [H_A, 66, 32], BF16)
    v2_s = B.dscr("v2_s", [H_A, 2, 16, 65], BF16)
    z2_s = B.dscr("z2_s", [32, DIN], BF16)
    xs2_s = B.dscr("xs2_s", [32, DIN], BF16)
    Bt2_s = B.dscr("Bt2_s", [32, 512], BF16)
    BT2_s = B.dscr("BT2_s", [512, 32], BF16)
    CT2_s = B.dscr("CT2_s", [512, 32], BF16)
    ga2_s = B.dscr("ga2_s", [D, 32], BF16)
    gs2_s = B.dscr("gs2_s", [D, 32], BF16)
    R_s2 = Res()
    xTall_d = B.din("xTall", [D, SEQ])
    kgA = B.dscr("kgA", [NCORES * H_A * 66, T], BF16)
    vgA = B.dscr("vgA", [NCORES * H_A * NT * 128, 65], BF16)
    kgA_v = kgA.rearrange("(j h r) t -> j h r t", j=NCORES, h=H_A)
    vgA_v = vgA.rearrange("(j h t p) d -> j h t p d", j=NCORES, h=H_A, t=NT)
    R_kgA, R_vgA = Res(), Res()
    alog_d = B.din("alog_bc", [128, 32])
    dsk_d = B.din("dskip_bc", [128, 32])
    gssd_d = B.din("gssd_r", [128, 16])
    tri_d = B.din("tri_in", [128, 128])
    mneg_d = B.din("maskneg_in", [128, 128])
    e0_d = B.din("e0row_in", [128, 128])
    sel_d = B.din("sel_in", [16, 16, 66])
    selc_d = B.din("selc_in", [128, 2])
    wa_d = B.din("wa_r", [64, 16, D])
    ws_d = B.din("ws_r", [128, 16, D])
    wo_d = B.din("wout_r", [128, 8, D])
    yT_o = B.dout("yT_o", [D, T])
    ssm_o = B.dout("ssm_o", [128, NSSD, 64])

    q_s = B.dscr("q_s", [H_A, 66, T], BF16)
    kg_s = B.dscr("kg_s", [H_A, 66, T], BF16)
    vg_s = B.dscr("vg_s", [H_A, NT, 128, 65], BF16)
    z_s = B.dscr("z_s", [T, DIN], BF16)
    xs_s = B.dscr("xs_s", [T, DIN], BF16)
    Bt_s = B.dscr("Bt_s", [T, 512], BF16)
    BT_s = B.dscr("BT_s", [512, T], BF16)
    CT_s = B.dscr("CT_s", [512, T], BF16)
    ga_s = B.dscr("ga_s", [D, T], BF16)
    gs_s = B.dscr("gs_s", [D, T], BF16)
    R_q, R_kg, R_vg, R_z, R_xs, R_Bt, R_BT, R_CT, R_ga, R_gs, R_x1 = [Res() for _ in range(11)]

    ones_f = B.sb("ones_f", [128, 128], F32)
    ident_b = B.sb("ident_b", [128, 128], BF16)
    ident_f = B.sb("ident_f", [128, 128], F32)
    epsc = B.sb("epsc", [128, 1], F32)
    onec = B.sb("onec", [128, 1], F32)
    R_const = Res()
    P.add("pool", lambda e: e.memset(ones_f[:], 1.0), w=[R_const])
    P.add("pool", lambda e: e.memset(epsc[:], EPS), w=[R_const])
    P.add("pool", lambda e: e.memset(onec[:], 1.0), w=[R_const])
    identf_d = B.din("ident_in", [128, 128])
    B.dma(ident_f[:], identf_d[:, :], w=[R_const])
    B.cp(ident_b[:], ident_f[:], r=[R_const], w=[R_const], eng="pool")

    B.init_wstream()

    banks = [(B.ps(f"bank{i}", [128, 512], F32), Res(True)) for i in range(7)]
    ptT = B.ps("ptT", [128, 1024], BF16)
    prT = Res(True)

    try:
        _build_body(B, locals())
    except StopBuild:
        pass
    B.finish()
    return B


def _build_body(B, L):
    globals_ = L
    nc, P = B.nc, B.P
    NBLK = T // TB
    NT = T // 128
    for k_, v_ in L.items():
        if k_ not in ("B", "nc", "P"):
            globals()[k_] = v_
    cT = B.sb("cT", [128, 8, 3], F32)
    cs = B.sb("cs", [128, 8, 3], BF16)
    bada = B.sb("bada", [128, 72], F32)
    modT = B.sb("modT", [128, 72, 3], F32)
    gsb = B.sb("gsb", [128, 4, 8], F32)
    R_c, R_mod, R_g = Res(), Res(), Res()
    B.dma(cT[:], cT_d[:, :, :], w=[R_c])
    B.dma(bada[:], bada_d[:, :], w=[R_c])
    for i, gd in enumerate([g1_d, gm_d, g2_d, gf_d]):
        B.dma(gsb[:, i, :], gd[:, :], w=[R_g])
    B.act(cs[:], cT[:], AF.Silu, r=[R_c], w=[R_c])
    for ch in range(72):
        wt, wr = B.load_w(wada_d, 8, ch * 128, 128)
        pt, pr = banks[ch % 2]
        B.mms(pt[:, 0:3], [(wt[:, kc, :], cs[:, kc, :]) for kc in range(8)], r=[wr, R_c], w=[pr])
        B.ts(modT[:, ch, :], pt[:, 0:3], bada[:, ch:ch + 1], None, ALU.add, r=[pr, R_c], w=[R_mod])
    Am = B.sb("Am", [128, 3, 8, 3], F32)
    Gm = B.sb("Gm", [128, 3, 8, 3], F32)
    R_AG = Res()
    for s in range(3):
        coef = 1.0 if s == 1 else 0.5
        for kc in range(8):
            B.ts(Am[:, s, kc, :], modT[:, (3 * s + 1) * 8 + kc, :], 1.0, gsb[:, s, kc:kc + 1], ALU.add, ALU.mult,
                 r=[R_mod, R_g], w=[R_AG])
            B.ts(Gm[:, s, kc, :], modT[:, (3 * s + 2) * 8 + kc, :], 1.0, coef, ALU.add, ALU.mult,
                 r=[R_mod], w=[R_AG])

    B.cut(1)

    def shiftp(s, kc, m):
        return modT[:, (3 * s) * 8 + kc, m:m + 1]

    xT = B.sb("xT", [128, 8, TB], F32)
    uT = B.sb("uT", [128, 8, TB], BF16)
    gT = B.sb("gT", [128, NH, TB], BF16)
    sq = B.sb("sq", [128, TB], F32)
    rstd = B.sb("rstd", [128, TB], F32)
    tmpA = [(B.sb(f"tmpA{i}", [128, TB], F32), Res()) for i in range(5)]
    tmpB = [(B.sb(f"tmpB{i}", [128, TB], BF16), Res()) for i in range(5)]
    R_x, R_u, R_gT, R_sq, R_rstd = Res(), Res(), Res(), Res(), Res()
    tai = [0]
    tbi = [0]

    def nextA():
        tai[0] += 1
        return tmpA[tai[0] % 5]

    def nextB():
        tbi[0] += 1
        return tmpB[tbi[0] % 5]

    def rms_mod(src, s, groups, ncols):
        pt, pr = banks[6]
        for kc in range(8):
            ta, tr = nextA()
            B.tt(ta[:, 0:ncols], src[:, kc, 0:ncols], src[:, kc, 0:ncols], ALU.mult, r=[R_x], w=[tr])
            P.add("pe", lambda e, ta=ta, kc=kc: e.matmul(pt[:, 0:ncols], ones_f[:], ta[:, 0:ncols],
                                                            start=(kc == 0), stop=(kc == 7)),
                  r=[tr, R_const], w=[pr])
        B.act(sq[:, 0:ncols], pt[:, 0:ncols], AF.Sqrt, r=[pr, R_const], w=[R_sq], bias=epsc[:], scale=1.0 / D)
        P.add("dve", lambda e: e.reciprocal(out=rstd[:, 0:ncols], in_=sq[:, 0:ncols]), r=[R_sq], w=[R_rstd])
        for kc in range(8):
            ta, tr = nextA()
            B.tt(ta[:, 0:ncols], src[:, kc, 0:ncols], rstd[:, 0:ncols], ALU.mult, r=[R_x, R_rstd], w=[tr])
            for (c0, n, m) in groups:
                B.ts(uT[:, kc, c0:c0 + n], ta[:, c0:c0 + n], Am[:, s, kc, m:m + 1], shiftp(s, kc, m),
                     ALU.mult, ALU.add, r=[tr, R_AG, R_mod], w=[R_u])

    def ffn(s, w1d, w3d, w2d, groups, ncols):
        for hc in range(NH):
            w1t, w1r = B.load_w(w1d, 8, hc * 128, 128)
            w3t, w3r = B.load_w(w3d, 8, hc * 128, 128)
            p1, r1 = banks[hc % 2]
            p3, r3 = banks[2 + hc % 2]
            B.mms(p1[:, 0:ncols], [(w1t[:, kc, :], uT[:, kc, 0:ncols]) for kc in range(8)], r=[w1r, R_u], w=[r1])
            B.mms(p3[:, 0:ncols], [(w3t[:, kc, :], uT[:, kc, 0:ncols]) for kc in range(8)], r=[w3r, R_u], w=[r3])
            ta, tr = nextA()
            B.act(ta[:, 0:ncols], p1[:, 0:ncols], AF.Silu, r=[r1], w=[tr])
            B.tt(gT[:, hc, 0:ncols], ta[:, 0:ncols], p3[:, 0:ncols], ALU.mult, r=[tr, r3], w=[R_gT])
        for oc in range(8):
            w2t, w2r = B.load_w(w2d, 11, oc * 128, 128)
            w2u, w2s = B.load_w(w2d, 11, oc * 128, 128, k0=11)
            po, ro = banks[4 + oc % 2]
            B.mms(po[:, 0:ncols], [(w2t[:, hc, :], gT[:, hc, 0:ncols]) for hc in range(11)]
                  + [(w2u[:, hc, :], gT[:, 11 + hc, 0:ncols]) for hc in range(11)], r=[w2r, w2s, R_gT], w=[ro])
            for (c0, n, m) in groups:
                B.stt(xT[:, oc, c0:c0 + n], po[:, c0:c0 + n], Gm[:, s, oc, m:m + 1], xT[:, oc, c0:c0 + n],
                      ALU.mult, ALU.add, r=[ro, R_AG], w=[R_x])

    logf = B.sb("logf", [128, NT + 2, 16], F32)
    dtsb = B.sb("dtsb", [128, NT + 2, 32], F32)
    bfb = B.sb("bfb", [128, 16], F32)
    dtb = B.sb("dtb", [128, 32], F32)
    cw = B.sb("cw", [128, 24, 4], F32)
    cbv = B.sb("cbv", [128, 24], F32)
    halo = B.sb("halo", [128, 24, 3], F32)
    R_lf, R_dt, R_sm, R_halo = Res(), Res(), Res(), Res()
    B.dma(bfb[:], bf_d[:, :], w=[R_sm])
    B.dma(dtb[:], dtb_d[:, :], w=[R_sm])
    B.dma(cw[:], cw_d[:, :, :], w=[R_sm])
    B.dma(cbv[:], cb_d[:, :], w=[R_sm])
    P.add("pool", lambda e: e.memset(halo[:], 0.0), w=[R_halo])
    flg = B.sb("flg", [128, 24], F32)
    B.dma(flg[:], flg_d[:, :], w=[R_sm])

    xb = [(B.sb(f"xb{i}", [128, TB + 3], F32), Res()) for i in range(2)]
    vst = [(B.sb(f"vst{i}", [128, 4, 65], BF16), Res()) for i in range(2)]
    kst = [(B.sb("kst0", [64, TB], F32), Res())] * 2
    onesrow = B.sb("onesrow", [66, TB], BF16)
    P.add("pool", lambda e: e.memset(onesrow[:], 1.0), w=[R_const])
    for i in range(2):
        P.add("pool", lambda e, i=i: e.memset(vst[i][0][:], 1.0), w=[vst[i][1]])
    for h in range(H_A):
        for bb in range(T // TB):
            B.dma(kg_s[h, 64:66, bb * TB:(bb + 1) * TB], onesrow[64:66, :], r=[R_const], w=[R_kg], q="pool")

    def softplus_to(out, in_ps, biasbc, n, r, w, neg_in=False, pn=128):
        ta, tr = nextA()
        tb2, tr2 = nextA()
        B.tt(ta[0:pn, 0:n], in_ps, biasbc, ALU.add, r=r, w=[tr])
        if neg_in:
            B.ts(ta[0:pn, 0:n], ta[0:pn, 0:n], -1.0, None, ALU.mult, r=[tr], w=[tr])
        B.stt(tb2[0:pn, 0:n], ta[0:pn, 0:n], -1.0, ta[0:pn, 0:n], ALU.mult, ALU.max, r=[tr], w=[tr2])
        B.act(tb2[0:pn, 0:n], tb2[0:pn, 0:n], AF.Exp, r=[tr2], w=[tr2], scale=-1.0)
        B.act(tb2[0:pn, 0:n], tb2[0:pn, 0:n], AF.Ln, r=[tr2, R_const], w=[tr2], bias=onec[0:pn, :], scale=1.0)
        B.stt(out, ta[0:pn, 0:n], 0.0, tb2[0:pn, 0:n], ALU.max, ALU.add, r=[tr, tr2], w=w)

    def win_block(blk, groups, ncols, ctxj=None):
        t0 = blk * TB
        ntt = ncols // 128
        B.cut(5)
        for h in range(H_A):
            if ctxj is None:
                wt, wr = B.load_w(win_d, 8, C_Q + h * 64, 64)
                pt, pr = banks[h % 2]
                B.mms(pt[0:64, 0:ncols], [(wt[:, kc, :], uT[:, kc, 0:ncols]) for kc in range(8)], r=[wr, R_u], w=[pr])
                tb_, tbr = nextB()
                B.ts(tb_[0:64, 0:ncols], pt[0:64, 0:ncols], 0.125, None, ALU.mult, r=[pr], w=[tbr])
                B.dma(q_s[h, 0:64, t0:t0 + ncols], tb_[0:64, 0:ncols], r=[tbr], w=[R_q], q="pool")
            wt, wr = B.load_w(win_d, 8, C_K + h * 64, 64)
            pt, pr = banks[2 + h % 2]
            B.mms(pt[0:64, 0:ncols], [(wt[:, kc, :], uT[:, kc, 0:ncols]) for kc in range(8)], r=[wr, R_u], w=[pr])
            if ctxj is None:
                kf, kr = kst[h % 2]
                B.cp(kf[:, 0:ncols], pt[0:64, 0:ncols], r=[pr], w=[kr])
                B.dma(kT_o[h * 64:(h + 1) * 64, t0:t0 + ncols], kf[:, 0:ncols], r=[kr], w=[], q="pool")
            tb_, tbr = nextB()
            B.cp(tb_[0:64, 0:ncols], pt[0:64, 0:ncols], r=[pr], w=[tbr], eng="act")
            if ctxj is None:
                B.dma(kg_s[h, 0:64, t0:t0 + ncols], tb_[0:64, 0:ncols], r=[tbr], w=[R_kg], q="pool")
            else:
                B.dma(kgA_v[ctxj, h, 0:64, t0:t0 + ncols], tb_[0:64, 0:ncols], r=[tbr], w=[R_kgA], q="pool")
        B.cut(6)
        for cg in range(4):
            wt, wr = B.load_w(win_d, 8, C_V + cg * 256, 256)
            for tt_ in range(ntt):
                pt, pr = banks[4 + tt_ % 2]
                B.mms(pt[:, 0:256], [(uT[:, kc, tt_ * 128:(tt_ + 1) * 128], wt[:, kc, :]) for kc in range(8)],
                      r=[wr, R_u], w=[pr])
                if ctxj is None:
                    ta, tr = nextA()
                    B.cp(ta[:, 0:256], pt[:, 0:256], r=[pr], w=[tr])
                    B.dma(v_o[t0 + tt_ * 128:t0 + (tt_ + 1) * 128, cg * 256:(cg + 1) * 256], ta[:, 0:256], r=[tr], w=[], q="pool")
                vs, vr = vst[(cg * ntt + tt_) % 2]
                B.cp(vs[:, 0:4, 0:64], pt[:, 0:256].rearrange("p (h d) -> p h d", h=4), r=[pr], w=[vr], eng="act")
                gt = (t0 // 128) + tt_
                if ctxj is None:
                    B.dma(vg_s[cg * 4:(cg + 1) * 4, gt, :, :].rearrange("h p d -> p h d"), vs[:, 0:4, :], r=[vr], w=[R_vg], q="pool")
                else:
                    B.dma(vgA_v[ctxj, cg * 4:(cg + 1) * 4, gt, :, :].rearrange("h p d -> p h d"), vs[:, 0:4, :], r=[vr], w=[R_vgA], q="pool")
        B.cut(7)
        wt, wr = B.load_w(win_d, 8, C_F, 16)
        for tt_ in range(ntt):
            gt = (t0 // 128) + tt_
            pt, pr = banks[6]
            B.mms(pt[:, 0:16], [(uT[:, kc, tt_ * 128:(tt_ + 1) * 128], wt[:, kc, :]) for kc in range(8)], r=[wr, R_u], w=[pr])
            ta, tr = nextA()
            softplus_to(ta[:, 0:16], pt[:, 0:16], bfb[:], 16, r=[pr, R_sm], w=[tr], neg_in=True)
            B.ts(logf[:, gt, :], ta[:, 0:16], -1.0, None, ALU.mult, r=[tr], w=[R_lf])
            if ctxj is None:
                B.dma(lf_o[gt * 128:(gt + 1) * 128, :], logf[:, gt, :], r=[R_lf], w=[], q="pool")
        if B.stage < 2:
            return
        wt, wr = B.load_w(win_d, 8, C_DT, 32)
        for tt_ in range(ntt):
            gt = (t0 // 128) + tt_
            pt, pr = banks[6]
            B.mms(pt[:, 0:32], [(uT[:, kc, tt_ * 128:(tt_ + 1) * 128], wt[:, kc, :]) for kc in range(8)], r=[wr, R_u], w=[pr])
            softplus_to(dtsb[:, gt, :], pt[:, 0:32], dtb[:], 32, r=[pr, R_sm], w=[R_dt])
        for cg in range(8 if ctxj is None else 0):
            wt, wr = B.load_w(win_d, 8, C_Z + cg * 256, 256)
            for tt_ in range(ntt):
                pt, pr = banks[4 + tt_ % 2]
                B.mms(pt[:, 0:256], [(uT[:, kc, tt_ * 128:(tt_ + 1) * 128], wt[:, kc, :]) for kc in range(8)],
                      r=[wr, R_u], w=[pr])
                tb_, tbr = nextB()
                B.cp(tb_[:, 0:256], pt[:, 0:256], r=[pr], w=[tbr], eng="act")
                B.dma(z_s[t0 + tt_ * 128:t0 + (tt_ + 1) * 128, cg * 256:(cg + 1) * 256], tb_[:, 0:256], r=[tbr], w=[R_z], q="pool")
        for c in range(24 if ctxj is None else 20):
            wt, wr = B.load_w(win_d, 8, C_X + c * 128, 128)
            pt, pr = banks[c % 2]
            B.mms(pt[:, 0:ncols], [(wt[:, kc, :], uT[:, kc, 0:ncols]) for kc in range(8)], r=[wr, R_u], w=[pr])
            xbt, xr = xb[c % 2]
            B.cp(xbt[:, 0:3], halo[:, c, :], r=[R_halo], w=[xr])
            B.cp(xbt[:, 3:3 + ncols], pt[:, 0:ncols], r=[pr], w=[xr])
            B.cp(halo[:, c, :], xbt[:, ncols:ncols + 3], r=[xr], w=[R_halo])
            if blk == NBLK - 1 and ctxj is None:
                B.dma(cv_o[c * 128:(c + 1) * 128, :], xbt[:, ncols:ncols + 3], r=[xr], w=[], q="pool")
            ta, tr = nextA()
            B.ts(ta[:, 0:ncols], xbt[:, 0:ncols], cw[:, c, 0:1], cbv[:, c:c + 1], ALU.mult, ALU.add, r=[xr, R_sm], w=[tr])
            for i in range(1, 4):
                B.stt(ta[:, 0:ncols], xbt[:, i:i + ncols], cw[:, c, i:i + 1], ta[:, 0:ncols], ALU.mult, ALU.add,
                      r=[xr, R_sm, tr], w=[tr])
            tb_, tbr = nextB()
            B.act(tb_[:, 0:ncols], ta[:, 0:ncols], AF.Silu, r=[tr], w=[tbr])
            if c >= 20:
                B.dma(CT_s[(c - 20) * 128:(c - 19) * 128, t0:t0 + ncols], tb_[:, 0:ncols], r=[tbr], w=[R_CT], q="pool")
                continue
            if c >= 16 and ctxj is None:
                B.dma(BT_s[(c - 16) * 128:(c - 15) * 128, t0:t0 + ncols], tb_[:, 0:ncols], r=[tbr], w=[R_BT], q="pool")
            for tt_ in range(ntt):
                P.add("pe", lambda e, tt_=tt_, tb_=tb_: e.transpose(ptT[:, tt_ * 128:(tt_ + 1) * 128],
                                                                     tb_[:, tt_ * 128:(tt_ + 1) * 128], ident_b[:]),
                      r=[tbr, R_const], w=[prT])
            tb2, tbr2 = nextB()
            B.cp(tb2[:, 0:ncols], ptT[:, 0:ncols], r=[prT], w=[tbr2])
            for tt_ in range(ntt):
                rows = slice(t0 + tt_ * 128, t0 + (tt_ + 1) * 128)
                if c < 16:
                    B.dma(xs_s[rows, c * 128:(c + 1) * 128], tb2[:, tt_ * 128:(tt_ + 1) * 128], r=[tbr2], w=[R_xs], q="pool")
                else:
                    B.dma(Bt_s[rows, (c - 16) * 128:(c - 15) * 128], tb2[:, tt_ * 128:(tt_ + 1) * 128], r=[tbr2], w=[R_Bt], q="pool")
        for gi, (c0, dst, rr) in enumerate([(C_GA, ga_s, R_ga), (C_GS, gs_s, R_gs)] if ctxj is None else []):
            for c in range(8):
                wt, wr = B.load_w(win_d, 8, c0 + c * 128, 128)
                pt, pr = banks[2 + c % 2]
                B.mms(pt[:, 0:ncols], [(wt[:, kc, :], uT[:, kc, 0:ncols]) for kc in range(8)], r=[wr, R_u], w=[pr])
                tb_, tbr = nextB()
                B.act(tb_[:, 0:ncols], pt[:, 0:ncols], AF.Sigmoid, r=[pr], w=[tbr])
                B.dma(dst[c * 128:(c + 1) * 128, t0:t0 + ncols], tb_[:, 0:ncols], r=[tbr], w=[rr], q="pool")


    xTd_v = xT_d.rearrange("(k p) t -> p k t", p=128)
    x1o_v = x1_o.rearrange("(k p) t -> p k t", p=128)

    xs1 = B.sb("xs1", [128, 8, 32], F32)
    sconv = B.sb("sconv", [128, 24, 2, 3], F32)
    R_xs1 = Res()

    def run_own_p1():
        NS_ = 32
        NX_ = 35
        sgroups = [(0, 16, 1), (16, 16, 2), (32, 3, 0)]
        B.dma(xT[:, :, 0:NX_], xsT_d.rearrange("(k p) t -> p k t", p=128), w=[R_x])
        rms_mod(xT, 0, sgroups, NX_)
        ffn(0, w1a_d, w3a_d, w2a_d, sgroups, NX_)
        rms_mod(xT, 1, sgroups, NX_)
        B.dma(sconv[:], scv_d[:, :, :, :], w=[R_sm])
        B.cp(xs1[:], xT[:, :, 0:NS_], r=[R_x], w=[R_xs1])
        for h in range(H_A):
            wt, wr = B.load_w(win_d, 8, C_Q + h * 64, 64)
            pt, pr = banks[h % 2]
            B.mms(pt[0:64, 0:NS_], [(wt[:, kc, :], uT[:, kc, 0:NS_]) for kc in range(8)], r=[wr, R_u], w=[pr])
            tb_, tbr = nextB()
            B.ts(tb_[0:64, 0:NS_], pt[0:64, 0:NS_], 0.125, None, ALU.mult, r=[pr], w=[tbr])
            B.dma(q2_s[h, 0:64, :], tb_[0:64, 0:NS_], r=[tbr], w=[R_s2], q="pool")
            wt, wr = B.load_w(win_d, 8, C_K + h * 64, 64)
            pt, pr = banks[2 + h % 2]
            B.mms(pt[0:64, 0:NS_], [(wt[:, kc, :], uT[:, kc, 0:NS_]) for kc in range(8)], r=[wr, R_u], w=[pr])
            kf, kr = kst[h % 2]
            B.cp(kf[:, 0:NS_], pt[0:64, 0:NS_], r=[pr], w=[kr])
            B.dma(ksT_o[h * 64:(h + 1) * 64, :], kf[:, 0:NS_], r=[kr], w=[], q="pool")
            tb_, tbr = nextB()
            B.cp(tb_[0:64, 0:NS_], pt[0:64, 0:NS_], r=[pr], w=[tbr], eng="act")
            B.dma(k2_s[h, 0:64, :], tb_[0:64, 0:NS_], r=[tbr], w=[R_s2], q="pool")
            B.dma(k2_s[h, 64:66, :], onesrow[64:66, 0:NS_], r=[R_const], w=[R_s2], q="pool")
        for cg in range(4):
            wt, wr = B.load_w(win_d, 8, C_V + cg * 256, 256)
            for sq_ in range(2):
                pt, pr = banks[4 + sq_]
                B.mms(pt[0:16, 0:256], [(uT[:, kc, sq_ * 16:(sq_ + 1) * 16], wt[:, kc, :]) for kc in range(8)], r=[wr, R_u], w=[pr])
                ta, tr = nextA()
                B.cp(ta[0:16, 0:256], pt[0:16, 0:256], r=[pr], w=[tr])
                B.dma(vs_o[sq_ * 16:(sq_ + 1) * 16, cg * 256:(cg + 1) * 256], ta[0:16, 0:256], r=[tr], w=[], q="pool")
                vs, vr = vst[sq_]
                B.cp(vs[0:16, 0:4, 0:64], pt[0:16, 0:256].rearrange("p (h d) -> p h d", h=4), r=[pr], w=[vr], eng="act")
                B.dma(v2_s[cg * 4:(cg + 1) * 4, sq_, :, :].rearrange("h p d -> p h d"), vs[0:16, 0:4, :], r=[vr], w=[R_s2], q="pool")
        wt, wr = B.load_w(win_d, 8, C_F, 16)
        for sq_ in range(2):
            pt, pr = banks[6]
            B.mms(pt[0:16, 0:16], [(uT[:, kc, sq_ * 16:(sq_ + 1) * 16], wt[:, kc, :]) for kc in range(8)], r=[wr, R_u], w=[pr])
            ta, tr = nextA()
            softplus_to(ta[0:16, 0:16], pt[0:16, 0:16], bfb[0:16, :], 16, r=[pr, R_sm], w=[tr], neg_in=True, pn=16)
            B.ts(logf[0:16, NT + sq_, :], ta[0:16, 0:16], -1.0, None, ALU.mult, r=[tr], w=[R_lf])
            B.dma(lfs_o[sq_ * 16:(sq_ + 1) * 16, :], logf[0:16, NT + sq_, :], r=[R_lf], w=[], q="pool")
        if B.stage >= 2:
            wt, wr = B.load_w(win_d, 8, C_DT, 32)
            for sq_ in range(2):
                pt, pr = banks[6]
                B.mms(pt[0:16, 0:32], [(uT[:, kc, sq_ * 16:(sq_ + 1) * 16], wt[:, kc, :]) for kc in range(8)], r=[wr, R_u], w=[pr])
                softplus_to(dtsb[0:16, NT + sq_, :], pt[0:16, 0:32], dtb[0:16, :], 32, r=[pr, R_sm], w=[R_dt], pn=16)
            for cg in range(8):
                wt, wr = B.load_w(win_d, 8, C_Z + cg * 256, 256)
                for sq_ in range(2):
                    pt, pr = banks[4 + sq_]
                    B.mms(pt[0:16, 0:256], [(uT[:, kc, sq_ * 16:(sq_ + 1) * 16], wt[:, kc, :]) for kc in range(8)], r=[wr, R_u], w=[pr])
                    tb_, tbr = nextB()
                    B.cp(tb_[0:16, 0:256], pt[0:16, 0:256], r=[pr], w=[tbr], eng="act")
                    B.dma(z2_s[sq_ * 16:(sq_ + 1) * 16, cg * 256:(cg + 1) * 256], tb_[0:16, 0:256], r=[tbr], w=[R_s2], q="pool")
            for c in range(24):
                wt, wr = B.load_w(win_d, 8, C_X + c * 128, 128)
                pt, pr = banks[c % 2]
                B.mms(pt[:, 0:NX_], [(wt[:, kc, :], uT[:, kc, 0:NX_]) for kc in range(8)], r=[wr, R_u], w=[pr])
                xbt, xr = xb[c % 2]
                B.cp(xbt[:, 0:3], sconv[:, c, 0, :], r=[R_sm], w=[xr])
                B.cp(xbt[:, 3:19], pt[:, 0:16], r=[pr], w=[xr])
                B.cp(xbt[:, 19:22], sconv[:, c, 1, :], r=[R_sm], w=[xr])
                B.cp(xbt[:, 22:38], pt[:, 16:32], r=[pr], w=[xr])
                B.ts(halo[:, c, :], pt[:, 32:35], flg[:, 16:17], None, ALU.mult, r=[pr, R_sm], w=[R_halo])
                B.dma(cvs_o[c * 128:(c + 1) * 128, 0:3], xbt[:, 16:19], r=[xr], w=[], q="pool")
                B.dma(cvs_o[c * 128:(c + 1) * 128, 3:6], xbt[:, 35:38], r=[xr], w=[], q="pool")
                ta, tr = nextA()
                B.ts(ta[:, 0:35], xbt[:, 0:35], cw[:, c, 0:1], cbv[:, c:c + 1], ALU.mult, ALU.add, r=[xr, R_sm], w=[tr])
                for i in range(1, 4):
                    B.stt(ta[:, 0:35], xbt[:, i:i + 35], cw[:, c, i:i + 1], ta[:, 0:35], ALU.mult, ALU.add, r=[xr, R_sm, tr], w=[tr])
                tb_, tbr = nextB()
                B.act(tb_[:, 0:35], ta[:, 0:35], AF.Silu, r=[tr], w=[tbr])
                offs = (0, 19)
                if c >= 20:
                    for sq_ in range(2):
                        B.dma(CT2_s[(c - 20) * 128:(c - 19) * 128, sq_ * 16:(sq_ + 1) * 16], tb_[:, offs[sq_]:offs[sq_] + 16], r=[tbr], w=[R_s2], q="pool")
                    continue
                if c >= 16:
                    for sq_ in range(2):
                        B.dma(BT2_s[(c - 16) * 128:(c - 15) * 128, sq_ * 16:(sq_ + 1) * 16], tb_[:, offs[sq_]:offs[sq_] + 16], r=[tbr], w=[R_s2], q="pool")
                for sq_ in range(2):
                    P.add("pe", lambda e, sq_=sq_, tb_=tb_: e.transpose(ptT[0:16, sq_ * 128:(sq_ + 1) * 128], tb_[:, offs[sq_]:offs[sq_] + 16], ident_b[:]),
                          r=[tbr, R_const], w=[prT])
                tb2, tbr2 = nextB()
                B.cp(tb2[0:16, 0:256], ptT[0:16, 0:256], r=[prT], w=[tbr2])
                for sq_ in range(2):
                    rows2 = slice(sq_ * 16, (sq_ + 1) * 16)
                    if c < 16:
                        B.dma(xs2_s[rows2, c * 128:(c + 1) * 128], tb2[0:16, sq_ * 128:(sq_ + 1) * 128], r=[tbr2], w=[R_s2], q="pool")
                    else:
                        B.dma(Bt2_s[rows2, (c - 16) * 128:(c - 15) * 128], tb2[0:16, sq_ * 128:(sq_ + 1) * 128], r=[tbr2], w=[R_s2], q="pool")
            for (c0, dst) in [(C_GA, ga2_s), (C_GS, gs2_s)]:
                for c in range(8):
                    wt, wr = B.load_w(win_d, 8, c0 + c * 128, 128)
                    pt, pr = banks[2 + c % 2]
                    B.mms(pt[:, 0:NS_], [(wt[:, kc, :], uT[:, kc, 0:NS_]) for kc in range(8)], r=[wr, R_u], w=[pr])
                    tb_, tbr = nextB()
                    B.act(tb_[:, 0:NS_], pt[:, 0:NS_], AF.Sigmoid, r=[pr], w=[tbr])
                    B.dma(dst[c * 128:(c + 1) * 128, :], tb_[:, 0:NS_], r=[tbr], w=[R_s2], q="pool")

        xTd_v = xT_d.rearrange("(k p) t -> p k t", p=128)
        x1o_v = x1_o.rearrange("(k p) t -> p k t", p=128)
        for blk in range(NBLK):
            t0 = blk * TB
            groups = [(0, TB, 0)]
            B.dma(xT[:, :, :], xTd_v[:, :, t0:t0 + TB], w=[R_x])
            if B.cutn <= 2:
                B.dma(x1o_v[:, :, t0:t0 + TB], xT[:, :, :], r=[R_x], w=[R_x1], q="pool")
            B.cut(2)
            rms_mod(xT, 0, groups, TB)
            if B.cutn <= 3:
                B.cp(xT[:, :, :], uT[:, :, :], r=[R_u], w=[R_x])
                B.dma(x1o_v[:, :, t0:t0 + TB], xT[:, :, :], r=[R_x], w=[R_x1], q="pool")
            B.cut(3)
            ffn(0, w1a_d, w3a_d, w2a_d, groups, TB)
            B.dma(x1o_v[:, :, t0:t0 + TB], xT[:, :, :], r=[R_x], w=[R_x1], q="pool")
            B.cut(4)
            rms_mod(xT, 1, groups, TB)
            win_block(blk, groups, TB)


    if B.stage < 3:
        run_own_p1()
        return
    tri = B.sb("tri", [128, 128], F32)
    mneg = B.sb("mneg", [128, 128], F32)
    e0row = B.sb("e0row", [128, 128], F32)
    selt = B.sb("selt", [16, 16, 66], F32)
    selc = B.sb("selc", [128, 2], F32)
    a_bc = B.sb("a_bc", [128, 32], F32)
    dsk = B.sb("dsk", [128, 32], F32)
    gssd = B.sb("gssd", [128, 16], F32)
    R_c3 = Res()
    B.dma(tri[:], tri_d[:, :], w=[R_c3])
    B.dma(mneg[:], mneg_d[:, :], w=[R_c3])
    B.dma(e0row[:], e0_d[:, :], w=[R_c3])
    B.dma(selt[:], sel_d[:, :, :], w=[R_c3])
    B.dma(selc[:], selc_d[:, :], w=[R_c3])
    B.dma(a_bc[:], alog_d[:, :], w=[R_c3])
    B.dma(dsk[:], dsk_d[:, :], w=[R_c3])
    B.dma(gssd[:], gssd_d[:, :], w=[R_c3])
    B.act(a_bc[:], a_bc[:], AF.Exp, r=[R_c3], w=[R_c3])
    B.ts(a_bc[:], a_bc[:], -1.0, None, ALU.mult, r=[R_c3], w=[R_c3])

    Fc = B.sb("Fc", [128, NT, 16], F32)
    carry = B.sb("carry", [128, 16], F32)
    R_F, R_carry = Res(), Res()
    arena = B.sb("arena", [128, 8192], BF16)
    R_ar = [Res() for _ in range(4)]
    xtm_s = [arena[:, 0:2048], arena[:, 2048:4096]]
    ztm_s = [arena[:, 4096:6144], arena[:, 6144:8192]]
    oT = arena[0:64, :].rearrange("p (h t) -> p h t", h=16)
    bc_s = [(B.sb(f"bcs{i}", [128, 3, 512], BF16), Res()) for i in range(2)]
    Hst = B.sb("Hst", [128, DIN], F32)
    Hb = B.sb("Hb", [128, DIN], BF16)
    yz = B.sb("yz", [128, DIN], F32)
    sm = B.sb("sm", [128, 8, 32], F32)
    cbm = B.sb("cbm", [128, 4, 128], F32)
    xd = B.sb("xd", [128, DIN], BF16)
    ynb = xd
    wTb = [(B.sb(f"wTb{i}", [128, 4, 128], BF16), Res()) for i in range(2)]
    D4 = [(B.sb(f"D4{i}", [128, 4, 128], F32), Res()) for i in range(2)]
    sg4 = [(B.sb(f"sg4{i}", [128, 4, 128], F32), Res()) for i in range(2)]
    ssq = B.sb("ssq", [128, 4], F32)
    R_H, R_Hb, R_yz, R_ynb, R_sm, R_cbm, R_xd, R_ssq = [Res() for _ in range(8)]
    R_ynb = R_xd
    P.add("pool", lambda e: e.memset(Hst[:], 0.0), w=[R_H])
    ynT = gT[:, 0:16, :]
    BT_v = BT_s.rearrange("(g n) t -> n g t", n=128)
    CT_v = CT_s.rearrange("(g n) t -> n g t", n=128)

    def bc3(ap2, n, m):
        return ap2.unsqueeze(2).broadcast_to([128, n, m])

    def ssd_tile(gt, want_y=True, L=128, samp=None, maskj=None):
        sl = gt % 2
        if samp is None:
            rows = slice(gt * 128, (gt + 1) * 128)
            src_x, src_z, src_Bt, src_BT, src_CT = xs_s, z_s, Bt_s, BT_v, CT_v
            rx_, rz_, rbt_, rBT_, rCT_ = R_xs, R_z, R_Bt, R_BT, R_CT
            dti = gt
            ycol0 = (gt % 4) * 128
        else:
            sl = samp
            rows = slice(samp * 16, samp * 16 + 16)
            src_x, src_z, src_Bt, src_BT, src_CT = xs2_s, z2_s, Bt2_s, BT2_v, CT2_v
            rx_ = rz_ = rbt_ = rBT_ = rCT_ = R_s2
            dti = NT + samp
            ycol0 = samp * 16
        xtm, ztm = xtm_s[sl], ztm_s[sl]
        Rx_, Rz_ = R_ar[sl], R_ar[2 + sl]
        bct, Rb_ = bc_s[sl]
        B.dma(xtm[0:L, :], src_x[rows, :], r=[rx_], w=[Rx_])
        if want_y:
            B.dma(ztm[0:L, :], src_z[rows, :], r=[rz_], w=[Rz_])
        B.dma(bct[0:L, 0, :], src_Bt[rows, :], r=[rbt_], w=[Rb_])
        if want_y:
            B.dma(bct[:, 1, 0:4 * L].rearrange("p (g t) -> p g t", g=4), src_BT[:, :, rows], r=[rBT_], w=[Rb_])
            B.dma(bct[:, 2, 0:4 * L].rearrange("p (g t) -> p g t", g=4), src_CT[:, :, rows], r=[rCT_], w=[Rb_])
        Btm = bct[0:L, 0, :].rearrange("p (g n) -> p g n", g=4)
        BTf = bct[:, 1, 0:4 * L].rearrange("p (g t) -> p g t", g=4)
        CTf = bct[:, 2, 0:4 * L].rearrange("p (g t) -> p g t", g=4)
        xt3 = xtm[0:L, :].rearrange("p (h d) -> p h d", h=32)
        dA, acs, ea, de, tmpv = [sm[0:L, j, :] for j in (0, 1, 3, 4, 6)]
        al, cd = sm[:, 2, :], sm[:, 5, :]
        dtv = dtsb[0:L, dti, :]
        B.tt(dA, dtv, a_bc[0:L, :], ALU.mult, r=[R_dt, R_c3], w=[R_sm])
        pt, pr = banks[0]
        P.add("pe", lambda e: e.matmul(pt[0:L, 0:32], tri[0:L, 0:L], dA, start=True, stop=True), r=[R_c3, R_sm], w=[pr])
        P.add("pe", lambda e: e.matmul(pt[:, 32:64], ones_f[0:L, :], dA, start=True, stop=True), r=[R_const, R_sm], w=[pr])
        B.cp(acs, pt[0:L, 0:32], r=[pr], w=[R_sm])
        B.cp(al, pt[:, 32:64], r=[pr], w=[R_sm])
        B.act(ea, acs, AF.Exp, r=[R_sm], w=[R_sm])
        B.act(cd, al, AF.Exp, r=[R_sm], w=[R_sm])
        if maskj is not None:
            B.ts(cd, cd, -1.0, flg[:, maskj:maskj + 1], ALU.add, ALU.mult, r=[R_sm], w=[R_sm])
            B.ts(cd, cd, 1.0, None, ALU.add, r=[R_sm], w=[R_sm])
        B.tt(tmpv, al[0:L, :], acs, ALU.subtract, r=[R_sm], w=[R_sm])
        B.act(tmpv, tmpv, AF.Exp, r=[R_sm], w=[R_sm])
        B.tt(de, tmpv, dtv, ALU.mult, r=[R_sm, R_dt], w=[R_sm])
        if want_y:
            B.cp(Hb[:], Hst[:], r=[R_H], w=[R_Hb], eng="act")
            pc, prc = banks[1]
            for g in range(4):
                P.add("pe", lambda e, g=g: e.matmul(pc[0:L, g * L:(g + 1) * L], BTf[:, g, :], CTf[:, g, :], start=True, stop=True),
                      r=[Rb_], w=[prc])
            cbv_ = cbm[:].rearrange("p g t -> p (g t)")[0:L, 0:4 * L].rearrange("p (g t) -> p g t", g=4)
            B.tt(cbv_, pc[0:L, 0:4 * L].rearrange("p (g t) -> p g t", g=4), tri[0:L, 0:L].unsqueeze(1).broadcast_to([L, 4, L]), ALU.mult,
                 r=[prc, R_c3], w=[R_cbm])
            for g in range(4):
                pyd, pryd = banks[4]
                for hb in range(2):
                    h0 = g * 8 + hb * 4
                    d4t, rd4 = D4[hb]
                    s4t, rs4 = sg4[hb]
                    wt4t, rw4 = wTb[hb]
                    d4 = d4t[:].rearrange("p g t -> p (g t)")[0:L, 0:4 * L].rearrange("p (g t) -> p g t", g=4)
                    s4 = s4t[:].rearrange("p g t -> p (g t)")[0:L, 0:4 * L].rearrange("p (g t) -> p g t", g=4)
                    wt4 = wt4t[:].rearrange("p g t -> p (g t)")[0:L, 0:4 * L].rearrange("p (g t) -> p g t", g=4)
                    B.tt(d4, ident_f[0:L, 0:L].unsqueeze(1).broadcast_to([L, 4, L]),
                         acs[:, h0:h0 + 4].unsqueeze(2).broadcast_to([L, 4, L]), ALU.mult, r=[R_const, R_sm], w=[rd4])
                    pb, prb = banks[2 + hb]
                    P.add("pe", lambda e, d4t=d4t, pb=pb: e.matmul(pb[0:L, 0:4 * L], ones_f[0:L, 0:L],
                                                                    d4t[:].rearrange("p g t -> p (g t)")[0:L, 0:4 * L], start=True, stop=True),
                          r=[R_const, rd4], w=[prb])
                    for j in range(4):
                        B.stt(s4[:, j, :], pb[0:L, j * L:(j + 1) * L], acs[:, h0 + j:h0 + j + 1], mneg[0:L, 0:L], ALU.subtract, ALU.add,
                              r=[prb, R_sm, R_c3], w=[rs4])
                    B.act(s4, s4, AF.Exp, r=[rs4], w=[rs4])
                    for j in range(4):
                        B.stt(wt4[:, j, :], s4[:, j, :], dtv[:, h0 + j:h0 + j + 1], cbv_[:, g, :], ALU.mult, ALU.mult,
                              r=[rs4, R_dt, R_cbm], w=[rw4])
                    for j in range(4):
                        hh = hb * 4 + j
                        P.add("pe", lambda e, j=j, hh=hh, wt4=wt4, h0=h0: e.matmul(pyd[0:L, hh * 64:(hh + 1) * 64], wt4[:, j, :], xt3[:, h0 + j, :],
                                                                                    start=True, stop=True), r=[rw4, Rx_], w=[pryd])
                pyo, pryo = banks[5]
                P.add("pe", lambda e, g=g, pyo=pyo: e.matmul(pyo[0:L, :], CTf[:, g, :], Hb[:, g * 512:(g + 1) * 512], start=True, stop=True),
                      r=[Rb_, R_Hb], w=[pryo])
                yg = yz[0:L, g * 512:(g + 1) * 512].rearrange("p (h d) -> p h d", h=8)
                B.tt(yg, pyo[0:L, :].rearrange("p (h d) -> p h d", h=8), ea[:, g * 8:(g + 1) * 8].unsqueeze(2).broadcast_to([L, 8, 64]),
                     ALU.mult, r=[pryo, R_sm], w=[R_yz])
                B.tt(yg, pyd[0:L, :].rearrange("p (h d) -> p h d", h=8), yg, ALU.add, r=[pryd, R_yz], w=[R_yz])
                ta, tr = nextA()
                ta3 = ta[0:L, :].rearrange("p (h d) -> p h d", h=8)
                B.tt(ta3, xt3[:, g * 8:(g + 1) * 8, :], dsk[0:L, g * 8:(g + 1) * 8].unsqueeze(2).broadcast_to([L, 8, 64]), ALU.mult,
                     r=[Rx_, R_c3], w=[tr])
                B.tt(yg, yg, ta3, ALU.add, r=[tr, R_yz], w=[R_yz])
        B.tt(xd[0:L, :].rearrange("p (h d) -> p h d", h=32), xt3, de.unsqueeze(2).broadcast_to([L, 32, 64]), ALU.mult,
             r=[Rx_, R_sm], w=[R_xd])
        for g in range(4):
            pS, prS = banks[6]
            P.add("pe", lambda e, g=g, pS=pS: e.matmul(pS[:, :], Btm[:, g, :], xd[0:L, g * 512:(g + 1) * 512], start=True, stop=True),
                  r=[Rb_, R_xd], w=[prS])
            Hg = Hst[:, g * 512:(g + 1) * 512].rearrange("p (h d) -> p h d", h=8)
            B.tt(Hg, Hg, bc3(cd[:, g * 8:(g + 1) * 8], 8, 64), ALU.mult, r=[R_sm, R_Hb], w=[R_H])
            if maskj is not None:
                B.stt(Hg, pS[:, :].rearrange("p (h d) -> p h d", h=8), flg[:, maskj:maskj + 1], Hg, ALU.mult, ALU.add, r=[prS], w=[R_H])
            else:
                B.tt(Hg, Hg, pS[:, :].rearrange("p (h d) -> p h d", h=8), ALU.add, r=[prS], w=[R_H])
        if not want_y:
            return
        for g in range(4):
            ta, tr = nextA()
            B.act(ta[0:L, :], ztm[0:L, g * 512:(g + 1) * 512], AF.Silu, r=[Rz_], w=[tr])
            B.tt(yz[0:L, g * 512:(g + 1) * 512], yz[0:L, g * 512:(g + 1) * 512], ta[0:L, :], ALU.mult, r=[tr, R_yz], w=[R_yz])
            ta2, tr2 = nextA()
            P.add("act", lambda e, g=g, ta2=ta2: e.activation(out=ta2[0:L, :], in_=yz[0:L, g * 512:(g + 1) * 512], func=AF.Square,
                                                               accum_out=ssq[0:L, g:g + 1]), r=[R_yz], w=[tr2, R_ssq])
        rs_ = sm[0:L, 7, 0:1]
        P.add("dve", lambda e: e.tensor_reduce(out=rs_, in_=ssq[0:L, 0:4], axis=AX.X, op=ALU.add), r=[R_ssq], w=[R_sm])
        B.act(rs_, rs_, AF.Sqrt, r=[R_sm, R_const], w=[R_sm], bias=epsc[0:L, :], scale=1.0 / DIN)
        P.add("dve", lambda e: e.reciprocal(out=rs_, in_=rs_), r=[R_sm], w=[R_sm])
        B.ts(ynb[0:L, :], yz[0:L, :], rs_, None, ALU.mult, r=[R_yz, R_sm], w=[R_ynb])
        for half in range(2):
            for c in range(8):
                cc = half * 8 + c
                P.add("pe", lambda e, c=c, cc=cc: e.transpose(ptT[:, c * 128:c * 128 + L], ynb[0:L, cc * 128:(cc + 1) * 128], ident_b[0:L, 0:L]),
                      r=[R_ynb, R_const], w=[prT])
            for c in range(8):
                cc = half * 8 + c
                B.ts(ynT[:, cc, ycol0:ycol0 + L], ptT[:, c * 128:c * 128 + L], gssd[:, cc:cc + 1], None, ALU.mult,
                     r=[prT, R_c3], w=[R_gT])

    Kt = [(B.sb(f"Kt{i}", [66, T], BF16), Res()) for i in range(2)]
    Vt = [(B.sb(f"Vt{i}", [128, NT, 65], BF16), Res()) for i in range(2)]
    Qa = [(B.sb(f"Qa{i}", [66, TB], BF16), Res()) for i in range(2)]
    Pt = [(B.sb(f"Pt{i}", [128, TB], BF16), Res()) for i in range(3)]
    FT = sq[0:16, :]
    rbc = B.sb("rbc", [128, 16], F32)
    biasT = B.sb("biasT", [128, NT, 16], F32)
    arb = B.sb("arb", [66, TB], BF16)
    R_FT, R_rbc, R_bias, R_arow, R_rl, R_rlb = [Res() for _ in range(6)]
    R_FT = R_sq
    pti = [0]

    def attn_block(blk):
        t0 = blk * TB
        nkt = 4 * (blk + 1)
        pf, prf = banks[6]
        for j in range(4):
            P.add("pe", lambda e, j=j: e.transpose(pf[0:16, j * 128:(j + 1) * 128], Fc[:, 4 * blk + j, :], ident_f[:]),
                  r=[R_F, R_const], w=[prf])
        B.cp(FT[:, :], pf[0:16, :], r=[prf], w=[R_FT])
        P.add("pe", lambda e: e.matmul(pf[:, 0:16], e0row[:], Fc[:, 4 * blk, :], start=True, stop=True), r=[R_c3, R_F], w=[prf])
        B.cp(rbc[:], pf[:, 0:16], r=[prf], w=[R_rbc])
        B.tt(biasT[:, 0:nkt, :], rbc[:].unsqueeze(1).broadcast_to([128, nkt, 16]), Fc[:, 0:nkt, :], ALU.subtract,
             r=[R_rbc, R_F], w=[R_bias])
        B.tt(rdj[:], delta[:, 0:NCORES, :], rbc[:].unsqueeze(1).broadcast_to([128, NCORES, 16]), ALU.add,
             r=[R_delta, R_rbc], w=[R_rdj])
        kvi = [0]
        for h in range(H_A):
            qa_, qr_ = Qa[h % 2]
            B.dma(qa_[0:64, :], q_s[h, 0:64, t0:t0 + TB], r=[R_q], w=[qr_])
            pa, pra = banks[4]
            P.add("pe", lambda e, h=h: e.matmul(pa[0:66, :], selt[:, h, :], FT[:, :], start=True, stop=True), r=[R_c3, R_FT], w=[pra])
            ar0, rr0 = nextA()
            ar1, rr1 = nextA()
            ar2, rr2 = nextA()
            B.cp(ar0[64:66, :], pa[64:66, :], r=[pra], w=[rr0])
            B.ts(ar1[64:66, :], ar0[64:66, :], ar0[64:66, 0:1], None, ALU.subtract, r=[rr0], w=[rr1])
            B.cp(arb[64:66, :], ar1[64:66, :], r=[rr1], w=[R_arow])
            B.tt(ar0[64:66, :], ar1[64:66, :], arb[64:66, :], ALU.subtract, r=[rr1, R_arow], w=[rr0])
            B.ts(ar2[64:66, :], arb[64:66, :], selc[64:66, 0:1], None, ALU.mult, r=[R_arow, R_c3], w=[rr2])
            B.stt(qa_[64:66, :], ar0[64:66, :], selc[64:66, 1:2], ar2[64:66, :], ALU.mult, ALU.add,
                  r=[rr0, rr2, R_c3], w=[qr_])
            po, pro = banks[2 + h % 2]
            tasks = []
            if CTX:
                for j in range(NCORES - 1):
                    kvi[0] += 1
                    kt_, kr_ = Kt[kvi[0] % 2]
                    vt_, vr_ = Vt[kvi[0] % 2]
                    bj, bjr = biasJ[kvi[0] % 2]

                    def pre(j=j, kt_=kt_, kr_=kr_, vt_=vt_, vr_=vr_, bj=bj, bjr=bjr, h=h):
                        B.dma(kt_[:, :], kgA_v[j, h, :, :], r=[R_kgA], w=[kr_])
                        B.dma(vt_[:, :, :], vgA_v[j, h, :, :, :].rearrange("t p d -> p t d"), r=[R_vgA], w=[vr_])
                        B.ts(bj[:, :], FcA[:, j, :].rearrange("p (t h) -> p t h", h=16)[:, :, h], -1.0, rdj[:, j, h:h + 1], ALU.mult, ALU.add,
                             r=[R_FcA, R_rdj], w=[bjr])
                        B.ts(bj[:, :], bj[:, :], flg[:, j:j + 1], flg[:, 8 + j:9 + j], ALU.mult, ALU.add, r=[R_sm], w=[bjr])
                    for kt in range(NT):
                        tasks.append(dict(pre=pre if kt == 0 else None, kt_=kt_, kr_=kr_, vt_=vt_, vr_=vr_, kc=kt * 128, vi=kt, c0=0,
                                          bias=bj[:, kt:kt + 1], bres=bjr, diag=False, stop=False))
            kvi[0] += 1
            kt_, kr_ = Kt[kvi[0] % 2]
            vt_, vr_ = Vt[kvi[0] % 2]

            def pre_own(kt_=kt_, kr_=kr_, vt_=vt_, vr_=vr_, h=h):
                B.dma(kt_[:, 0:t0 + TB], kg_s[h, :, 0:t0 + TB], r=[R_kg], w=[kr_])
                B.dma(vt_[:, 0:nkt, :], vg_s[h, 0:nkt, :, :].rearrange("t p d -> p t d"), r=[R_vg], w=[vr_])
            for kt in range(nkt):
                j = kt - 4 * blk
                tasks.append(dict(pre=pre_own if kt == 0 else None, kt_=kt_, kr_=kr_, vt_=vt_, vr_=vr_, kc=kt * 128, vi=kt,
                                  c0=(128 * j if j > 0 else 0), bias=biasT[:, kt, h:h + 1], bres=R_bias, diag=(j >= 0),
                                  stop=(kt == nkt - 1)))
            ntask = len(tasks)

            def emitS(i, qa_=qa_, qr_=qr_):
                t = tasks[i]
                ps_, prs = banks[i % 2]
                if t["pre"] is not None:
                    t["pre"]()
                c0 = t["c0"]
                P.add("pe", lambda e, t=t, ps_=ps_, c0=c0: e.matmul(ps_[:, c0:TB], t["kt_"][:, t["kc"]:t["kc"] + 128], qa_[:, c0:TB],
                                                                    start=True, stop=True), r=[t["kr_"], qr_], w=[prs])

            def emitEPV(i, po=po, pro=pro):
                t = tasks[i]
                ps_, prs = banks[i % 2]
                pT_, prp = Pt[i % 3]
                c0 = t["c0"]
                if t["diag"]:
                    ta, tr = nextA()
                    B.tt(ta[:, c0:c0 + 128], ps_[:, c0:c0 + 128], mneg[:], ALU.add, r=[prs, R_c3], w=[tr])
                    B.act(pT_[:, c0:c0 + 128], ta[:, c0:c0 + 128], AF.Exp, r=[tr, t["bres"]], w=[prp], bias=t["bias"], scale=1.0)
                    if c0 + 128 < TB:
                        B.act(pT_[:, c0 + 128:TB], ps_[:, c0 + 128:TB], AF.Exp, r=[prs, t["bres"]], w=[prp], bias=t["bias"], scale=1.0)
                else:
                    B.act(pT_[:, c0:TB], ps_[:, c0:TB], AF.Exp, r=[prs, t["bres"]], w=[prp], bias=t["bias"], scale=1.0)
                st_ = (i == 0)
                sp_ = t["stop"]
                P.add("pe", lambda e, t=t, pT_=pT_, c0=c0, st_=st_, sp_=sp_: e.matmul(po[0:65, c0:TB], t["vt_"][:, t["vi"], :], pT_[:, c0:TB],
                                                                                    start=st_, stop=sp_, skip_group_check=True),
                      r=[t["vr_"], prp], w=[pro])
            emitS(0)
            for i in range(ntask):
                if i + 1 < ntask:
                    emitS(i + 1)
                emitEPV(i)
            rl, R_rl = nextA()
            rlb, R_rlb = nextA()
            P.add("dve", lambda e, po=po, rl=rl: e.reciprocal(out=rl[64:65, :], in_=po[64:65, :]), r=[pro], w=[R_rl])
            pb2, prb2 = banks[5]
            P.add("pe", lambda e, rl=rl: e.matmul(pb2[0:64, :], ones_f[64:65, 0:64], rl[64:65, :], start=True, stop=True), r=[R_const, R_rl], w=[prb2])
            B.cp(rlb[0:64, :], pb2[0:64, :], r=[prb2], w=[R_rlb])
            B.tt(oT[:, h, :], po[0:64, :], rlb[0:64, :], ALU.mult, r=[pro, R_rlb], w=R_ar)

    gab = [(B.sb(f"gab{i}", [128, 2, TB], BF16), Res()) for i in range(2)]
    mT = uT
    R_mT = R_u
    ga_v = ga_s.rearrange("(k p) t -> p k t", p=128)
    gs_v = gs_s.rearrange("(k p) t -> p k t", p=128)
    yTo_v = yT_o.rearrange("(k p) t -> p k t", p=128)

    def dense_block(blk, ncols=TB, groups=None, samp=False):
        t0 = 0 if samp else blk * TB
        if groups is None:
            groups = [(0, TB, 0)]
        gav = ga2_v if samp else ga_v
        gsv = gs2_v if samp else gs_v
        rga, rgs = (R_s2, R_s2) if samp else (R_ga, R_gs)
        for oc in range(8):
            wat, war = B.load_w(wa_d, 16, oc * 128, 128, pn=64)
            wst_, wsr = B.load_w(ws_d, 16, oc * 128, 128)
            pa_, pra_ = banks[oc % 2]
            ps2, prs2 = banks[2 + oc % 2]
            B.mms(pa_[:, 0:ncols], [(wat[:, h, :], oT[:, h, 0:ncols]) for h in range(16)], r=[war] + R_ar, w=[pra_])
            B.mms(ps2[:, 0:ncols], [(wst_[:, kc, :], ynT[:, kc, 0:ncols]) for kc in range(16)], r=[wsr, R_gT], w=[prs2])
            gt_, gr_ = gab[oc % 2]
            B.dma(gt_[:, 0, 0:ncols], gav[:, oc, t0:t0 + ncols], r=[rga], w=[gr_])
            B.dma(gt_[:, 1, 0:ncols], gsv[:, oc, t0:t0 + ncols], r=[rgs], w=[gr_])
            ta, tr = nextA()
            B.tt(ta[:, 0:ncols], pa_[:, 0:ncols], gt_[:, 0, 0:ncols], ALU.mult, r=[pra_, gr_], w=[tr])
            ta2, tr2 = nextA()
            B.tt(ta2[:, 0:ncols], ps2[:, 0:ncols], gt_[:, 1, 0:ncols], ALU.mult, r=[prs2, gr_], w=[tr2])
            B.tt(mT[:, oc, 0:ncols], ta[:, 0:ncols], ta2[:, 0:ncols], ALU.add, r=[tr, tr2], w=[R_mT])
        if samp:
            B.cp(xT[:, :, 0:ncols], xs1[:, :, 0:ncols], r=[R_xs1], w=[R_x])
        else:
            B.dma(xT[:, :, :], x1o_v[:, :, t0:t0 + TB], r=[R_x1], w=[R_x])
        for oc in range(8):
            wot, wor = B.load_w(wo_d, 8, oc * 128, 128)
            po_, pro_ = banks[4 + oc % 2]
            B.mms(po_[:, 0:ncols], [(wot[:, kc, :], mT[:, kc, 0:ncols]) for kc in range(8)], r=[wor, R_mT], w=[pro_])
            for (c0_, n_, m_) in groups:
                B.stt(xT[:, oc, c0_:c0_ + n_], po_[:, c0_:c0_ + n_], Gm[:, 1, oc, m_:m_ + 1], xT[:, oc, c0_:c0_ + n_],
                      ALU.mult, ALU.add, r=[pro_, R_AG], w=[R_x])
        rms_mod(xT, 2, groups, ncols)
        ffn(2, w1b_d, w3b_d, w2b_d, groups, ncols)
        pt, pr = banks[6]
        for kc in range(8):
            ta, tr = nextA()
            B.tt(ta[:, 0:ncols], xT[:, kc, 0:ncols], xT[:, kc, 0:ncols], ALU.mult, r=[R_x], w=[tr])
            P.add("pe", lambda e, ta=ta, kc=kc: e.matmul(pt[:, 0:ncols], ones_f[:], ta[:, 0:ncols], start=(kc == 0), stop=(kc == 7)),
                  r=[tr, R_const], w=[pr])
        B.act(sq[:, 0:ncols], pt[:, 0:ncols], AF.Sqrt, r=[pr, R_const], w=[R_sq], bias=epsc[:], scale=1.0 / D)
        P.add("dve", lambda e: e.reciprocal(out=rstd[:, 0:ncols], in_=sq[:, 0:ncols]), r=[R_sq], w=[R_rstd])
        for kc in range(8):
            B.stt(xT[:, kc, 0:ncols], xT[:, kc, 0:ncols], gsb[:, 3, kc:kc + 1], rstd[:, 0:ncols], ALU.mult, ALU.mult,
                  r=[R_x, R_g, R_rstd], w=[R_x])
        if samp:
            B.dma(ysT_o.rearrange("(k p) t -> p k t", p=128), xT[:, :, 0:ncols], r=[R_x], w=[], q="pool")
        else:
            B.dma(yTo_v[:, :, t0:t0 + TB], xT[:, :, :], r=[R_x], w=[], q="pool")

    ga2_v = ga2_s.rearrange("(k p) t -> p k t", p=128)
    gs2_v = gs2_s.rearrange("(k p) t -> p k t", p=128)
    BT2_v = BT2_s.rearrange("(g n) t -> n g t", n=128)
    CT2_v = CT2_s.rearrange("(g n) t -> n g t", n=128)

    FcA = B.sb("FcA", [128, NCORES, 256], F32)
    smAll = B.sb("smAll", [128, NCORES, 16], F32)
    delta = B.sb("delta", [128, NCORES + 1, 16], F32)
    R_FcA, R_smAll, R_delta = Res(), Res(), Res()
    xTall_v = xTall_d.rearrange("(k p) t -> p k t", p=128)
    NCTX = NCORES - 1
    P.add("dve", lambda e: e.memset(smAll[:], 0.0), w=[R_smAll])
    for j in range(NCTX):
        for h in range(H_A):
            for bb in range(T // TB):
                B.dma(kgA_v[j, h, 64:66, bb * TB:(bb + 1) * TB], onesrow[64:66, :], r=[R_const], w=[R_kgA], q="pool")
    P.add("pool", lambda e: e.memset(halo[:], 0.0), w=[R_halo])
    for j in range(NCTX):
        for blk in range(NBLK):
            c0_ = j * T + blk * TB
            B.dma(xT[:, :, :], xTall_v[:, :, c0_:c0_ + TB], w=[R_x])
            rms_mod(xT, 0, [(0, TB, 0)], TB)
            ffn(0, w1a_d, w3a_d, w2a_d, [(0, TB, 0)], TB)
            rms_mod(xT, 1, [(0, TB, 0)], TB)
            win_block(blk, [(0, TB, 0)], TB, ctxj=j)
        P.add("dve", lambda e: e.memset(carry[:], 0.0), w=[R_carry])
        for i in range(NT):
            pt, pr = banks[6]
            P.add("pe", lambda e, i=i: e.matmul(pt[:, 0:16], tri[:], logf[:, i, :], start=True, stop=True), r=[R_c3, R_lf], w=[pr])
            B.tt(FcA[:, j, i * 16:(i + 1) * 16], pt[:, 0:16], carry[:], ALU.add, r=[pr, R_carry], w=[R_FcA])
            P.add("pe", lambda e, i=i: e.matmul(pt[:, 16:32], ones_f[:], logf[:, i, :], start=True, stop=True), r=[R_const, R_lf], w=[pr])
            B.tt(carry[:], pt[:, 16:32], carry[:], ALU.add, r=[pr], w=[R_carry])
        B.cp(smAll[:, j, :], carry[:], r=[R_carry], w=[R_smAll])
        for gt in range(NT):
            ssd_tile(gt, want_y=False, maskj=j)
    P.add("dve", lambda e: e.memset(delta[:], 0.0), w=[R_delta])
    for j in range(NCORES - 1, -1, -1):
        B.stt(delta[:, j, :], smAll[:, j, 0:16], flg[:, j:j + 1], delta[:, j + 1, :], ALU.mult, ALU.add,
              r=[R_smAll, R_delta], w=[R_delta])
    run_own_p1()
    P.add("dve", lambda e: e.memset(carry[:], 0.0), w=[R_carry])
    for i in range(NT):
        pt, pr = banks[6]
        P.add("pe", lambda e, i=i: e.matmul(pt[:, 0:16], tri[:], logf[:, i, :], start=True, stop=True), r=[R_c3, R_lf], w=[pr])
        B.tt(Fc[:, i, :], pt[:, 0:16], carry[:], ALU.add, r=[pr, R_carry], w=[R_F])
        P.add("pe", lambda e, i=i: e.matmul(pt[:, 16:32], ones_f[:], logf[:, i, :], start=True, stop=True), r=[R_const, R_lf], w=[pr])
        B.tt(carry[:], pt[:, 16:32], carry[:], ALU.add, r=[pr], w=[R_carry])
    biasJ = [(B.sb(f"biasJ{i}", [128, NT], F32), Res()) for i in range(2)]
    rdj = B.sb("rdj", [128, NCORES, 16], F32)
    R_rdj = Res()
    for blk in range(NBLK):
        for lt in range(4):
            ssd_tile(blk * 4 + lt)
        attn_block(blk)
        dense_block(blk)
    B.dma(ssm_o[:, :, :], Hst[:].rearrange("p (h d) -> p h d", h=32), r=[R_H], w=[], q="pool")

    lfc = B.sb("lfc", [128, 8, 16], F32)
    Fs = B.sb("Fs", [128, 9, 16], F32)
    biasS = B.sb("biasS", [128, 9, 16], F32)
    R_lfc, R_Fs, R_bS = Res(), Res(), Res()

    def attn_sample(sq_):
        cs_ = slice(sq_ * 16, (sq_ + 1) * 16)
        B.dma(lfc[:], clf_d[sq_, :, :].rearrange("(t p) h -> p t h", p=128), w=[R_lfc])
        P.add("dve", lambda e: e.memset(carry[:], 0.0), w=[R_carry])
        P.add("dve", lambda e: e.memset(Fs[:], 0.0), w=[R_Fs])
        pt, pr = banks[6]
        for i in range(8):
            P.add("pe", lambda e, i=i: e.matmul(pt[:, 0:16], tri[:], lfc[:, i, :], start=True, stop=True), r=[R_c3, R_lfc], w=[pr])
            B.tt(Fs[:, i, :], pt[:, 0:16], carry[:], ALU.add, r=[pr, R_carry], w=[R_Fs])
            P.add("pe", lambda e, i=i: e.matmul(pt[:, 16:32], ones_f[:], lfc[:, i, :], start=True, stop=True), r=[R_const, R_lfc], w=[pr])
            B.tt(carry[:], pt[:, 16:32], carry[:], ALU.add, r=[pr], w=[R_carry])
        P.add("pe", lambda e: e.matmul(pt[0:16, 0:16], tri[0:16, 0:16], logf[0:16, NT + sq_, :], start=True, stop=True), r=[R_c3, R_lf], w=[pr])
        B.tt(Fs[0:16, 8, :], pt[0:16, 0:16], carry[0:16, :], ALU.add, r=[pr, R_carry], w=[R_Fs])
        pf, prf = banks[6]
        P.add("pe", lambda e: e.transpose(pf[0:16, 0:16], Fs[0:16, 8, :], ident_f[0:16, 0:16]), r=[R_Fs, R_const], w=[prf])
        B.cp(FT[:, 0:16], pf[0:16, 0:16], r=[prf], w=[R_FT])
        P.add("pe", lambda e: e.matmul(pf[:, 0:16], e0row[0:16, :], Fs[0:16, 8, :], start=True, stop=True), r=[R_c3, R_Fs], w=[prf])
        B.cp(rbc[:], pf[:, 0:16], r=[prf], w=[R_rbc])
        B.tt(biasS[:], rbc[:].unsqueeze(1).broadcast_to([128, 9, 16]), Fs[:], ALU.subtract, r=[R_rbc, R_Fs], w=[R_bS])
        for h in range(H_A):
            kt_, kr_ = Kt[h % 2]
            vt_, vr_ = Vt[h % 2]
            qa_, qr_ = Qa[h % 2]
            B.dma(yz[0:64, 0:1024], ckT_d[sq_, h, :, :], w=[R_yz])
            B.cp(kt_[0:64, 0:1024], yz[0:64, 0:1024], r=[R_yz], w=[kr_], eng="pool")
            P.add("pool", lambda e, kt_=kt_: e.memset(kt_[64:66, 0:1040], 1.0), w=[kr_])
            B.dma(kt_[0:64, 1024:1040], k2_s[h, 0:64, cs_], r=[R_s2], w=[kr_])
            ta, tr = nextA()
            B.dma(ta[:, :].rearrange("p (t d) -> p t d", t=8),
                  cv_d[sq_, :, :].rearrange("(t p) (h d) -> p t h d", p=128, h=16)[:, :, h, :], w=[tr])
            B.cp(vt_[:, 0:8, 0:64], ta[:, :].rearrange("p (t d) -> p t d", t=8), r=[tr], w=[vr_], eng="pool")
            P.add("pool", lambda e, vt_=vt_: e.memset(vt_[:, 0:9, 64:65], 1.0), w=[vr_])
            B.dma(vt_[0:16, 8, :], v2_s[h, sq_, :, :], r=[R_s2], w=[vr_])
            B.dma(qa_[0:64, 0:16], q2_s[h, 0:64, cs_], r=[R_s2], w=[qr_])
            pa, pra = banks[4]
            P.add("pe", lambda e, h=h: e.matmul(pa[0:66, 0:16], selt[:, h, :], FT[:, 0:16], start=True, stop=True), r=[R_c3, R_FT], w=[pra])
            ar0, rr0 = nextA()
            ar1, rr1 = nextA()
            ar2, rr2 = nextA()
            B.cp(ar0[64:66, 0:16], pa[64:66, 0:16], r=[pra], w=[rr0])
            B.ts(ar1[64:66, 0:16], ar0[64:66, 0:16], ar0[64:66, 0:1], None, ALU.subtract, r=[rr0], w=[rr1])
            B.cp(arb[64:66, 0:16], ar1[64:66, 0:16], r=[rr1], w=[R_arow])
            B.tt(ar0[64:66, 0:16], ar1[64:66, 0:16], arb[64:66, 0:16], ALU.subtract, r=[rr1, R_arow], w=[rr0])
            B.ts(ar2[64:66, 0:16], arb[64:66, 0:16], selc[64:66, 0:1], None, ALU.mult, r=[R_arow, R_c3], w=[rr2])
            B.stt(qa_[64:66, 0:16], ar0[64:66, 0:16], selc[64:66, 1:2], ar2[64:66, 0:16], ALU.mult, ALU.add, r=[rr0, rr2, R_c3], w=[qr_])
            po, pro = banks[2 + h % 2]
            for kt in range(9):
                Lk = 128 if kt < 8 else 16
                ps_, prs = banks[kt % 2]
                P.add("pe", lambda e, kt=kt, Lk=Lk, ps_=ps_, kt_=kt_, qa_=qa_: e.matmul(ps_[0:Lk, 0:16], kt_[:, kt * 128:kt * 128 + Lk], qa_[:, 0:16],
                                                                                       start=True, stop=True), r=[kr_, qr_], w=[prs])
                pti[0] += 1
                pT_, prp = Pt[pti[0] % 3]
                if kt == 8:
                    ta, tr = nextA()
                    B.tt(ta[0:16, 0:16], ps_[0:16, 0:16], mneg[0:16, 0:16], ALU.add, r=[prs, R_c3], w=[tr])
                    B.act(pT_[0:16, 0:16], ta[0:16, 0:16], AF.Exp, r=[tr, R_bS], w=[prp], bias=biasS[0:16, 8, h:h + 1], scale=1.0)
                else:
                    B.act(pT_[:, 0:16], ps_[:, 0:16], AF.Exp, r=[prs, R_bS], w=[prp], bias=biasS[:, kt, h:h + 1], scale=1.0)
                P.add("pe", lambda e, kt=kt, Lk=Lk, pT_=pT_, vt_=vt_, po=po: e.matmul(po[0:65, 0:16], vt_[0:Lk, kt, :], pT_[0:Lk, 0:16],
                                                                                    start=(kt == 0), stop=(kt == 8), skip_group_check=True),
                      r=[vr_, prp], w=[pro])
            rl, R_rl = nextA()
            rlb, R_rlb = nextA()
            P.add("dve", lambda e, po=po, rl=rl: e.reciprocal(out=rl[64:65, 0:16], in_=po[64:65, 0:16]), r=[pro], w=[R_rl])
            pb2, prb2 = banks[5]
            P.add("pe", lambda e, rl=rl: e.matmul(pb2[0:64, 0:16], ones_f[64:65, 0:64], rl[64:65, 0:16], start=True, stop=True), r=[R_const, R_rl], w=[prb2])
            B.cp(rlb[0:64, 0:16], pb2[0:64, 0:16], r=[prb2], w=[R_rlb])
            B.tt(oT[:, h, cs_], po[0:64, 0:16], rlb[0:64, 0:16], ALU.mult, r=[pro, R_rlb], w=R_ar)

    for sq_ in range(2):
        B.dma(Hst[:], ssmin_d[sq_, :, :], w=[R_H])
        ssd_tile(0, want_y=True, L=16, samp=sq_)
        B.dma(ssms_o[sq_, :, :], Hst[:], r=[R_H], w=[], q="pool")
    for sq_ in range(2):
        attn_sample(sq_)
    dense_block(0, ncols=32, groups=[(0, 16, 1), (16, 16, 2)], samp=True)


_CACHE = {}


def _r(w, kc):
    K, N = w.shape
    return np.ascontiguousarray(w.reshape(kc, K // kc, N).transpose(1, 0, 2))


def prep_inputs(inp, stage):
    f = np.float32
    xp = np.asarray(inp["x_prompt"], f)[0]
    maps = []
    shared = {}
    shared["w_ada_r"] = _r(np.asarray(inp["w_ada"], f)[0], 8)
    shared["b_ada_r"] = np.ascontiguousarray(np.asarray(inp["b_ada"], f)[0].reshape(72, 128).T)
    for nm, key in [("g_ffn1_r", "g_ffn1"), ("g_mix_r", "g_mix"), ("g_ffn2_r", "g_ffn2")]:
        shared[nm] = np.ascontiguousarray(np.asarray(inp[key], f)[0].reshape(8, 128).T)
    shared["g_final_r"] = np.ascontiguousarray(np.asarray(inp["g_final"], f).reshape(8, 128).T)
    shared["w1a"] = _r(np.asarray(inp["w1_ffn1"], f)[0], 8)
    shared["w3a"] = _r(np.asarray(inp["w3_ffn1"], f)[0], 8)
    shared["w2a"] = _r(np.asarray(inp["w2_ffn1"], f)[0], NH)
    shared["w1b"] = _r(np.asarray(inp["w1_ffn2"], f)[0], 8)
    shared["w3b"] = _r(np.asarray(inp["w3_ffn2"], f)[0], 8)
    shared["w2b"] = _r(np.asarray(inp["w2_ffn2"], f)[0], NH)
    shared["win_r"] = _r(np.asarray(inp["w_in"], f)[0], 8)
    shared["bf_bc"] = np.ascontiguousarray(np.broadcast_to(np.asarray(inp["b_f"], f)[0][None, :], (128, 16)))
    shared["convw_r"] = np.ascontiguousarray(np.asarray(inp["conv_w"], f)[0].reshape(4, 24, 128).transpose(2, 1, 0))
    shared["convb_r"] = np.ascontiguousarray(np.asarray(inp["conv_b"], f)[0].reshape(24, 128).T)
    shared["dtb_bc"] = np.ascontiguousarray(np.broadcast_to(np.asarray(inp["dt_bias"], f)[0][None, :], (128, 32)))
    shared["ident_in"] = np.eye(128, dtype=f)
    shared["alog_bc"] = np.ascontiguousarray(np.broadcast_to(np.asarray(inp["a_log"], f)[0][None, :], (128, 32)))
    shared["dskip_bc"] = np.ascontiguousarray(np.broadcast_to(np.asarray(inp["d_skip"], f)[0][None, :], (128, 32)))
    shared["gssd_r"] = np.ascontiguousarray(np.asarray(inp["g_ssd"], f)[0].reshape(16, 128).T)
    tri = np.triu(np.ones((128, 128), f))
    shared["tri_in"] = tri
    shared["maskneg_in"] = ((1.0 - tri) * -1e9).astype(f)
    e0 = np.zeros((128, 128), f); e0[0, :] = 1.0
    shared["e0row_in"] = e0
    sel = np.zeros((16, 16, 66), f)
    for h in range(16):
        sel[h, h, 64] = 1.0; sel[h, h, 65] = 1.0
    shared["sel_in"] = sel
    selc = np.zeros((128, 2), f); selc[64, 0] = 1.0; selc[65, 1] = 1.0
    shared["selc_in"] = selc
    shared["wa_r"] = np.ascontiguousarray(np.asarray(inp["w_a"], f)[0].reshape(16, 64, D).transpose(1, 0, 2))
    shared["ws_r"] = _r(np.asarray(inp["w_s"], f)[0], 16)
    shared["wout_r"] = _r(np.asarray(inp["w_out"], f)[0], 8)
    xpT = np.ascontiguousarray(xp.T)
    xsm = np.asarray(inp["x_sample"], f)
    cp = np.asarray(inp["c_prompt"], f)[0]
    cs = np.asarray(inp["c_sample"], f)
    for c in range(NCORES):
        m = dict(shared)
        m["xT"] = np.ascontiguousarray(xp[c * T:(c + 1) * T].T)
        m["xTall"] = xpT
        hal = xp[c * T - 3:c * T] if c > 0 else np.zeros((3, D), f)
        m["xsT"] = np.ascontiguousarray(np.concatenate([xsm[2 * c:2 * c + 2].reshape(32, D), hal], 0).T)
        fl = np.zeros((128, 24), f)
        for j in range(8):
            fl[:, j] = 1.0 if j < c else 0.0
            fl[:, 8 + j] = 0.0 if j < c else NEG
        fl[:, 16] = 1.0 if c > 0 else 0.0
        m["flags"] = fl
        sc = np.asarray(inp["state_conv"], f)[0, 2 * c:2 * c + 2]
        m["scv"] = np.ascontiguousarray(sc.transpose(2, 0, 1).reshape(24, 128, 2, 3).transpose(1, 0, 2, 3))
        ss = np.asarray(inp["state_ssm"], f)[0, 2 * c:2 * c + 2]
        m["ssmin"] = np.ascontiguousarray(ss.transpose(0, 3, 1, 2).reshape(2, 128, DIN))
        ck = np.asarray(inp["cache_k"], f)[0, 2 * c:2 * c + 2]
        m["ckT"] = np.ascontiguousarray(ck.transpose(0, 2, 3, 1))
        m["cvc"] = np.ascontiguousarray(np.asarray(inp["cache_v"], f)[0, 2 * c:2 * c + 2].reshape(2, 1024, D))
        m["clf"] = np.ascontiguousarray(np.asarray(inp["cache_logf"], f)[0, 2 * c:2 * c + 2])
        c3 = np.stack([cp, cs[2 * c], cs[2 * c + 1]], axis=1)
        m["cT"] = np.ascontiguousarray(c3.reshape(8, 128, 3).transpose(1, 0, 2))
        maps.append(m)
    return maps


def run(inp, stage=99, ncores=NCORES):
    if stage not in _CACHE:
        B = build(stage)
        _CACHE[stage] = B
    B = _CACHE[stage]
    maps = prep_inputs(inp, stage)
    maps = [{k: v for k, v in m.items() if k in B.ins} for m in maps]
    res = run_bass_kernel_spmd(B.nc, maps[:ncores], core_ids=list(range(ncores)))
    return res.results


def kernel(**inp):
    res = run(inp, stage=3)
    f = np.float32
    y_prompt = np.zeros((1, SEQ, D), f)
    y_sample = np.zeros((16, 16, D), f)
    k_prompt = np.zeros((1, 1, SEQ, H_A, HD), f)
    v_prompt = np.zeros((1, 1, SEQ, H_A, HD), f)
    logf_prompt = np.zeros((1, 1, SEQ, H_A), f)
    ssm_prompt = np.zeros((1, 1, NSSD, 64, NST), f)
    conv_prompt = np.zeros((1, 1, 3, CONVD), f)
    k_sample = np.zeros((1, 16, 16, H_A, HD), f)
    v_sample = np.zeros((1, 16, 16, H_A, HD), f)
    logf_sample = np.zeros((1, 16, 16, H_A), f)
    ssm_sample = np.zeros((1, 16, NSSD, 64, NST), f)
    conv_sample = np.zeros((1, 16, 3, CONVD), f)
    for c in range(NCORES):
        r = res[c]
        sl = slice(c * T, (c + 1) * T)
        y_prompt[0, sl] = np.asarray(r["yT_o"]).T
        k_prompt[0, 0, sl] = np.asarray(r["kT_o"]).T.reshape(T, H_A, HD)
        v_prompt[0, 0, sl] = np.asarray(r["v_o"]).reshape(T, H_A, HD)
        logf_prompt[0, 0, sl] = np.asarray(r["lf_o"])
        k_sample[0, 2 * c:2 * c + 2] = np.asarray(r["ksT_o"]).T.reshape(2, 16, H_A, HD)
        v_sample[0, 2 * c:2 * c + 2] = np.asarray(r["vs_o"]).reshape(2, 16, H_A, HD)
        logf_sample[0, 2 * c:2 * c + 2] = np.asarray(r["lfs_o"]).reshape(2, 16, H_A)
        y_sample[2 * c:2 * c + 2] = np.asarray(r["ysT_o"]).T.reshape(2, 16, D)
        ssm_sample[0, 2 * c:2 * c + 2] = np.asarray(r["ssms_o"]).reshape(2, 128, NSSD, 64).transpose(0, 2, 3, 1)
        cvs = np.asarray(r["cvs_o"])
        conv_sample[0, 2 * c] = cvs[:, 0:3].T
        conv_sample[0, 2 * c + 1] = cvs[:, 3:6].T
    conv_prompt[0, 0] = np.asarray(res[NCORES - 1]["cv_o"]).T
    ssm_prompt[0, 0] = np.asarray(res[NCORES - 1]["ssm_o"]).transpose(1, 2, 0)
    return (y_prompt, y_sample, k_prompt, v_prompt, logf_prompt, ssm_prompt, conv_prompt,
            k_sample, v_sample, logf_sample, ssm_sample, conv_sample)
```

```python
from contextlib import ExitStack
import numpy as np
import concourse.bass as bass
import concourse.mybir as mybir
from concourse.bass_utils import run_bass_kernel_spmd

F32 = mybir.dt.float32
BF16 = mybir.dt.bfloat16
AF = mybir.ActivationFunctionType
ALU = mybir.AluOpType
AX = mybir.AxisListType

NCORES = 8
D = 1024
SEQ = 16384
T = SEQ // NCORES
TB = 512
DFF = 2816
NH = 22
H_A = 16
HD = 64
DIN = 2048
NSSD = 32
NST = 128
CONVD = 3072
DINP = 10288
C_Q, C_K, C_V, C_F, C_Z, C_X, C_DT, C_GA, C_GS = 0, 1024, 2048, 3072, 3088, 5136, 8208, 8240, 9264
EPS = 1e-6
NEG = -30000.0
import os
CTX = os.environ.get("CTX", "1") == "1"

ENG = ["pe", "act", "dve", "pool", "sp"]


class Res:
    __slots__ = ("lw", "rd", "excl")

    def __init__(self, excl=False):
        self.lw = None
        self.rd = []
        self.excl = excl


class Op:
    __slots__ = ("eng", "fn", "deps", "dma", "need", "sv", "dsem", "dval", "idx", "cc", "seng")


class Prog:
    NS = 16

    def __init__(self):
        self.ops = []
        self.dq = {"sp": [], "pool": [], "act": []}
        self.ncc = 0

    def add(self, eng, fn, r=(), w=(), dma=False, cc=False):
        op = Op()
        op.eng, op.fn, op.dma, op.need, op.idx = eng, fn, dma, False, len(self.ops)
        op.sv = 0
        op.cc = cc
        op.seng = eng
        if cc:
            op.dma = dma = True
        deps = {}
        w = list(w) + [R for R in r if R.excl]
        r = [R for R in r if not R.excl]

        def dep(p, war=False):
            if p is None:
                return
            if (not p.dma) and p.eng == eng and eng == "pe":
                return
            deps[p.idx] = p

        for R in r:
            dep(R.lw)
        for R in w:
            dep(R.lw)
            for q in R.rd:
                dep(q, True)
        if cc:
            op.seng = "cc"
            op.dsem = self.ncc
            op.dval = 1
            self.ncc += 1
        elif dma:
            lst = self.dq[eng]
            n = len(lst)
            op.dsem = n % self.NS
            op.dval = 16 * (n // self.NS + 1)
            if n >= self.NS:
                deps[lst[n - self.NS].idx] = lst[n - self.NS]
            lst.append(op)
        op.deps = list(deps.values())
        for p in op.deps:
            if not p.dma:
                p.need = True
        for R in r:
            R.rd.append(op)
        for R in w:
            R.lw = op
            R.rd = []
        self.ops.append(op)
        return op

    def emit(self, nc, sems, dsems):
        cnt = {e: 0 for e in ENG}
        for op in self.ops:
            if (not op.dma) and op.need:
                cnt[op.eng] += 1
                op.sv = cnt[op.eng]
        per = {e: [o for o in self.ops if o.eng == e] for e in ENG}

        def run(ename, e):
            waited = {}
            for op in per[ename]:
                for p in op.deps:
                    if p.dma:
                        key, val, sem = ("d", p.seng, p.dsem), p.dval, dsems[p.seng][p.dsem]
                    else:
                        key, val, sem = ("c", p.eng), p.sv, sems[p.eng]
                    if waited.get(key, 0) >= val:
                        continue
                    waited[key] = val
                    e.wait_ge(sem, val)
                ins = op.fn(e)
                if op.cc:
                    ins.then_inc(dsems["cc"][op.dsem], 1)
                elif op.dma:
                    ins.then_inc(dsems[ename][op.dsem], 16)
                elif op.need:
                    ins.then_inc(sems[ename], 1)
            if ename in self.dq:
                lst = self.dq[ename]
                last = {}
                for o in lst:
                    last[o.dsem] = o.dval
                for s, v in last.items():
                    e.wait_ge(dsems[ename][s], v)

        with nc.Block() as block:
            @block.tensor
            def _(e):
                run("pe", e)

            @block.scalar
            def _(e):
                run("act", e)

            @block.vector
            def _(e):
                run("dve", e)

            @block.gpsimd
            def _(e):
                run("pool", e)

            @block.sync
            def _(e):
                run("sp", e)


class StopBuild(Exception):
    pass


class Builder:
    def __init__(self, stage=99):
        import os
        self.cutn = float(os.environ.get("DBG_CUT", "999"))
        self.stage = stage
        self.nc = bass.Bass("TRN2", target_bir_lowering=False)
        try:
            self.nc.allow_low_precision("bf16 matmul operands by design")
        except Exception:
            pass
        self.P = Prog()
        self.es = ExitStack()
        self.ins = {}
        self.outs = {}
        self.uid = 0
        self.wrr = 0

    def finish(self):
        nc = self.nc
        sems = {e: self.es.enter_context(nc.semaphore(f"s_{e}")) for e in ENG}
        dsems = {q: [self.es.enter_context(nc.semaphore(f"d_{q}{i}")) for i in range(Prog.NS)]
                 for q in ("sp", "pool", "act")}
        dsems["cc"] = [self.es.enter_context(nc.semaphore(f"d_cc{i}")) for i in range(max(1, self.P.ncc))]
        self.P.emit(nc, sems, dsems)
        self.es.close()

    def cut(self, n):
        if self.cutn <= n:
            raise StopBuild()

    def din(self, name, shape, dt=F32):
        t = self.nc.dram_tensor(name, list(shape), dt, kind="ExternalInput").ap()
        self.ins[name] = t
        return t

    def dout(self, name, shape, dt=F32):
        t = self.nc.dram_tensor(name, list(shape), dt, kind="ExternalOutput").ap()
        self.outs[name] = t
        return t

    def dscr(self, name, shape, dt):
        return self.nc.dram_tensor(name, list(shape), dt).ap()

    def sb(self, name, shape, dt=F32):
        return self.es.enter_context(self.nc.sbuf_tensor("sb_" + name, list(shape), dt))

    def ps(self, name, shape, dt=F32):
        return self.es.enter_context(self.nc.psum_tensor("ps_" + name, list(shape), dt))

    def dma(self, out, in_, r=(), w=(), q="sp"):
        return self.P.add(q, lambda e: e.dma_start(out=out, in_=in_), r=r, w=w, dma=True)

    def act(self, out, in_, func, r=(), w=(), bias=None, scale=None):
        kw = {}
        if bias is not None:
            kw["bias"] = bias
        if scale is not None:
            kw["scale"] = scale
        return self.P.add("act", lambda e: e.activation(out=out, in_=in_, func=func, **kw), r=r, w=w)

    def tt(self, out, a, b, op, r=(), w=(), eng="dve"):
        return self.P.add(eng, lambda e: e.tensor_tensor(out=out, in0=a, in1=b, op=op), r=r, w=w)

    def ts(self, out, a, s1, s2, op0, op1=None, r=(), w=(), eng="dve"):
        if op1 is None:
            return self.P.add(eng, lambda e: e.tensor_scalar(out=out, in0=a, scalar1=s1, scalar2=None, op0=op0), r=r, w=w)
        return self.P.add(eng, lambda e: e.tensor_scalar(out=out, in0=a, scalar1=s1, scalar2=s2, op0=op0, op1=op1), r=r, w=w)

    def stt(self, out, a, s, b, op0, op1, r=(), w=(), eng="dve"):
        return self.P.add(eng, lambda e: e.scalar_tensor_tensor(out=out, in0=a, scalar=s, in1=b, op0=op0, op1=op1), r=r, w=w)

    def cp(self, out, in_, r=(), w=(), eng="dve"):
        if eng == "act":
            return self.P.add("act", lambda e: e.copy(out=out, in_=in_), r=r, w=w)
        return self.P.add(eng, lambda e: e.tensor_copy(out=out, in_=in_), r=r, w=w)

    def mms(self, out, pairs, r=(), w=()):
        n = len(pairs)

        def fn(e):
            ins = None
            for i, (l, rh) in enumerate(pairs):
                ins = e.matmul(out, l, rh, start=(i == 0), stop=(i == n - 1))
            return ins
        return self.P.add("pe", fn, r=r, w=w)

    def init_wstream(self):
        self.WSZ = 2048
        self.wst = [(self.sb(f"wst{i}", [128, self.WSZ], F32), Res()) for i in range(2)]
        self.wbf = [(self.sb(f"wbf{i}", [128, self.WSZ], BF16), Res()) for i in range(2)]
        self.wi = 0
        self.wj = 0

    def load_w(self, wd, kcn, c0, n, pn=128, k0=0):
        st, sr = self.wst[self.wi % 2]
        self.wi += 1
        bf, br = self.wbf[self.wj % 2]
        self.wj += 1
        sz = kcn * n
        assert sz <= self.WSZ
        stv = st[0:pn, 0:sz].rearrange("p (k n) -> p k n", k=kcn)
        bfv = bf[0:pn, 0:sz].rearrange("p (k n) -> p k n", k=kcn)
        self.dma(stv, wd[0:pn, k0:k0 + kcn, c0:c0 + n], w=[sr])
        ceng = "dve" if (self.wj % 3 == 0) else "act"
        self.cp(bf[0:pn, 0:sz], st[0:pn, 0:sz], r=[sr], w=[br], eng=ceng)
        return bfv, br


def wpairs(B, wd, kcn, base, n, count, pn=128):
    cache = {}

    def get(i):
        g = i // 2
        if g not in cache:
            cache.clear()
            ncol = n * min(2, count - 2 * g)
            cache[g] = B.load_w(wd, kcn, base + 2 * g * n, ncol, pn=pn)
        wt, wr = cache[g]
        o = (i % 2) * n
        return wt[:, :, o:o + n], wr
    return get


def build(stage=99):
    B = Builder(stage)
    nc, P = B.nc, B.P
    NBLK = T // TB
    NT = T // 128

    xT_d = B.din("xT", [D, T])
    cT_d = B.din("cT", [128, 8, 3])
    wada_d = B.din("w_ada_r", [128, 8, 9 * D])
    bada_d = B.din("b_ada_r", [128, 72])
    g1_d = B.din("g_ffn1_r", [128, 8])
    gm_d = B.din("g_mix_r", [128, 8])
    g2_d = B.din("g_ffn2_r", [128, 8])
    gf_d = B.din("g_final_r", [128, 8])
    w1a_d = B.din("w1a", [128, 8, DFF])
    w3a_d = B.din("w3a", [128, 8, DFF])
    w2a_d = B.din("w2a", [128, NH, D])
    w1b_d = B.din("w1b", [128, 8, DFF])
    w3b_d = B.din("w3b", [128, 8, DFF])
    w2b_d = B.din("w2b", [128, NH, D])
    win_d = B.din("win_r", [128, 8, DINP])
    bf_d = B.din("bf_bc", [128, 16])
    cw_d = B.din("convw_r", [128, 24, 4])
    cb_d = B.din("convb_r", [128, 24])
    dtb_d = B.din("dtb_bc", [128, 32])

    kT_o = B.dout("kT_o", [D, T])
    v_o = B.dout("v_o", [T, D])
    lf_o = B.dout("lf_o", [T, 16])
    cv_o = B.dout("cv_o", [CONVD, 3])
    xsT_d = B.din("xsT", [D, 35])
    flg_d = B.din("flags", [128, 24])
    ksT_o = B.dout("ksT_o", [D, 32])
    vs_o = B.dout("vs_o", [32, D])
    lfs_o = B.dout("lfs_o", [32, 16])
    cvs_o = B.dout("cvs_o", [CONVD, 6])
    x1_o = B.dout("x1_o", [D, T])
    scv_d = B.din("scv", [128, 24, 2, 3])
    ssmin_d = B.din("ssmin", [2, 128, DIN])
    ckT_d = B.din("ckT", [2, H_A, 64, 1024])
    cv_d = B.din("cvc", [2, 1024, D])
    clf_d = B.din("clf", [2, 1024, 16])
    ysT_o = B.dout("ysT_o", [D, 32])
    ssms_o = B.dout("ssms_o", [2, 128, DIN])
    q2_s = B.dscr("q2_s", [H_A, 66, 32], BF16)
    k2_s = B.dscr("k2_s", [H_A, 66, 32], BF16)
    v2_s = B.dscr("v2_s", [H_A, 2, 16, 65], BF16)
    z2_s = B.dscr("z2_s", [32, DIN], BF16)
    xs2_s = B.dscr("xs2_s", [32, DIN], BF16)
    Bt2_s = B.dscr("Bt2_s", [32, 512], BF16)
    BT2_s = B.dscr("BT2_s", [512, 32], BF16)
    CT2_s = B.dscr("CT2_s", [512, 32], BF16)
    ga2_s = B.dscr("ga2_s", [D, 32], BF16)
    gs2_s = B.dscr("gs2_s", [D, 32], BF16)
    R_s2 = Res()
    xTall_d = B.din("xTall", [D, SEQ])
    kgA = B.dscr("kgA", [NCORES * H_A * 66, T], BF16)
    vgA = B.dscr("vgA", [NCORES * H_A * NT * 128, 65], BF16)
    kgA_v = kgA.rearrange("(j h r) t -> j h r t", j=NCORES, h=H_A)
    vgA_v = vgA.rearrange("(j h t p) d -> j h t p d", j=NCORES, h=H_A, t=NT)
    R_kgA, R_vgA = Res(), Res()
    alog_d = B.din("alog_bc", [128, 32])
    dsk_d = B.din("dskip_bc", [128, 32])
    gssd_d = B.din("gssd_r", [128, 16])
    tri_d = B.din("tri_in", [128, 128])
    mneg_d = B.din("maskneg_in", [128, 128])
    e0_d = B.din("e0row_in", [128, 128])
    sel_d = B.din("sel_in", [16, 16, 66])
    selc_d = B.din("selc_in", [128, 2])
    wa_d = B.din("wa_r", [64, 16, D])
    ws_d = B.din("ws_r", [128, 16, D])
    wo_d = B.din("wout_r", [128, 8, D])
    yT_o = B.dout("yT_o", [D, T])
    ssm_o = B.dout("ssm_o", [128, NSSD, 64])

    q_s = B.dscr("q_s", [H_A, 66, T], BF16)
    kg_s = B.dscr("kg_s", [H_A, 66, T], BF16)
    vg_s = B.dscr("vg_s", [H_A, NT, 128, 65], BF16)
    z_s = B.dscr("z_s", [T, DIN], BF16)
    xs_s = B.dscr("xs_s", [T, DIN], BF16)
    Bt_s = B.dscr("Bt_s", [T, 512], BF16)
    BT_s = B.dscr("BT_s", [512, T], BF16)
    CT_s = B.dscr("CT_s", [512, T], BF16)
    ga_s = B.dscr("ga_s", [D, T], BF16)
    gs_s = B.dscr("gs_s", [D, T], BF16)
    R_q, R_kg, R_vg, R_z, R_xs, R_Bt, R_BT, R_CT, R_ga, R_gs, R_x1 = [Res() for _ in range(11)]

    ones_f = B.sb("ones_f", [128, 128], F32)
    ident_b = B.sb("ident_b", [128, 128], BF16)
    ident_f = B.sb("ident_f", [128, 128], F32)
    epsc = B.sb("epsc", [128, 1], F32)
    onec = B.sb("onec", [128, 1], F32)
    R_const = Res()
    P.add("pool", lambda e: e.memset(ones_f[:], 1.0), w=[R_const])
    P.add("pool", lambda e: e.memset(epsc[:], EPS), w=[R_const])
    P.add("pool", lambda e: e.memset(onec[:], 1.0), w=[R_const])
    identf_d = B.din("ident_in", [128, 128])
    B.dma(ident_f[:], identf_d[:, :], w=[R_const])
    B.cp(ident_b[:], ident_f[:], r=[R_const], w=[R_const], eng="pool")

    B.init_wstream()

    banks = [(B.ps(f"bank{i}", [128, 512], F32), Res(True)) for i in range(7)]
    ptT = B.ps("ptT", [128, 1024], BF16)
    prT = Res(True)

    try:
        _build_body(B, locals())
    except StopBuild:
        pass
    B.finish()
    return B


def _build_body(B, L):
    globals_ = L
    nc, P = B.nc, B.P
    NBLK = T // TB
    NT = T // 128
    for k_, v_ in L.items():
        if k_ not in ("B", "nc", "P"):
            globals()[k_] = v_
    cT = B.sb("cT", [128, 8, 3], F32)
    cs = B.sb("cs", [128, 8, 3], BF16)
    bada = B.sb("bada", [128, 72], F32)
    modT = B.sb("modT", [128, 72, 3], F32)
    gsb = B.sb("gsb", [128, 4, 8], F32)
    R_c, R_mod, R_g = Res(), Res(), Res()
    B.dma(cT[:], cT_d[:, :, :], w=[R_c])
    B.dma(bada[:], bada_d[:, :], w=[R_c])
    for i, gd in enumerate([g1_d, gm_d, g2_d, gf_d]):
        B.dma(gsb[:, i, :], gd[:, :], w=[R_g])
    B.act(cs[:], cT[:], AF.Silu, r=[R_c], w=[R_c])
    for ch in range(72):
        wt, wr = B.load_w(wada_d, 8, ch * 128, 128)
        pt, pr = banks[ch % 2]
        B.mms(pt[:, 0:3], [(wt[:, kc, :], cs[:, kc, :]) for kc in range(8)], r=[wr, R_c], w=[pr])
        B.ts(modT[:, ch, :], pt[:, 0:3], bada[:, ch:ch + 1], None, ALU.add, r=[pr, R_c], w=[R_mod])
    Am = B.sb("Am", [128, 3, 8, 3], F32)
    Gm = B.sb("Gm", [128, 3, 8, 3], F32)
    R_AG = Res()
    for s in range(3):
        coef = 1.0 if s == 1 else 0.5
        for kc in range(8):
            B.ts(Am[:, s, kc, :], modT[:, (3 * s + 1) * 8 + kc, :], 1.0, gsb[:, s, kc:kc + 1], ALU.add, ALU.mult,
                 r=[R_mod, R_g], w=[R_AG])
            B.ts(Gm[:, s, kc, :], modT[:, (3 * s + 2) * 8 + kc, :], 1.0, coef, ALU.add, ALU.mult,
                 r=[R_mod], w=[R_AG])

    B.cut(1)

    def shiftp(s, kc, m):
        return modT[:, (3 * s) * 8 + kc, m:m + 1]

    xT = B.sb("xT", [128, 8, TB], F32)
    uT = B.sb("uT", [128, 8, TB], BF16)
    gT = B.sb("gT", [128, NH, TB], BF16)
    sq = B.sb("sq", [128, TB], F32)
    rstd = B.sb("rstd", [128, TB], F32)
    tmpA = [(B.sb(f"tmpA{i}", [128, TB], F32), Res()) for i in range(5)]
    tmpB = [(B.sb(f"tmpB{i}", [128, TB], BF16), Res()) for i in range(5)]
    R_x, R_u, R_gT, R_sq, R_rstd = Res(), Res(), Res(), Res(), Res()
    tai = [0]
    tbi = [0]

    def nextA():
        tai[0] += 1
        return tmpA[tai[0] % 5]

    def nextB():
        tbi[0] += 1
        return tmpB[tbi[0] % 5]

    def rms_mod(src, s, groups, ncols):
        pt, pr = banks[6]
        for kc in range(8):
            ta, tr = nextA()
            B.tt(ta[:, 0:ncols], src[:, kc, 0:ncols], src[:, kc, 0:ncols], ALU.mult, r=[R_x], w=[tr])
            P.add("pe", lambda e, ta=ta, kc=kc: e.matmul(pt[:, 0:ncols], ones_f[:], ta[:, 0:ncols],
                                                            start=(kc == 0), stop=(kc == 7)),
                  r=[tr, R_const], w=[pr])
        B.act(sq[:, 0:ncols], pt[:, 0:ncols], AF.Sqrt, r=[pr, R_const], w=[R_sq], bias=epsc[:], scale=1.0 / D)
        P.add("dve", lambda e: e.reciprocal(out=rstd[:, 0:ncols], in_=sq[:, 0:ncols]), r=[R_sq], w=[R_rstd])
        for kc in range(8):
            ta, tr = nextA()
            B.tt(ta[:, 0:ncols], src[:, kc, 0:ncols], rstd[:, 0:ncols], ALU.mult, r=[R_x, R_rstd], w=[tr])
            for (c0, n, m) in groups:
                B.ts(uT[:, kc, c0:c0 + n], ta[:, c0:c0 + n], Am[:, s, kc, m:m + 1], shiftp(s, kc, m),
                     ALU.mult, ALU.add, r=[tr, R_AG, R_mod], w=[R_u])

    def ffn(s, w1d, w3d, w2d, groups, ncols):
        g1 = wpairs(B, w1d, 8, 0, 128, NH)
        g3 = wpairs(B, w3d, 8, 0, 128, NH)
        for hc in range(NH):
            w1t, w1r = g1(hc)
            w3t, w3r = g3(hc)
            p1, r1 = banks[hc % 2]
            p3, r3 = banks[2 + hc % 2]
            B.mms(p1[:, 0:ncols], [(w1t[:, kc, :], uT[:, kc, 0:ncols]) for kc in range(8)], r=[w1r, R_u], w=[r1])
            B.mms(p3[:, 0:ncols], [(w3t[:, kc, :], uT[:, kc, 0:ncols]) for kc in range(8)], r=[w3r, R_u], w=[r3])
            ta, tr = nextA()
            B.act(ta[:, 0:ncols], p1[:, 0:ncols], AF.Silu, r=[r1], w=[tr])
            B.tt(gT[:, hc, 0:ncols], ta[:, 0:ncols], p3[:, 0:ncols], ALU.mult, r=[tr, r3], w=[R_gT])
        for oc in range(8):
            w2t, w2r = B.load_w(w2d, 11, oc * 128, 128)
            w2u, w2s = B.load_w(w2d, 11, oc * 128, 128, k0=11)
            po, ro = banks[4 + oc % 2]
            B.mms(po[:, 0:ncols], [(w2t[:, hc, :], gT[:, hc, 0:ncols]) for hc in range(11)]
                  + [(w2u[:, hc, :], gT[:, 11 + hc, 0:ncols]) for hc in range(11)], r=[w2r, w2s, R_gT], w=[ro])
            for (c0, n, m) in groups:
                B.stt(xT[:, oc, c0:c0 + n], po[:, c0:c0 + n], Gm[:, s, oc, m:m + 1], xT[:, oc, c0:c0 + n],
                      ALU.mult, ALU.add, r=[ro, R_AG], w=[R_x])

    logf = B.sb("logf", [128, NT + 2, 16], F32)
    dtsb = B.sb("dtsb", [128, NT + 2, 32], F32)
    bfb = B.sb("bfb", [128, 16], F32)
    dtb = B.sb("dtb", [128, 32], F32)
    cw = B.sb("cw", [128, 24, 4], F32)
    cbv = B.sb("cbv", [128, 24], F32)
    halo = B.sb("halo", [128, 24, 3], F32)
    R_lf, R_dt, R_sm, R_halo = Res(), Res(), Res(), Res()
    B.dma(bfb[:], bf_d[:, :], w=[R_sm])
    B.dma(dtb[:], dtb_d[:, :], w=[R_sm])
    B.dma(cw[:], cw_d[:, :, :], w=[R_sm])
    B.dma(cbv[:], cb_d[:, :], w=[R_sm])
    P.add("pool", lambda e: e.memset(halo[:], 0.0), w=[R_halo])
    flg = B.sb("flg", [128, 24], F32)
    B.dma(flg[:], flg_d[:, :], w=[R_sm])

    xb = [(B.sb(f"xb{i}", [128, TB + 3], F32), Res()) for i in range(2)]
    vst = [(B.sb(f"vst{i}", [128, 4, 65], BF16), Res()) for i in range(2)]
    kst = [(B.sb("kst0", [64, TB], F32), Res())] * 2
    onesrow = B.sb("onesrow", [66, TB], BF16)
    P.add("pool", lambda e: e.memset(onesrow[:], 1.0), w=[R_const])
    for i in range(2):
        P.add("pool", lambda e, i=i: e.memset(vst[i][0][:], 1.0), w=[vst[i][1]])
    for h in range(H_A):
        for bb in range(T // TB):
            B.dma(kg_s[h, 64:66, bb * TB:(bb + 1) * TB], onesrow[64:66, :], r=[R_const], w=[R_kg], q="pool")

    def softplus_to(out, in_ps, biasbc, n, r, w, neg_in=False, pn=128):
        ta, tr = nextA()
        tb2, tr2 = nextA()
        B.tt(ta[0:pn, 0:n], in_ps, biasbc, ALU.add, r=r, w=[tr])
        if neg_in:
            B.ts(ta[0:pn, 0:n], ta[0:pn, 0:n], -1.0, None, ALU.mult, r=[tr], w=[tr])
        B.stt(tb2[0:pn, 0:n], ta[0:pn, 0:n], -1.0, ta[0:pn, 0:n], ALU.mult, ALU.max, r=[tr], w=[tr2])
        B.act(tb2[0:pn, 0:n], tb2[0:pn, 0:n], AF.Exp, r=[tr2], w=[tr2], scale=-1.0)
        B.act(tb2[0:pn, 0:n], tb2[0:pn, 0:n], AF.Ln, r=[tr2, R_const], w=[tr2], bias=onec[0:pn, :], scale=1.0)
        B.stt(out, ta[0:pn, 0:n], 0.0, tb2[0:pn, 0:n], ALU.max, ALU.add, r=[tr, tr2], w=w)

    def win_block(blk, groups, ncols, ctxj=None):
        t0 = blk * TB
        ntt = ncols // 128
        B.cut(5)
        gq = wpairs(B, win_d, 8, C_Q, 64, H_A)
        gk = wpairs(B, win_d, 8, C_K, 64, H_A)
        for h in range(H_A):
            if ctxj is None:
                wt, wr = gq(h)
                pt, pr = banks[h % 2]
                B.mms(pt[0:64, 0:ncols], [(wt[:, kc, :], uT[:, kc, 0:ncols]) for kc in range(8)], r=[wr, R_u], w=[pr])
                tb_, tbr = nextB()
                B.ts(tb_[0:64, 0:ncols], pt[0:64, 0:ncols], 0.125, None, ALU.mult, r=[pr], w=[tbr])
                B.dma(q_s[h, 0:64, t0:t0 + ncols], tb_[0:64, 0:ncols], r=[tbr], w=[R_q], q="pool")
            wt, wr = gk(h)
            pt, pr = banks[2 + h % 2]
            B.mms(pt[0:64, 0:ncols], [(wt[:, kc, :], uT[:, kc, 0:ncols]) for kc in range(8)], r=[wr, R_u], w=[pr])
            if ctxj is None:
                kf, kr = kst[h % 2]
                B.cp(kf[:, 0:ncols], pt[0:64, 0:ncols], r=[pr], w=[kr])
                B.dma(kT_o[h * 64:(h + 1) * 64, t0:t0 + ncols], kf[:, 0:ncols], r=[kr], w=[], q="pool")
            tb_, tbr = nextB()
            B.cp(tb_[0:64, 0:ncols], pt[0:64, 0:ncols], r=[pr], w=[tbr], eng="act")
            if ctxj is None:
                B.dma(kg_s[h, 0:64, t0:t0 + ncols], tb_[0:64, 0:ncols], r=[tbr], w=[R_kg], q="pool")
            else:
                B.dma(kgA_v[ctxj, h, 0:64, t0:t0 + ncols], tb_[0:64, 0:ncols], r=[tbr], w=[R_kgA], q="pool")
        B.cut(6)
        for cg in range(4):
            wt, wr = B.load_w(win_d, 8, C_V + cg * 256, 256)
            for tt_ in range(ntt):
                pt, pr = banks[4 + tt_ % 2]
                B.mms(pt[:, 0:256], [(uT[:, kc, tt_ * 128:(tt_ + 1) * 128], wt[:, kc, :]) for kc in range(8)],
                      r=[wr, R_u], w=[pr])
                if ctxj is None:
                    ta, tr = nextA()
                    B.cp(ta[:, 0:256], pt[:, 0:256], r=[pr], w=[tr])
                    B.dma(v_o[t0 + tt_ * 128:t0 + (tt_ + 1) * 128, cg * 256:(cg + 1) * 256], ta[:, 0:256], r=[tr], w=[], q="pool")
                vs, vr = vst[(cg * ntt + tt_) % 2]
                B.cp(vs[:, 0:4, 0:64], pt[:, 0:256].rearrange("p (h d) -> p h d", h=4), r=[pr], w=[vr], eng="act")
                gt = (t0 // 128) + tt_
                if ctxj is None:
                    B.dma(vg_s[cg * 4:(cg + 1) * 4, gt, :, :].rearrange("h p d -> p h d"), vs[:, 0:4, :], r=[vr], w=[R_vg], q="pool")
                else:
                    B.dma(vgA_v[ctxj, cg * 4:(cg + 1) * 4, gt, :, :].rearrange("h p d -> p h d"), vs[:, 0:4, :], r=[vr], w=[R_vgA], q="pool")
        B.cut(7)
        wt, wr = B.load_w(win_d, 8, C_F, 16)
        for tt_ in range(ntt):
            gt = (t0 // 128) + tt_
            pt, pr = banks[6]
            B.mms(pt[:, 0:16], [(uT[:, kc, tt_ * 128:(tt_ + 1) * 128], wt[:, kc, :]) for kc in range(8)], r=[wr, R_u], w=[pr])
            ta, tr = nextA()
            softplus_to(ta[:, 0:16], pt[:, 0:16], bfb[:], 16, r=[pr, R_sm], w=[tr], neg_in=True)
            B.ts(logf[:, gt, :], ta[:, 0:16], -1.0, None, ALU.mult, r=[tr], w=[R_lf])
            if ctxj is None:
                B.dma(lf_o[gt * 128:(gt + 1) * 128, :], logf[:, gt, :], r=[R_lf], w=[], q="pool")
        if B.stage < 2:
            return
        wt, wr = B.load_w(win_d, 8, C_DT, 32)
        for tt_ in range(ntt):
            gt = (t0 // 128) + tt_
            pt, pr = banks[6]
            B.mms(pt[:, 0:32], [(uT[:, kc, tt_ * 128:(tt_ + 1) * 128], wt[:, kc, :]) for kc in range(8)], r=[wr, R_u], w=[pr])
            softplus_to(dtsb[:, gt, :], pt[:, 0:32], dtb[:], 32, r=[pr, R_sm], w=[R_dt])
        for cg in range(8 if ctxj is None else 0):
            wt, wr = B.load_w(win_d, 8, C_Z + cg * 256, 256)
            for tt_ in range(ntt):
                pt, pr = banks[4 + tt_ % 2]
                B.mms(pt[:, 0:256], [(uT[:, kc, tt_ * 128:(tt_ + 1) * 128], wt[:, kc, :]) for kc in range(8)],
                      r=[wr, R_u], w=[pr])
                tb_, tbr = nextB()
                B.cp(tb_[:, 0:256], pt[:, 0:256], r=[pr], w=[tbr], eng="act")
                B.dma(z_s[t0 + tt_ * 128:t0 + (tt_ + 1) * 128, cg * 256:(cg + 1) * 256], tb_[:, 0:256], r=[tbr], w=[R_z], q="pool")
        nxc = 24 if ctxj is None else 20
        gx = wpairs(B, win_d, 8, C_X, 128, nxc)
        for c in range(nxc):
            wt, wr = gx(c)
            pt, pr = banks[c % 2]
            B.mms(pt[:, 0:ncols], [(wt[:, kc, :], uT[:, kc, 0:ncols]) for kc in range(8)], r=[wr, R_u], w=[pr])
            xbt, xr = xb[c % 2]
            B.cp(xbt[:, 0:3], halo[:, c, :], r=[R_halo], w=[xr])
            B.cp(xbt[:, 3:3 + ncols], pt[:, 0:ncols], r=[pr], w=[xr])
            B.cp(halo[:, c, :], xbt[:, ncols:ncols + 3], r=[xr], w=[R_halo])
            if blk == NBLK - 1 and ctxj is None:
                B.dma(cv_o[c * 128:(c + 1) * 128, :], xbt[:, ncols:ncols + 3], r=[xr], w=[], q="pool")
            ta, tr = nextA()
            B.ts(ta[:, 0:ncols], xbt[:, 0:ncols], cw[:, c, 0:1], cbv[:, c:c + 1], ALU.mult, ALU.add, r=[xr, R_sm], w=[tr])
            for i in range(1, 4):
                B.stt(ta[:, 0:ncols], xbt[:, i:i + ncols], cw[:, c, i:i + 1], ta[:, 0:ncols], ALU.mult, ALU.add,
                      r=[xr, R_sm, tr], w=[tr])
            tb_, tbr = nextB()
            B.act(tb_[:, 0:ncols], ta[:, 0:ncols], AF.Silu, r=[tr], w=[tbr])
            if c >= 20:
                B.dma(CT_s[(c - 20) * 128:(c - 19) * 128, t0:t0 + ncols], tb_[:, 0:ncols], r=[tbr], w=[R_CT], q="pool")
                continue
            if c >= 16 and ctxj is None:
                B.dma(BT_s[(c - 16) * 128:(c - 15) * 128, t0:t0 + ncols], tb_[:, 0:ncols], r=[tbr], w=[R_BT], q="pool")
            for tt_ in range(ntt):
                P.add("pe", lambda e, tt_=tt_, tb_=tb_: e.transpose(ptT[:, tt_ * 128:(tt_ + 1) * 128],
                                                                     tb_[:, tt_ * 128:(tt_ + 1) * 128], ident_b[:]),
                      r=[tbr, R_const], w=[prT])
            tb2, tbr2 = nextB()
            B.cp(tb2[:, 0:ncols], ptT[:, 0:ncols], r=[prT], w=[tbr2])
            for tt_ in range(ntt):
                rows = slice(t0 + tt_ * 128, t0 + (tt_ + 1) * 128)
                if c < 16:
                    B.dma(xs_s[rows, c * 128:(c + 1) * 128], tb2[:, tt_ * 128:(tt_ + 1) * 128], r=[tbr2], w=[R_xs], q="pool")
                else:
                    B.dma(Bt_s[rows, (c - 16) * 128:(c - 15) * 128], tb2[:, tt_ * 128:(tt_ + 1) * 128], r=[tbr2], w=[R_Bt], q="pool")
        for gi, (c0, dst, rr) in enumerate([(C_GA, ga_s, R_ga), (C_GS, gs_s, R_gs)] if ctxj is None else []):
            gg = wpairs(B, win_d, 8, c0, 128, 8)
            for c in range(8):
                wt, wr = gg(c)
                pt, pr = banks[2 + c % 2]
                B.mms(pt[:, 0:ncols], [(wt[:, kc, :], uT[:, kc, 0:ncols]) for kc in range(8)], r=[wr, R_u], w=[pr])
                tb_, tbr = nextB()
                B.act(tb_[:, 0:ncols], pt[:, 0:ncols], AF.Sigmoid, r=[pr], w=[tbr])
                B.dma(dst[c * 128:(c + 1) * 128, t0:t0 + ncols], tb_[:, 0:ncols], r=[tbr], w=[rr], q="pool")


    xTd_v = xT_d.rearrange("(k p) t -> p k t", p=128)
    x1o_v = x1_o.rearrange("(k p) t -> p k t", p=128)

    xs1 = B.sb("xs1", [128, 8, 32], F32)
    sconv = B.sb("sconv", [128, 24, 2, 3], F32)
    R_xs1 = Res()

    def run_own_p1():
        NS_ = 32
        NX_ = 35
        sgroups = [(0, 16, 1), (16, 16, 2), (32, 3, 0)]
        B.dma(xT[:, :, 0:NX_], xsT_d.rearrange("(k p) t -> p k t", p=128), w=[R_x])
        rms_mod(xT, 0, sgroups, NX_)
        ffn(0, w1a_d, w3a_d, w2a_d, sgroups, NX_)
        rms_mod(xT, 1, sgroups, NX_)
        B.dma(sconv[:], scv_d[:, :, :, :], w=[R_sm])
        B.cp(xs1[:], xT[:, :, 0:NS_], r=[R_x], w=[R_xs1])
        for h in range(H_A):
            wt, wr = B.load_w(win_d, 8, C_Q + h * 64, 64)
            pt, pr = banks[h % 2]
            B.mms(pt[0:64, 0:NS_], [(wt[:, kc, :], uT[:, kc, 0:NS_]) for kc in range(8)], r=[wr, R_u], w=[pr])
            tb_, tbr = nextB()
            B.ts(tb_[0:64, 0:NS_], pt[0:64, 0:NS_], 0.125, None, ALU.mult, r=[pr], w=[tbr])
            B.dma(q2_s[h, 0:64, :], tb_[0:64, 0:NS_], r=[tbr], w=[R_s2], q="pool")
            wt, wr = B.load_w(win_d, 8, C_K + h * 64, 64)
            pt, pr = banks[2 + h % 2]
            B.mms(pt[0:64, 0:NS_], [(wt[:, kc, :], uT[:, kc, 0:NS_]) for kc in range(8)], r=[wr, R_u], w=[pr])
            kf, kr = kst[h % 2]
            B.cp(kf[:, 0:NS_], pt[0:64, 0:NS_], r=[pr], w=[kr])
            B.dma(ksT_o[h * 64:(h + 1) * 64, :], kf[:, 0:NS_], r=[kr], w=[], q="pool")
            tb_, tbr = nextB()
            B.cp(tb_[0:64, 0:NS_], pt[0:64, 0:NS_], r=[pr], w=[tbr], eng="act")
            B.dma(k2_s[h, 0:64, :], tb_[0:64, 0:NS_], r=[tbr], w=[R_s2], q="pool")
            B.dma(k2_s[h, 64:66, :], onesrow[64:66, 0:NS_], r=[R_const], w=[R_s2], q="pool")
        for cg in range(4):
            wt, wr = B.load_w(win_d, 8, C_V + cg * 256, 256)
            for sq_ in range(2):
                pt, pr = banks[4 + sq_]
                B.mms(pt[0:16, 0:256], [(uT[:, kc, sq_ * 16:(sq_ + 1) * 16], wt[:, kc, :]) for kc in range(8)], r=[wr, R_u], w=[pr])
                ta, tr = nextA()
                B.cp(ta[0:16, 0:256], pt[0:16, 0:256], r=[pr], w=[tr])
                B.dma(vs_o[sq_ * 16:(sq_ + 1) * 16, cg * 256:(cg + 1) * 256], ta[0:16, 0:256], r=[tr], w=[], q="pool")
                vs, vr = vst[sq_]
                B.cp(vs[0:16, 0:4, 0:64], pt[0:16, 0:256].rearrange("p (h d) -> p h d", h=4), r=[pr], w=[vr], eng="act")
                B.dma(v2_s[cg * 4:(cg + 1) * 4, sq_, :, :].rearrange("h p d -> p h d"), vs[0:16, 0:4, :], r=[vr], w=[R_s2], q="pool")
        wt, wr = B.load_w(win_d, 8, C_F, 16)
        for sq_ in range(2):
            pt, pr = banks[6]
            B.mms(pt[0:16, 0:16], [(uT[:, kc, sq_ * 16:(sq_ + 1) * 16], wt[:, kc, :]) for kc in range(8)], r=[wr, R_u], w=[pr])
            ta, tr = nextA()
            softplus_to(ta[0:16, 0:16], pt[0:16, 0:16], bfb[0:16, :], 16, r=[pr, R_sm], w=[tr], neg_in=True, pn=16)
            B.ts(logf[0:16, NT + sq_, :], ta[0:16, 0:16], -1.0, None, ALU.mult, r=[tr], w=[R_lf])
            B.dma(lfs_o[sq_ * 16:(sq_ + 1) * 16, :], logf[0:16, NT + sq_, :], r=[R_lf], w=[], q="pool")
        if B.stage >= 2:
            wt, wr = B.load_w(win_d, 8, C_DT, 32)
            for sq_ in range(2):
                pt, pr = banks[6]
                B.mms(pt[0:16, 0:32], [(uT[:, kc, sq_ * 16:(sq_ + 1) * 16], wt[:, kc, :]) for kc in range(8)], r=[wr, R_u], w=[pr])
                softplus_to(dtsb[0:16, NT + sq_, :], pt[0:16, 0:32], dtb[0:16, :], 32, r=[pr, R_sm], w=[R_dt], pn=16)
            for cg in range(8):
                wt, wr = B.load_w(win_d, 8, C_Z + cg * 256, 256)
                for sq_ in range(2):
                    pt, pr = banks[4 + sq_]
                    B.mms(pt[0:16, 0:256], [(uT[:, kc, sq_ * 16:(sq_ + 1) * 16], wt[:, kc, :]) for kc in range(8)], r=[wr, R_u], w=[pr])
                    tb_, tbr = nextB()
                    B.cp(tb_[0:16, 0:256], pt[0:16, 0:256], r=[pr], w=[tbr], eng="act")
                    B.dma(z2_s[sq_ * 16:(sq_ + 1) * 16, cg * 256:(cg + 1) * 256], tb_[0:16, 0:256], r=[tbr], w=[R_s2], q="pool")
            for c in range(24):
                wt, wr = B.load_w(win_d, 8, C_X + c * 128, 128)
                pt, pr = banks[c % 2]
                B.mms(pt[:, 0:NX_], [(wt[:, kc, :], uT[:, kc, 0:NX_]) for kc in range(8)], r=[wr, R_u], w=[pr])
                xbt, xr = xb[c % 2]
                B.cp(xbt[:, 0:3], sconv[:, c, 0, :], r=[R_sm], w=[xr])
                B.cp(xbt[:, 3:19], pt[:, 0:16], r=[pr], w=[xr])
                B.cp(xbt[:, 19:22], sconv[:, c, 1, :], r=[R_sm], w=[xr])
                B.cp(xbt[:, 22:38], pt[:, 16:32], r=[pr], w=[xr])
                B.ts(halo[:, c, :], pt[:, 32:35], flg[:, 16:17], None, ALU.mult, r=[pr, R_sm], w=[R_halo])
                B.dma(cvs_o[c * 128:(c + 1) * 128, 0:3], xbt[:, 16:19], r=[xr], w=[], q="pool")
                B.dma(cvs_o[c * 128:(c + 1) * 128, 3:6], xbt[:, 35:38], r=[xr], w=[], q="pool")
                ta, tr = nextA()
                B.ts(ta[:, 0:35], xbt[:, 0:35], cw[:, c, 0:1], cbv[:, c:c + 1], ALU.mult, ALU.add, r=[xr, R_sm], w=[tr])
                for i in range(1, 4):
                    B.stt(ta[:, 0:35], xbt[:, i:i + 35], cw[:, c, i:i + 1], ta[:, 0:35], ALU.mult, ALU.add, r=[xr, R_sm, tr], w=[tr])
                tb_, tbr = nextB()
                B.act(tb_[:, 0:35], ta[:, 0:35], AF.Silu, r=[tr], w=[tbr])
                offs = (0, 19)
                if c >= 20:
                    for sq_ in range(2):
                        B.dma(CT2_s[(c - 20) * 128:(c - 19) * 128, sq_ * 16:(sq_ + 1) * 16], tb_[:, offs[sq_]:offs[sq_] + 16], r=[tbr], w=[R_s2], q="pool")
                    continue
                if c >= 16:
                    for sq_ in range(2):
                        B.dma(BT2_s[(c - 16) * 128:(c - 15) * 128, sq_ * 16:(sq_ + 1) * 16], tb_[:, offs[sq_]:offs[sq_] + 16], r=[tbr], w=[R_s2], q="pool")
                for sq_ in range(2):
                    P.add("pe", lambda e, sq_=sq_, tb_=tb_: e.transpose(ptT[0:16, sq_ * 128:(sq_ + 1) * 128], tb_[:, offs[sq_]:offs[sq_] + 16], ident_b[:]),
                          r=[tbr, R_const], w=[prT])
                tb2, tbr2 = nextB()
                B.cp(tb2[0:16, 0:256], ptT[0:16, 0:256], r=[prT], w=[tbr2])
                for sq_ in range(2):
                    rows2 = slice(sq_ * 16, (sq_ + 1) * 16)
                    if c < 16:
                        B.dma(xs2_s[rows2, c * 128:(c + 1) * 128], tb2[0:16, sq_ * 128:(sq_ + 1) * 128], r=[tbr2], w=[R_s2], q="pool")
                    else:
                        B.dma(Bt2_s[rows2, (c - 16) * 128:(c - 15) * 128], tb2[0:16, sq_ * 128:(sq_ + 1) * 128], r=[tbr2], w=[R_s2], q="pool")
            for (c0, dst) in [(C_GA, ga2_s), (C_GS, gs2_s)]:
                for c in range(8):
                    wt, wr = B.load_w(win_d, 8, c0 + c * 128, 128)
                    pt, pr = banks[2 + c % 2]
                    B.mms(pt[:, 0:NS_], [(wt[:, kc, :], uT[:, kc, 0:NS_]) for kc in range(8)], r=[wr, R_u], w=[pr])
                    tb_, tbr = nextB()
                    B.act(tb_[:, 0:NS_], pt[:, 0:NS_], AF.Sigmoid, r=[pr], w=[tbr])
                    B.dma(dst[c * 128:(c + 1) * 128, :], tb_[:, 0:NS_], r=[tbr], w=[R_s2], q="pool")

        xTd_v = xT_d.rearrange("(k p) t -> p k t", p=128)
        x1o_v = x1_o.rearrange("(k p) t -> p k t", p=128)
        for blk in range(NBLK):
            t0 = blk * TB
            groups = [(0, TB, 0)]
            B.dma(xT[:, :, :], xTd_v[:, :, t0:t0 + TB], w=[R_x])
            if B.cutn <= 2:
                B.dma(x1o_v[:, :, t0:t0 + TB], xT[:, :, :], r=[R_x], w=[R_x1], q="pool")
            B.cut(2)
            rms_mod(xT, 0, groups, TB)
            if B.cutn <= 3:
                B.cp(xT[:, :, :], uT[:, :, :], r=[R_u], w=[R_x])
                B.dma(x1o_v[:, :, t0:t0 + TB], xT[:, :, :], r=[R_x], w=[R_x1], q="pool")
            B.cut(3)
            ffn(0, w1a_d, w3a_d, w2a_d, groups, TB)
            B.dma(x1o_v[:, :, t0:t0 + TB], xT[:, :, :], r=[R_x], w=[R_x1], q="pool")
            B.cut(4)
            rms_mod(xT, 1, groups, TB)
            win_block(blk, groups, TB)


    if B.stage < 3:
        run_own_p1()
        return
    tri = B.sb("tri", [128, 128], F32)
    mneg = B.sb("mneg", [128, 128], F32)
    e0row = B.sb("e0row", [128, 128], F32)
    selt = B.sb("selt", [16, 16, 66], F32)
    selc = B.sb("selc", [128, 2], F32)
    a_bc = B.sb("a_bc", [128, 32], F32)
    dsk = B.sb("dsk", [128, 32], F32)
    gssd = B.sb("gssd", [128, 16], F32)
    R_c3 = Res()
    B.dma(tri[:], tri_d[:, :], w=[R_c3])
    B.dma(mneg[:], mneg_d[:, :], w=[R_c3])
    B.dma(e0row[:], e0_d[:, :], w=[R_c3])
    B.dma(selt[:], sel_d[:, :, :], w=[R_c3])
    B.dma(selc[:], selc_d[:, :], w=[R_c3])
    B.dma(a_bc[:], alog_d[:, :], w=[R_c3])
    B.dma(dsk[:], dsk_d[:, :], w=[R_c3])
    B.dma(gssd[:], gssd_d[:, :], w=[R_c3])
    B.act(a_bc[:], a_bc[:], AF.Exp, r=[R_c3], w=[R_c3])
    B.ts(a_bc[:], a_bc[:], -1.0, None, ALU.mult, r=[R_c3], w=[R_c3])

    Fc = B.sb("Fc", [128, NT, 16], F32)
    carry = B.sb("carry", [128, 16], F32)
    R_F, R_carry = Res(), Res()
    arena = B.sb("arena", [128, 8192], BF16)
    R_ar = [Res() for _ in range(4)]
    xtm_s = [arena[:, 0:2048], arena[:, 2048:4096]]
    ztm_s = [arena[:, 4096:6144], arena[:, 6144:8192]]
    oT = arena[0:64, :].rearrange("p (h t) -> p h t", h=16)
    bc_s = [(B.sb(f"bcs{i}", [128, 3, 512], BF16), Res()) for i in range(2)]
    Hst = B.sb("Hst", [128, DIN], F32)
    Hb = B.sb("Hb", [128, DIN], BF16)
    yz = B.sb("yz", [128, DIN], F32)
    sm = B.sb("sm", [128, 8, 32], F32)
    cbm = B.sb("cbm", [128, 4, 128], F32)
    xd = B.sb("xd", [128, DIN], BF16)
    ynb = xd
    wTb = [(B.sb(f"wTb{i}", [128, 4, 128], BF16), Res()) for i in range(2)]
    D4 = [(B.sb(f"D4{i}", [128, 4, 128], F32), Res()) for i in range(2)]
    sg4 = [(B.sb(f"sg4{i}", [128, 4, 128], F32), Res()) for i in range(2)]
    ssq = B.sb("ssq", [128, 4], F32)
    R_H, R_Hb, R_yz, R_ynb, R_sm, R_cbm, R_xd, R_ssq = [Res() for _ in range(8)]
    R_ynb = R_xd
    P.add("pool", lambda e: e.memset(Hst[:], 0.0), w=[R_H])
    ynT = gT[:, 0:16, :]
    BT_v = BT_s.rearrange("(g n) t -> n g t", n=128)
    CT_v = CT_s.rearrange("(g n) t -> n g t", n=128)

    def bc3(ap2, n, m):
        return ap2.unsqueeze(2).broadcast_to([128, n, m])

    def ssd_tile(gt, want_y=True, L=128, samp=None, maskj=None):
        sl = gt % 2
        if samp is None:
            rows = slice(gt * 128, (gt + 1) * 128)
            src_x, src_z, src_Bt, src_BT, src_CT = xs_s, z_s, Bt_s, BT_v, CT_v
            rx_, rz_, rbt_, rBT_, rCT_ = R_xs, R_z, R_Bt, R_BT, R_CT
            dti = gt
            ycol0 = (gt % 4) * 128
        else:
            sl = samp
            rows = slice(samp * 16, samp * 16 + 16)
            src_x, src_z, src_Bt, src_BT, src_CT = xs2_s, z2_s, Bt2_s, BT2_v, CT2_v
            rx_ = rz_ = rbt_ = rBT_ = rCT_ = R_s2
            dti = NT + samp
            ycol0 = samp * 16
        xtm, ztm = xtm_s[sl], ztm_s[sl]
        Rx_, Rz_ = R_ar[sl], R_ar[2 + sl]
        bct, Rb_ = bc_s[sl]
        B.dma(xtm[0:L, :], src_x[rows, :], r=[rx_], w=[Rx_])
        if want_y:
            B.dma(ztm[0:L, :], src_z[rows, :], r=[rz_], w=[Rz_])
        B.dma(bct[0:L, 0, :], src_Bt[rows, :], r=[rbt_], w=[Rb_])
        if want_y:
            B.dma(bct[:, 1, 0:4 * L].rearrange("p (g t) -> p g t", g=4), src_BT[:, :, rows], r=[rBT_], w=[Rb_])
            B.dma(bct[:, 2, 0:4 * L].rearrange("p (g t) -> p g t", g=4), src_CT[:, :, rows], r=[rCT_], w=[Rb_])
        Btm = bct[0:L, 0, :].rearrange("p (g n) -> p g n", g=4)
        BTf = bct[:, 1, 0:4 * L].rearrange("p (g t) -> p g t", g=4)
        CTf = bct[:, 2, 0:4 * L].rearrange("p (g t) -> p g t", g=4)
        xt3 = xtm[0:L, :].rearrange("p (h d) -> p h d", h=32)
        dA, acs, ea, de, tmpv = [sm[0:L, j, :] for j in (0, 1, 3, 4, 6)]
        al, cd = sm[:, 2, :], sm[:, 5, :]
        dtv = dtsb[0:L, dti, :]
        B.tt(dA, dtv, a_bc[0:L, :], ALU.mult, r=[R_dt, R_c3], w=[R_sm])
        pt, pr = banks[0]
        P.add("pe", lambda e: e.matmul(pt[0:L, 0:32], tri[0:L, 0:L], dA, start=True, stop=True), r=[R_c3, R_sm], w=[pr])
        P.add("pe", lambda e: e.matmul(pt[:, 32:64], ones_f[0:L, :], dA, start=True, stop=True), r=[R_const, R_sm], w=[pr])
        B.cp(acs, pt[0:L, 0:32], r=[pr], w=[R_sm])
        B.cp(al, pt[:, 32:64], r=[pr], w=[R_sm])
        B.act(ea, acs, AF.Exp, r=[R_sm], w=[R_sm])
        B.act(cd, al, AF.Exp, r=[R_sm], w=[R_sm])
        if maskj is not None:
            B.ts(cd, cd, -1.0, flg[:, maskj:maskj + 1], ALU.add, ALU.mult, r=[R_sm], w=[R_sm])
            B.ts(cd, cd, 1.0, None, ALU.add, r=[R_sm], w=[R_sm])
        B.tt(tmpv, al[0:L, :], acs, ALU.subtract, r=[R_sm], w=[R_sm])
        B.act(tmpv, tmpv, AF.Exp, r=[R_sm], w=[R_sm])
        B.tt(de, tmpv, dtv, ALU.mult, r=[R_sm, R_dt], w=[R_sm])
        if want_y:
            B.cp(Hb[:], Hst[:], r=[R_H], w=[R_Hb], eng="act")
            pc, prc = banks[1]
            for g in range(4):
                P.add("pe", lambda e, g=g: e.matmul(pc[0:L, g * L:(g + 1) * L], BTf[:, g, :], CTf[:, g, :], start=True, stop=True),
                      r=[Rb_], w=[prc])
            cbv_ = cbm[:].rearrange("p g t -> p (g t)")[0:L, 0:4 * L].rearrange("p (g t) -> p g t", g=4)
            B.tt(cbv_, pc[0:L, 0:4 * L].rearrange("p (g t) -> p g t", g=4), tri[0:L, 0:L].unsqueeze(1).broadcast_to([L, 4, L]), ALU.mult,
                 r=[prc, R_c3], w=[R_cbm])
            for g in range(4):
                pyd, pryd = banks[4]
                for hb in range(2):
                    h0 = g * 8 + hb * 4
                    d4t, rd4 = D4[hb]
                    s4t, rs4 = sg4[hb]
                    wt4t, rw4 = wTb[hb]
                    d4 = d4t[:].rearrange("p g t -> p (g t)")[0:L, 0:4 * L].rearrange("p (g t) -> p g t", g=4)
                    s4 = s4t[:].rearrange("p g t -> p (g t)")[0:L, 0:4 * L].rearrange("p (g t) -> p g t", g=4)
                    wt4 = wt4t[:].rearrange("p g t -> p (g t)")[0:L, 0:4 * L].rearrange("p (g t) -> p g t", g=4)
                    B.tt(d4, ident_f[0:L, 0:L].unsqueeze(1).broadcast_to([L, 4, L]),
                         acs[:, h0:h0 + 4].unsqueeze(2).broadcast_to([L, 4, L]), ALU.mult, r=[R_const, R_sm], w=[rd4])
                    pb, prb = banks[2 + hb]
                    P.add("pe", lambda e, d4t=d4t, pb=pb: e.matmul(pb[0:L, 0:4 * L], ones_f[0:L, 0:L],
                                                                    d4t[:].rearrange("p g t -> p (g t)")[0:L, 0:4 * L], start=True, stop=True),
                          r=[R_const, rd4], w=[prb])
                    for j in range(4):
                        B.stt(s4[:, j, :], pb[0:L, j * L:(j + 1) * L], acs[:, h0 + j:h0 + j + 1], mneg[0:L, 0:L], ALU.subtract, ALU.add,
                              r=[prb, R_sm, R_c3], w=[rs4])
                    B.act(s4, s4, AF.Exp, r=[rs4], w=[rs4])
                    for j in range(4):
                        B.stt(wt4[:, j, :], s4[:, j, :], dtv[:, h0 + j:h0 + j + 1], cbv_[:, g, :], ALU.mult, ALU.mult,
                              r=[rs4, R_dt, R_cbm], w=[rw4])
                    for j in range(4):
                        hh = hb * 4 + j
                        P.add("pe", lambda e, j=j, hh=hh, wt4=wt4, h0=h0: e.matmul(pyd[0:L, hh * 64:(hh + 1) * 64], wt4[:, j, :], xt3[:, h0 + j, :],
                                                                                    start=True, stop=True), r=[rw4, Rx_], w=[pryd])
                pyo, pryo = banks[5]
                P.add("pe", lambda e, g=g, pyo=pyo: e.matmul(pyo[0:L, :], CTf[:, g, :], Hb[:, g * 512:(g + 1) * 512], start=True, stop=True),
                      r=[Rb_, R_Hb], w=[pryo])
                yg = yz[0:L, g * 512:(g + 1) * 512].rearrange("p (h d) -> p h d", h=8)
                B.tt(yg, pyo[0:L, :].rearrange("p (h d) -> p h d", h=8), ea[:, g * 8:(g + 1) * 8].unsqueeze(2).broadcast_to([L, 8, 64]),
                     ALU.mult, r=[pryo, R_sm], w=[R_yz])
                B.tt(yg, pyd[0:L, :].rearrange("p (h d) -> p h d", h=8), yg, ALU.add, r=[pryd, R_yz], w=[R_yz])
                ta, tr = nextA()
                ta3 = ta[0:L, :].rearrange("p (h d) -> p h d", h=8)
                B.tt(ta3, xt3[:, g * 8:(g + 1) * 8, :], dsk[0:L, g * 8:(g + 1) * 8].unsqueeze(2).broadcast_to([L, 8, 64]), ALU.mult,
                     r=[Rx_, R_c3], w=[tr])
                B.tt(yg, yg, ta3, ALU.add, r=[tr, R_yz], w=[R_yz])
        B.tt(xd[0:L, :].rearrange("p (h d) -> p h d", h=32), xt3, de.unsqueeze(2).broadcast_to([L, 32, 64]), ALU.mult,
             r=[Rx_, R_sm], w=[R_xd])
        for g in range(4):
            pS, prS = banks[6]
            P.add("pe", lambda e, g=g, pS=pS: e.matmul(pS[:, :], Btm[:, g, :], xd[0:L, g * 512:(g + 1) * 512], start=True, stop=True),
                  r=[Rb_, R_xd], w=[prS])
            Hg = Hst[:, g * 512:(g + 1) * 512].rearrange("p (h d) -> p h d", h=8)
            B.tt(Hg, Hg, bc3(cd[:, g * 8:(g + 1) * 8], 8, 64), ALU.mult, r=[R_sm, R_Hb], w=[R_H])
            if maskj is not None:
                B.stt(Hg, pS[:, :].rearrange("p (h d) -> p h d", h=8), flg[:, maskj:maskj + 1], Hg, ALU.mult, ALU.add, r=[prS], w=[R_H])
            else:
                B.tt(Hg, Hg, pS[:, :].rearrange("p (h d) -> p h d", h=8), ALU.add, r=[prS], w=[R_H])
        if not want_y:
            return
        for g in range(4):
            ta, tr = nextA()
            B.act(ta[0:L, :], ztm[0:L, g * 512:(g + 1) * 512], AF.Silu, r=[Rz_], w=[tr])
            B.tt(yz[0:L, g * 512:(g + 1) * 512], yz[0:L, g * 512:(g + 1) * 512], ta[0:L, :], ALU.mult, r=[tr, R_yz], w=[R_yz])
            ta2, tr2 = nextA()
            P.add("act", lambda e, g=g, ta2=ta2: e.activation(out=ta2[0:L, :], in_=yz[0:L, g * 512:(g + 1) * 512], func=AF.Square,
                                                               accum_out=ssq[0:L, g:g + 1]), r=[R_yz], w=[tr2, R_ssq])
        rs_ = sm[0:L, 7, 0:1]
        P.add("dve", lambda e: e.tensor_reduce(out=rs_, in_=ssq[0:L, 0:4], axis=AX.X, op=ALU.add), r=[R_ssq], w=[R_sm])
        B.act(rs_, rs_, AF.Sqrt, r=[R_sm, R_const], w=[R_sm], bias=epsc[0:L, :], scale=1.0 / DIN)
        P.add("dve", lambda e: e.reciprocal(out=rs_, in_=rs_), r=[R_sm], w=[R_sm])
        B.ts(ynb[0:L, :], yz[0:L, :], rs_, None, ALU.mult, r=[R_yz, R_sm], w=[R_ynb])
        for half in range(2):
            for c in range(8):
                cc = half * 8 + c
                P.add("pe", lambda e, c=c, cc=cc: e.transpose(ptT[:, c * 128:c * 128 + L], ynb[0:L, cc * 128:(cc + 1) * 128], ident_b[0:L, 0:L]),
                      r=[R_ynb, R_const], w=[prT])
            for c in range(8):
                cc = half * 8 + c
                B.ts(ynT[:, cc, ycol0:ycol0 + L], ptT[:, c * 128:c * 128 + L], gssd[:, cc:cc + 1], None, ALU.mult,
                     r=[prT, R_c3], w=[R_gT])

    Kt = [(B.sb(f"Kt{i}", [66, T], BF16), Res()) for i in range(2)]
    Vt = [(B.sb(f"Vt{i}", [128, NT, 65], BF16), Res()) for i in range(2)]
    Qa = [(B.sb(f"Qa{i}", [66, TB], BF16), Res()) for i in range(2)]
    Pt = [(B.sb(f"Pt{i}", [128, TB], BF16), Res()) for i in range(3)]
    FT = sq[0:16, :]
    rbc = B.sb("rbc", [128, 16], F32)
    biasT = B.sb("biasT", [128, NT, 16], F32)
    arb = B.sb("arb", [66, TB], BF16)
    R_FT, R_rbc, R_bias, R_arow, R_rl, R_rlb = [Res() for _ in range(6)]
    R_FT = R_sq
    pti = [0]

    def attn_block(blk):
        t0 = blk * TB
        nkt = 4 * (blk + 1)
        pf, prf = banks[6]
        for j in range(4):
            P.add("pe", lambda e, j=j: e.transpose(pf[0:16, j * 128:(j + 1) * 128], Fc[:, 4 * blk + j, :], ident_f[:]),
                  r=[R_F, R_const], w=[prf])
        B.cp(FT[:, :], pf[0:16, :], r=[prf], w=[R_FT])
        P.add("pe", lambda e: e.matmul(pf[:, 0:16], e0row[:], Fc[:, 4 * blk, :], start=True, stop=True), r=[R_c3, R_F], w=[prf])
        B.cp(rbc[:], pf[:, 0:16], r=[prf], w=[R_rbc])
        B.tt(biasT[:, 0:nkt, :], rbc[:].unsqueeze(1).broadcast_to([128, nkt, 16]), Fc[:, 0:nkt, :], ALU.subtract,
             r=[R_rbc, R_F], w=[R_bias])
        B.tt(rdj[:], delta[:, 0:NCORES, :], rbc[:].unsqueeze(1).broadcast_to([128, NCORES, 16]), ALU.add,
             r=[R_delta, R_rbc], w=[R_rdj])
        kvi = [0]
        for h in range(H_A):
            qa_, qr_ = Qa[h % 2]
            B.dma(qa_[0:64, :], q_s[h, 0:64, t0:t0 + TB], r=[R_q], w=[qr_])
            pa, pra = banks[4]
            P.add("pe", lambda e, h=h: e.matmul(pa[0:66, :], selt[:, h, :], FT[:, :], start=True, stop=True), r=[R_c3, R_FT], w=[pra])
            ar0, rr0 = nextA()
            ar1, rr1 = nextA()
            ar2, rr2 = nextA()
            B.cp(ar0[64:66, :], pa[64:66, :], r=[pra], w=[rr0])
            B.ts(ar1[64:66, :], ar0[64:66, :], ar0[64:66, 0:1], None, ALU.subtract, r=[rr0], w=[rr1])
            B.cp(arb[64:66, :], ar1[64:66, :], r=[rr1], w=[R_arow])
            B.tt(ar0[64:66, :], ar1[64:66, :], arb[64:66, :], ALU.subtract, r=[rr1, R_arow], w=[rr0])
            B.ts(ar2[64:66, :], arb[64:66, :], selc[64:66, 0:1], None, ALU.mult, r=[R_arow, R_c3], w=[rr2])
            B.stt(qa_[64:66, :], ar0[64:66, :], selc[64:66, 1:2], ar2[64:66, :], ALU.mult, ALU.add,
                  r=[rr0, rr2, R_c3], w=[qr_])
            po, pro = banks[2 + h % 2]
            tasks = []
            if CTX:
                for j in range(NCORES - 1):
                    kvi[0] += 1
                    kt_, kr_ = Kt[kvi[0] % 2]
                    vt_, vr_ = Vt[kvi[0] % 2]
                    bj, bjr = biasJ[kvi[0] % 2]

                    def pre(j=j, kt_=kt_, kr_=kr_, vt_=vt_, vr_=vr_, bj=bj, bjr=bjr, h=h):
                        B.dma(kt_[:, :], kgA_v[j, h, :, :], r=[R_kgA], w=[kr_])
                        B.dma(vt_[:, :, :], vgA_v[j, h, :, :, :].rearrange("t p d -> p t d"), r=[R_vgA], w=[vr_])
                        B.ts(bj[:, :], FcA[:, j, :].rearrange("p (t h) -> p t h", h=16)[:, :, h], -1.0, rdj[:, j, h:h + 1], ALU.mult, ALU.add,
                             r=[R_FcA, R_rdj], w=[bjr])
                        B.ts(bj[:, :], bj[:, :], flg[:, j:j + 1], flg[:, 8 + j:9 + j], ALU.mult, ALU.add, r=[R_sm], w=[bjr])
                    for kt in range(NT):
                        tasks.append(dict(pre=pre if kt == 0 else None, kt_=kt_, kr_=kr_, vt_=vt_, vr_=vr_, kc=kt * 128, vi=kt, c0=0,
                                          bias=bj[:, kt:kt + 1], bres=bjr, diag=False, stop=False))
            kvi[0] += 1
            kt_, kr_ = Kt[kvi[0] % 2]
            vt_, vr_ = Vt[kvi[0] % 2]

            def pre_own(kt_=kt_, kr_=kr_, vt_=vt_, vr_=vr_, h=h):
                B.dma(kt_[:, 0:t0 + TB], kg_s[h, :, 0:t0 + TB], r=[R_kg], w=[kr_])
                B.dma(vt_[:, 0:nkt, :], vg_s[h, 0:nkt, :, :].rearrange("t p d -> p t d"), r=[R_vg], w=[vr_])
            for kt in range(nkt):
                j = kt - 4 * blk
                tasks.append(dict(pre=pre_own if kt == 0 else None, kt_=kt_, kr_=kr_, vt_=vt_, vr_=vr_, kc=kt * 128, vi=kt,
                                  c0=(128 * j if j > 0 else 0), bias=biasT[:, kt, h:h + 1], bres=R_bias, diag=(j >= 0),
                                  stop=(kt == nkt - 1)))
            ntask = len(tasks)

            def emitS(i, qa_=qa_, qr_=qr_):
                t = tasks[i]
                ps_, prs = banks[i % 2]
                if t["pre"] is not None:
                    t["pre"]()
                c0 = t["c0"]
                P.add("pe", lambda e, t=t, ps_=ps_, c0=c0: e.matmul(ps_[:, c0:TB], t["kt_"][:, t["kc"]:t["kc"] + 128], qa_[:, c0:TB],
                                                                    start=True, stop=True), r=[t["kr_"], qr_], w=[prs])

            def emitEPV(i, po=po, pro=pro):
                t = tasks[i]
                ps_, prs = banks[i % 2]
                pT_, prp = Pt[i % 3]
                c0 = t["c0"]
                if t["diag"]:
                    ta, tr = nextA()
                    B.tt(ta[:, c0:c0 + 128], ps_[:, c0:c0 + 128], mneg[:], ALU.add, r=[prs, R_c3], w=[tr])
                    B.act(pT_[:, c0:c0 + 128], ta[:, c0:c0 + 128], AF.Exp, r=[tr, t["bres"]], w=[prp], bias=t["bias"], scale=1.0)
                    if c0 + 128 < TB:
                        B.act(pT_[:, c0 + 128:TB], ps_[:, c0 + 128:TB], AF.Exp, r=[prs, t["bres"]], w=[prp], bias=t["bias"], scale=1.0)
                else:
                    B.act(pT_[:, c0:TB], ps_[:, c0:TB], AF.Exp, r=[prs, t["bres"]], w=[prp], bias=t["bias"], scale=1.0)
                st_ = (i == 0)
                sp_ = t["stop"]
                P.add("pe", lambda e, t=t, pT_=pT_, c0=c0, st_=st_, sp_=sp_: e.matmul(po[0:65, c0:TB], t["vt_"][:, t["vi"], :], pT_[:, c0:TB],
                                                                                    start=st_, stop=sp_, skip_group_check=True),
                      r=[t["vr_"], prp], w=[pro])
            emitS(0)
            for i in range(ntask):
                if i + 1 < ntask:
                    emitS(i + 1)
                emitEPV(i)
            rl, R_rl = nextA()
            rlb, R_rlb = nextA()
            P.add("dve", lambda e, po=po, rl=rl: e.reciprocal(out=rl[64:65, :], in_=po[64:65, :]), r=[pro], w=[R_rl])
            pb2, prb2 = banks[5]
            P.add("pe", lambda e, rl=rl: e.matmul(pb2[0:64, :], ones_f[64:65, 0:64], rl[64:65, :], start=True, stop=True), r=[R_const, R_rl], w=[prb2])
            B.cp(rlb[0:64, :], pb2[0:64, :], r=[prb2], w=[R_rlb])
            B.tt(oT[:, h, :], po[0:64, :], rlb[0:64, :], ALU.mult, r=[pro, R_rlb], w=R_ar)

    gab = [(B.sb(f"gab{i}", [128, 2, TB], BF16), Res()) for i in range(2)]
    mT = uT
    R_mT = R_u
    ga_v = ga_s.rearrange("(k p) t -> p k t", p=128)
    gs_v = gs_s.rearrange("(k p) t -> p k t", p=128)
    yTo_v = yT_o.rearrange("(k p) t -> p k t", p=128)

    def dense_block(blk, ncols=TB, groups=None, samp=False):
        t0 = 0 if samp else blk * TB
        if groups is None:
            groups = [(0, TB, 0)]
        gav = ga2_v if samp else ga_v
        gsv = gs2_v if samp else gs_v
        rga, rgs = (R_s2, R_s2) if samp else (R_ga, R_gs)
        for oc in range(8):
            wat, war = B.load_w(wa_d, 16, oc * 128, 128, pn=64)
            wst_, wsr = B.load_w(ws_d, 16, oc * 128, 128)
            pa_, pra_ = banks[oc % 2]
            ps2, prs2 = banks[2 + oc % 2]
            B.mms(pa_[:, 0:ncols], [(wat[:, h, :], oT[:, h, 0:ncols]) for h in range(16)], r=[war] + R_ar, w=[pra_])
            B.mms(ps2[:, 0:ncols], [(wst_[:, kc, :], ynT[:, kc, 0:ncols]) for kc in range(16)], r=[wsr, R_gT], w=[prs2])
            gt_, gr_ = gab[oc % 2]
            B.dma(gt_[:, 0, 0:ncols], gav[:, oc, t0:t0 + ncols], r=[rga], w=[gr_])
            B.dma(gt_[:, 1, 0:ncols], gsv[:, oc, t0:t0 + ncols], r=[rgs], w=[gr_])
            ta, tr = nextA()
            B.tt(ta[:, 0:ncols], pa_[:, 0:ncols], gt_[:, 0, 0:ncols], ALU.mult, r=[pra_, gr_], w=[tr])
            ta2, tr2 = nextA()
            B.tt(ta2[:, 0:ncols], ps2[:, 0:ncols], gt_[:, 1, 0:ncols], ALU.mult, r=[prs2, gr_], w=[tr2])
            B.tt(mT[:, oc, 0:ncols], ta[:, 0:ncols], ta2[:, 0:ncols], ALU.add, r=[tr, tr2], w=[R_mT])
        if samp:
            B.cp(xT[:, :, 0:ncols], xs1[:, :, 0:ncols], r=[R_xs1], w=[R_x])
        else:
            B.dma(xT[:, :, :], x1o_v[:, :, t0:t0 + TB], r=[R_x1], w=[R_x])
        for oc in range(8):
            wot, wor = B.load_w(wo_d, 8, oc * 128, 128)
            po_, pro_ = banks[4 + oc % 2]
            B.mms(po_[:, 0:ncols], [(wot[:, kc, :], mT[:, kc, 0:ncols]) for kc in range(8)], r=[wor, R_mT], w=[pro_])
            for (c0_, n_, m_) in groups:
                B.stt(xT[:, oc, c0_:c0_ + n_], po_[:, c0_:c0_ + n_], Gm[:, 1, oc, m_:m_ + 1], xT[:, oc, c0_:c0_ + n_],
                      ALU.mult, ALU.add, r=[pro_, R_AG], w=[R_x])
        rms_mod(xT, 2, groups, ncols)
        ffn(2, w1b_d, w3b_d, w2b_d, groups, ncols)
        pt, pr = banks[6]
        for kc in range(8):
            ta, tr = nextA()
            B.tt(ta[:, 0:ncols], xT[:, kc, 0:ncols], xT[:, kc, 0:ncols], ALU.mult, r=[R_x], w=[tr])
            P.add("pe", lambda e, ta=ta, kc=kc: e.matmul(pt[:, 0:ncols], ones_f[:], ta[:, 0:ncols], start=(kc == 0), stop=(kc == 7)),
                  r=[tr, R_const], w=[pr])
        B.act(sq[:, 0:ncols], pt[:, 0:ncols], AF.Sqrt, r=[pr, R_const], w=[R_sq], bias=epsc[:], scale=1.0 / D)
        P.add("dve", lambda e: e.reciprocal(out=rstd[:, 0:ncols], in_=sq[:, 0:ncols]), r=[R_sq], w=[R_rstd])
        for kc in range(8):
            B.stt(xT[:, kc, 0:ncols], xT[:, kc, 0:ncols], gsb[:, 3, kc:kc + 1], rstd[:, 0:ncols], ALU.mult, ALU.mult,
                  r=[R_x, R_g, R_rstd], w=[R_x])
        if samp:
            B.dma(ysT_o.rearrange("(k p) t -> p k t", p=128), xT[:, :, 0:ncols], r=[R_x], w=[], q="pool")
        else:
            B.dma(yTo_v[:, :, t0:t0 + TB], xT[:, :, :], r=[R_x], w=[], q="pool")

    ga2_v = ga2_s.rearrange("(k p) t -> p k t", p=128)
    gs2_v = gs2_s.rearrange("(k p) t -> p k t", p=128)
    BT2_v = BT2_s.rearrange("(g n) t -> n g t", n=128)
    CT2_v = CT2_s.rearrange("(g n) t -> n g t", n=128)

    FcA = B.sb("FcA", [128, NCORES, 256], F32)
    smAll = B.sb("smAll", [128, NCORES, 16], F32)
    delta = B.sb("delta", [128, NCORES + 1, 16], F32)
    R_FcA, R_smAll, R_delta = Res(), Res(), Res()
    xTall_v = xTall_d.rearrange("(k p) t -> p k t", p=128)
    NCTX = NCORES - 1
    P.add("dve", lambda e: e.memset(smAll[:], 0.0), w=[R_smAll])
    for j in range(NCTX):
        for h in range(H_A):
            for bb in range(T // TB):
                B.dma(kgA_v[j, h, 64:66, bb * TB:(bb + 1) * TB], onesrow[64:66, :], r=[R_const], w=[R_kgA], q="pool")
    P.add("pool", lambda e: e.memset(halo[:], 0.0), w=[R_halo])
    for j in range(NCTX):
        for blk in range(NBLK):
            c0_ = j * T + blk * TB
            B.dma(xT[:, :, :], xTall_v[:, :, c0_:c0_ + TB], w=[R_x])
            rms_mod(xT, 0, [(0, TB, 0)], TB)
            ffn(0, w1a_d, w3a_d, w2a_d, [(0, TB, 0)], TB)
            rms_mod(xT, 1, [(0, TB, 0)], TB)
            win_block(blk, [(0, TB, 0)], TB, ctxj=j)
        P.add("dve", lambda e: e.memset(carry[:], 0.0), w=[R_carry])
        for i in range(NT):
            pt, pr = banks[6]
            P.add("pe", lambda e, i=i: e.matmul(pt[:, 0:16], tri[:], logf[:, i, :], start=True, stop=True), r=[R_c3, R_lf], w=[pr])
            B.tt(FcA[:, j, i * 16:(i + 1) * 16], pt[:, 0:16], carry[:], ALU.add, r=[pr, R_carry], w=[R_FcA])
            P.add("pe", lambda e, i=i: e.matmul(pt[:, 16:32], ones_f[:], logf[:, i, :], start=True, stop=True), r=[R_const, R_lf], w=[pr])
            B.tt(carry[:], pt[:, 16:32], carry[:], ALU.add, r=[pr], w=[R_carry])
        B.cp(smAll[:, j, :], carry[:], r=[R_carry], w=[R_smAll])
        for gt in range(NT):
            ssd_tile(gt, want_y=False, maskj=j)
    P.add("dve", lambda e: e.memset(delta[:], 0.0), w=[R_delta])
    for j in range(NCORES - 1, -1, -1):
        B.stt(delta[:, j, :], smAll[:, j, 0:16], flg[:, j:j + 1], delta[:, j + 1, :], ALU.mult, ALU.add,
              r=[R_smAll, R_delta], w=[R_delta])
    run_own_p1()
    P.add("dve", lambda e: e.memset(carry[:], 0.0), w=[R_carry])
    for i in range(NT):
        pt, pr = banks[6]
        P.add("pe", lambda e, i=i: e.matmul(pt[:, 0:16], tri[:], logf[:, i, :], start=True, stop=True), r=[R_c3, R_lf], w=[pr])
        B.tt(Fc[:, i, :], pt[:, 0:16], carry[:], ALU.add, r=[pr, R_carry], w=[R_F])
        P.add("pe", lambda e, i=i: e.matmul(pt[:, 16:32], ones_f[:], logf[:, i, :], start=True, stop=True), r=[R_const, R_lf], w=[pr])
        B.tt(carry[:], pt[:, 16:32], carry[:], ALU.add, r=[pr], w=[R_carry])
    biasJ = [(B.sb(f"biasJ{i}", [128, NT], F32), Res()) for i in range(2)]
    rdj = B.sb("rdj", [128, NCORES, 16], F32)
    R_rdj = Res()
    for blk in range(NBLK):
        for lt in range(4):
            ssd_tile(blk * 4 + lt)
        attn_block(blk)
        dense_block(blk)
    B.dma(ssm_o[:, :, :], Hst[:].rearrange("p (h d) -> p h d", h=32), r=[R_H], w=[], q="pool")

    lfc = B.sb("lfc", [128, 8, 16], F32)
    Fs = B.sb("Fs", [128, 9, 16], F32)
    biasS = B.sb("biasS", [128, 9, 16], F32)
    R_lfc, R_Fs, R_bS = Res(), Res(), Res()

    def attn_sample(sq_):
        cs_ = slice(sq_ * 16, (sq_ + 1) * 16)
        B.dma(lfc[:], clf_d[sq_, :, :].rearrange("(t p) h -> p t h", p=128), w=[R_lfc])
        P.add("dve", lambda e: e.memset(carry[:], 0.0), w=[R_carry])
        P.add("dve", lambda e: e.memset(Fs[:], 0.0), w=[R_Fs])
        pt, pr = banks[6]
        for i in range(8):
            P.add("pe", lambda e, i=i: e.matmul(pt[:, 0:16], tri[:], lfc[:, i, :], start=True, stop=True), r=[R_c3, R_lfc], w=[pr])
            B.tt(Fs[:, i, :], pt[:, 0:16], carry[:], ALU.add, r=[pr, R_carry], w=[R_Fs])
            P.add("pe", lambda e, i=i: e.matmul(pt[:, 16:32], ones_f[:], lfc[:, i, :], start=True, stop=True), r=[R_const, R_lfc], w=[pr])
            B.tt(carry[:], pt[:, 16:32], carry[:], ALU.add, r=[pr], w=[R_carry])
        P.add("pe", lambda e: e.matmul(pt[0:16, 0:16], tri[0:16, 0:16], logf[0:16, NT + sq_, :], start=True, stop=True), r=[R_c3, R_lf], w=[pr])
        B.tt(Fs[0:16, 8, :], pt[0:16, 0:16], carry[0:16, :], ALU.add, r=[pr, R_carry], w=[R_Fs])
        pf, prf = banks[6]
        P.add("pe", lambda e: e.transpose(pf[0:16, 0:16], Fs[0:16, 8, :], ident_f[0:16, 0:16]), r=[R_Fs, R_const], w=[prf])
        B.cp(FT[:, 0:16], pf[0:16, 0:16], r=[prf], w=[R_FT])
        P.add("pe", lambda e: e.matmul(pf[:, 0:16], e0row[0:16, :], Fs[0:16, 8, :], start=True, stop=True), r=[R_c3, R_Fs], w=[prf])
        B.cp(rbc[:], pf[:, 0:16], r=[prf], w=[R_rbc])
        B.tt(biasS[:], rbc[:].unsqueeze(1).broadcast_to([128, 9, 16]), Fs[:], ALU.subtract, r=[R_rbc, R_Fs], w=[R_bS])
        for h in range(H_A):
            kt_, kr_ = Kt[h % 2]
            vt_, vr_ = Vt[h % 2]
            qa_, qr_ = Qa[h % 2]
            B.dma(yz[0:64, 0:1024], ckT_d[sq_, h, :, :], w=[R_yz])
            B.cp(kt_[0:64, 0:1024], yz[0:64, 0:1024], r=[R_yz], w=[kr_], eng="pool")
            P.add("pool", lambda e, kt_=kt_: e.memset(kt_[64:66, 0:1040], 1.0), w=[kr_])
            B.dma(kt_[0:64, 1024:1040], k2_s[h, 0:64, cs_], r=[R_s2], w=[kr_])
            ta, tr = nextA()
            B.dma(ta[:, :].rearrange("p (t d) -> p t d", t=8),
                  cv_d[sq_, :, :].rearrange("(t p) (h d) -> p t h d", p=128, h=16)[:, :, h, :], w=[tr])
            B.cp(vt_[:, 0:8, 0:64], ta[:, :].rearrange("p (t d) -> p t d", t=8), r=[tr], w=[vr_], eng="pool")
            P.add("pool", lambda e, vt_=vt_: e.memset(vt_[:, 0:9, 64:65], 1.0), w=[vr_])
            B.dma(vt_[0:16, 8, :], v2_s[h, sq_, :, :], r=[R_s2], w=[vr_])
            B.dma(qa_[0:64, 0:16], q2_s[h, 0:64, cs_], r=[R_s2], w=[qr_])
            pa, pra = banks[4]
            P.add("pe", lambda e, h=h: e.matmul(pa[0:66, 0:16], selt[:, h, :], FT[:, 0:16], start=True, stop=True), r=[R_c3, R_FT], w=[pra])
            ar0, rr0 = nextA()
            ar1, rr1 = nextA()
            ar2, rr2 = nextA()
            B.cp(ar0[64:66, 0:16], pa[64:66, 0:16], r=[pra], w=[rr0])
            B.ts(ar1[64:66, 0:16], ar0[64:66, 0:16], ar0[64:66, 0:1], None, ALU.subtract, r=[rr0], w=[rr1])
            B.cp(arb[64:66, 0:16], ar1[64:66, 0:16], r=[rr1], w=[R_arow])
            B.tt(ar0[64:66, 0:16], ar1[64:66, 0:16], arb[64:66, 0:16], ALU.subtract, r=[rr1, R_arow], w=[rr0])
            B.ts(ar2[64:66, 0:16], arb[64:66, 0:16], selc[64:66, 0:1], None, ALU.mult, r=[R_arow, R_c3], w=[rr2])
            B.stt(qa_[64:66, 0:16], ar0[64:66, 0:16], selc[64:66, 1:2], ar2[64:66, 0:16], ALU.mult, ALU.add, r=[rr0, rr2, R_c3], w=[qr_])
            po, pro = banks[2 + h % 2]
            for kt in range(9):
                Lk = 128 if kt < 8 else 16
                ps_, prs = banks[kt % 2]
                P.add("pe", lambda e, kt=kt, Lk=Lk, ps_=ps_, kt_=kt_, qa_=qa_: e.matmul(ps_[0:Lk, 0:16], kt_[:, kt * 128:kt * 128 + Lk], qa_[:, 0:16],
                                                                                       start=True, stop=True), r=[kr_, qr_], w=[prs])
                pti[0] += 1
                pT_, prp = Pt[pti[0] % 3]
                if kt == 8:
                    ta, tr = nextA()
                    B.tt(ta[0:16, 0:16], ps_[0:16, 0:16], mneg[0:16, 0:16], ALU.add, r=[prs, R_c3], w=[tr])
                    B.act(pT_[0:16, 0:16], ta[0:16, 0:16], AF.Exp, r=[tr, R_bS], w=[prp], bias=biasS[0:16, 8, h:h + 1], scale=1.0)
                else:
                    B.act(pT_[:, 0:16], ps_[:, 0:16], AF.Exp, r=[prs, R_bS], w=[prp], bias=biasS[:, kt, h:h + 1], scale=1.0)
                P.add("pe", lambda e, kt=kt, Lk=Lk, pT_=pT_, vt_=vt_, po=po: e.matmul(po[0:65, 0:16], vt_[0:Lk, kt, :], pT_[0:Lk, 0:16],
                                                                                    start=(kt == 0), stop=(kt == 8), skip_group_check=True),
                      r=[vr_, prp], w=[pro])
            rl, R_rl = nextA()
            rlb, R_rlb = nextA()
            P.add("dve", lambda e, po=po, rl=rl: e.reciprocal(out=rl[64:65, 0:16], in_=po[64:65, 0:16]), r=[pro], w=[R_rl])
            pb2, prb2 = banks[5]
            P.add("pe", lambda e, rl=rl: e.matmul(pb2[0:64, 0:16], ones_f[64:65, 0:64], rl[64:65, 0:16], start=True, stop=True), r=[R_const, R_rl], w=[prb2])
            B.cp(rlb[0:64, 0:16], pb2[0:64, 0:16], r=[prb2], w=[R_rlb])
            B.tt(oT[:, h, cs_], po[0:64, 0:16], rlb[0:64, 0:16], ALU.mult, r=[pro, R_rlb], w=R_ar)

    for sq_ in range(2):
        B.dma(Hst[:], ssmin_d[sq_, :, :], w=[R_H])
        ssd_tile(0, want_y=True, L=16, samp=sq_)
        B.dma(ssms_o[sq_, :, :], Hst[:], r=[R_H], w=[], q="pool")
    for sq_ in range(2):
        attn_sample(sq_)
    dense_block(0, ncols=32, groups=[(0, 16, 1), (16, 16, 2)], samp=True)


_CACHE = {}


def _r(w, kc):
    K, N = w.shape
    return np.ascontiguousarray(w.reshape(kc, K // kc, N).transpose(1, 0, 2))


def prep_inputs(inp, stage):
    f = np.float32
    xp = np.asarray(inp["x_prompt"], f)[0]
    maps = []
    shared = {}
    shared["w_ada_r"] = _r(np.asarray(inp["w_ada"], f)[0], 8)
    shared["b_ada_r"] = np.ascontiguousarray(np.asarray(inp["b_ada"], f)[0].reshape(72, 128).T)
    for nm, key in [("g_ffn1_r", "g_ffn1"), ("g_mix_r", "g_mix"), ("g_ffn2_r", "g_ffn2")]:
        shared[nm] = np.ascontiguousarray(np.asarray(inp[key], f)[0].reshape(8, 128).T)
    shared["g_final_r"] = np.ascontiguousarray(np.asarray(inp["g_final"], f).reshape(8, 128).T)
    shared["w1a"] = _r(np.asarray(inp["w1_ffn1"], f)[0], 8)
    shared["w3a"] = _r(np.asarray(inp["w3_ffn1"], f)[0], 8)
    shared["w2a"] = _r(np.asarray(inp["w2_ffn1"], f)[0], NH)
    shared["w1b"] = _r(np.asarray(inp["w1_ffn2"], f)[0], 8)
    shared["w3b"] = _r(np.asarray(inp["w3_ffn2"], f)[0], 8)
    shared["w2b"] = _r(np.asarray(inp["w2_ffn2"], f)[0], NH)
    shared["win_r"] = _r(np.asarray(inp["w_in"], f)[0], 8)
    shared["bf_bc"] = np.ascontiguousarray(np.broadcast_to(np.asarray(inp["b_f"], f)[0][None, :], (128, 16)))
    shared["convw_r"] = np.ascontiguousarray(np.asarray(inp["conv_w"], f)[0].reshape(4, 24, 128).transpose(2, 1, 0))
    shared["convb_r"] = np.ascontiguousarray(np.asarray(inp["conv_b"], f)[0].reshape(24, 128).T)
    shared["dtb_bc"] = np.ascontiguousarray(np.broadcast_to(np.asarray(inp["dt_bias"], f)[0][None, :], (128, 32)))
    shared["ident_in"] = np.eye(128, dtype=f)
    shared["alog_bc"] = np.ascontiguousarray(np.broadcast_to(np.asarray(inp["a_log"], f)[0][None, :], (128, 32)))
    shared["dskip_bc"] = np.ascontiguousarray(np.broadcast_to(np.asarray(inp["d_skip"], f)[0][None, :], (128, 32)))
    shared["gssd_r"] = np.ascontiguousarray(np.asarray(inp["g_ssd"], f)[0].reshape(16, 128).T)
    tri = np.triu(np.ones((128, 128), f))
    shared["tri_in"] = tri
    shared["maskneg_in"] = ((1.0 - tri) * -1e9).astype(f)
    e0 = np.zeros((128, 128), f); e0[0, :] = 1.0
    shared["e0row_in"] = e0
    sel = np.zeros((16, 16, 66), f)
    for h in range(16):
        sel[h, h, 64] = 1.0; sel[h, h, 65] = 1.0
    shared["sel_in"] = sel
    selc = np.zeros((128, 2), f); selc[64, 0] = 1.0; selc[65, 1] = 1.0
    shared["selc_in"] = selc
    shared["wa_r"] = np.ascontiguousarray(np.asarray(inp["w_a"], f)[0].reshape(16, 64, D).transpose(1, 0, 2))
    shared["ws_r"] = _r(np.asarray(inp["w_s"], f)[0], 16)
    shared["wout_r"] = _r(np.asarray(inp["w_out"], f)[0], 8)
    xpT = np.ascontiguousarray(xp.T)
    xsm = np.asarray(inp["x_sample"], f)
    cp = np.asarray(inp["c_prompt"], f)[0]
    cs = np.asarray(inp["c_sample"], f)
    for c in range(NCORES):
        m = dict(shared)
        m["xT"] = np.ascontiguousarray(xp[c * T:(c + 1) * T].T)
        m["xTall"] = xpT
        hal = xp[c * T - 3:c * T] if c > 0 else np.zeros((3, D), f)
        m["xsT"] = np.ascontiguousarray(np.concatenate([xsm[2 * c:2 * c + 2].reshape(32, D), hal], 0).T)
        fl = np.zeros((128, 24), f)
        for j in range(8):
            fl[:, j] = 1.0 if j < c else 0.0
            fl[:, 8 + j] = 0.0 if j < c else NEG
        fl[:, 16] = 1.0 if c > 0 else 0.0
        m["flags"] = fl
        sc = np.asarray(inp["state_conv"], f)[0, 2 * c:2 * c + 2]
        m["scv"] = np.ascontiguousarray(sc.transpose(2, 0, 1).reshape(24, 128, 2, 3).transpose(1, 0, 2, 3))
        ss = np.asarray(inp["state_ssm"], f)[0, 2 * c:2 * c + 2]
        m["ssmin"] = np.ascontiguousarray(ss.transpose(0, 3, 1, 2).reshape(2, 128, DIN))
        ck = np.asarray(inp["cache_k"], f)[0, 2 * c:2 * c + 2]
        m["ckT"] = np.ascontiguousarray(ck.transpose(0, 2, 3, 1))
        m["cvc"] = np.ascontiguousarray(np.asarray(inp["cache_v"], f)[0, 2 * c:2 * c + 2].reshape(2, 1024, D))
        m["clf"] = np.ascontiguousarray(np.asarray(inp["cache_logf"], f)[0, 2 * c:2 * c + 2])
        c3 = np.stack([cp, cs[2 * c], cs[2 * c + 1]], axis=1)
        m["cT"] = np.ascontiguousarray(c3.reshape(8, 128, 3).transpose(1, 0, 2))
        maps.append(m)
    return maps


def run(inp, stage=99, ncores=NCORES):
    if stage not in _CACHE:
        B = build(stage)
        _CACHE[stage] = B
    B = _CACHE[stage]
    maps = prep_inputs(inp, stage)
    maps = [{k: v for k, v in m.items() if k in B.ins} for m in maps]
    res = run_bass_kernel_spmd(B.nc, maps[:ncores], core_ids=list(range(ncores)))
    return res.results


def kernel(**inp):
    res = run(inp, stage=3)
    f = np.float32
    y_prompt = np.zeros((1, SEQ, D), f)
    y_sample = np.zeros((16, 16, D), f)
    k_prompt = np.zeros((1, 1, SEQ, H_A, HD), f)
    v_prompt = np.zeros((1, 1, SEQ, H_A, HD), f)
    logf_prompt = np.zeros((1, 1, SEQ, H_A), f)
    ssm_prompt = np.zeros((1, 1, NSSD, 64, NST), f)
    conv_prompt = np.zeros((1, 1, 3, CONVD), f)
    k_sample = np.zeros((1, 16, 16, H_A, HD), f)
    v_sample = np.zeros((1, 16, 16, H_A, HD), f)
    logf_sample = np.zeros((1, 16, 16, H_A), f)
    ssm_sample = np.zeros((1, 16, NSSD, 64, NST), f)
    conv_sample = np.zeros((1, 16, 3, CONVD), f)
    for c in range(NCORES):
        r = res[c]
        sl = slice(c * T, (c + 1) * T)
        y_prompt[0, sl] = np.asarray(r["yT_o"]).T
        k_prompt[0, 0, sl] = np.asarray(r["kT_o"]).T.reshape(T, H_A, HD)
        v_prompt[0, 0, sl] = np.asarray(r["v_o"]).reshape(T, H_A, HD)
        logf_prompt[0, 0, sl] = np.asarray(r["lf_o"])
        k_sample[0, 2 * c:2 * c + 2] = np.asarray(r["ksT_o"]).T.reshape(2, 16, H_A, HD)
        v_sample[0, 2 * c:2 * c + 2] = np.asarray(r["vs_o"]).reshape(2, 16, H_A, HD)
        logf_sample[0, 2 * c:2 * c + 2] = np.asarray(r["lfs_o"]).reshape(2, 16, H_A)
        y_sample[2 * c:2 * c + 2] = np.asarray(r["ysT_o"]).T.reshape(2, 16, D)
        ssm_sample[0, 2 * c:2 * c + 2] = np.asarray(r["ssms_o"]).reshape(2, 128, NSSD, 64).transpose(0, 2, 3, 1)
        cvs = np.asarray(r["cvs_o"])
        conv_sample[0, 2 * c] = cvs[:, 0:3].T
        conv_sample[0, 2 * c + 1] = cvs[:, 3:6].T
    conv_prompt[0, 0] = np.asarray(res[NCORES - 1]["cv_o"]).T
    ssm_prompt[0, 0] = np.asarray(res[NCORES - 1]["ssm_o"]).transpose(1, 2, 0)
    return (y_prompt, y_sample, k_prompt, v_prompt, logf_prompt, ssm_prompt, conv_prompt,
            k_sample, v_sample, logf_sample, ssm_sample, conv_sample)
```

```python
from contextlib import ExitStack
import numpy as np
import concourse.bass as bass
import concourse.mybir as mybir
from concourse.bass_utils import run_bass_kernel_spmd

F32 = mybir.dt.float32
BF16 = mybir.dt.bfloat16
AF = mybir.ActivationFunctionType
ALU = mybir.AluOpType
AX = mybir.AxisListType

NCORES = 8
D = 1024
SEQ = 16384
T = SEQ // NCORES
TB = 512
DFF = 2816
NH = 22
H_A = 16
HD = 64
DIN = 2048
NSSD = 32
NST = 128
CONVD = 3072
DINP = 10288
C_Q, C_K, C_V, C_F, C_Z, C_X, C_DT, C_GA, C_GS = 0, 1024, 2048, 3072, 3088, 5136, 8208, 8240, 9264
EPS = 1e-6
NEG = -30000.0
import os
CTX = os.environ.get("CTX", "1") == "1"

ENG = ["pe", "act", "dve", "pool", "sp"]


class Res:
    __slots__ = ("lw", "rd", "excl")

    def __init__(self, excl=False):
        self.lw = None
        self.rd = []
        self.excl = excl


class Op:
    __slots__ = ("eng", "fn", "deps", "dma", "need", "sv", "dsem", "dval", "idx", "cc", "seng")


class Prog:
    NS = 16

    def __init__(self):
        self.ops = []
        self.dq = {"sp": [], "pool": [], "act": []}
        self.ncc = 0

    def add(self, eng, fn, r=(), w=(), dma=False, cc=False):
        op = Op()
        op.eng, op.fn, op.dma, op.need, op.idx = eng, fn, dma, False, len(self.ops)
        op.sv = 0
        op.cc = cc
        op.seng = eng
        if cc:
            op.dma = dma = True
        deps = {}
        w = list(w) + [R for R in r if R.excl]
        r = [R for R in r if not R.excl]

        def dep(p, war=False):
            if p is None:
                return
            if (not p.dma) and p.eng == eng and eng == "pe":
                return
            deps[p.idx] = p

        for R in r:
            dep(R.lw)
        for R in w:
            dep(R.lw)
            for q in R.rd:
                dep(q, True)
        if cc:
            op.seng = "cc"
            op.dsem = self.ncc
            op.dval = 1
            self.ncc += 1
        elif dma:
            lst = self.dq[eng]
            n = len(lst)
            op.dsem = n % self.NS
            op.dval = 16 * (n // self.NS + 1)
            if n >= self.NS:
                deps[lst[n - self.NS].idx] = lst[n - self.NS]
            lst.append(op)
        op.deps = list(deps.values())
        for p in op.deps:
            if not p.dma:
                p.need = True
        for R in r:
            R.rd.append(op)
        for R in w:
            R.lw = op
            R.rd = []
        self.ops.append(op)
        return op

    def emit(self, nc, sems, dsems):
        cnt = {e: 0 for e in ENG}
        for op in self.ops:
            if (not op.dma) and op.need:
                cnt[op.eng] += 1
                op.sv = cnt[op.eng]
        per = {e: [o for o in self.ops if o.eng == e] for e in ENG}

        def run(ename, e):
            waited = {}
            for op in per[ename]:
                for p in op.deps:
                    if p.dma:
                        key, val, sem = ("d", p.seng, p.dsem), p.dval, dsems[p.seng][p.dsem]
                    else:
                        key, val, sem = ("c", p.eng), p.sv, sems[p.eng]
                    if waited.get(key, 0) >= val:
                        continue
                    waited[key] = val
                    e.wait_ge(sem, val)
                ins = op.fn(e)
                if op.cc:
                    ins.then_inc(dsems["cc"][op.dsem], 1)
                elif op.dma:
                    ins.then_inc(dsems[ename][op.dsem], 16)
                elif op.need:
                    ins.then_inc(sems[ename], 1)
            if ename in self.dq:
                lst = self.dq[ename]
                last = {}
                for o in lst:
                    last[o.dsem] = o.dval
                for s, v in last.items():
                    e.wait_ge(dsems[ename][s], v)

        with nc.Block() as block:
            @block.tensor
            def _(e):
                run("pe", e)

            @block.scalar
            def _(e):
                run("act", e)

            @block.vector
            def _(e):
                run("dve", e)

            @block.gpsimd
            def _(e):
                run("pool", e)

            @block.sync
            def _(e):
                run("sp", e)


class StopBuild(Exception):
    pass


class Builder:
    def __init__(self, stage=99):
        import os
        self.cutn = float(os.environ.get("DBG_CUT", "999"))
        self.stage = stage
        self.nc = bass.Bass("TRN2", target_bir_lowering=False)
        try:
            self.nc.allow_low_precision("bf16 matmul operands by design")
        except Exception:
            pass
        self.P = Prog()
        self.es = ExitStack()
        self.ins = {}
        self.outs = {}
        self.uid = 0
        self.wrr = 0

    def finish(self):
        nc = self.nc
        sems = {e: self.es.enter_context(nc.semaphore(f"s_{e}")) for e in ENG}
        dsems = {q: [self.es.enter_context(nc.semaphore(f"d_{q}{i}")) for i in range(Prog.NS)]
                 for q in ("sp", "pool", "act")}
        dsems["cc"] = [self.es.enter_context(nc.semaphore(f"d_cc{i}")) for i in range(max(1, self.P.ncc))]
        self.P.emit(nc, sems, dsems)
        self.es.close()

    def cut(self, n):
        if self.cutn <= n:
            raise StopBuild()

    def din(self, name, shape, dt=F32):
        t = self.nc.dram_tensor(name, list(shape), dt, kind="ExternalInput").ap()
        self.ins[name] = t
        return t

    def dout(self, name, shape, dt=F32):
        t = self.nc.dram_tensor(name, list(shape), dt, kind="ExternalOutput").ap()
        self.outs[name] = t
        return t

    def dscr(self, name, shape, dt):
        return self.nc.dram_tensor(name, list(shape), dt).ap()

    def sb(self, name, shape, dt=F32):
        return self.es.enter_context(self.nc.sbuf_tensor("sb_" + name, list(shape), dt))

    def ps(self, name, shape, dt=F32):
        return self.es.enter_context(self.nc.psum_tensor("ps_" + name, list(shape), dt))

    def dma(self, out, in_, r=(), w=(), q="sp"):
        return self.P.add(q, lambda e: e.dma_start(out=out, in_=in_), r=r, w=w, dma=True)

    def act(self, out, in_, func, r=(), w=(), bias=None, scale=None):
        kw = {}
        if bias is not None:
            kw["bias"] = bias
        if scale is not None:
            kw["scale"] = scale
        return self.P.add("act", lambda e: e.activation(out=out, in_=in_, func=func, **kw), r=r, w=w)

    def tt(self, out, a, b, op, r=(), w=(), eng="dve"):
        return self.P.add(eng, lambda e: e.tensor_tensor(out=out, in0=a, in1=b, op=op), r=r, w=w)

    def ts(self, out, a, s1, s2, op0, op1=None, r=(), w=(), eng="dve"):
        if op1 is None:
            return self.P.add(eng, lambda e: e.tensor_scalar(out=out, in0=a, scalar1=s1, scalar2=None, op0=op0), r=r, w=w)
        return self.P.add(eng, lambda e: e.tensor_scalar(out=out, in0=a, scalar1=s1, scalar2=s2, op0=op0, op1=op1), r=r, w=w)

    def stt(self, out, a, s, b, op0, op1, r=(), w=(), eng="dve"):
        return self.P.add(eng, lambda e: e.scalar_tensor_tensor(out=out, in0=a, scalar=s, in1=b, op0=op0, op1=op1), r=r, w=w)

    def cp(self, out, in_, r=(), w=(), eng="dve"):
        if eng == "act":
            return self.P.add("act", lambda e: e.copy(out=out, in_=in_), r=r, w=w)
        return self.P.add(eng, lambda e: e.tensor_copy(out=out, in_=in_), r=r, w=w)

    def mms(self, out, pairs, r=(), w=()):
        n = len(pairs)

        def fn(e):
            ins = None
            for i, (l, rh) in enumerate(pairs):
                ins = e.matmul(out, l, rh, start=(i == 0), stop=(i == n - 1))
            return ins
        return self.P.add("pe", fn, r=r, w=w)

    def init_wstream(self):
        self.WSZ = 2048
        self.wst = [(self.sb(f"wst{i}", [128, self.WSZ], F32), Res()) for i in range(2)]
        self.wbf = [(self.sb(f"wbf{i}", [128, self.WSZ], BF16), Res()) for i in range(2)]
        self.wi = 0
        self.wj = 0

    def load_w(self, wd, kcn, c0, n, pn=128, k0=0):
        st, sr = self.wst[self.wi % 2]
        self.wi += 1
        bf, br = self.wbf[self.wj % 2]
        self.wj += 1
        sz = kcn * n
        assert sz <= self.WSZ
        stv = st[0:pn, 0:sz].rearrange("p (k n) -> p k n", k=kcn)
        bfv = bf[0:pn, 0:sz].rearrange("p (k n) -> p k n", k=kcn)
        self.dma(stv, wd[0:pn, k0:k0 + kcn, c0:c0 + n], w=[sr])
        ceng = "dve" if (self.wj % 3 == 0) else "act"
        self.cp(bf[0:pn, 0:sz], st[0:pn, 0:sz], r=[sr], w=[br], eng=ceng)
        return bfv, br


def wpairs(B, wd, kcn, base, n, count, pn=128):
    cache = {}

    def get(i):
        g = i // 2
        if g not in cache:
            cache.clear()
            ncol = n * min(2, count - 2 * g)
            cache[g] = B.load_w(wd, kcn, base + 2 * g * n, ncol, pn=pn)
        wt, wr = cache[g]
        o = (i % 2) * n
        return wt[:, :, o:o + n], wr
    return get


def build(stage=99):
    B = Builder(stage)
    nc, P = B.nc, B.P
    NBLK = T // TB
    NT = T // 128

    xT_d = B.din("xT", [D, T])
    cT_d = B.din("cT", [128, 8, 3])
    wada_d = B.din("w_ada_r", [128, 8, 9 * D])
    bada_d = B.din("b_ada_r", [128, 72])
    g1_d = B.din("g_ffn1_r", [128, 8])
    gm_d = B.din("g_mix_r", [128, 8])
    g2_d = B.din("g_ffn2_r", [128, 8])
    gf_d = B.din("g_final_r", [128, 8])
    w1a_d = B.din("w1a", [128, 8, DFF])
    w3a_d = B.din("w3a", [128, 8, DFF])
    w2a_d = B.din("w2a", [128, NH, D])
    w1b_d = B.din("w1b", [128, 8, DFF])
    w3b_d = B.din("w3b", [128, 8, DFF])
    w2b_d = B.din("w2b", [128, NH, D])
    win_d = B.din("win_r", [128, 8, DINP])
    bf_d = B.din("bf_bc", [128, 16])
    cw_d = B.din("convw_r", [128, 24, 4])
    cb_d = B.din("convb_r", [128, 24])
    dtb_d = B.din("dtb_bc", [128, 32])

    kT_o = B.dout("kT_o", [D, T])
    v_o = B.dout("v_o", [T, D])
    lf_o = B.dout("lf_o", [T, 16])
    cv_o = B.dout("cv_o", [CONVD, 3])
    xsT_d = B.din("xsT", [D, 35])
    flg_d = B.din("flags", [128, 24])
    ksT_o = B.dout("ksT_o", [D, 32])
    vs_o = B.dout("vs_o", [32, D])
    lfs_o = B.dout("lfs_o", [32, 16])
    cvs_o = B.dout("cvs_o", [CONVD, 6])
    x1_o = B.dout("x1_o", [D, T])
    scv_d = B.din("scv", [128, 24, 2, 3])
    ssmin_d = B.din("ssmin", [2, 128, DIN])
    ckT_d = B.din("ckT", [2, H_A, 64, 1024])
    cv_d = B.din("cvc", [2, 1024, D])
    clf_d = B.din("clf", [2, 1024, 16])
    ysT_o = B.dout("ysT_o", [D, 32])
    ssms_o = B.dout("ssms_o", [2, 128, DIN])
    q2_s = B.dscr("q2_s", [H_A, 66, 32], BF16)
    k2_s = B.dscr("k2_s", [H_A, 66, 32], BF16)
    v2_s = B.dscr("v2_s", [H_A, 2, 16, 65], BF16)
    z2_s = B.dscr("z2_s", [32, DIN], BF16)
    xs2_s = B.dscr("xs2_s", [32, DIN], BF16)
    Bt2_s = B.dscr("Bt2_s", [32, 512], BF16)
    BT2_s = B.dscr("BT2_s", [512, 32], BF16)
    CT2_s = B.dscr("CT2_s", [512, 32], BF16)
    ga2_s = B.dscr("ga2_s", [D, 32], BF16)
    gs2_s = B.dscr("gs2_s", [D, 32], BF16)
    R_s2 = Res()
    xTall_d = B.din("xTall", [D, SEQ])
    kgA = B.dscr("kgA", [NCORES * H_A * 66, T], BF16)
    vgA = B.dscr("vgA", [NCORES * H_A * NT * 128, 65], BF16)
    kgA_v = kgA.rearrange("(j h r) t -> j h r t", j=NCORES, h=H_A)
    vgA_v = vgA.rearrange("(j h t p) d -> j h t p d", j=NCORES, h=H_A, t=NT)
    R_kgA, R_vgA = Res(), Res()
    alog_d = B.din("alog_bc", [128, 32])
    dsk_d = B.din("dskip_bc", [128, 32])
    gssd_d = B.din("gssd_r", [128, 16])
    tri_d = B.din("tri_in", [128, 128])
    mneg_d = B.din("maskneg_in", [128, 128])
    e0_d = B.din("e0row_in", [128, 128])
    sel_d = B.din("sel_in", [16, 16, 66])
    selc_d = B.din("selc_in", [128, 2])
    wa_d = B.din("wa_r", [64, 16, D])
    ws_d = B.din("ws_r", [128, 16, D])
    wo_d = B.din("wout_r", [128, 8, D])
    yT_o = B.dout("yT_o", [D, T])
    ssm_o = B.dout("ssm_o", [128, NSSD, 64])

    q_s = B.dscr("q_s", [H_A, 66, T], BF16)
    kg_s = B.dscr("kg_s", [H_A, 66, T], BF16)
    vg_s = B.dscr("vg_s", [H_A, NT, 128, 65], BF16)
    z_s = B.dscr("z_s", [T, DIN], BF16)
    xs_s = B.dscr("xs_s", [T, DIN], BF16)
    Bt_s = B.dscr("Bt_s", [T, 512], BF16)
    BT_s = B.dscr("BT_s", [512, T], BF16)
    CT_s = B.dscr("CT_s", [512, T], BF16)
    ga_s = B.dscr("ga_s", [D, T], BF16)
    gs_s = B.dscr("gs_s", [D, T], BF16)
    R_q, R_kg, R_vg, R_z, R_xs, R_Bt, R_BT, R_CT, R_ga, R_gs, R_x1 = [Res() for _ in range(11)]

    ones_f = B.sb("ones_f", [128, 128], F32)
    ident_b = B.sb("ident_b", [128, 128], BF16)
    ident_f = B.sb("ident_f", [128, 128], F32)
    epsc = B.sb("epsc", [128, 1], F32)
    onec = B.sb("onec", [128, 1], F32)
    R_const = Res()
    P.add("pool", lambda e: e.memset(ones_f[:], 1.0), w=[R_const])
    P.add("pool", lambda e: e.memset(epsc[:], EPS), w=[R_const])
    P.add("pool", lambda e: e.memset(onec[:], 1.0), w=[R_const])
    identf_d = B.din("ident_in", [128, 128])
    B.dma(ident_f[:], identf_d[:, :], w=[R_const])
    B.cp(ident_b[:], ident_f[:], r=[R_const], w=[R_const], eng="pool")

    B.init_wstream()

    banks = [(B.ps(f"bank{i}", [128, 512], F32), Res(True)) for i in range(7)]
    ptT = B.ps("ptT", [128, 1024], BF16)
    prT = Res(True)

    try:
        _build_body(B, locals())
    except StopBuild:
        pass
    B.finish()
    return B


def _build_body(B, L):
    globals_ = L
    nc, P = B.nc, B.P
    NBLK = T // TB
    NT = T // 128
    for k_, v_ in L.items():
        if k_ not in ("B", "nc", "P"):
            globals()[k_] = v_
    cT = B.sb("cT", [128, 8, 3], F32)
    cs = B.sb("cs", [128, 8, 3], BF16)
    bada = B.sb("bada", [128, 72], F32)
    modT = B.sb("modT", [128, 72, 3], F32)
    gsb = B.sb("gsb", [128, 4, 8], F32)
    R_c, R_mod, R_g = Res(), Res(), Res()
    B.dma(cT[:], cT_d[:, :, :], w=[R_c])
    B.dma(bada[:], bada_d[:, :], w=[R_c])
    for i, gd in enumerate([g1_d, gm_d, g2_d, gf_d]):
        B.dma(gsb[:, i, :], gd[:, :], w=[R_g])
    B.act(cs[:], cT[:], AF.Silu, r=[R_c], w=[R_c])
    for ch in range(72):
        wt, wr = B.load_w(wada_d, 8, ch * 128, 128)
        pt, pr = banks[ch % 2]
        B.mms(pt[:, 0:3], [(wt[:, kc, :], cs[:, kc, :]) for kc in range(8)], r=[wr, R_c], w=[pr])
        B.ts(modT[:, ch, :], pt[:, 0:3], bada[:, ch:ch + 1], None, ALU.add, r=[pr, R_c], w=[R_mod])
    Am = B.sb("Am", [128, 3, 8, 3], F32)
    Gm = B.sb("Gm", [128, 3, 8, 3], F32)
    R_AG = Res()
    for s in range(3):
        coef = 1.0 if s == 1 else 0.5
        for kc in range(8):
            B.ts(Am[:, s, kc, :], modT[:, (3 * s + 1) * 8 + kc, :], 1.0, gsb[:, s, kc:kc + 1], ALU.add, ALU.mult,
                 r=[R_mod, R_g], w=[R_AG])
            B.ts(Gm[:, s, kc, :], modT[:, (3 * s + 2) * 8 + kc, :], 1.0, coef, ALU.add, ALU.mult,
                 r=[R_mod], w=[R_AG])

    B.cut(1)

    def shiftp(s, kc, m):
        return modT[:, (3 * s) * 8 + kc, m:m + 1]

    xT = B.sb("xT", [128, 8, TB], F32)
    uT = B.sb("uT", [128, 8, TB], BF16)
    gT = B.sb("gT", [128, NH, TB], BF16)
    sq = B.sb("sq", [128, TB], F32)
    rstd = B.sb("rstd", [128, TB], F32)
    tmpA = [(B.sb(f"tmpA{i}", [128, TB], F32), Res()) for i in range(5)]
    tmpB = [(B.sb(f"tmpB{i}", [128, TB], BF16), Res()) for i in range(5)]
    R_x, R_u, R_gT, R_sq, R_rstd = Res(), Res(), Res(), Res(), Res()
    tai = [0]
    tbi = [0]

    def nextA():
        tai[0] += 1
        return tmpA[tai[0] % 5]

    def nextB():
        tbi[0] += 1
        return tmpB[tbi[0] % 5]

    def rms_mod(src, s, groups, ncols):
        pt, pr = banks[6]
        for kc in range(8):
            ta, tr = nextA()
            B.tt(ta[:, 0:ncols], src[:, kc, 0:ncols], src[:, kc, 0:ncols], ALU.mult, r=[R_x], w=[tr])
            P.add("pe", lambda e, ta=ta, kc=kc: e.matmul(pt[:, 0:ncols], ones_f[:], ta[:, 0:ncols],
                                                            start=(kc == 0), stop=(kc == 7)),
                  r=[tr, R_const], w=[pr])
        B.act(sq[:, 0:ncols], pt[:, 0:ncols], AF.Sqrt, r=[pr, R_const], w=[R_sq], bias=epsc[:], scale=1.0 / D)
        P.add("dve", lambda e: e.reciprocal(out=rstd[:, 0:ncols], in_=sq[:, 0:ncols]), r=[R_sq], w=[R_rstd])
        for kc in range(8):
            ta, tr = nextA()
            B.tt(ta[:, 0:ncols], src[:, kc, 0:ncols], rstd[:, 0:ncols], ALU.mult, r=[R_x, R_rstd], w=[tr])
            for (c0, n, m) in groups:
                B.ts(uT[:, kc, c0:c0 + n], ta[:, c0:c0 + n], Am[:, s, kc, m:m + 1], shiftp(s, kc, m),
                     ALU.mult, ALU.add, r=[tr, R_AG, R_mod], w=[R_u])

    def ffn(s, w1d, w3d, w2d, groups, ncols):
        g1 = wpairs(B, w1d, 8, 0, 128, NH)
        g3 = wpairs(B, w3d, 8, 0, 128, NH)
        for hc in range(NH):
            w1t, w1r = g1(hc)
            w3t, w3r = g3(hc)
            p1, r1 = banks[hc % 2]
            p3, r3 = banks[2 + hc % 2]
            B.mms(p1[:, 0:ncols], [(w1t[:, kc, :], uT[:, kc, 0:ncols]) for kc in range(8)], r=[w1r, R_u], w=[r1])
            B.mms(p3[:, 0:ncols], [(w3t[:, kc, :], uT[:, kc, 0:ncols]) for kc in range(8)], r=[w3r, R_u], w=[r3])
            ta, tr = nextA()
            B.act(ta[:, 0:ncols], p1[:, 0:ncols], AF.Silu, r=[r1], w=[tr])
            B.tt(gT[:, hc, 0:ncols], ta[:, 0:ncols], p3[:, 0:ncols], ALU.mult, r=[tr, r3], w=[R_gT])
        for oc in range(8):
            w2t, w2r = B.load_w(w2d, 11, oc * 128, 128)
            w2u, w2s = B.load_w(w2d, 11, oc * 128, 128, k0=11)
            po, ro = banks[4 + oc % 2]
            B.mms(po[:, 0:ncols], [(w2t[:, hc, :], gT[:, hc, 0:ncols]) for hc in range(11)]
                  + [(w2u[:, hc, :], gT[:, 11 + hc, 0:ncols]) for hc in range(11)], r=[w2r, w2s, R_gT], w=[ro])
            for (c0, n, m) in groups:
                B.stt(xT[:, oc, c0:c0 + n], po[:, c0:c0 + n], Gm[:, s, oc, m:m + 1], xT[:, oc, c0:c0 + n],
                      ALU.mult, ALU.add, r=[ro, R_AG], w=[R_x])

    logf = B.sb("logf", [128, NT + 2, 16], F32)
    dtsb = B.sb("dtsb", [128, NT + 2, 32], F32)
    bfb = B.sb("bfb", [128, 16], F32)
    dtb = B.sb("dtb", [128, 32], F32)
    cw = B.sb("cw", [128, 24, 4], F32)
    cbv = B.sb("cbv", [128, 24], F32)
    halo = B.sb("halo", [128, 24, 3], F32)
    R_lf, R_dt, R_sm, R_halo = Res(), Res(), Res(), Res()
    B.dma(bfb[:], bf_d[:, :], w=[R_sm])
    B.dma(dtb[:], dtb_d[:, :], w=[R_sm])
    B.dma(cw[:], cw_d[:, :, :], w=[R_sm])
    B.dma(cbv[:], cb_d[:, :], w=[R_sm])
    P.add("pool", lambda e: e.memset(halo[:], 0.0), w=[R_halo])
    flg = B.sb("flg", [128, 24], F32)
    B.dma(flg[:], flg_d[:, :], w=[R_sm])

    xb = [(B.sb(f"xb{i}", [128, TB + 3], F32), Res()) for i in range(2)]
    vst = [(B.sb(f"vst{i}", [128, 4, 65], BF16), Res()) for i in range(2)]
    kst = [(B.sb("kst0", [64, TB], F32), Res())] * 2
    onesrow = B.sb("onesrow", [66, TB], BF16)
    P.add("pool", lambda e: e.memset(onesrow[:], 1.0), w=[R_const])
    for i in range(2):
        P.add("pool", lambda e, i=i: e.memset(vst[i][0][:], 1.0), w=[vst[i][1]])
    for h in range(H_A):
        for bb in range(T // TB):
            B.dma(kg_s[h, 64:66, bb * TB:(bb + 1) * TB], onesrow[64:66, :], r=[R_const], w=[R_kg], q="pool")

    def softplus_to(out, in_ps, biasbc, n, r, w, neg_in=False, pn=128):
        ta, tr = nextA()
        tb2, tr2 = nextA()
        B.tt(ta[0:pn, 0:n], in_ps, biasbc, ALU.add, r=r, w=[tr])
        if neg_in:
            B.ts(ta[0:pn, 0:n], ta[0:pn, 0:n], -1.0, None, ALU.mult, r=[tr], w=[tr])
        B.stt(tb2[0:pn, 0:n], ta[0:pn, 0:n], -1.0, ta[0:pn, 0:n], ALU.mult, ALU.max, r=[tr], w=[tr2])
        B.act(tb2[0:pn, 0:n], tb2[0:pn, 0:n], AF.Exp, r=[tr2], w=[tr2], scale=-1.0)
        B.act(tb2[0:pn, 0:n], tb2[0:pn, 0:n], AF.Ln, r=[tr2, R_const], w=[tr2], bias=onec[0:pn, :], scale=1.0)
        B.stt(out, ta[0:pn, 0:n], 0.0, tb2[0:pn, 0:n], ALU.max, ALU.add, r=[tr, tr2], w=w)

    def win_block(blk, groups, ncols, ctxj=None):
        t0 = blk * TB
        ntt = ncols // 128
        B.cut(5)
        gq = wpairs(B, win_d, 8, C_Q, 64, H_A)
        gk = wpairs(B, win_d, 8, C_K, 64, H_A)
        for h in range(H_A):
            if ctxj is None:
                wt, wr = gq(h)
                pt, pr = banks[h % 2]
                B.mms(pt[0:64, 0:ncols], [(wt[:, kc, :], uT[:, kc, 0:ncols]) for kc in range(8)], r=[wr, R_u], w=[pr])
                tb_, tbr = nextB()
                B.ts(tb_[0:64, 0:ncols], pt[0:64, 0:ncols], 0.125, None, ALU.mult, r=[pr], w=[tbr])
                B.dma(q_s[h, 0:64, t0:t0 + ncols], tb_[0:64, 0:ncols], r=[tbr], w=[R_q], q="pool")
            wt, wr = gk(h)
            pt, pr = banks[2 + h % 2]
            B.mms(pt[0:64, 0:ncols], [(wt[:, kc, :], uT[:, kc, 0:ncols]) for kc in range(8)], r=[wr, R_u], w=[pr])
            if ctxj is None:
                kf, kr = kst[h % 2]
                B.cp(kf[:, 0:ncols], pt[0:64, 0:ncols], r=[pr], w=[kr])
                B.dma(kT_o[h * 64:(h + 1) * 64, t0:t0 + ncols], kf[:, 0:ncols], r=[kr], w=[], q="pool")
            tb_, tbr = nextB()
            B.cp(tb_[0:64, 0:ncols], pt[0:64, 0:ncols], r=[pr], w=[tbr], eng="act")
            if ctxj is None:
                B.dma(kg_s[h, 0:64, t0:t0 + ncols], tb_[0:64, 0:ncols], r=[tbr], w=[R_kg], q="pool")
            else:
                B.dma(kgA_v[ctxj, h, 0:64, t0:t0 + ncols], tb_[0:64, 0:ncols], r=[tbr], w=[R_kgA], q="pool")
        B.cut(6)
        for cg in range(4):
            wt, wr = B.load_w(win_d, 8, C_V + cg * 256, 256)
            for tt_ in range(ntt):
                pt, pr = banks[4 + tt_ % 2]
                B.mms(pt[:, 0:256], [(uT[:, kc, tt_ * 128:(tt_ + 1) * 128], wt[:, kc, :]) for kc in range(8)],
                      r=[wr, R_u], w=[pr])
                if ctxj is None:
                    ta, tr = nextA()
                    B.cp(ta[:, 0:256], pt[:, 0:256], r=[pr], w=[tr])
                    B.dma(v_o[t0 + tt_ * 128:t0 + (tt_ + 1) * 128, cg * 256:(cg + 1) * 256], ta[:, 0:256], r=[tr], w=[], q="pool")
                vs, vr = vst[(cg * ntt + tt_) % 2]
                B.cp(vs[:, 0:4, 0:64], pt[:, 0:256].rearrange("p (h d) -> p h d", h=4), r=[pr], w=[vr], eng="act")
                gt = (t0 // 128) + tt_
                if ctxj is None:
                    B.dma(vg_s[cg * 4:(cg + 1) * 4, gt, :, :].rearrange("h p d -> p h d"), vs[:, 0:4, :], r=[vr], w=[R_vg], q="pool")
                else:
                    B.dma(vgA_v[ctxj, cg * 4:(cg + 1) * 4, gt, :, :].rearrange("h p d -> p h d"), vs[:, 0:4, :], r=[vr], w=[R_vgA], q="pool")
        B.cut(7)
        wt, wr = B.load_w(win_d, 8, C_F, 16)
        for tt_ in range(ntt):
            gt = (t0 // 128) + tt_
            pt, pr = banks[6]
            B.mms(pt[:, 0:16], [(uT[:, kc, tt_ * 128:(tt_ + 1) * 128], wt[:, kc, :]) for kc in range(8)], r=[wr, R_u], w=[pr])
            ta, tr = nextA()
            softplus_to(ta[:, 0:16], pt[:, 0:16], bfb[:], 16, r=[pr, R_sm], w=[tr], neg_in=True)
            B.ts(logf[:, gt, :], ta[:, 0:16], -1.0, None, ALU.mult, r=[tr], w=[R_lf])
            if ctxj is None:
                B.dma(lf_o[gt * 128:(gt + 1) * 128, :], logf[:, gt, :], r=[R_lf], w=[], q="pool")
        if B.stage < 2:
            return
        wt, wr = B.load_w(win_d, 8, C_DT, 32)
        for tt_ in range(ntt):
            gt = (t0 // 128) + tt_
            pt, pr = banks[6]
            B.mms(pt[:, 0:32], [(uT[:, kc, tt_ * 128:(tt_ + 1) * 128], wt[:, kc, :]) for kc in range(8)], r=[wr, R_u], w=[pr])
            softplus_to(dtsb[:, gt, :], pt[:, 0:32], dtb[:], 32, r=[pr, R_sm], w=[R_dt])
        for cg in range(8 if ctxj is None else 0):
            wt, wr = B.load_w(win_d, 8, C_Z + cg * 256, 256)
            for tt_ in range(ntt):
                pt, pr = banks[4 + tt_ % 2]
                B.mms(pt[:, 0:256], [(uT[:, kc, tt_ * 128:(tt_ + 1) * 128], wt[:, kc, :]) for kc in range(8)],
                      r=[wr, R_u], w=[pr])
                tb_, tbr = nextB()
                B.cp(tb_[:, 0:256], pt[:, 0:256], r=[pr], w=[tbr], eng="act")
                B.dma(z_s[t0 + tt_ * 128:t0 + (tt_ + 1) * 128, cg * 256:(cg + 1) * 256], tb_[:, 0:256], r=[tbr], w=[R_z], q="pool")
        nxc = 24 if ctxj is None else 20
        gx = wpairs(B, win_d, 8, C_X, 128, nxc)
        for c in range(nxc):
            wt, wr = gx(c)
            pt, pr = banks[c % 2]
            B.mms(pt[:, 0:ncols], [(wt[:, kc, :], uT[:, kc, 0:ncols]) for kc in range(8)], r=[wr, R_u], w=[pr])
            xbt, xr = xb[c % 2]
            B.cp(xbt[:, 0:3], halo[:, c, :], r=[R_halo], w=[xr])
            B.cp(xbt[:, 3:3 + ncols], pt[:, 0:ncols], r=[pr], w=[xr])
            B.cp(halo[:, c, :], xbt[:, ncols:ncols + 3], r=[xr], w=[R_halo])
            if blk == NBLK - 1 and ctxj is None:
                B.dma(cv_o[c * 128:(c + 1) * 128, :], xbt[:, ncols:ncols + 3], r=[xr], w=[], q="pool")
            ta, tr = nextA()
            B.ts(ta[:, 0:ncols], xbt[:, 0:ncols], cw[:, c, 0:1], cbv[:, c:c + 1], ALU.mult, ALU.add, r=[xr, R_sm], w=[tr])
            for i in range(1, 4):
                B.stt(ta[:, 0:ncols], xbt[:, i:i + ncols], cw[:, c, i:i + 1], ta[:, 0:ncols], ALU.mult, ALU.add,
                      r=[xr, R_sm, tr], w=[tr])
            tb_, tbr = nextB()
            B.act(tb_[:, 0:ncols], ta[:, 0:ncols], AF.Silu, r=[tr], w=[tbr])
            if c >= 20:
                B.dma(CT_s[(c - 20) * 128:(c - 19) * 128, t0:t0 + ncols], tb_[:, 0:ncols], r=[tbr], w=[R_CT], q="pool")
                continue
            if c >= 16 and ctxj is None:
                B.dma(BT_s[(c - 16) * 128:(c - 15) * 128, t0:t0 + ncols], tb_[:, 0:ncols], r=[tbr], w=[R_BT], q="pool")
            for tt_ in range(ntt):
                P.add("pe", lambda e, tt_=tt_, tb_=tb_: e.transpose(ptT[:, tt_ * 128:(tt_ + 1) * 128],
                                                                     tb_[:, tt_ * 128:(tt_ + 1) * 128], ident_b[:]),
                      r=[tbr, R_const], w=[prT])
            tb2, tbr2 = nextB()
            B.cp(tb2[:, 0:ncols], ptT[:, 0:ncols], r=[prT], w=[tbr2])
            for tt_ in range(ntt):
                rows = slice(t0 + tt_ * 128, t0 + (tt_ + 1) * 128)
                if c < 16:
                    B.dma(xs_s[rows, c * 128:(c + 1) * 128], tb2[:, tt_ * 128:(tt_ + 1) * 128], r=[tbr2], w=[R_xs], q="pool")
                else:
                    B.dma(Bt_s[rows, (c - 16) * 128:(c - 15) * 128], tb2[:, tt_ * 128:(tt_ + 1) * 128], r=[tbr2], w=[R_Bt], q="pool")
        for gi, (c0, dst, rr) in enumerate([(C_GA, ga_s, R_ga), (C_GS, gs_s, R_gs)] if ctxj is None else []):
            gg = wpairs(B, win_d, 8, c0, 128, 8)
            for c in range(8):
                wt, wr = gg(c)
                pt, pr = banks[2 + c % 2]
                B.mms(pt[:, 0:ncols], [(wt[:, kc, :], uT[:, kc, 0:ncols]) for kc in range(8)], r=[wr, R_u], w=[pr])
                tb_, tbr = nextB()
                B.act(tb_[:, 0:ncols], pt[:, 0:ncols], AF.Sigmoid, r=[pr], w=[tbr])
                B.dma(dst[c * 128:(c + 1) * 128, t0:t0 + ncols], tb_[:, 0:ncols], r=[tbr], w=[rr], q="pool")


    xTd_v = xT_d.rearrange("(k p) t -> p k t", p=128)
    x1o_v = x1_o.rearrange("(k p) t -> p k t", p=128)

    xs1 = B.sb("xs1", [128, 8, 32], F32)
    sconv = B.sb("sconv", [128, 24, 2, 3], F32)
    R_xs1 = Res()

    def run_own_p1():
        NS_ = 32
        NX_ = 35
        sgroups = [(0, 16, 1), (16, 16, 2), (32, 3, 0)]
        B.dma(xT[:, :, 0:NX_], xsT_d.rearrange("(k p) t -> p k t", p=128), w=[R_x])
        rms_mod(xT, 0, sgroups, NX_)
        ffn(0, w1a_d, w3a_d, w2a_d, sgroups, NX_)
        rms_mod(xT, 1, sgroups, NX_)
        B.dma(sconv[:], scv_d[:, :, :, :], w=[R_sm])
        B.cp(xs1[:], xT[:, :, 0:NS_], r=[R_x], w=[R_xs1])
        for h in range(H_A):
            wt, wr = B.load_w(win_d, 8, C_Q + h * 64, 64)
            pt, pr = banks[h % 2]
            B.mms(pt[0:64, 0:NS_], [(wt[:, kc, :], uT[:, kc, 0:NS_]) for kc in range(8)], r=[wr, R_u], w=[pr])
            tb_, tbr = nextB()
            B.ts(tb_[0:64, 0:NS_], pt[0:64, 0:NS_], 0.125, None, ALU.mult, r=[pr], w=[tbr])
            B.dma(q2_s[h, 0:64, :], tb_[0:64, 0:NS_], r=[tbr], w=[R_s2], q="pool")
            wt, wr = B.load_w(win_d, 8, C_K + h * 64, 64)
            pt, pr = banks[2 + h % 2]
            B.mms(pt[0:64, 0:NS_], [(wt[:, kc, :], uT[:, kc, 0:NS_]) for kc in range(8)], r=[wr, R_u], w=[pr])
            kf, kr = kst[h % 2]
            B.cp(kf[:, 0:NS_], pt[0:64, 0:NS_], r=[pr], w=[kr])
            B.dma(ksT_o[h * 64:(h + 1) * 64, :], kf[:, 0:NS_], r=[kr], w=[], q="pool")
            tb_, tbr = nextB()
            B.cp(tb_[0:64, 0:NS_], pt[0:64, 0:NS_], r=[pr], w=[tbr], eng="act")
            B.dma(k2_s[h, 0:64, :], tb_[0:64, 0:NS_], r=[tbr], w=[R_s2], q="pool")
            B.dma(k2_s[h, 64:66, :], onesrow[64:66, 0:NS_], r=[R_const], w=[R_s2], q="pool")
        for cg in range(4):
            wt, wr = B.load_w(win_d, 8, C_V + cg * 256, 256)
            for sq_ in range(2):
                pt, pr = banks[4 + sq_]
                B.mms(pt[0:16, 0:256], [(uT[:, kc, sq_ * 16:(sq_ + 1) * 16], wt[:, kc, :]) for kc in range(8)], r=[wr, R_u], w=[pr])
                ta, tr = nextA()
                B.cp(ta[0:16, 0:256], pt[0:16, 0:256], r=[pr], w=[tr])
                B.dma(vs_o[sq_ * 16:(sq_ + 1) * 16, cg * 256:(cg + 1) * 256], ta[0:16, 0:256], r=[tr], w=[], q="pool")
                vs, vr = vst[sq_]
                B.cp(vs[0:16, 0:4, 0:64], pt[0:16, 0:256].rearrange("p (h d) -> p h d", h=4), r=[pr], w=[vr], eng="act")
                B.dma(v2_s[cg * 4:(cg + 1) * 4, sq_, :, :].rearrange("h p d -> p h d"), vs[0:16, 0:4, :], r=[vr], w=[R_s2], q="pool")
        wt, wr = B.load_w(win_d, 8, C_F, 16)
        for sq_ in range(2):
            pt, pr = banks[6]
            B.mms(pt[0:16, 0:16], [(uT[:, kc, sq_ * 16:(sq_ + 1) * 16], wt[:, kc, :]) for kc in range(8)], r=[wr, R_u], w=[pr])
            ta, tr = nextA()
            softplus_to(ta[0:16, 0:16], pt[0:16, 0:16], bfb[0:16, :], 16, r=[pr, R_sm], w=[tr], neg_in=True, pn=16)
            B.ts(logf[0:16, NT + sq_, :], ta[0:16, 0:16], -1.0, None, ALU.mult, r=[tr], w=[R_lf])
            B.dma(lfs_o[sq_ * 16:(sq_ + 1) * 16, :], logf[0:16, NT + sq_, :], r=[R_lf], w=[], q="pool")
        if B.stage >= 2:
            wt, wr = B.load_w(win_d, 8, C_DT, 32)
            for sq_ in range(2):
                pt, pr = banks[6]
                B.mms(pt[0:16, 0:32], [(uT[:, kc, sq_ * 16:(sq_ + 1) * 16], wt[:, kc, :]) for kc in range(8)], r=[wr, R_u], w=[pr])
                softplus_to(dtsb[0:16, NT + sq_, :], pt[0:16, 0:32], dtb[0:16, :], 32, r=[pr, R_sm], w=[R_dt], pn=16)
            for cg in range(8):
                wt, wr = B.load_w(win_d, 8, C_Z + cg * 256, 256)
                for sq_ in range(2):
                    pt, pr = banks[4 + sq_]
                    B.mms(pt[0:16, 0:256], [(uT[:, kc, sq_ * 16:(sq_ + 1) * 16], wt[:, kc, :]) for kc in range(8)], r=[wr, R_u], w=[pr])
                    tb_, tbr = nextB()
                    B.cp(tb_[0:16, 0:256], pt[0:16, 0:256], r=[pr], w=[tbr], eng="act")
                    B.dma(z2_s[sq_ * 16:(sq_ + 1) * 16, cg * 256:(cg + 1) * 256], tb_[0:16, 0:256], r=[tbr], w=[R_s2], q="pool")
            for c in range(24):
                wt, wr = B.load_w(win_d, 8, C_X + c * 128, 128)
                pt, pr = banks[c % 2]
                B.mms(pt[:, 0:NX_], [(wt[:, kc, :], uT[:, kc, 0:NX_]) for kc in range(8)], r=[wr, R_u], w=[pr])
                xbt, xr = xb[c % 2]
                B.cp(xbt[:, 0:3], sconv[:, c, 0, :], r=[R_sm], w=[xr])
                B.cp(xbt[:, 3:19], pt[:, 0:16], r=[pr], w=[xr])
                B.cp(xbt[:, 19:22], sconv[:, c, 1, :], r=[R_sm], w=[xr])
                B.cp(xbt[:, 22:38], pt[:, 16:32], r=[pr], w=[xr])
                B.ts(halo[:, c, :], pt[:, 32:35], flg[:, 16:17], None, ALU.mult, r=[pr, R_sm], w=[R_halo])
                B.dma(cvs_o[c * 128:(c + 1) * 128, 0:3], xbt[:, 16:19], r=[xr], w=[], q="pool")
                B.dma(cvs_o[c * 128:(c + 1) * 128, 3:6], xbt[:, 35:38], r=[xr], w=[], q="pool")
                ta, tr = nextA()
                B.ts(ta[:, 0:35], xbt[:, 0:35], cw[:, c, 0:1], cbv[:, c:c + 1], ALU.mult, ALU.add, r=[xr, R_sm], w=[tr])
                for i in range(1, 4):
                    B.stt(ta[:, 0:35], xbt[:, i:i + 35], cw[:, c, i:i + 1], ta[:, 0:35], ALU.mult, ALU.add, r=[xr, R_sm, tr], w=[tr])
                tb_, tbr = nextB()
                B.act(tb_[:, 0:35], ta[:, 0:35], AF.Silu, r=[tr], w=[tbr])
                offs = (0, 19)
                if c >= 20:
                    for sq_ in range(2):
                        B.dma(CT2_s[(c - 20) * 128:(c - 19) * 128, sq_ * 16:(sq_ + 1) * 16], tb_[:, offs[sq_]:offs[sq_] + 16], r=[tbr], w=[R_s2], q="pool")
                    continue
                if c >= 16:
                    for sq_ in range(2):
                        B.dma(BT2_s[(c - 16) * 128:(c - 15) * 128, sq_ * 16:(sq_ + 1) * 16], tb_[:, offs[sq_]:offs[sq_] + 16], r=[tbr], w=[R_s2], q="pool")
                for sq_ in range(2):
                    P.add("pe", lambda e, sq_=sq_, tb_=tb_: e.transpose(ptT[0:16, sq_ * 128:(sq_ + 1) * 128], tb_[:, offs[sq_]:offs[sq_] + 16], ident_b[:]),
                          r=[tbr, R_const], w=[prT])
                tb2, tbr2 = nextB()
                B.cp(tb2[0:16, 0:256], ptT[0:16, 0:256], r=[prT], w=[tbr2])
                for sq_ in range(2):
                    rows2 = slice(sq_ * 16, (sq_ + 1) * 16)
                    if c < 16:
                        B.dma(xs2_s[rows2, c * 128:(c + 1) * 128], tb2[0:16, sq_ * 128:(sq_ + 1) * 128], r=[tbr2], w=[R_s2], q="pool")
                    else:
                        B.dma(Bt2_s[rows2, (c - 16) * 128:(c - 15) * 128], tb2[0:16, sq_ * 128:(sq_ + 1) * 128], r=[tbr2], w=[R_s2], q="pool")
            for (c0, dst) in [(C_GA, ga2_s), (C_GS, gs2_s)]:
                for c in range(8):
                    wt, wr = B.load_w(win_d, 8, c0 + c * 128, 128)
                    pt, pr = banks[2 + c % 2]
                    B.mms(pt[:, 0:NS_], [(wt[:, kc, :], uT[:, kc, 0:NS_]) for kc in range(8)], r=[wr, R_u], w=[pr])
                    tb_, tbr = nextB()
                    B.act(tb_[:, 0:NS_], pt[:, 0:NS_], AF.Sigmoid, r=[pr], w=[tbr])
                    B.dma(dst[c * 128:(c + 1) * 128, :], tb_[:, 0:NS_], r=[tbr], w=[R_s2], q="pool")

        xTd_v = xT_d.rearrange("(k p) t -> p k t", p=128)
        x1o_v = x1_o.rearrange("(k p) t -> p k t", p=128)
        for blk in range(NBLK):
            t0 = blk * TB
            groups = [(0, TB, 0)]
            B.dma(xT[:, :, :], xTd_v[:, :, t0:t0 + TB], w=[R_x])
            if B.cutn <= 2:
                B.dma(x1o_v[:, :, t0:t0 + TB], xT[:, :, :], r=[R_x], w=[R_x1], q="pool")
            B.cut(2)
            rms_mod(xT, 0, groups, TB)
            if B.cutn <= 3:
                B.cp(xT[:, :, :], uT[:, :, :], r=[R_u], w=[R_x])
                B.dma(x1o_v[:, :, t0:t0 + TB], xT[:, :, :], r=[R_x], w=[R_x1], q="pool")
            B.cut(3)
            ffn(0, w1a_d, w3a_d, w2a_d, groups, TB)
            B.dma(x1o_v[:, :, t0:t0 + TB], xT[:, :, :], r=[R_x], w=[R_x1], q="pool")
            B.cut(4)
            rms_mod(xT, 1, groups, TB)
            win_block(blk, groups, TB)


    if B.stage < 3:
        run_own_p1()
        return
    tri = B.sb("tri", [128, 128], F32)
    mneg = B.sb("mneg", [128, 128], F32)
    e0row = B.sb("e0row", [128, 128], F32)
    selt = B.sb("selt", [16, 16, 66], F32)
    selc = B.sb("selc", [128, 2], F32)
    a_bc = B.sb("a_bc", [128, 32], F32)
    dsk = B.sb("dsk", [128, 32], F32)
    gssd = B.sb("gssd", [128, 16], F32)
    R_c3 = Res()
    B.dma(tri[:], tri_d[:, :], w=[R_c3])
    B.dma(mneg[:], mneg_d[:, :], w=[R_c3])
    B.dma(e0row[:], e0_d[:, :], w=[R_c3])
    B.dma(selt[:], sel_d[:, :, :], w=[R_c3])
    B.dma(selc[:], selc_d[:, :], w=[R_c3])
    B.dma(a_bc[:], alog_d[:, :], w=[R_c3])
    B.dma(dsk[:], dsk_d[:, :], w=[R_c3])
    B.dma(gssd[:], gssd_d[:, :], w=[R_c3])
    B.act(a_bc[:], a_bc[:], AF.Exp, r=[R_c3], w=[R_c3])
    B.ts(a_bc[:], a_bc[:], -1.0, None, ALU.mult, r=[R_c3], w=[R_c3])

    Fc = B.sb("Fc", [128, NT, 16], F32)
    carry = B.sb("carry", [128, 16], F32)
    R_F, R_carry = Res(), Res()
    arena = B.sb("arena", [128, 8192], BF16)
    R_ar = [Res() for _ in range(4)]
    xtm_s = [arena[:, 0:2048], arena[:, 2048:4096]]
    ztm_s = [arena[:, 4096:6144], arena[:, 6144:8192]]
    oT = arena[0:64, :].rearrange("p (h t) -> p h t", h=16)
    bc_s = [(B.sb(f"bcs{i}", [128, 3, 512], BF16), Res()) for i in range(2)]
    Hst = B.sb("Hst", [128, DIN], F32)
    Hb = B.sb("Hb", [128, DIN], BF16)
    yz = B.sb("yz", [128, DIN], F32)
    sm = B.sb("sm", [128, 8, 32], F32)
    cbm = B.sb("cbm", [128, 4, 128], F32)
    xd = B.sb("xd", [128, DIN], BF16)
    ynb = xd
    wTb = [(B.sb(f"wTb{i}", [128, 4, 128], BF16), Res()) for i in range(2)]
    D4 = [(B.sb(f"D4{i}", [128, 4, 128], F32), Res()) for i in range(2)]
    sg4 = [(B.sb(f"sg4{i}", [128, 4, 128], F32), Res()) for i in range(2)]
    ssq = B.sb("ssq", [128, 4], F32)
    R_H, R_Hb, R_yz, R_ynb, R_sm, R_cbm, R_xd, R_ssq = [Res() for _ in range(8)]
    R_ynb = R_xd
    P.add("pool", lambda e: e.memset(Hst[:], 0.0), w=[R_H])
    ynT = gT[:, 0:16, :]
    BT_v = BT_s.rearrange("(g n) t -> n g t", n=128)
    CT_v = CT_s.rearrange("(g n) t -> n g t", n=128)

    def bc3(ap2, n, m):
        return ap2.unsqueeze(2).broadcast_to([128, n, m])

    def ssd_tile(gt, want_y=True, L=128, samp=None, maskj=None):
        sl = gt % 2
        if samp is None:
            rows = slice(gt * 128, (gt + 1) * 128)
            src_x, src_z, src_Bt, src_BT, src_CT = xs_s, z_s, Bt_s, BT_v, CT_v
            rx_, rz_, rbt_, rBT_, rCT_ = R_xs, R_z, R_Bt, R_BT, R_CT
            dti = gt
            ycol0 = (gt % 4) * 128
        else:
            sl = samp
            rows = slice(samp * 16, samp * 16 + 16)
            src_x, src_z, src_Bt, src_BT, src_CT = xs2_s, z2_s, Bt2_s, BT2_v, CT2_v
            rx_ = rz_ = rbt_ = rBT_ = rCT_ = R_s2
            dti = NT + samp
            ycol0 = samp * 16
        xtm, ztm = xtm_s[sl], ztm_s[sl]
        Rx_, Rz_ = R_ar[sl], R_ar[2 + sl]
        bct, Rb_ = bc_s[sl]
        B.dma(xtm[0:L, :], src_x[rows, :], r=[rx_], w=[Rx_])
        if want_y:
            B.dma(ztm[0:L, :], src_z[rows, :], r=[rz_], w=[Rz_])
        B.dma(bct[0:L, 0, :], src_Bt[rows, :], r=[rbt_], w=[Rb_])
        if want_y:
            B.dma(bct[:, 1, 0:4 * L].rearrange("p (g t) -> p g t", g=4), src_BT[:, :, rows], r=[rBT_], w=[Rb_])
            B.dma(bct[:, 2, 0:4 * L].rearrange("p (g t) -> p g t", g=4), src_CT[:, :, rows], r=[rCT_], w=[Rb_])
        Btm = bct[0:L, 0, :].rearrange("p (g n) -> p g n", g=4)
        BTf = bct[:, 1, 0:4 * L].rearrange("p (g t) -> p g t", g=4)
        CTf = bct[:, 2, 0:4 * L].rearrange("p (g t) -> p g t", g=4)
        xt3 = xtm[0:L, :].rearrange("p (h d) -> p h d", h=32)
        dA, acs, ea, de, tmpv = [sm[0:L, j, :] for j in (0, 1, 3, 4, 6)]
        al, cd = sm[:, 2, :], sm[:, 5, :]
        dtv = dtsb[0:L, dti, :]
        B.tt(dA, dtv, a_bc[0:L, :], ALU.mult, r=[R_dt, R_c3], w=[R_sm])
        pt, pr = banks[0]
        P.add("pe", lambda e: e.matmul(pt[0:L, 0:32], tri[0:L, 0:L], dA, start=True, stop=True), r=[R_c3, R_sm], w=[pr])
        P.add("pe", lambda e: e.matmul(pt[:, 32:64], ones_f[0:L, :], dA, start=True, stop=True), r=[R_const, R_sm], w=[pr])
        B.cp(acs, pt[0:L, 0:32], r=[pr], w=[R_sm])
        B.cp(al, pt[:, 32:64], r=[pr], w=[R_sm])
        B.act(ea, acs, AF.Exp, r=[R_sm], w=[R_sm])
        B.act(cd, al, AF.Exp, r=[R_sm], w=[R_sm])
        if maskj is not None:
            B.ts(cd, cd, -1.0, flg[:, maskj:maskj + 1], ALU.add, ALU.mult, r=[R_sm], w=[R_sm])
            B.ts(cd, cd, 1.0, None, ALU.add, r=[R_sm], w=[R_sm])
        B.tt(tmpv, al[0:L, :], acs, ALU.subtract, r=[R_sm], w=[R_sm])
        B.act(tmpv, tmpv, AF.Exp, r=[R_sm], w=[R_sm])
        B.tt(de, tmpv, dtv, ALU.mult, r=[R_sm, R_dt], w=[R_sm])
        if want_y:
            B.cp(Hb[:], Hst[:], r=[R_H], w=[R_Hb], eng="act")
            pc, prc = banks[1]
            for g in range(4):
                P.add("pe", lambda e, g=g: e.matmul(pc[0:L, g * L:(g + 1) * L], BTf[:, g, :], CTf[:, g, :], start=True, stop=True),
                      r=[Rb_], w=[prc])
            cbv_ = cbm[:].rearrange("p g t -> p (g t)")[0:L, 0:4 * L].rearrange("p (g t) -> p g t", g=4)
            B.tt(cbv_, pc[0:L, 0:4 * L].rearrange("p (g t) -> p g t", g=4), tri[0:L, 0:L].unsqueeze(1).broadcast_to([L, 4, L]), ALU.mult,
                 r=[prc, R_c3], w=[R_cbm])
            for g in range(4):
                pyd, pryd = banks[4]
                for hb in range(2):
                    h0 = g * 8 + hb * 4
                    d4t, rd4 = D4[hb]
                    s4t, rs4 = sg4[hb]
                    wt4t, rw4 = wTb[hb]
                    d4 = d4t[:].rearrange("p g t -> p (g t)")[0:L, 0:4 * L].rearrange("p (g t) -> p g t", g=4)
                    s4 = s4t[:].rearrange("p g t -> p (g t)")[0:L, 0:4 * L].rearrange("p (g t) -> p g t", g=4)
                    wt4 = wt4t[:].rearrange("p g t -> p (g t)")[0:L, 0:4 * L].rearrange("p (g t) -> p g t", g=4)
                    B.tt(d4, ident_f[0:L, 0:L].unsqueeze(1).broadcast_to([L, 4, L]),
                         acs[:, h0:h0 + 4].unsqueeze(2).broadcast_to([L, 4, L]), ALU.mult, r=[R_const, R_sm], w=[rd4])
                    pb, prb = banks[2 + hb]
                    P.add("pe", lambda e, d4t=d4t, pb=pb: e.matmul(pb[0:L, 0:4 * L], ones_f[0:L, 0:L],
                                                                    d4t[:].rearrange("p g t -> p (g t)")[0:L, 0:4 * L], start=True, stop=True),
                          r=[R_const, rd4], w=[prb])
                    for j in range(4):
                        B.stt(s4[:, j, :], pb[0:L, j * L:(j + 1) * L], acs[:, h0 + j:h0 + j + 1], mneg[0:L, 0:L], ALU.subtract, ALU.add,
                              r=[prb, R_sm, R_c3], w=[rs4])
                    B.act(s4, s4, AF.Exp, r=[rs4], w=[rs4])
                    for j in range(4):
                        B.stt(wt4[:, j, :], s4[:, j, :], dtv[:, h0 + j:h0 + j + 1], cbv_[:, g, :], ALU.mult, ALU.mult,
                              r=[rs4, R_dt, R_cbm], w=[rw4])
                    for j in range(4):
                        hh = hb * 4 + j
                        P.add("pe", lambda e, j=j, hh=hh, wt4=wt4, h0=h0: e.matmul(pyd[0:L, hh * 64:(hh + 1) * 64], wt4[:, j, :], xt3[:, h0 + j, :],
                                                                                    start=True, stop=True), r=[rw4, Rx_], w=[pryd])
                pyo, pryo = banks[5]
                P.add("pe", lambda e, g=g, pyo=pyo: e.matmul(pyo[0:L, :], CTf[:, g, :], Hb[:, g * 512:(g + 1) * 512], start=True, stop=True),
                      r=[Rb_, R_Hb], w=[pryo])
                yg = yz[0:L, g * 512:(g + 1) * 512].rearrange("p (h d) -> p h d", h=8)
                B.tt(yg, pyo[0:L, :].rearrange("p (h d) -> p h d", h=8), ea[:, g * 8:(g + 1) * 8].unsqueeze(2).broadcast_to([L, 8, 64]),
                     ALU.mult, r=[pryo, R_sm], w=[R_yz])
                B.tt(yg, pyd[0:L, :].rearrange("p (h d) -> p h d", h=8), yg, ALU.add, r=[pryd, R_yz], w=[R_yz])
                ta, tr = nextA()
                ta3 = ta[0:L, :].rearrange("p (h d) -> p h d", h=8)
                B.tt(ta3, xt3[:, g * 8:(g + 1) * 8, :], dsk[0:L, g * 8:(g + 1) * 8].unsqueeze(2).broadcast_to([L, 8, 64]), ALU.mult,
                     r=[Rx_, R_c3], w=[tr])
                B.tt(yg, yg, ta3, ALU.add, r=[tr, R_yz], w=[R_yz])
        B.tt(xd[0:L, :].rearrange("p (h d) -> p h d", h=32), xt3, de.unsqueeze(2).broadcast_to([L, 32, 64]), ALU.mult,
             r=[Rx_, R_sm], w=[R_xd])
        for g in range(4):
            pS, prS = banks[6]
            P.add("pe", lambda e, g=g, pS=pS: e.matmul(pS[:, :], Btm[:, g, :], xd[0:L, g * 512:(g + 1) * 512], start=True, stop=True),
                  r=[Rb_, R_xd], w=[prS])
            Hg = Hst[:, g * 512:(g + 1) * 512].rearrange("p (h d) -> p h d", h=8)
            B.tt(Hg, Hg, bc3(cd[:, g * 8:(g + 1) * 8], 8, 64), ALU.mult, r=[R_sm, R_Hb], w=[R_H])
            if maskj is not None:
                B.stt(Hg, pS[:, :].rearrange("p (h d) -> p h d", h=8), flg[:, maskj:maskj + 1], Hg, ALU.mult, ALU.add, r=[prS], w=[R_H])
            else:
                B.tt(Hg, Hg, pS[:, :].rearrange("p (h d) -> p h d", h=8), ALU.add, r=[prS], w=[R_H])
        if not want_y:
            return
        for g in range(4):
            ta, tr = nextA()
            B.act(ta[0:L, :], ztm[0:L, g * 512:(g + 1) * 512], AF.Silu, r=[Rz_], w=[tr])
            B.tt(yz[0:L, g * 512:(g + 1) * 512], yz[0:L, g * 512:(g + 1) * 512], ta[0:L, :], ALU.mult, r=[tr, R_yz], w=[R_yz])
            ta2, tr2 = nextA()
            P.add("act", lambda e, g=g, ta2=ta2: e.activation(out=ta2[0:L, :], in_=yz[0:L, g * 512:(g + 1) * 512], func=AF.Square,
                                                               accum_out=ssq[0:L, g:g + 1]), r=[R_yz], w=[tr2, R_ssq])
        rs_ = sm[0:L, 7, 0:1]
        P.add("dve", lambda e: e.tensor_reduce(out=rs_, in_=ssq[0:L, 0:4], axis=AX.X, op=ALU.add), r=[R_ssq], w=[R_sm])
        B.act(rs_, rs_, AF.Sqrt, r=[R_sm, R_const], w=[R_sm], bias=epsc[0:L, :], scale=1.0 / DIN)
        P.add("dve", lambda e: e.reciprocal(out=rs_, in_=rs_), r=[R_sm], w=[R_sm])
        B.ts(ynb[0:L, :], yz[0:L, :], rs_, None, ALU.mult, r=[R_yz, R_sm], w=[R_ynb])
        for half in range(2):
            for c in range(8):
                cc = half * 8 + c
                P.add("pe", lambda e, c=c, cc=cc: e.transpose(ptT[:, c * 128:c * 128 + L], ynb[0:L, cc * 128:(cc + 1) * 128], ident_b[0:L, 0:L]),
                      r=[R_ynb, R_const], w=[prT])
            for c in range(8):
                cc = half * 8 + c
                B.ts(ynT[:, cc, ycol0:ycol0 + L], ptT[:, c * 128:c * 128 + L], gssd[:, cc:cc + 1], None, ALU.mult,
                     r=[prT, R_c3], w=[R_gT])

    Kt = [(B.sb(f"Kt{i}", [66, T], BF16), Res()) for i in range(2)]
    Vt = [(B.sb(f"Vt{i}", [128, NT, 65], BF16), Res()) for i in range(2)]
    Qa = [(B.sb(f"Qa{i}", [66, TB], BF16), Res()) for i in range(2)]
    Pt = [(B.sb(f"Pt{i}", [128, TB], BF16), Res()) for i in range(3)]
    FT = sq[0:16, :]
    rbc = B.sb("rbc", [128, 16], F32)
    biasT = B.sb("biasT", [128, NT, 16], F32)
    arb = B.sb("arb", [66, TB], BF16)
    R_FT, R_rbc, R_bias, R_arow, R_rl, R_rlb = [Res() for _ in range(6)]
    R_FT = R_sq
    pti = [0]

    def attn_block(blk):
        t0 = blk * TB
        nkt = 4 * (blk + 1)
        pf, prf = banks[6]
        for j in range(4):
            P.add("pe", lambda e, j=j: e.transpose(pf[0:16, j * 128:(j + 1) * 128], Fc[:, 4 * blk + j, :], ident_f[:]),
                  r=[R_F, R_const], w=[prf])
        B.cp(FT[:, :], pf[0:16, :], r=[prf], w=[R_FT])
        P.add("pe", lambda e: e.matmul(pf[:, 0:16], e0row[:], Fc[:, 4 * blk, :], start=True, stop=True), r=[R_c3, R_F], w=[prf])
        B.cp(rbc[:], pf[:, 0:16], r=[prf], w=[R_rbc])
        B.tt(biasT[:, 0:nkt, :], rbc[:].unsqueeze(1).broadcast_to([128, nkt, 16]), Fc[:, 0:nkt, :], ALU.subtract,
             r=[R_rbc, R_F], w=[R_bias])
        B.tt(rdj[:], delta[:, 0:NCORES, :], rbc[:].unsqueeze(1).broadcast_to([128, NCORES, 16]), ALU.add,
             r=[R_delta, R_rbc], w=[R_rdj])
        kvi = [0]
        for h in range(H_A):
            qa_, qr_ = Qa[h % 2]
            B.dma(qa_[0:64, :], q_s[h, 0:64, t0:t0 + TB], r=[R_q], w=[qr_])
            pa, pra = banks[4]
            P.add("pe", lambda e, h=h: e.matmul(pa[0:66, :], selt[:, h, :], FT[:, :], start=True, stop=True), r=[R_c3, R_FT], w=[pra])
            ar0, rr0 = nextA()
            ar1, rr1 = nextA()
            ar2, rr2 = nextA()
            B.cp(ar0[64:66, :], pa[64:66, :], r=[pra], w=[rr0])
            B.ts(ar1[64:66, :], ar0[64:66, :], ar0[64:66, 0:1], None, ALU.subtract, r=[rr0], w=[rr1])
            B.cp(arb[64:66, :], ar1[64:66, :], r=[rr1], w=[R_arow])
            B.tt(ar0[64:66, :], ar1[64:66, :], arb[64:66, :], ALU.subtract, r=[rr1, R_arow], w=[rr0])
            B.ts(ar2[64:66, :], arb[64:66, :], selc[64:66, 0:1], None, ALU.mult, r=[R_arow, R_c3], w=[rr2])
            B.stt(qa_[64:66, :], ar0[64:66, :], selc[64:66, 1:2], ar2[64:66, :], ALU.mult, ALU.add,
                  r=[rr0, rr2, R_c3], w=[qr_])
            po, pro = banks[2 + h % 2]
            tasks = []
            if CTX:
                for j in range(NCORES - 1):
                    kvi[0] += 1
                    kt_, kr_ = Kt[kvi[0] % 2]
                    vt_, vr_ = Vt[kvi[0] % 2]
                    bj, bjr = biasJ[kvi[0] % 2]

                    def pre(j=j, kt_=kt_, kr_=kr_, vt_=vt_, vr_=vr_, bj=bj, bjr=bjr, h=h):
                        B.dma(kt_[:, :], kgA_v[j, h, :, :], r=[R_kgA], w=[kr_])
                        B.dma(vt_[:, :, :], vgA_v[j, h, :, :, :].rearrange("t p d -> p t d"), r=[R_vgA], w=[vr_])
                        B.ts(bj[:, :], FcA[:, j, :].rearrange("p (t h) -> p t h", h=16)[:, :, h], -1.0, rdj[:, j, h:h + 1], ALU.mult, ALU.add,
                             r=[R_FcA, R_rdj], w=[bjr])
                        B.ts(bj[:, :], bj[:, :], flg[:, j:j + 1], flg[:, 8 + j:9 + j], ALU.mult, ALU.add, r=[R_sm], w=[bjr])
                    for kt in range(NT):
                        tasks.append(dict(pre=pre if kt == 0 else None, kt_=kt_, kr_=kr_, vt_=vt_, vr_=vr_, kc=kt * 128, vi=kt, c0=0,
                                          bias=bj[:, kt:kt + 1], bres=bjr, diag=False, stop=False))
            kvi[0] += 1
            kt_, kr_ = Kt[kvi[0] % 2]
            vt_, vr_ = Vt[kvi[0] % 2]

            def pre_own(kt_=kt_, kr_=kr_, vt_=vt_, vr_=vr_, h=h):
                B.dma(kt_[:, 0:t0 + TB], kg_s[h, :, 0:t0 + TB], r=[R_kg], w=[kr_])
                B.dma(vt_[:, 0:nkt, :], vg_s[h, 0:nkt, :, :].rearrange("t p d -> p t d"), r=[R_vg], w=[vr_])
            for kt in range(nkt):
                j = kt - 4 * blk
                tasks.append(dict(pre=pre_own if kt == 0 else None, kt_=kt_, kr_=kr_, vt_=vt_, vr_=vr_, kc=kt * 128, vi=kt,
                                  c0=(128 * j if j > 0 else 0), bias=biasT[:, kt, h:h + 1], bres=R_bias, diag=(j >= 0),
                                  stop=(kt == nkt - 1)))
            ntask = len(tasks)

            sbanks = [banks[0], banks[1], banks[6]]

            def emitS(i, qa_=qa_, qr_=qr_):
                t = tasks[i]
                ps_, prs = sbanks[i % 3]
                if t["pre"] is not None:
                    t["pre"]()
                c0 = t["c0"]
                P.add("pe", lambda e, t=t, ps_=ps_, c0=c0: e.matmul(ps_[:, c0:TB], t["kt_"][:, t["kc"]:t["kc"] + 128], qa_[:, c0:TB],
                                                                    start=True, stop=True), r=[t["kr_"], qr_], w=[prs])

            def emitEPV(i, po=po, pro=pro):
                t = tasks[i]
                ps_, prs = sbanks[i % 3]
                pT_, prp = Pt[i % 3]
                c0 = t["c0"]
                if t["diag"]:
                    ta, tr = nextA()
                    B.tt(ta[:, c0:c0 + 128], ps_[:, c0:c0 + 128], mneg[:], ALU.add, r=[prs, R_c3], w=[tr])
                    B.act(pT_[:, c0:c0 + 128], ta[:, c0:c0 + 128], AF.Exp, r=[tr, t["bres"]], w=[prp], bias=t["bias"], scale=1.0)
                    if c0 + 128 < TB:
                        B.act(pT_[:, c0 + 128:TB], ps_[:, c0 + 128:TB], AF.Exp, r=[prs, t["bres"]], w=[prp], bias=t["bias"], scale=1.0)
                else:
                    B.act(pT_[:, c0:TB], ps_[:, c0:TB], AF.Exp, r=[prs, t["bres"]], w=[prp], bias=t["bias"], scale=1.0)
                st_ = (i == 0)
                sp_ = t["stop"]
                P.add("pe", lambda e, t=t, pT_=pT_, c0=c0, st_=st_, sp_=sp_: e.matmul(po[0:65, c0:TB], t["vt_"][:, t["vi"], :], pT_[:, c0:TB],
                                                                                    start=st_, stop=sp_, skip_group_check=True),
                      r=[t["vr_"], prp], w=[pro])
            emitS(0)
            if ntask > 1:
                emitS(1)
            for i in range(ntask):
                if i + 2 < ntask:
                    emitS(i + 2)
                emitEPV(i)
            rl, R_rl = nextA()
            rlb, R_rlb = nextA()
            P.add("dve", lambda e, po=po, rl=rl: e.reciprocal(out=rl[64:65, :], in_=po[64:65, :]), r=[pro], w=[R_rl])
            pb2, prb2 = banks[5]
            P.add("pe", lambda e, rl=rl: e.matmul(pb2[0:64, :], ones_f[64:65, 0:64], rl[64:65, :], start=True, stop=True), r=[R_const, R_rl], w=[prb2])
            B.cp(rlb[0:64, :], pb2[0:64, :], r=[prb2], w=[R_rlb])
            B.tt(oT[:, h, :], po[0:64, :], rlb[0:64, :], ALU.mult, r=[pro, R_rlb], w=R_ar)

    gab = [(B.sb(f"gab{i}", [128, 2, TB], BF16), Res()) for i in range(2)]
    mT = uT
    R_mT = R_u
    ga_v = ga_s.rearrange("(k p) t -> p k t", p=128)
    gs_v = gs_s.rearrange("(k p) t -> p k t", p=128)
    yTo_v = yT_o.rearrange("(k p) t -> p k t", p=128)

    def dense_block(blk, ncols=TB, groups=None, samp=False):
        t0 = 0 if samp else blk * TB
        if groups is None:
            groups = [(0, TB, 0)]
        gav = ga2_v if samp else ga_v
        gsv = gs2_v if samp else gs_v
        rga, rgs = (R_s2, R_s2) if samp else (R_ga, R_gs)
        for oc in range(8):
            wat, war = B.load_w(wa_d, 16, oc * 128, 128, pn=64)
            wst_, wsr = B.load_w(ws_d, 16, oc * 128, 128)
            pa_, pra_ = banks[oc % 2]
            ps2, prs2 = banks[2 + oc % 2]
            B.mms(pa_[:, 0:ncols], [(wat[:, h, :], oT[:, h, 0:ncols]) for h in range(16)], r=[war] + R_ar, w=[pra_])
            B.mms(ps2[:, 0:ncols], [(wst_[:, kc, :], ynT[:, kc, 0:ncols]) for kc in range(16)], r=[wsr, R_gT], w=[prs2])
            gt_, gr_ = gab[oc % 2]
            B.dma(gt_[:, 0, 0:ncols], gav[:, oc, t0:t0 + ncols], r=[rga], w=[gr_])
            B.dma(gt_[:, 1, 0:ncols], gsv[:, oc, t0:t0 + ncols], r=[rgs], w=[gr_])
            ta, tr = nextA()
            B.tt(ta[:, 0:ncols], pa_[:, 0:ncols], gt_[:, 0, 0:ncols], ALU.mult, r=[pra_, gr_], w=[tr])
            ta2, tr2 = nextA()
            B.tt(ta2[:, 0:ncols], ps2[:, 0:ncols], gt_[:, 1, 0:ncols], ALU.mult, r=[prs2, gr_], w=[tr2])
            B.tt(mT[:, oc, 0:ncols], ta[:, 0:ncols], ta2[:, 0:ncols], ALU.add, r=[tr, tr2], w=[R_mT])
        if samp:
            B.cp(xT[:, :, 0:ncols], xs1[:, :, 0:ncols], r=[R_xs1], w=[R_x])
        else:
            B.dma(xT[:, :, :], x1o_v[:, :, t0:t0 + TB], r=[R_x1], w=[R_x])
        for oc in range(8):
            wot, wor = B.load_w(wo_d, 8, oc * 128, 128)
            po_, pro_ = banks[4 + oc % 2]
            B.mms(po_[:, 0:ncols], [(wot[:, kc, :], mT[:, kc, 0:ncols]) for kc in range(8)], r=[wor, R_mT], w=[pro_])
            for (c0_, n_, m_) in groups:
                B.stt(xT[:, oc, c0_:c0_ + n_], po_[:, c0_:c0_ + n_], Gm[:, 1, oc, m_:m_ + 1], xT[:, oc, c0_:c0_ + n_],
                      ALU.mult, ALU.add, r=[pro_, R_AG], w=[R_x])
        rms_mod(xT, 2, groups, ncols)
        ffn(2, w1b_d, w3b_d, w2b_d, groups, ncols)
        pt, pr = banks[6]
        for kc in range(8):
            ta, tr = nextA()
            B.tt(ta[:, 0:ncols], xT[:, kc, 0:ncols], xT[:, kc, 0:ncols], ALU.mult, r=[R_x], w=[tr])
            P.add("pe", lambda e, ta=ta, kc=kc: e.matmul(pt[:, 0:ncols], ones_f[:], ta[:, 0:ncols], start=(kc == 0), stop=(kc == 7)),
                  r=[tr, R_const], w=[pr])
        B.act(sq[:, 0:ncols], pt[:, 0:ncols], AF.Sqrt, r=[pr, R_const], w=[R_sq], bias=epsc[:], scale=1.0 / D)
        P.add("dve", lambda e: e.reciprocal(out=rstd[:, 0:ncols], in_=sq[:, 0:ncols]), r=[R_sq], w=[R_rstd])
        for kc in range(8):
            B.stt(xT[:, kc, 0:ncols], xT[:, kc, 0:ncols], gsb[:, 3, kc:kc + 1], rstd[:, 0:ncols], ALU.mult, ALU.mult,
                  r=[R_x, R_g, R_rstd], w=[R_x])
        if samp:
            B.dma(ysT_o.rearrange("(k p) t -> p k t", p=128), xT[:, :, 0:ncols], r=[R_x], w=[], q="pool")
        else:
            B.dma(yTo_v[:, :, t0:t0 + TB], xT[:, :, :], r=[R_x], w=[], q="pool")

    ga2_v = ga2_s.rearrange("(k p) t -> p k t", p=128)
    gs2_v = gs2_s.rearrange("(k p) t -> p k t", p=128)
    BT2_v = BT2_s.rearrange("(g n) t -> n g t", n=128)
    CT2_v = CT2_s.rearrange("(g n) t -> n g t", n=128)

    FcA = B.sb("FcA", [128, NCORES, 256], F32)
    smAll = B.sb("smAll", [128, NCORES, 16], F32)
    delta = B.sb("delta", [128, NCORES + 1, 16], F32)
    R_FcA, R_smAll, R_delta = Res(), Res(), Res()
    xTall_v = xTall_d.rearrange("(k p) t -> p k t", p=128)
    NCTX = NCORES - 1
    P.add("dve", lambda e: e.memset(smAll[:], 0.0), w=[R_smAll])
    for j in range(NCTX):
        for h in range(H_A):
            for bb in range(T // TB):
                B.dma(kgA_v[j, h, 64:66, bb * TB:(bb + 1) * TB], onesrow[64:66, :], r=[R_const], w=[R_kgA], q="pool")
    P.add("pool", lambda e: e.memset(halo[:], 0.0), w=[R_halo])
    for j in range(NCTX):
        for blk in range(NBLK):
            c0_ = j * T + blk * TB
            B.dma(xT[:, :, :], xTall_v[:, :, c0_:c0_ + TB], w=[R_x])
            rms_mod(xT, 0, [(0, TB, 0)], TB)
            ffn(0, w1a_d, w3a_d, w2a_d, [(0, TB, 0)], TB)
            rms_mod(xT, 1, [(0, TB, 0)], TB)
            win_block(blk, [(0, TB, 0)], TB, ctxj=j)
        P.add("dve", lambda e: e.memset(carry[:], 0.0), w=[R_carry])
        for i in range(NT):
            pt, pr = banks[6]
            P.add("pe", lambda e, i=i: e.matmul(pt[:, 0:16], tri[:], logf[:, i, :], start=True, stop=True), r=[R_c3, R_lf], w=[pr])
            B.tt(FcA[:, j, i * 16:(i + 1) * 16], pt[:, 0:16], carry[:], ALU.add, r=[pr, R_carry], w=[R_FcA])
            P.add("pe", lambda e, i=i: e.matmul(pt[:, 16:32], ones_f[:], logf[:, i, :], start=True, stop=True), r=[R_const, R_lf], w=[pr])
            B.tt(carry[:], pt[:, 16:32], carry[:], ALU.add, r=[pr], w=[R_carry])
        B.cp(smAll[:, j, :], carry[:], r=[R_carry], w=[R_smAll])
        for gt in range(NT):
            ssd_tile(gt, want_y=False, maskj=j)
    P.add("dve", lambda e: e.memset(delta[:], 0.0), w=[R_delta])
    for j in range(NCORES - 1, -1, -1):
        B.stt(delta[:, j, :], smAll[:, j, 0:16], flg[:, j:j + 1], delta[:, j + 1, :], ALU.mult, ALU.add,
              r=[R_smAll, R_delta], w=[R_delta])
    run_own_p1()
    P.add("dve", lambda e: e.memset(carry[:], 0.0), w=[R_carry])
    for i in range(NT):
        pt, pr = banks[6]
        P.add("pe", lambda e, i=i: e.matmul(pt[:, 0:16], tri[:], logf[:, i, :], start=True, stop=True), r=[R_c3, R_lf], w=[pr])
        B.tt(Fc[:, i, :], pt[:, 0:16], carry[:], ALU.add, r=[pr, R_carry], w=[R_F])
        P.add("pe", lambda e, i=i: e.matmul(pt[:, 16:32], ones_f[:], logf[:, i, :], start=True, stop=True), r=[R_const, R_lf], w=[pr])
        B.tt(carry[:], pt[:, 16:32], carry[:], ALU.add, r=[pr], w=[R_carry])
    biasJ = [(B.sb(f"biasJ{i}", [128, NT], F32), Res()) for i in range(2)]
    rdj = B.sb("rdj", [128, NCORES, 16], F32)
    R_rdj = Res()
    for blk in range(NBLK):
        for lt in range(4):
            ssd_tile(blk * 4 + lt)
        attn_block(blk)
        dense_block(blk)
    B.dma(ssm_o[:, :, :], Hst[:].rearrange("p (h d) -> p h d", h=32), r=[R_H], w=[], q="pool")

    lfc = B.sb("lfc", [128, 8, 16], F32)
    Fs = B.sb("Fs", [128, 9, 16], F32)
    biasS = B.sb("biasS", [128, 9, 16], F32)
    R_lfc, R_Fs, R_bS = Res(), Res(), Res()

    def attn_sample(sq_):
        cs_ = slice(sq_ * 16, (sq_ + 1) * 16)
        B.dma(lfc[:], clf_d[sq_, :, :].rearrange("(t p) h -> p t h", p=128), w=[R_lfc])
        P.add("dve", lambda e: e.memset(carry[:], 0.0), w=[R_carry])
        P.add("dve", lambda e: e.memset(Fs[:], 0.0), w=[R_Fs])
        pt, pr = banks[6]
        for i in range(8):
            P.add("pe", lambda e, i=i: e.matmul(pt[:, 0:16], tri[:], lfc[:, i, :], start=True, stop=True), r=[R_c3, R_lfc], w=[pr])
            B.tt(Fs[:, i, :], pt[:, 0:16], carry[:], ALU.add, r=[pr, R_carry], w=[R_Fs])
            P.add("pe", lambda e, i=i: e.matmul(pt[:, 16:32], ones_f[:], lfc[:, i, :], start=True, stop=True), r=[R_const, R_lfc], w=[pr])
            B.tt(carry[:], pt[:, 16:32], carry[:], ALU.add, r=[pr], w=[R_carry])
        P.add("pe", lambda e: e.matmul(pt[0:16, 0:16], tri[0:16, 0:16], logf[0:16, NT + sq_, :], start=True, stop=True), r=[R_c3, R_lf], w=[pr])
        B.tt(Fs[0:16, 8, :], pt[0:16, 0:16], carry[0:16, :], ALU.add, r=[pr, R_carry], w=[R_Fs])
        pf, prf = banks[6]
        P.add("pe", lambda e: e.transpose(pf[0:16, 0:16], Fs[0:16, 8, :], ident_f[0:16, 0:16]), r=[R_Fs, R_const], w=[prf])
        B.cp(FT[:, 0:16], pf[0:16, 0:16], r=[prf], w=[R_FT])
        P.add("pe", lambda e: e.matmul(pf[:, 0:16], e0row[0:16, :], Fs[0:16, 8, :], start=True, stop=True), r=[R_c3, R_Fs], w=[prf])
        B.cp(rbc[:], pf[:, 0:16], r=[prf], w=[R_rbc])
        B.tt(biasS[:], rbc[:].unsqueeze(1).broadcast_to([128, 9, 16]), Fs[:], ALU.subtract, r=[R_rbc, R_Fs], w=[R_bS])
        for h in range(H_A):
            kt_, kr_ = Kt[h % 2]
            vt_, vr_ = Vt[h % 2]
            qa_, qr_ = Qa[h % 2]
            B.dma(yz[0:64, 0:1024], ckT_d[sq_, h, :, :], w=[R_yz])
            B.cp(kt_[0:64, 0:1024], yz[0:64, 0:1024], r=[R_yz], w=[kr_], eng="pool")
            P.add("pool", lambda e, kt_=kt_: e.memset(kt_[64:66, 0:1040], 1.0), w=[kr_])
            B.dma(kt_[0:64, 1024:1040], k2_s[h, 0:64, cs_], r=[R_s2], w=[kr_])
            ta, tr = nextA()
            B.dma(ta[:, :].rearrange("p (t d) -> p t d", t=8),
                  cv_d[sq_, :, :].rearrange("(t p) (h d) -> p t h d", p=128, h=16)[:, :, h, :], w=[tr])
            B.cp(vt_[:, 0:8, 0:64], ta[:, :].rearrange("p (t d) -> p t d", t=8), r=[tr], w=[vr_], eng="pool")
            P.add("pool", lambda e, vt_=vt_: e.memset(vt_[:, 0:9, 64:65], 1.0), w=[vr_])
            B.dma(vt_[0:16, 8, :], v2_s[h, sq_, :, :], r=[R_s2], w=[vr_])
            B.dma(qa_[0:64, 0:16], q2_s[h, 0:64, cs_], r=[R_s2], w=[qr_])
            pa, pra = banks[4]
            P.add("pe", lambda e, h=h: e.matmul(pa[0:66, 0:16], selt[:, h, :], FT[:, 0:16], start=True, stop=True), r=[R_c3, R_FT], w=[pra])
            ar0, rr0 = nextA()
            ar1, rr1 = nextA()
            ar2, rr2 = nextA()
            B.cp(ar0[64:66, 0:16], pa[64:66, 0:16], r=[pra], w=[rr0])
            B.ts(ar1[64:66, 0:16], ar0[64:66, 0:16], ar0[64:66, 0:1], None, ALU.subtract, r=[rr0], w=[rr1])
            B.cp(arb[64:66, 0:16], ar1[64:66, 0:16], r=[rr1], w=[R_arow])
            B.tt(ar0[64:66, 0:16], ar1[64:66, 0:16], arb[64:66, 0:16], ALU.subtract, r=[rr1, R_arow], w=[rr0])
            B.ts(ar2[64:66, 0:16], arb[64:66, 0:16], selc[64:66, 0:1], None, ALU.mult, r=[R_arow, R_c3], w=[rr2])
            B.stt(qa_[64:66, 0:16], ar0[64:66, 0:16], selc[64:66, 1:2], ar2[64:66, 0:16], ALU.mult, ALU.add, r=[rr0, rr2, R_c3], w=[qr_])
            po, pro = banks[2 + h % 2]
            for kt in range(9):
                Lk = 128 if kt < 8 else 16
                ps_, prs = banks[kt % 2]
                P.add("pe", lambda e, kt=kt, Lk=Lk, ps_=ps_, kt_=kt_, qa_=qa_: e.matmul(ps_[0:Lk, 0:16], kt_[:, kt * 128:kt * 128 + Lk], qa_[:, 0:16],
                                                                                       start=True, stop=True), r=[kr_, qr_], w=[prs])
                pti[0] += 1
                pT_, prp = Pt[pti[0] % 3]
                if kt == 8:
                    ta, tr = nextA()
                    B.tt(ta[0:16, 0:16], ps_[0:16, 0:16], mneg[0:16, 0:16], ALU.add, r=[prs, R_c3], w=[tr])
                    B.act(pT_[0:16, 0:16], ta[0:16, 0:16], AF.Exp, r=[tr, R_bS], w=[prp], bias=biasS[0:16, 8, h:h + 1], scale=1.0)
                else:
                    B.act(pT_[:, 0:16], ps_[:, 0:16], AF.Exp, r=[prs, R_bS], w=[prp], bias=biasS[:, kt, h:h + 1], scale=1.0)
                P.add("pe", lambda e, kt=kt, Lk=Lk, pT_=pT_, vt_=vt_, po=po: e.matmul(po[0:65, 0:16], vt_[0:Lk, kt, :], pT_[0:Lk, 0:16],
                                                                                    start=(kt == 0), stop=(kt == 8), skip_group_check=True),
                      r=[vr_, prp], w=[pro])
            rl, R_rl = nextA()
            rlb, R_rlb = nextA()
            P.add("dve", lambda e, po=po, rl=rl: e.reciprocal(out=rl[64:65, 0:16], in_=po[64:65, 0:16]), r=[pro], w=[R_rl])
            pb2, prb2 = banks[5]
            P.add("pe", lambda e, rl=rl: e.matmul(pb2[0:64, 0:16], ones_f[64:65, 0:64], rl[64:65, 0:16], start=True, stop=True), r=[R_const, R_rl], w=[prb2])
            B.cp(rlb[0:64, 0:16], pb2[0:64, 0:16], r=[prb2], w=[R_rlb])
            B.tt(oT[:, h, cs_], po[0:64, 0:16], rlb[0:64, 0:16], ALU.mult, r=[pro, R_rlb], w=R_ar)

    for sq_ in range(2):
        B.dma(Hst[:], ssmin_d[sq_, :, :], w=[R_H])
        ssd_tile(0, want_y=True, L=16, samp=sq_)
        B.dma(ssms_o[sq_, :, :], Hst[:], r=[R_H], w=[], q="pool")
    for sq_ in range(2):
        attn_sample(sq_)
    dense_block(0, ncols=32, groups=[(0, 16, 1), (16, 16, 2)], samp=True)


_CACHE = {}


def _r(w, kc):
    K, N = w.shape
    return np.ascontiguousarray(w.reshape(kc, K // kc, N).transpose(1, 0, 2))


def prep_inputs(inp, stage):
    f = np.float32
    xp = np.asarray(inp["x_prompt"], f)[0]
    maps = []
    shared = {}
    shared["w_ada_r"] = _r(np.asarray(inp["w_ada"], f)[0], 8)
    shared["b_ada_r"] = np.ascontiguousarray(np.asarray(inp["b_ada"], f)[0].reshape(72, 128).T)
    for nm, key in [("g_ffn1_r", "g_ffn1"), ("g_mix_r", "g_mix"), ("g_ffn2_r", "g_ffn2")]:
        shared[nm] = np.ascontiguousarray(np.asarray(inp[key], f)[0].reshape(8, 128).T)
    shared["g_final_r"] = np.ascontiguousarray(np.asarray(inp["g_final"], f).reshape(8, 128).T)
    shared["w1a"] = _r(np.asarray(inp["w1_ffn1"], f)[0], 8)
    shared["w3a"] = _r(np.asarray(inp["w3_ffn1"], f)[0], 8)
    shared["w2a"] = _r(np.asarray(inp["w2_ffn1"], f)[0], NH)
    shared["w1b"] = _r(np.asarray(inp["w1_ffn2"], f)[0], 8)
    shared["w3b"] = _r(np.asarray(inp["w3_ffn2"], f)[0], 8)
    shared["w2b"] = _r(np.asarray(inp["w2_ffn2"], f)[0], NH)
    shared["win_r"] = _r(np.asarray(inp["w_in"], f)[0], 8)
    shared["bf_bc"] = np.ascontiguousarray(np.broadcast_to(np.asarray(inp["b_f"], f)[0][None, :], (128, 16)))
    shared["convw_r"] = np.ascontiguousarray(np.asarray(inp["conv_w"], f)[0].reshape(4, 24, 128).transpose(2, 1, 0))
    shared["convb_r"] = np.ascontiguousarray(np.asarray(inp["conv_b"], f)[0].reshape(24, 128).T)
    shared["dtb_bc"] = np.ascontiguousarray(np.broadcast_to(np.asarray(inp["dt_bias"], f)[0][None, :], (128, 32)))
    shared["ident_in"] = np.eye(128, dtype=f)
    shared["alog_bc"] = np.ascontiguousarray(np.broadcast_to(np.asarray(inp["a_log"], f)[0][None, :], (128, 32)))
    shared["dskip_bc"] = np.ascontiguousarray(np.broadcast_to(np.asarray(inp["d_skip"], f)[0][None, :], (128, 32)))
    shared["gssd_r"] = np.ascontiguousarray(np.asarray(inp["g_ssd"], f)[0].reshape(16, 128).T)
    tri = np.triu(np.ones((128, 128), f))
    shared["tri_in"] = tri
    shared["maskneg_in"] = ((1.0 - tri) * -1e9).astype(f)
    e0 = np.zeros((128, 128), f); e0[0, :] = 1.0
    shared["e0row_in"] = e0
    sel = np.zeros((16, 16, 66), f)
    for h in range(16):
        sel[h, h, 64] = 1.0; sel[h, h, 65] = 1.0
    shared["sel_in"] = sel
    selc = np.zeros((128, 2), f); selc[64, 0] = 1.0; selc[65, 1] = 1.0
    shared["selc_in"] = selc
    shared["wa_r"] = np.ascontiguousarray(np.asarray(inp["w_a"], f)[0].reshape(16, 64, D).transpose(1, 0, 2))
    shared["ws_r"] = _r(np.asarray(inp["w_s"], f)[0], 16)
    shared["wout_r"] = _r(np.asarray(inp["w_out"], f)[0], 8)
    xpT = np.ascontiguousarray(xp.T)
    xsm = np.asarray(inp["x_sample"], f)
    cp = np.asarray(inp["c_prompt"], f)[0]
    cs = np.asarray(inp["c_sample"], f)
    for c in range(NCORES):
        m = dict(shared)
        m["xT"] = np.ascontiguousarray(xp[c * T:(c + 1) * T].T)
        m["xTall"] = xpT
        hal = xp[c * T - 3:c * T] if c > 0 else np.zeros((3, D), f)
        m["xsT"] = np.ascontiguousarray(np.concatenate([xsm[2 * c:2 * c + 2].reshape(32, D), hal], 0).T)
        fl = np.zeros((128, 24), f)
        for j in range(8):
            fl[:, j] = 1.0 if j < c else 0.0
            fl[:, 8 + j] = 0.0 if j < c else NEG
        fl[:, 16] = 1.0 if c > 0 else 0.0
        m["flags"] = fl
        sc = np.asarray(inp["state_conv"], f)[0, 2 * c:2 * c + 2]
        m["scv"] = np.ascontiguousarray(sc.transpose(2, 0, 1).reshape(24, 128, 2, 3).transpose(1, 0, 2, 3))
        ss = np.asarray(inp["state_ssm"], f)[0, 2 * c:2 * c + 2]
        m["ssmin"] = np.ascontiguousarray(ss.transpose(0, 3, 1, 2).reshape(2, 128, DIN))
        ck = np.asarray(inp["cache_k"], f)[0, 2 * c:2 * c + 2]
        m["ckT"] = np.ascontiguousarray(ck.transpose(0, 2, 3, 1))
        m["cvc"] = np.ascontiguousarray(np.asarray(inp["cache_v"], f)[0, 2 * c:2 * c + 2].reshape(2, 1024, D))
        m["clf"] = np.ascontiguousarray(np.asarray(inp["cache_logf"], f)[0, 2 * c:2 * c + 2])
        c3 = np.stack([cp, cs[2 * c], cs[2 * c + 1]], axis=1)
        m["cT"] = np.ascontiguousarray(c3.reshape(8, 128, 3).transpose(1, 0, 2))
        maps.append(m)
    return maps


def run(inp, stage=99, ncores=NCORES):
    if stage not in _CACHE:
        B = build(stage)
        _CACHE[stage] = B
    B = _CACHE[stage]
    maps = prep_inputs(inp, stage)
    maps = [{k: v for k, v in m.items() if k in B.ins} for m in maps]
    res = run_bass_kernel_spmd(B.nc, maps[:ncores], core_ids=list(range(ncores)))
    return res.results


def kernel(**inp):
    res = run(inp, stage=3)
    f = np.float32
    y_prompt = np.zeros((1, SEQ, D), f)
    y_sample = np.zeros((16, 16, D), f)
    k_prompt = np.zeros((1, 1, SEQ, H_A, HD), f)
    v_prompt = np.zeros((1, 1, SEQ, H_A, HD), f)
    logf_prompt = np.zeros((1, 1, SEQ, H_A), f)
    ssm_prompt = np.zeros((1, 1, NSSD, 64, NST), f)
    conv_prompt = np.zeros((1, 1, 3, CONVD), f)
    k_sample = np.zeros((1, 16, 16, H_A, HD), f)
    v_sample = np.zeros((1, 16, 16, H_A, HD), f)
    logf_sample = np.zeros((1, 16, 16, H_A), f)
    ssm_sample = np.zeros((1, 16, NSSD, 64, NST), f)
    conv_sample = np.zeros((1, 16, 3, CONVD), f)
    for c in range(NCORES):
        r = res[c]
        sl = slice(c * T, (c + 1) * T)
        y_prompt[0, sl] = np.asarray(r["yT_o"]).T
        k_prompt[0, 0, sl] = np.asarray(r["kT_o"]).T.reshape(T, H_A, HD)
        v_prompt[0, 0, sl] = np.asarray(r["v_o"]).reshape(T, H_A, HD)
        logf_prompt[0, 0, sl] = np.asarray(r["lf_o"])
        k_sample[0, 2 * c:2 * c + 2] = np.asarray(r["ksT_o"]).T.reshape(2, 16, H_A, HD)
        v_sample[0, 2 * c:2 * c + 2] = np.asarray(r["vs_o"]).reshape(2, 16, H_A, HD)
        logf_sample[0, 2 * c:2 * c + 2] = np.asarray(r["lfs_o"]).reshape(2, 16, H_A)
        y_sample[2 * c:2 * c + 2] = np.asarray(r["ysT_o"]).T.reshape(2, 16, D)
        ssm_sample[0, 2 * c:2 * c + 2] = np.asarray(r["ssms_o"]).reshape(2, 128, NSSD, 64).transpose(0, 2, 3, 1)
        cvs = np.asarray(r["cvs_o"])
        conv_sample[0, 2 * c] = cvs[:, 0:3].T
        conv_sample[0, 2 * c + 1] = cvs[:, 3:6].T
    conv_prompt[0, 0] = np.asarray(res[NCORES - 1]["cv_o"]).T
    ssm_prompt[0, 0] = np.asarray(res[NCORES - 1]["ssm_o"]).transpose(1, 2, 0)
    return (y_prompt, y_sample, k_prompt, v_prompt, logf_prompt, ssm_prompt, conv_prompt,
            k_sample, v_sample, logf_sample, ssm_sample, conv_sample)
```
